# Optimizing a Trainium2 kernel written in Bass

```python
import math
import jax, jax.numpy as jnp
from jax import lax
import numpy as np

D_MODEL = 1024
BATCH = 8
SEQ = 2048
DEPTH = 4
DEC_BATCH = 128
DEC_SEQ = 8
PAST_LEN = 16384
PAGE_SIZE = 128

W_GROUP = D_MODEL // 4
A_HEADS = 4
A_DK = W_GROUP // A_HEADS
A_DV = W_GROUP // A_HEADS
A_CONV = 4
A_CHUNK = 64
B_CH = W_GROUP
B_GROUPS = 4
B_CONV = 31
C_CH = W_GROUP
POOL_WINDOWS = (2, 4, 8, 16)
C_GROUP = C_CH // len(POOL_WINDOWS)
POOL_BUF = max(POOL_WINDOWS) - 1
D_CH = W_GROUP
D_BLOCKS = 4
D_BLOCK = D_CH // D_BLOCKS
D_CONV = 4
LRU_C = 8.0
A_QKV = A_HEADS * (2 * A_DK + A_DV)
OFF_ARAW = A_QKV
OFF_BETA = OFF_ARAW + A_HEADS
OFF_GATE = OFF_BETA + A_HEADS
OFF_B = OFF_GATE + A_HEADS * A_DV
OFF_C = OFF_B + 2 * B_CH
OFF_D = OFF_C + C_CH
IN_COLS = OFF_D + 2 * D_CH
MIX = A_HEADS * A_DV + B_CH + C_CH + D_CH
N_MEM = 256
X_HEADS = 4
X_HD = D_MODEL // X_HEADS
FFN = -(-8 * D_MODEL // (3 * 256)) * 256
EPS = 1e-6
F32 = jnp.float32

kernel_name = 'hymba_style_delta_conformer_pool_rglru_decoder_step'


def rmsnorm(x, g):
    xf = x.astype(F32)
    y = xf * lax.rsqrt(jnp.mean(xf * xf, axis=-1, keepdims=True) + EPS)
    return (y * g.astype(F32)).astype(x.dtype)


def l2norm(x):
    xf = x.astype(F32)
    return xf * lax.rsqrt(jnp.sum(xf * xf, axis=-1, keepdims=True) + EPS)


def causal_dwconv(buf, x, w):
    width = w.shape[0]
    xp = jnp.concatenate([buf.astype(x.dtype), x], axis=1)
    y = lax.conv_general_dilated(xp, w[:, None, :].astype(x.dtype), window_strides=(1,), padding='VALID',
                                 dimension_numbers=('NWC', 'WIO', 'NWC'), feature_group_count=x.shape[-1])
    return y, xp[:, xp.shape[1] - (width - 1):]


def gated_delta_chunked(q, k, v, g, beta, s0):
    Bn, T, H, DK = q.shape
    DV = v.shape[-1]
    n = -(-T // A_CHUNK)
    pad = n * A_CHUNK - T

    def chunk(a):
        a = jnp.pad(a.astype(F32), [(0, 0), (0, pad)] + [(0, 0)] * (a.ndim - 2))
        a = a.reshape((Bn, n, A_CHUNK) + a.shape[2:])
        return jnp.moveaxis(a, 3, 1)

    qc, kc, vc, gc, bc = (chunk(a) for a in (q, k, v, g, beta))
    gcum = jnp.cumsum(gc, axis=-1)
    idx = jnp.arange(A_CHUNK)
    causal = idx[:, None] >= idx[None, :]
    strict = idx[:, None] > idx[None, :]
    decay_mat = jnp.exp(jnp.where(causal, gcum[..., :, None] - gcum[..., None, :], -jnp.inf))
    kk = jnp.einsum('bhncd,bhnmd->bhncm', kc, kc)
    amat = jnp.where(strict, bc[..., None] * kk * decay_mat, 0.0)
    eye = jnp.eye(A_CHUNK, dtype=F32)
    rhs = jnp.concatenate([vc * bc[..., None], kc * (bc * jnp.exp(gcum))[..., None]], axis=-1)
    sol = lax.linalg.triangular_solve(eye + amat, rhs, left_side=True, lower=True, unit_diagonal=True)
    w_val, w_key = sol[..., :DV], sol[..., DV:]
    qk = jnp.einsum('bhncd,bhnmd->bhncm', qc, kc) * decay_mat

    def step(S, inp):
        qi, ki, wv, wk, qki, gi = inp
        u = wv - jnp.einsum('bhcd,bhde->bhce', wk, S)
        o = jnp.einsum('bhcd,bhde->bhce', qi * jnp.exp(gi)[..., None], S) + jnp.einsum('bhcm,bhme->bhce', qki, u)
        glast = gi[..., -1:]
        S = S * jnp.exp(glast)[..., None] + jnp.einsum('bhcd,bhce->bhde', ki * jnp.exp(glast - gi)[..., None], u)
        return S, o

    xs = tuple(jnp.moveaxis(a, 2, 0) for a in (qc, kc, w_val, w_key, qk, gcum))
    s_fin, o = lax.scan(step, s0.astype(F32), xs)
    o = jnp.moveaxis(o, 0, 2).reshape(Bn, H, n * A_CHUNK, DV)
    o = jnp.swapaxes(o, 1, 2)[:, :T]
    return o, s_fin.astype(s0.dtype)


def conformer_conv(u, buf, dw_w, dw_b, gn_g, gn_b, w_pw):
    glu = u[..., :B_CH] * jax.nn.sigmoid(u[..., B_CH:])
    y, new_buf = causal_dwconv(buf, glu, dw_w)
    y = (y + dw_b).astype(F32)
    yg = y.reshape(y.shape[:-1] + (B_GROUPS, B_CH // B_GROUPS))
    mu = jnp.mean(yg, axis=-1, keepdims=True)
    var = jnp.mean(jnp.square(yg - mu), axis=-1, keepdims=True)
    yn = ((yg - mu) * lax.rsqrt(var + EPS)).reshape(y.shape) * gn_g.astype(F32) + gn_b.astype(F32)
    return jax.nn.silu(yn).astype(u.dtype) @ w_pw, new_buf


def multiscale_pool(u, buf, pos0, w_pool, scale):
    T = u.shape[1]
    xp = jnp.concatenate([buf.astype(u.dtype), u], axis=1)
    cs = jnp.pad(jnp.cumsum(xp.astype(F32), axis=1), ((0, 0), (1, 0), (0, 0)))
    pos = pos0 + jnp.arange(T)
    outs = []
    for gi, w in enumerate(POOL_WINDOWS):
        sl = slice(gi * C_GROUP, (gi + 1) * C_GROUP)
        end = cs[:, POOL_BUF + 1:POOL_BUF + 1 + T, sl]
        start = cs[:, POOL_BUF + 1 - w:POOL_BUF + 1 - w + T, sl]
        cnt = jnp.minimum(pos + 1, w).astype(F32)[None, :, None]
        d = ((end - start) / cnt - u[..., sl].astype(F32)).astype(u.dtype)
        outs.append(d @ w_pool[gi])
    y = jnp.concatenate(outs, axis=-1) * scale
    return y, xp[:, xp.shape[1] - POOL_BUF:]


def rglru_block(u, buf, h0, conv_w, conv_b, w_rg, b_rg, w_ig, b_ig, lam):
    Bn, T, _ = u.shape
    gate = jax.nn.gelu(u[..., :D_CH].astype(F32))
    xr, new_buf = causal_dwconv(buf, u[..., D_CH:], conv_w)
    xr = (xr + conv_b).astype(F32)
    xb = xr.reshape(Bn, T, D_BLOCKS, D_BLOCK)
    r = jax.nn.sigmoid(jnp.einsum('btki,kij->btkj', xb, w_rg.astype(F32)).reshape(Bn, T, D_CH) + b_rg.astype(F32))
    i = jax.nn.sigmoid(jnp.einsum('btki,kij->btkj', xb, w_ig.astype(F32)).reshape(Bn, T, D_CH) + b_ig.astype(F32))
    log_a = -LRU_C * r * jax.nn.softplus(-lam.astype(F32))
    a = jnp.exp(log_a)
    b = jnp.sqrt(-jnp.expm1(2.0 * log_a)) * (i * xr)
    b = b.at[:, 0].add(a[:, 0] * h0.astype(F32))

    def combine(left, right):
        a_l, b_l = left
        a_r, b_r = right
        return a_l * a_r, a_r * b_l + b_r

    _, h = lax.associative_scan(combine, (a, b), axis=1)
    return (gate * h).astype(u.dtype), new_buf, h[:, -1].astype(h0.dtype)


def memory_kv(mem, g, w_kv):
    m = rmsnorm(mem, g) @ w_kv
    Bn = mem.shape[0]
    k = m[..., :X_HEADS * X_HD].reshape(Bn, -1, X_HEADS, X_HD)
    v = m[..., X_HEADS * X_HD:].reshape(Bn, -1, X_HEADS, X_HD)
    return k, v


def layer(x, mem_k, mem_v, state, pos0, p):
    dconv, dS, bconv, pbuf, lconv, lh = state
    Bn, T, _ = x.shape
    dt = x.dtype
    z = rmsnorm(x, p['norm_mix_pre']) @ p['w_in']
    y, dconv_new = causal_dwconv(dconv, z[..., :OFF_ARAW], p['conv_qkv'])
    y = jax.nn.silu(y)
    qa = l2norm(y[..., :A_HEADS * A_DK].reshape(Bn, T, A_HEADS, A_DK)) * (A_DK ** -0.5)
    ka = l2norm(y[..., A_HEADS * A_DK:2 * A_HEADS * A_DK].reshape(Bn, T, A_HEADS, A_DK))
    va = y[..., 2 * A_HEADS * A_DK:].reshape(Bn, T, A_HEADS, A_DV)
    g = -jnp.exp(p['a_log'].astype(F32)) * jax.nn.softplus(z[..., OFF_ARAW:OFF_BETA].astype(F32) + p['dt_bias'].astype(F32))
    beta = jax.nn.sigmoid(z[..., OFF_BETA:OFF_GATE].astype(F32))
    oa, dS_new = gated_delta_chunked(qa, ka, va, g, beta, dS)
    oa = rmsnorm(oa.astype(dt), p['onorm_a']) * jax.nn.silu(z[..., OFF_GATE:OFF_B].reshape(Bn, T, A_HEADS, A_DV))
    oa = oa.reshape(Bn, T, A_HEADS * A_DV)
    ob, bconv_new = conformer_conv(z[..., OFF_B:OFF_C], bconv, p['dw_b'], p['dwbias_b'], p['gn_gain_b'], p['gn_bias_b'], p['w_pw_b'])
    oc, pbuf_new = multiscale_pool(z[..., OFF_C:OFF_D], pbuf, pos0, p['w_pool'], p['scale_pool'])
    od, lconv_new, lh_new = rglru_block(z[..., OFF_D:IN_COLS], lconv, lh, p['conv_d'], p['conv_bias_d'],
                                        p['w_rg'], p['b_rg'], p['w_ig'], p['b_ig'], p['lam_d'])
    mix = jnp.concatenate([oa, ob, oc, od], axis=-1)
    x = x + rmsnorm(mix @ p['w_out'], p['norm_mix_post'])
    h = rmsnorm(x, p['norm_x_pre'])
    q = (h @ p['w_xq']).reshape(Bn, T, X_HEADS, X_HD)
    s = jnp.einsum('bthd,bmhd->bhtm', q, mem_k.astype(dt)).astype(F32) * (X_HD ** -0.5)
    pr = jax.nn.softmax(s, axis=-1).astype(dt)
    o = jnp.einsum('bhtm,bmhd->bthd', pr, mem_v.astype(dt)).reshape(Bn, T, X_HEADS * X_HD)
    x = x + rmsnorm(o @ p['w_xo'], p['norm_x_post'])
    h = rmsnorm(x, p['norm_ffn_pre'])
    gu = h @ p['w_ffn_in']
    ff = (jax.nn.silu(gu[..., :FFN]) * gu[..., FFN:]) @ p['w_ffn_out']
    x = x + rmsnorm(ff, p['norm_ffn_post'])
    return x, (dconv_new, dS_new, bconv_new, pbuf_new, lconv_new, lh_new)


def setup_inputs(seed: int = 0) -> dict:
    key = jax.random.key(seed)
    ks = iter(jax.random.split(key, 64))
    L = DEPTH

    def nrm(shape, s):
        return jax.random.normal(next(ks), shape, F32) * s

    def gain(n):
        return 1.0 + nrm((L, n), 0.02)

    inp = {}
    inp['x_prompt'] = nrm((BATCH, SEQ, D_MODEL), 1.0)
    inp['x_sample'] = nrm((DEC_BATCH, DEC_SEQ, D_MODEL), 1.0)
    inp['mem_prompt'] = nrm((BATCH, N_MEM, D_MODEL), 1.0)
    inp['state_delta'] = nrm((L, DEC_BATCH, A_HEADS, A_DK, A_DV), A_DK ** -0.5)
    inp['state_delta_conv'] = nrm((L, DEC_BATCH, A_CONV - 1, A_QKV), 1.0)
    inp['state_conf_conv'] = nrm((L, DEC_BATCH, B_CONV - 1, B_CH), 0.5)
    inp['state_pool'] = nrm((L, DEC_BATCH, POOL_BUF, C_CH), 1.0)
    inp['state_lru_conv'] = nrm((L, DEC_BATCH, D_CONV - 1, D_CH), 1.0)
    inp['state_lru_h'] = nrm((L, DEC_BATCH, D_CH), 0.5)
    inp['cache_mem_k'] = nrm((L, DEC_BATCH, N_MEM, X_HEADS, X_HD), 1.0)
    inp['cache_mem_v'] = nrm((L, DEC_BATCH, N_MEM, X_HEADS, X_HD), 1.0)
    inp['norm_mix_pre'] = gain(D_MODEL)
    inp['norm_mix_post'] = gain(D_MODEL)
    inp['w_in'] = nrm((L, D_MODEL, IN_COLS), D_MODEL ** -0.5)
    inp['conv_qkv'] = nrm((L, A_CONV, A_QKV), A_CONV ** -0.5)
    inp['a_log'] = jnp.log(jax.random.uniform(next(ks), (L, A_HEADS), F32, 1.0, 16.0))
    dtv = jnp.exp(jax.random.uniform(next(ks), (L, A_HEADS), F32, math.log(1e-3), math.log(1e-1)))
    inp['dt_bias'] = dtv + jnp.log(-jnp.expm1(-dtv))
    inp['onorm_a'] = gain(A_DV)
    inp['dw_b'] = nrm((L, B_CONV, B_CH), B_CONV ** -0.5)
    inp['dwbias_b'] = nrm((L, B_CH), 0.02)
    inp['gn_gain_b'] = gain(B_CH)
    inp['gn_bias_b'] = nrm((L, B_CH), 0.02)
    inp['w_pw_b'] = nrm((L, B_CH, B_CH), B_CH ** -0.5)
    inp['w_pool'] = nrm((L, len(POOL_WINDOWS), C_GROUP, C_GROUP), C_GROUP ** -0.5)
    inp['scale_pool'] = 1.0 + nrm((L, C_CH), 0.1)
    inp['conv_d'] = nrm((L, D_CONV, D_CH), D_CONV ** -0.5)
    inp['conv_bias_d'] = nrm((L, D_CH), 0.02)
    inp['w_rg'] = nrm((L, D_BLOCKS, D_BLOCK, D_BLOCK), D_BLOCK ** -0.5)
    inp['b_rg'] = nrm((L, D_CH), 0.02)
    inp['w_ig'] = nrm((L, D_BLOCKS, D_BLOCK, D_BLOCK), D_BLOCK ** -0.5)
    inp['b_ig'] = nrm((L, D_CH), 0.02)
    a0 = jax.random.uniform(next(ks), (L, D_CH), F32, 0.9, 0.999)
    inp['lam_d'] = jnp.log(a0) - jnp.log1p(-a0)
    inp['w_out'] = nrm((L, MIX, D_MODEL), MIX ** -0.5)
    inp['norm_x_pre'] = gain(D_MODEL)
    inp['norm_x_post'] = gain(D_MODEL)
    inp['norm_mem'] = gain(D_MODEL)
    inp['w_xq'] = nrm((L, D_MODEL, X_HEADS * X_HD), D_MODEL ** -0.5)
    inp['w_xkv'] = nrm((L, D_MODEL, 2 * X_HEADS * X_HD), D_MODEL ** -0.5)
    inp['w_xo'] = nrm((L, X_HEADS * X_HD, D_MODEL), (X_HEADS * X_HD) ** -0.5)
    inp['norm_ffn_pre'] = gain(D_MODEL)
    inp['norm_ffn_post'] = gain(D_MODEL)
    inp['w_ffn_in'] = nrm((L, D_MODEL, 2 * FFN), D_MODEL ** -0.5)
    inp['w_ffn_out'] = nrm((L, FFN, D_MODEL), FFN ** -0.5)
    return inp


def reference(x_prompt, x_sample, mem_prompt, state_delta, state_delta_conv, state_conf_conv, state_pool,
              state_lru_conv, state_lru_h, cache_mem_k, cache_mem_v, norm_mix_pre, norm_mix_post, w_in, conv_qkv,
              a_log, dt_bias, onorm_a, dw_b, dwbias_b, gn_gain_b, gn_bias_b, w_pw_b, w_pool, scale_pool, conv_d,
              conv_bias_d, w_rg, b_rg, w_ig, b_ig, lam_d, w_out, norm_x_pre, norm_x_post, norm_mem, w_xq, w_xkv,
              w_xo, norm_ffn_pre, norm_ffn_post, w_ffn_in, w_ffn_out):
    dt = x_prompt.dtype
    bp = x_prompt.shape[0]
    xp, xs = x_prompt, x_sample
    new_p = [[] for _ in range(6)]
    new_s = [[] for _ in range(6)]
    mem_k_list, mem_v_list = [], []
    for l in range(DEPTH):
        p = {'norm_mix_pre': norm_mix_pre[l], 'norm_mix_post': norm_mix_post[l], 'w_in': w_in[l],
             'conv_qkv': conv_qkv[l], 'a_log': a_log[l], 'dt_bias': dt_bias[l], 'onorm_a': onorm_a[l],
             'dw_b': dw_b[l], 'dwbias_b': dwbias_b[l], 'gn_gain_b': gn_gain_b[l], 'gn_bias_b': gn_bias_b[l],
             'w_pw_b': w_pw_b[l], 'w_pool': w_pool[l], 'scale_pool': scale_pool[l], 'conv_d': conv_d[l],
             'conv_bias_d': conv_bias_d[l], 'w_rg': w_rg[l], 'b_rg': b_rg[l], 'w_ig': w_ig[l], 'b_ig': b_ig[l],
             'lam_d': lam_d[l], 'w_out': w_out[l], 'norm_x_pre': norm_x_pre[l], 'norm_x_post': norm_x_post[l],
             'w_xq': w_xq[l], 'w_xo': w_xo[l], 'norm_ffn_pre': norm_ffn_pre[l],
             'norm_ffn_post': norm_ffn_post[l], 'w_ffn_in': w_ffn_in[l], 'w_ffn_out': w_ffn_out[l]}
        mk_p, mv_p = memory_kv(mem_prompt, norm_mem[l], w_xkv[l])
        zero_state = (jnp.zeros((bp, A_CONV - 1, A_QKV), dt),
                      jnp.zeros((bp, A_HEADS, A_DK, A_DV), dt),
                      jnp.zeros((bp, B_CONV - 1, B_CH), dt),
                      jnp.zeros((bp, POOL_BUF, C_CH), dt),
                      jnp.zeros((bp, D_CONV - 1, D_CH), dt),
                      jnp.zeros((bp, D_CH), dt))
        xp, st_p = layer(xp, mk_p, mv_p, zero_state, 0, p)
        st_in = (state_delta_conv[l], state_delta[l], state_conf_conv[l], state_pool[l],
                 state_lru_conv[l], state_lru_h[l])
        xs, st_s = layer(xs, cache_mem_k[l], cache_mem_v[l], st_in, PAST_LEN, p)
        for j in range(6):
            new_p[j].append(st_p[j])
            new_s[j].append(st_s[j])
        mem_k_list.append(mk_p)
        mem_v_list.append(mv_p)
    dconv_p, delta_p, conf_p, pool_p, lconv_p, lh_p = (jnp.stack(a) for a in new_p)
    dconv_s, delta_s, conf_s, pool_s, lconv_s, lh_s = (jnp.stack(a) for a in new_s)
    mem_k_p = jnp.stack(mem_k_list)
    mem_v_p = jnp.stack(mem_v_list)
    return (xp, xs, delta_p, delta_s, dconv_p, dconv_s, conf_p, conf_s, pool_p, pool_s,
            lconv_p, lconv_s, lh_p, lh_s, mem_k_p, mem_v_p)
```

```python
import numpy as np
from contextlib import ExitStack
import concourse.bass as bass
import concourse.mybir as mybir
from concourse.bass_utils import run_bass_kernel_spmd

F32 = mybir.dt.float32
BF16 = mybir.dt.bfloat16
I32 = mybir.dt.int32
AF = mybir.ActivationFunctionType
ALU = mybir.AluOpType
AX = mybir.AxisListType

EPS = 1e-6
NLAYER = 4
OFF_B, OFF_C, OFF_D = 1032, 1544, 1800
FFN = 2816


class Rec:
    __slots__ = ("w", "r", "wm", "dsem", "dcount")

    def __init__(self):
        self.w = None
        self.r = {}
        self.wm = {}
        self.dsem = None
        self.dcount = 0


class Buf:
    def __init__(self, t, name):
        self.t = t
        self.name = name
        self.whole = Rec()
        self.parts = {}

    def __getitem__(self, k):
        return self.t[k]

    def recs(self, key):
        if key is None:
            return [self.whole] + list(self.parts.values())
        if key not in self.parts:
            self.parts[key] = Rec()
        return [self.whole, self.parts[key]]

    def own(self, key):
        if key is None:
            return self.whole
        if key not in self.parts:
            self.parts[key] = Rec()
        return self.parts[key]

    def collapse(self):
        for rec in self.parts.values():
            for (sem, val) in rec.r.values():
                k = id(sem)
                if k not in self.whole.r or self.whole.r[k][1] < val:
                    self.whole.r[k] = (sem, val)
            wevs = list(rec.wm.values())
            if rec.w is not None:
                wevs.append(rec.w)
            for (sem, val) in wevs:
                k = id(sem)
                if k not in self.whole.wm or self.whole.wm[k][1] < val:
                    self.whole.wm[k] = (sem, val)
        self.parts = {}


class Eng:
    def __init__(self, name, h):
        self.name = name
        self.h = h
        self.sem = None
        self.count = 0
        self.seen = {}


class FW:
    def __init__(self, nc, stack):
        self.nc = nc
        self.stack = stack
        self.pe = Eng("pe", nc.tensor)
        self.act = Eng("act", nc.scalar)
        self.dve = Eng("dve", nc.vector)
        self.pool = Eng("pool", nc.gpsimd)
        self.sp = Eng("sp", nc.sync)
        self.engs = [self.pe, self.act, self.dve, self.pool, self.sp]
        for e in self.engs:
            e.sem = stack.enter_context(nc.semaphore("s_" + e.name))
        self.n_dma_sems = 0
        self.out_events = {}
        self.nbuf = 0
        self.ninst = 0
        self.dead = False
        self.hook = None
        self.stream_idx = {}
        self.yield_now = None

    def wait_until(self, cond):
        if cond():
            return
        if self.yield_now is None:
            raise RuntimeError("wait_until outside interleaved emission")
        n = 0
        while not cond():
            self.yield_now()
            n += 1
            if n > 10_000_000:
                raise RuntimeError("wait_until: never satisfied")

    def sbuf(self, shape, dtype, name=None):
        self.nbuf += 1
        name = name or f"b{self.nbuf}"
        t = self.stack.enter_context(self.nc.sbuf_tensor(name, list(shape), dtype))
        return Buf(t, name)

    def psum(self, shape, dtype, name=None):
        self.nbuf += 1
        name = name or f"p{self.nbuf}"
        t = self.stack.enter_context(self.nc.psum_tensor(name, list(shape), dtype))
        return Buf(t, name)

    def _dsem(self, rec):
        if rec.dsem is None:
            self.n_dma_sems += 1
            rec.dsem = self.stack.enter_context(self.nc.semaphore(f"d{self.n_dma_sems}"))
        return rec.dsem

    def _collect(self, reads, writes):
        need = {}

        def add(ev):
            if ev is None:
                return
            sem, val = ev
            k = id(sem)
            if k not in need or need[k][1] < val:
                need[k] = (sem, val)

        for (b, key) in reads:
            for rec in b.recs(key):
                add(rec.w)
                for ev in rec.wm.values():
                    add(ev)
        for (b, key) in writes:
            for rec in b.recs(key):
                add(rec.w)
                for ev in rec.wm.values():
                    add(ev)
                for ev in rec.r.values():
                    add(ev)
        return need

    def _emit_waits(self, eng, need):
        for k, (sem, val) in need.items():
            if eng is self.pe and sem is self.pe.sem:
                continue
            if eng.seen.get(k, 0) >= val:
                continue
            eng.seen[k] = val
            eng.h.wait_ge(sem, val)

    @staticmethod
    def _norm(lst):
        out = []
        for x in lst:
            if isinstance(x, Buf):
                out.append((x, None))
            else:
                out.append(x)
        return out

    def op(self, eng, fn, reads=(), writes=()):
        if self.dead:
            return None
        reads = self._norm(reads)
        writes = self._norm(writes)
        need = self._collect(reads, writes)
        self._emit_waits(eng, need)
        ins = fn(eng.h)
        self.ninst += 1
        eng.count += 1
        ins.then_inc(eng.sem, 1)
        ev = (eng.sem, eng.count)
        for (b, key) in reads:
            b.own(key).r[id(eng.sem)] = ev
        for (b, key) in writes:
            if key is None:
                b.parts = {}
            rec = b.own(key)
            rec.w = ev
            rec.r = {}
            if key is None:
                rec.wm = {}
        if self.hook is not None:
            self.hook()
        return ins

    def dma(self, q, fn, reads=(), writes=(), is_out=False):
        if self.dead:
            return None
        reads = self._norm(reads)
        writes = self._norm(writes)
        need = self._collect(reads, writes)
        self._emit_waits(q, need)
        ins = fn(q.h)
        self.ninst += 1
        if writes:
            b, key = writes[0]
        else:
            b, key = reads[0]
        rec = b.own(key)
        sem = self._dsem(rec)
        rec.dcount += 16
        ins.then_inc(sem, 16)
        ev = (sem, rec.dcount)
        for (b2, key2) in reads:
            b2.own(key2).r[id(sem)] = ev
        for (b2, key2) in writes:
            if key2 is None:
                b2.parts = {}
            r2 = b2.own(key2)
            if r2 is not rec and key2 is None:
                r2.dsem = r2.dsem
            r2.w = ev
            r2.r = {}
            if key2 is None:
                r2.wm = {}
        if is_out:
            self.out_events[id(sem)] = ev
        return ins

    def finish(self):
        for sem, val in self.out_events.values():
            self.sp.h.wait_ge(sem, val)
        for e in self.engs:
            if e is not self.sp and e.count > 0:
                self.sp.h.wait_ge(e.sem, e.count)


IN_SHAPES = dict(
    xp=[2048, 1024], xs=[128, 1024], mem=[256, 1024],
    st_delta=[4, 16, 4, 64, 64], st_dconv=[4, 48, 768], st_bconv=[4, 480, 256], st_pool=[4, 240, 256],
    st_lconv=[4, 48, 256], st_lh=[4, 16, 256], ck=[4, 16, 256, 1024], cv=[4, 16, 256, 1024],
    norm_mix_pre=[4, 1024], norm_mix_post=[4, 1024], w_in=[4, 1024, 2312], conv_qkv=[4, 4, 768], a_log=[4, 4],
    dt_bias=[4, 4], onorm_a=[4, 64], dw_b=[4, 31, 256], dwbias_b=[4, 256], gn_gain_b=[4, 256], gn_bias_b=[4, 256],
    w_pw_b=[4, 256, 256], w_pool=[4, 4, 64, 64], scale_pool=[4, 256], conv_d=[4, 4, 256], conv_bias_d=[4, 256],
    w_rg=[4, 4, 64, 64], b_rg=[4, 256], w_ig=[4, 4, 64, 64], b_ig=[4, 256], lam_d=[4, 256], w_out=[4, 1024, 1024],
    norm_x_pre=[4, 1024], norm_x_post=[4, 1024], norm_mem=[4, 1024], w_xq=[4, 1024, 1024], w_xkv=[4, 1024, 2048],
    w_xo=[4, 1024, 1024], norm_ffn_pre=[4, 1024], norm_ffn_post=[4, 1024], w_ffn_in=[4, 1024, 5632],
    w_ffn_out=[4, 2816, 1024])
OUT_SHAPES = dict(
    y_p=[2048, 1024], y_s=[128, 1024], delta_p=[4, 4, 64, 64], delta_s=[4, 16, 4, 64, 64], dconv_p=[4, 3, 768],
    dconv_s=[4, 48, 768], conf_p=[4, 30, 256], conf_s=[4, 480, 256], pool_p=[4, 15, 256], pool_s=[4, 240, 256],
    lconv_p=[4, 3, 256], lconv_s=[4, 48, 256], lh_p=[4, 1, 256], lh_s=[4, 16, 256], mk_p=[4, 256, 1024],
    mv_p=[4, 256, 1024])


class _Stop(Exception):
    pass


import threading


def run_streams(fw, fns, weights=None):
    n = len(fns)
    sems = [threading.Semaphore(0) for _ in range(n)]
    main_sem = threading.Semaphore(0)
    done = [False] * n
    errs = []
    idx = {}
    cnt = [0] * n
    weights = weights or [1] * n

    def hook(force=False):
        i = idx.get(threading.get_ident())
        if i is None:
            return
        cnt[i] += 1
        if cnt[i] < weights[i] and not force:
            return
        cnt[i] = 0
        j = i
        for d in range(1, n + 1):
            j = (i + d) % n
            if not done[j]:
                break
        if j == i:
            if force:
                raise RuntimeError("yield_now: no other live stream (emission-order deadlock)")
            return
        sems[j].release()
        sems[i].acquire()

    def runner(i):
        idx[threading.get_ident()] = i
        sems[i].acquire()
        try:
            fns[i]()
        except BaseException as e:
            errs.append(e)
        done[i] = True
        alive = [j for j in range(n) if not done[j]]
        if alive:
            sems[alive[0]].release()
        else:
            main_sem.release()

    ths = [threading.Thread(target=runner, args=(i,)) for i in range(n)]
    old = fw.hook
    fw.hook = hook
    fw.yield_now = lambda: hook(True)
    fw.stream_idx = idx
    for t in ths:
        t.start()
    sems[0].release()
    main_sem.acquire()
    for t in ths:
        t.join()
    fw.hook = old
    fw.yield_now = None
    fw.stream_idx = {}
    if errs:
        raise errs[0]


import os
STOP = float(os.environ.get("KSTOP", "99"))
INTERLEAVE = os.environ.get("KINTER", "1") == "1"


def _chk(fw, level):
    if STOP <= level:
        fw.dead = True


def build(NL=NLAYER, tts=None):
    nc = bass.Bass("TRN2", target_bir_lowering=False)
    D = {}
    for k, s in IN_SHAPES.items():
        D[k] = nc.dram_tensor(k, list(s), F32, kind="ExternalInput").ap()
    for k, s in OUT_SHAPES.items():
        D[k] = nc.dram_tensor(k, list(s), F32, kind="ExternalOutput").ap()
    if tts is None:
        tts = [("p", i) for i in range(4)] + [("s", 0)]

    with ExitStack() as st:
        fw = FW(nc, st)
        pe, act, dve, pool, sp = fw.pe, fw.act, fw.dve, fw.pool, fw.sp

        def V(fn, r=(), w=()):
            return fw.op(dve, fn, r, w)

        def A(fn, r=(), w=()):
            return fw.op(act, fn, r, w)

        def P(fn, r=(), w=()):
            return fw.op(pe, fn, r, w)

        def G(fn, r=(), w=()):
            return fw.op(pool, fn, r, w)

        X = fw.sbuf([128, 4, 1024], F32, "X")
        NW = 4
        WS = [fw.sbuf([128, 8, 1024], BF16, f"WS{i}") for i in range(NW)]
        XN = fw.sbuf([128, 8, 512], BF16, "XN")
        MIX = fw.sbuf([128, 8, 512], BF16, "MIX")
        A1 = fw.sbuf([128, 6200], F32, "A1")
        A2 = fw.sbuf([128, 6800], F32, "A2")
        A3 = fw.sbuf([128, 6144], F32, "A3")
        GBC = [fw.sbuf([128, 1024], F32, f"GBC{i}") for i in range(2)]
        STG = fw.sbuf([128, 1024], F32, "STG")
        SM = fw.sbuf([128, 256], F32, "SM")
        TST = fw.sbuf([128, 128], F32, "TST")
        TST2 = fw.sbuf([128, 128], F32, "TST2")
        PQ = [fw.psum([128, 4, 512], F32, f"PQ{i}") for i in range(2)]
        IDENT = fw.sbuf([128, 128], F32, "IDENT")
        ONES = fw.sbuf([128, 128], F32, "ONES")
        TRIU = fw.sbuf([128, 128], F32, "TRIU")
        NEGSL = fw.sbuf([128, 128], F32, "NEGSL")
        BONES = fw.sbuf([128, 128], F32, "BONES")
        INVC0 = fw.sbuf([128, 2, 16], F32, "INVC0")
        SSB = fw.sbuf([64, 4, 64], F32, "SSB")
        GPRE = fw.sbuf([128, NLAYER, 4, 8], F32, "GPRE")
        CWA = fw.sbuf([128, NLAYER, 6, 4], F32, "CWA")
        DWB = fw.sbuf([128, NLAYER, 2, 31], F32, "DWB")
        CWD = fw.sbuf([128, NLAYER, 2, 4], F32, "CWD")
        VEC = fw.sbuf([128, NLAYER, 8, 2], F32, "VEC")
        NSP = fw.sbuf([128, NLAYER, 2], F32, "NSP")
        WPW = fw.sbuf([128, NLAYER, 2, 256], BF16, "WPW")
        WBD = fw.sbuf([128, NLAYER, 3, 2, 128], BF16, "WBD")
        TOKC = fw.sbuf([128, NLAYER, 72], F32, "TOKC")
        TA = fw.sbuf([128, NLAYER, 6, 3], F32, "TA")
        TB = fw.sbuf([128, NLAYER, 2, 30], F32, "TB")
        TC = fw.sbuf([128, NLAYER, 2, 15], F32, "TC")
        TD = fw.sbuf([128, NLAYER, 2, 3], F32, "TD")
        HST = fw.sbuf([128, NLAYER, 2], F32, "HST")
        SST = fw.sbuf([64, NLAYER, 4, 64], F32, "SST")

        _pb = [0, 0, 0]

        def bank():
            si = fw.stream_idx.get(threading.get_ident())
            if si is None:
                i = _pb[0] % 8
                _pb[0] += 1
                return PQ[i // 4], i % 4
            i = _pb[1 + si] % 4
            _pb[1 + si] += 1
            return PQ[si], i

        _pq = [0]

        def quad():
            si = fw.stream_idx.get(threading.get_ident())
            if si is not None:
                return PQ[si]
            i = _pq[0] % 2
            _pq[0] += 1
            return PQ[i]

        _ws = [0]

        def wslot():
            i = _ws[0] % NW
            _ws[0] += 1
            return WS[i]

        def load_w(src_ap, ncols, col0=0, slot=None, rows=None):
            if slot is None:
                slot = wslot()
            nk = src_ap.shape[0] // 128
            fw.dma(pool, lambda e: e.dma_start(out=slot[:, 0:nk, col0:col0 + ncols],
                                               in_=src_ap.rearrange("(k p) n -> p k n", p=128)), writes=[slot])
            return slot

        G(lambda e: e.memset(ONES[:, :], 1.0), w=[ONES])
        G(lambda e: e.affine_select(out=IDENT[:, :], in_=ONES[:, :], pattern=[[-1, 128]], compare_op=ALU.is_equal,
                                    fill=0.0, base=0, channel_multiplier=1), r=[ONES], w=[IDENT])
        G(lambda e: e.affine_select(out=TRIU[:, :], in_=ONES[:, :], pattern=[[1, 128]], compare_op=ALU.is_ge,
                                    fill=0.0, base=0, channel_multiplier=-1), r=[ONES], w=[TRIU])
        G(lambda e: e.memset(BONES[:, :], -1.0), w=[BONES])
        G(lambda e: e.affine_select(out=NEGSL[:, :], in_=BONES[:, :], pattern=[[-1, 128]], compare_op=ALU.is_ge,
                                    fill=0.0, base=-1, channel_multiplier=1), r=[BONES], w=[NEGSL])
        G(lambda e: e.memset(BONES[:, :], 0.0), w=[BONES])
        G(lambda e: e.memset(BONES[0:64, 0:64], 1.0), w=[BONES])
        G(lambda e: e.memset(BONES[64:128, 64:128], 1.0), w=[BONES])
        for b_ in (TA, TB, TC, TD, HST, SST, WBD, SM):
            G(lambda e, b_=b_: e.memset(b_.t[:].rearrange(" ".join(["p"] + [f"a{i}" for i in range(len(b_.t.shape) - 1)]) + " -> p (" + " ".join(
                [f"a{i}" for i in range(len(b_.t.shape) - 1)]) + ")"), 0.0), w=[b_])
        WIN = [[2, 4], [8, 16]]
        IOT = A3
        G(lambda e: e.iota(IOT.t[:, 0:16].bitcast(I32), pattern=[[1, 16]], base=1, channel_multiplier=0), w=[IOT])
        V(lambda e: e.tensor_copy(out=IOT.t[:, 16:32], in_=IOT.t[:, 0:16].bitcast(I32)), r=[IOT], w=[IOT])
        for cc in range(2):
            for hf in range(2):
                ps = slice(64 * hf, 64 * hf + 64)
                wv = float(WIN[cc][hf])
                V(lambda e, ps=ps, cc=cc, wv=wv: e.tensor_scalar(out=INVC0[ps, cc, :], in0=IOT.t[ps, 16:32], scalar1=wv,
                                                                 scalar2=None, op0=ALU.min), r=[IOT], w=[INVC0])
        V(lambda e: e.reciprocal(out=INVC0[:, :, :], in_=INVC0[:, :, :]), r=[INVC0], w=[INVC0])

        def sdma(out_ap, in_ap, wbuf, q=None):
            fw.dma(q or sp, lambda e: e.dma_start(out=out_ap, in_=in_ap, allow_slow_non_contiguous=True), writes=[wbuf])

        for l in range(NL):
            for i, nm in enumerate(["norm_mix_pre", "norm_x_pre", "norm_ffn_pre", "norm_mem"]):
                sdma(GPRE[:, l, i, :], D[nm][l].rearrange("(c p) -> p c", p=128), GPRE)
            for j in range(4):
                sdma(CWA[:, l, :, j], D["conv_qkv"][l, j].rearrange("(c p) -> p c", p=128), CWA)
                sdma(CWD[:, l, :, j], D["conv_d"][l, j].rearrange("(c p) -> p c", p=128), CWD)
            for cc in range(2):
                sdma(DWB[:, l, cc, :], D["dw_b"][l, :, cc * 128:(cc + 1) * 128].rearrange("j p -> p j"), DWB)
            for i, nm in enumerate(["dwbias_b", "gn_gain_b", "gn_bias_b", "scale_pool", "conv_bias_d", "b_rg", "b_ig", "lam_d"]):
                sdma(VEC[:, l, i, :], D[nm][l].rearrange("(c p) -> p c", p=128), VEC)
            sdma(WPW[:, l, :, :], D["w_pw_b"][l].rearrange("(c p) n -> p c n", p=128), WPW, q=pool)
            for i, nm in enumerate(["w_pool", "w_rg", "w_ig"]):
                for gi in range(4):
                    hf, cc = gi % 2, gi // 2
                    sdma(WBD[64 * hf:64 * hf + 64, l, i, cc, 64 * hf:64 * hf + 64], D[nm][l, gi], WBD, q=pool)
            sdma(TOKC[:, l, 0:4], D["dt_bias"][l:l + 1, :].broadcast_to([128, 4]), TOKC)
            sdma(TOKC[:, l, 4:8], D["a_log"][l:l + 1, :].broadcast_to([128, 4]), TOKC)
            sdma(TOKC[:, l, 8:72], D["onorm_a"][l:l + 1, :].broadcast_to([128, 64]), TOKC)
        for l in range(NL):
            A(lambda e, l=l: e.activation(out=TOKC[:, l, 4:8], in_=TOKC[:, l, 4:8], func=AF.Exp), r=[TOKC], w=[TOKC])
            V(lambda e, l=l: e.tensor_scalar(out=TOKC[:, l, 4:8], in0=TOKC[:, l, 4:8], scalar1=-1.0, scalar2=None, op0=ALU.mult),
              r=[TOKC], w=[TOKC])
            A(lambda e, l=l: e.activation(out=NSP[:, l, :], in_=VEC[:, l, 7, :], func=AF.Exp, scale=-1.0), r=[VEC], w=[NSP])
            V(lambda e, l=l: e.tensor_scalar(out=NSP[:, l, :], in0=NSP[:, l, :], scalar1=1.0, scalar2=None, op0=ALU.add), r=[NSP], w=[NSP])
            A(lambda e, l=l: e.activation(out=NSP[:, l, :], in_=NSP[:, l, :], func=AF.Ln), r=[NSP], w=[NSP])
            V(lambda e, l=l: e.tensor_scalar(out=NSP[:, l, :], in0=NSP[:, l, :], scalar1=-8.0, scalar2=None, op0=ALU.mult), r=[NSP], w=[NSP])

        _chk(fw, 1)

        def rstd_from_ss(ss_ap, n, out_ap, bufs_r, buf_w):
            V(lambda e: e.tensor_scalar(out=out_ap, in0=ss_ap, scalar1=1.0 / n, scalar2=EPS, op0=ALU.mult, op1=ALU.add), r=bufs_r, w=[buf_w])
            A(lambda e: e.activation(out=out_ap, in_=out_ap, func=AF.Sqrt), r=[buf_w], w=[buf_w])
            V(lambda e: e.reciprocal(out=out_ap, in_=out_ap), r=[buf_w], w=[buf_w])

        def tr(out_ap, in_ap, np_, rbufs, wbuf):
            P(lambda e: e.transpose(out=out_ap, in_=in_ap, identity=IDENT[0:np_, 0:np_]), r=list(rbufs) + [IDENT], w=[wbuf])

        def prenorm(src_tm, NB, gi, l, dst=XN):
            for tb in range(NB):
                junk = A2
                A(lambda e, tb=tb: e.activation(out=junk.t[:, 0:1024], in_=src_tm(tb), func=AF.Square, accum_out=SM[:, tb:tb + 1]),
                  r=[X], w=[(junk, "junk"), (SM, "ss")])
                rstd_from_ss(SM[:, tb:tb + 1], 1024.0, SM[:, 8 + tb:9 + tb], [(SM, "ss")], (SM, "rs"))
                V(lambda e, tb=tb: e.tensor_scalar(out=junk.t[:, 1024:2048], in0=src_tm(tb), scalar1=SM[:, 8 + tb:9 + tb], scalar2=None,
                                                   op0=ALU.mult), r=[X, (SM, "rs")], w=[(junk, "xs")])
                pq = quad()
                for c in range(8):
                    tr(pq[:, c // 4, (c % 4) * 128:(c % 4) * 128 + 128], junk.t[:, 1024 + c * 128:1024 + (c + 1) * 128], 128, [(junk, "xs")], pq)
                src = pq.t[:, 0:2, :].rearrange("p a (b t) -> p (a b) t", b=4)
                V(lambda e, tb=tb, src=src: e.tensor_tensor(out=dst[:, :, tb * 128:(tb + 1) * 128], in0=src,
                                                            in1=GPRE[:, l, gi, :].unsqueeze(2).broadcast_to([128, 8, 128]), op=ALU.mult),
                  r=[pq, GPRE], w=[(dst, tb)])
            A2.collapse()

        _gb = [0]

        def out_proj(src_fm, slots, nk, gain_name, l, NB, src_key=None):
            A2.collapse()
            _gb[0] += 1
            gb = GBC[_gb[0] % 2]
            fw.dma(sp, lambda e: e.dma_start(out=gb[:, :], in_=D[gain_name][l:l + 1, :].broadcast_to([128, 1024])), writes=[gb])
            for tb in range(NB):
                pq = quad()
                for n in range(2):
                    for k in range(nk):
                        P(lambda e, n=n, k=k, tb=tb: e.matmul(pq[:, n, :], lhsT=src_fm[:, k, tb * 128:(tb + 1) * 128],
                                                             rhs=slots[k // 8][:, k % 8, n * 512:(n + 1) * 512], start=(k == 0), stop=(k == nk - 1)),
                          r=[(src_fm, src_key) if src_key is None else (src_fm, tb), slots[k // 8]], w=[pq])
                for n in range(2):
                    A(lambda e, n=n: e.activation(out=A2.t[:, 1024 + n * 512:1024 + (n + 1) * 512], in_=pq[:, n, :], func=AF.Copy), r=[pq], w=[(A2, "ycp")])
                    A(lambda e, n=n: e.activation(out=A2.t[:, 0:512], in_=A2.t[:, 1024 + n * 512:1024 + (n + 1) * 512], func=AF.Square, accum_out=SM[:, 16 + n:17 + n]),
                      r=[(A2, "ycp")], w=[(A2, "junk"), (SM, "ss2")])
                V(lambda e: e.tensor_tensor(out=SM[:, 18:19], in0=SM[:, 16:17], in1=SM[:, 17:18], op=ALU.add), r=[(SM, "ss2")], w=[(SM, "ss3")])
                rstd_from_ss(SM[:, 18:19], 1024.0, SM[:, 19:20], [(SM, "ss3")], (SM, "rs3"))
                for n in range(2):
                    V(lambda e, n=n: e.scalar_tensor_tensor(out=A2.t[:, 2048 + n * 512:2048 + (n + 1) * 512], in0=A2.t[:, 1024 + n * 512:1024 + (n + 1) * 512], scalar=SM[:, 19:20],
                                                            in1=gb[:, n * 512:(n + 1) * 512], op0=ALU.mult, op1=ALU.mult),
                      r=[(A2, "ycp"), (SM, "rs3"), gb], w=[(A2, "yn")])
                V(lambda e, tb=tb: e.tensor_tensor(out=X[:, tb, :], in0=X[:, tb, :], in1=A2.t[:, 2048:3072], op=ALU.add), r=[X, (A2, "yn")], w=[X])

        def _stg():
            si = fw.stream_idx.get(threading.get_ident())
            if si == 1:
                return 768, (STG, "b"), TST2
            if si == 0:
                return 0, (STG, "a"), TST
            return 0, (STG, None), TST

        def load_tm_to_fm(dram_ap, R, ncc, dst_fn, dst_bufs, sg=None):
            c0, sk, _ = _stg()
            fw.dma(sp, lambda e: e.dma_start(out=STG[0:R, c0:c0 + ncc * 128], in_=dram_ap), writes=[sk])
            for cc in range(ncc):
                pb, bk = bank()
                tr(pb[:, bk, 0:R], STG[0:R, c0 + cc * 128:c0 + (cc + 1) * 128], R, [sk], (pb, bk))
                srcp = pb[:, bk, 0:R] if sg is None else pb[:, bk, 0:R].rearrange("p (s w) -> p s w", s=sg)
                A(lambda e, cc=cc, srcp=srcp: e.activation(out=dst_fn(cc), in_=srcp, func=AF.Copy), r=[(pb, bk)], w=dst_bufs)

        def store_fm_to_tm(src_fn, src_bufs, R, ncc, dram_ap, sg=None):
            c0, sk, tst = _stg()
            for cc in range(ncc):
                pb, bk = bank()
                if sg is None:
                    tr(pb[0:R, bk, 0:128], src_fn(cc), 128, src_bufs, (pb, bk))
                else:
                    V(lambda e, cc=cc: e.tensor_copy(out=tst[:, 0:R].rearrange("p (s w) -> p s w", s=sg), in_=src_fn(cc)), r=src_bufs, w=[tst])
                    tr(pb[0:R, bk, 0:128], tst[:, 0:R], 128, [tst], (pb, bk))
                A(lambda e, cc=cc, pb=pb, bk=bk: e.activation(out=STG[0:R, c0 + cc * 128:c0 + (cc + 1) * 128], in_=pb[0:R, bk, 0:128], func=AF.Copy),
                  r=[(pb, bk)], w=[sk])
            fw.dma(sp, lambda e: e.dma_start(out=dram_ap, in_=STG[0:R, c0:c0 + ncc * 128]), reads=[sk], is_out=True)

        def conv_fm(xpv, ncc, T, W, wfn, bfn, outv, rbufs, wbuf):
            for cc in range(ncc):
                V(lambda e, cc=cc: e.tensor_scalar(out=outv(cc), in0=xpv(cc, 0, T), scalar1=wfn(cc, 0), scalar2=(bfn(cc) if bfn else None),
                                                   op0=ALU.mult, op1=(ALU.add if bfn else ALU.bypass)), r=rbufs, w=[wbuf])
                for j in range(1, W):
                    V(lambda e, cc=cc, j=j: e.scalar_tensor_tensor(out=outv(cc), in0=xpv(cc, j, T), scalar=wfn(cc, j), in1=outv(cc),
                                                                  op0=ALU.mult, op1=ALU.add), r=rbufs, w=[wbuf])

        for l in range(NL):
            fw.dma(sp, lambda e: e.dma_start(out=X[:, 0:2, :], in_=D["mem"].rearrange("(tb p) d -> p tb d", p=128)), writes=[X])
            prenorm(lambda tb: X[:, tb, :], 2, 3, l)
            slots = [load_w(D["w_xkv"][l][:, 0:1024], 1024), load_w(D["w_xkv"][l][:, 1024:2048], 1024)]
            for kv in range(2):
                for tb in range(2):
                    pq = quad()
                    for n in range(2):
                        for k in range(8):
                            P(lambda e, n=n, k=k, tb=tb, kv=kv: e.matmul(pq[:, n, :], lhsT=XN[:, k, tb * 128:(tb + 1) * 128],
                                                                        rhs=slots[kv][:, k, n * 512:(n + 1) * 512], start=(k == 0), stop=(k == 7)),
                              r=[(XN, tb), slots[kv]], w=[pq])
                    A(lambda e, pq=pq: e.activation(out=STG[:, :], in_=pq.t[:, 0:2, :].rearrange("p a b -> p (a b)"), func=AF.Copy), r=[pq], w=[STG])
                    dst = D["mk_p" if kv == 0 else "mv_p"][l, tb * 128:(tb + 1) * 128, :]
                    fw.dma(sp, lambda e, dst=dst: e.dma_start(out=dst, in_=STG[:, :]), reads=[STG], is_out=True)
        for sem, val in list(fw.out_events.values()):
            if not fw.dead:
                sp.h.wait_ge(sem, val)
                pool.h.wait_ge(sem, val)
        _chk(fw, 2)

        for (kind, ti) in tts:
            if kind == "p":
                ntok, NB, nseq, T = 512, 4, 1, 512
                xsrc = D["xp"][ti * 512:(ti + 1) * 512, :]
                ydst = D["y_p"][ti * 512:(ti + 1) * 512, :]
            else:
                ntok, NB, nseq, T = 128, 1, 16, 8
                xsrc = D["xs"]
                ydst = D["y_s"]
            first = (kind == "p" and ti == 0)
            last_p = (kind == "p" and ti == 3)
            fw.dma(sp, lambda e: e.dma_start(out=X[:, 0:NB, :], in_=xsrc.rearrange("(tb p) d -> p tb d", p=128)), writes=[X])

            for l in range(NL):
                prenorm(lambda tb: X[:, tb, :], NB, 0, l)
                w0 = load_w(D["w_in"][l][:, 0:768], 768)
                w1 = load_w(D["w_in"][l][:, 768:1544], 776)
                w2 = load_w(D["w_in"][l][:, 1544:2312], 768)
                wo = load_w(D["w_out"][l], 1024)
                for a_ in (A1, A2, A3):
                    a_.collapse()
                _chk(fw, 2.2)

                def xpview(arena, off, ncc, W):
                    sz = ncc * nseq * (W - 1 + T)
                    return arena.t[:, off:off + sz].rearrange("p (c s w) -> p c s w", c=ncc, s=nseq)

                def proj_fm(slot, col, M=128):
                    pb, bk = bank()
                    for k in range(8):
                        P(lambda e, k=k: e.matmul(pb[0:M, bk, 0:ntok], lhsT=slot[:, k, col:col + M], rhs=XN[:, k, 0:ntok], start=(k == 0), stop=(k == 7)),
                          r=[XN, slot], w=[(pb, bk)])
                    return pb, bk

                def ps3(pb, bk, M=128):
                    return pb[0:M, bk, 0:ntok].rearrange("p (s t) -> p s t", s=nseq)

                def tails_in(xv, TT_, st_name, W, ncc, key):
                    if kind == "p":
                        V(lambda e: e.tensor_copy(out=xv[:, :, 0, 0:W - 1], in_=TT_[:, l, :, :]), r=[TT_], w=[key])
                    else:
                        R_all = 16 * (W - 1)
                        ng = 1 if R_all <= 128 else R_all // 120
                        sg = 16 // ng
                        for g in range(ng):
                            R = sg * (W - 1)
                            load_tm_to_fm(D[st_name][l, g * R:(g + 1) * R, :], R, ncc,
                                          lambda cc, g=g: xv[:, cc, g * sg:(g + 1) * sg, 0:W - 1], [key], sg=sg)

                def tails_out(xv, TT_, out_p, out_s, W, ncc, key):
                    if kind == "p":
                        V(lambda e: e.tensor_copy(out=TT_[:, l, :, :], in_=xv[:, :, 0, T:T + W - 1]), r=[key], w=[TT_])
                        if last_p:
                            store_fm_to_tm(lambda cc: xv[:, cc, 0, T:T + W - 1], [key], W - 1, ncc, D[out_p][l])
                    else:
                        R_all = 16 * (W - 1)
                        ng = 1 if R_all <= 128 else R_all // 120
                        sg = 16 // ng
                        for g in range(ng):
                            R = sg * (W - 1)
                            store_fm_to_tm(lambda cc, g=g: xv[:, cc, g * sg:(g + 1) * sg, T:T + W - 1], [key], R, ncc,
                                           D[out_s][l, g * R:(g + 1) * R, :], sg=sg)

                C = 64 if kind == "p" else 8
                LV = 6 if kind == "p" else 3
                nbatch = ntok // (2 * C)
                flags = {}
                turn = [0]
                YAf_g = A1.t[:, 3100:3100 + 6 * ntok].rearrange("p (c n) -> p c n", c=6)
                kYA_g = (A1, "YA")

                def delta_batch(bt, AR, smo, sfx, SSv, SSk):
                    YAf, kYA = YAf_g, kYA_g
                    AR.collapse()
                    t0 = bt * 2 * C
                    tcol = [t0 + ci * C for ci in range(2)]
                    QKV = AR.t[0:C, 0:1536].rearrange("p (a n) -> p a n", a=2)
                    kQKV = (AR, "QKV")
                    pq = quad()
                    for ci in range(2):
                        for cc in range(6):
                            col = cc * 128
                            tr(pq[0:C, 2 * ci + col // 512, col % 512:col % 512 + 128], YAf[:, cc, tcol[ci]:tcol[ci] + C], 128, [kYA], pq)
                    src = pq.t[0:C, :, :].rearrange("p (a b) n -> p a (b n)", a=2)[:, :, 0:768]
                    A(lambda e, src=src: e.activation(out=QKV, in_=src, func=AF.Copy), r=[pq], w=[kQKV])
                    SQ = AR.t[0:C, 1536:2560].rearrange("p (g d) -> p g d", d=64)
                    QK3 = AR.t[0:C, 0:1536].rearrange("p (a n) -> p a n", a=2)[:, :, 0:512].rearrange("p a (g d) -> p a g d", d=64)
                    SQ4 = AR.t[0:C, 1536:2560].rearrange("p (a g d) -> p a g d", a=2, d=64)
                    V(lambda e: e.tensor_tensor(out=SQ4, in0=QK3, in1=QK3, op=ALU.mult), r=[kQKV], w=[(AR, "SQ")])
                    V(lambda e: e.tensor_reduce(out=SM[0:C, smo + 32:smo + 48], in_=SQ, axis=AX.X, op=ALU.add), r=[(AR, "SQ")], w=[(SM, "l2" + sfx)])
                    V(lambda e: e.tensor_scalar(out=SM[0:C, smo + 32:smo + 48], in0=SM[0:C, smo + 32:smo + 48], scalar1=EPS, scalar2=None, op0=ALU.add), r=[(SM, "l2" + sfx)], w=[(SM, "l2" + sfx)])
                    A(lambda e: e.activation(out=SM[0:C, smo + 32:smo + 48], in_=SM[0:C, smo + 32:smo + 48], func=AF.Sqrt), r=[(SM, "l2" + sfx)], w=[(SM, "l2" + sfx)])
                    V(lambda e: e.reciprocal(out=SM[0:C, smo + 32:smo + 48], in_=SM[0:C, smo + 32:smo + 48]), r=[(SM, "l2" + sfx)], w=[(SM, "l2" + sfx)])
                    l2v = SM[0:C, smo + 32:smo + 48].rearrange("p (a g) -> p a g", a=2)
                    V(lambda e: e.tensor_scalar(out=l2v[:, :, 0:4], in0=l2v[:, :, 0:4], scalar1=0.125, scalar2=None, op0=ALU.mult), r=[(SM, "l2" + sfx)], w=[(SM, "l2" + sfx)])
                    V(lambda e: e.tensor_tensor(out=QK3, in0=QK3, in1=l2v.unsqueeze(3).broadcast_to([C, 2, 8, 64]), op=ALU.mult), r=[kQKV, (SM, "l2" + sfx)], w=[kQKV])
                    QKF = AR.t[0:64, 1536:1536 + 16 * C].rearrange("p (a g t) -> p a g t", a=2, g=8)
                    pq = quad()
                    for ci in range(2):
                        for g in range(8):
                            tr(pq[0:64, ci, g * C:(g + 1) * C], QKV[:, ci, g * 64:(g + 1) * 64], C, [kQKV], pq)
                    A(lambda e, pq=pq: e.activation(out=QKF, in_=pq.t[0:64, 0:2, 0:8 * C].rearrange("p a (g t) -> p a g t", g=8), func=AF.Copy),
                      r=[pq, (AR, "SQ")], w=[(AR, "QKF")])
                    pb, bk = bank()
                    for ci in range(2):
                        for k in range(8):
                            P(lambda e, ci=ci, k=k: e.matmul(pb[0:C, bk, ci * 8:ci * 8 + 8], lhsT=XN[:, k, tcol[ci]:tcol[ci] + C], rhs=w1[:, k, 0:8],
                                                            start=(k == 0), stop=(k == 7)), r=[XN, w1], w=[(pb, bk)])
                    GBv = pb[0:C, bk, 0:16].rearrange("p (a g) -> p a g", a=2)
                    gg = SM[0:C, smo + 48:smo + 56].rearrange("p (a g) -> p a g", a=2)
                    be = SM[0:C, smo + 56:smo + 64].rearrange("p (a g) -> p a g", a=2)
                    V(lambda e: e.tensor_tensor(out=gg, in0=GBv[:, :, 0:4], in1=TOKC[0:C, l, 0:4].unsqueeze(1).broadcast_to([C, 2, 4]), op=ALU.add),
                      r=[(pb, bk), TOKC], w=[(SM, "gg" + sfx)])
                    A(lambda e: e.activation(out=be, in_=GBv[:, :, 4:8], func=AF.Sigmoid), r=[(pb, bk)], w=[(SM, "be" + sfx)])
                    V(lambda e: e.tensor_scalar(out=gg, in0=gg, scalar1=30.0, scalar2=None, op0=ALU.min), r=[(SM, "gg" + sfx)], w=[(SM, "gg" + sfx)])
                    A(lambda e: e.activation(out=gg, in_=gg, func=AF.Exp), r=[(SM, "gg" + sfx)], w=[(SM, "gg" + sfx)])
                    V(lambda e: e.tensor_scalar(out=gg, in0=gg, scalar1=1.0, scalar2=None, op0=ALU.add), r=[(SM, "gg" + sfx)], w=[(SM, "gg" + sfx)])
                    A(lambda e: e.activation(out=gg, in_=gg, func=AF.Ln), r=[(SM, "gg" + sfx)], w=[(SM, "gg" + sfx)])
                    V(lambda e: e.tensor_tensor(out=gg, in0=gg, in1=TOKC[0:C, l, 4:8].unsqueeze(1).broadcast_to([C, 2, 4]), op=ALU.mult),
                      r=[(SM, "gg" + sfx), TOKC], w=[(SM, "gg" + sfx)])
                    pb, bk = bank()
                    P(lambda e: e.matmul(pb[0:C, bk, 0:8], lhsT=TRIU[0:C, 0:C], rhs=SM[0:C, smo + 48:smo + 56], start=True, stop=True), r=[TRIU, (SM, "gg" + sfx)], w=[(pb, bk)])
                    P(lambda e: e.matmul(pb[0:64, bk, 8:16], lhsT=ONES[0:C, 0:64], rhs=SM[0:C, smo + 48:smo + 56], start=True, stop=True), r=[ONES, (SM, "gg" + sfx)], w=[(pb, bk)])
                    gc = SM[0:C, smo + 64:smo + 72]
                    gt = SM[0:64, smo + 72:smo + 80]
                    V(lambda e: e.tensor_copy(out=gc, in_=pb[0:C, bk, 0:8]), r=[(pb, bk)], w=[(SM, "gc" + sfx)])
                    V(lambda e: e.tensor_copy(out=gt, in_=pb[0:64, bk, 8:16]), r=[(pb, bk)], w=[(SM, "gt" + sfx)])
                    egc = SM[0:C, smo + 80:smo + 88]
                    egd = SM[0:C, smo + 88:smo + 96]
                    egt = SM[0:64, smo + 96:smo + 104]
                    kf = SM[0:C, smo + 104:smo + 112]
                    A(lambda e: e.activation(out=egc, in_=gc, func=AF.Exp), r=[(SM, "gc" + sfx)], w=[(SM, "egc" + sfx)])
                    A(lambda e: e.activation(out=egt, in_=gt, func=AF.Exp), r=[(SM, "gt" + sfx)], w=[(SM, "egt" + sfx)])
                    V(lambda e: e.tensor_tensor(out=egd, in0=SM[0:C, smo + 72:smo + 80], in1=gc, op=ALU.subtract), r=[(SM, "gt" + sfx), (SM, "gc" + sfx)], w=[(SM, "egd" + sfx)])
                    A(lambda e: e.activation(out=egd, in_=egd, func=AF.Exp), r=[(SM, "egd" + sfx)], w=[(SM, "egd" + sfx)])
                    V(lambda e: e.tensor_tensor(out=kf, in0=SM[0:C, smo + 56:smo + 64], in1=egc, op=ALU.mult), r=[(SM, "be" + sfx), (SM, "egc" + sfx)], w=[(SM, "kf" + sfx)])
                    CC8 = 8 * C

                    def u3(i):
                        return AR.t[0:C, i * 512:i * 512 + CC8].rearrange("p (g f) -> p g f", g=8)
                    DG = u3(5)
                    V(lambda e: e.tensor_tensor(out=DG, in0=IDENT[0:C, 0:C].unsqueeze(1).broadcast_to([C, 8, C]),
                                                in1=gc.unsqueeze(2).broadcast_to([C, 8, C]), op=ALU.mult), r=[IDENT, (SM, "gc" + sfx)], w=[(AR, "u5")])
                    pb, bk = bank()
                    for g in range(8):
                        P(lambda e, g=g: e.matmul(pb[0:C, bk, g * C:(g + 1) * C], lhsT=ONES[0:C, 0:C], rhs=DG[:, g, :], start=True, stop=True),
                          r=[ONES, (AR, "u5")], w=[(pb, bk)])
                    EE = u3(6)
                    V(lambda e: e.tensor_tensor(out=EE, in0=pb[0:C, bk, 0:CC8].rearrange("p (g f) -> p g f", g=8), in1=gc.unsqueeze(2).broadcast_to([C, 8, C]),
                                                op=ALU.subtract), r=[(pb, bk), (SM, "gc" + sfx)], w=[(AR, "u6")])
                    A(lambda e: e.activation(out=EE, in_=EE, func=AF.Abs), r=[(AR, "u6")], w=[(AR, "u6")])
                    A(lambda e: e.activation(out=EE, in_=EE, func=AF.Exp, scale=-1.0), r=[(AR, "u6")], w=[(AR, "u6")])
                    EN = u3(5)
                    EQ = u3(7)
                    V(lambda e: e.tensor_tensor(out=EN, in0=EE, in1=NEGSL[0:C, 0:C].unsqueeze(1).broadcast_to([C, 8, C]), op=ALU.mult),
                      r=[(AR, "u6"), NEGSL], w=[(AR, "u5")])
                    V(lambda e: e.tensor_tensor(out=EQ, in0=EE, in1=TRIU[0:C, 0:C].unsqueeze(1).broadcast_to([C, 8, C]), op=ALU.mult),
                      r=[(AR, "u6"), TRIU], w=[(AR, "u7")])
                    pbk, bkk = bank()
                    pbq, bkq = bank()
                    for ci in range(2):
                        for h in range(4):
                            g = ci * 4 + h
                            P(lambda e, ci=ci, h=h, g=g: e.matmul(pbk[0:C, bkk, g * C:(g + 1) * C], lhsT=QKF[:, ci, 4 + h, :], rhs=QKF[:, ci, 4 + h, :],
                                                                 start=True, stop=True), r=[(AR, "QKF")], w=[(pbk, bkk)])
                            P(lambda e, ci=ci, h=h, g=g: e.matmul(pbq[0:C, bkq, g * C:(g + 1) * C], lhsT=QKF[:, ci, 4 + h, :], rhs=QKF[:, ci, h, :],
                                                                 start=True, stop=True), r=[(AR, "QKF")], w=[(pbq, bkq)])
                    NN = u3(6)
                    V(lambda e: e.tensor_tensor(out=NN, in0=pbk[0:C, bkk, 0:CC8].rearrange("p (g f) -> p g f", g=8), in1=EN, op=ALU.mult),
                      r=[(pbk, bkk), (AR, "u5")], w=[(AR, "u6")])
                    V(lambda e: e.tensor_tensor(out=NN, in0=NN, in1=SM[0:C, smo + 56:smo + 64].unsqueeze(2).broadcast_to([C, 8, C]), op=ALU.mult),
                      r=[(AR, "u6"), (SM, "be" + sfx)], w=[(AR, "u6")])
                    QKT = u3(7)
                    V(lambda e: e.tensor_tensor(out=QKT, in0=pbq[0:C, bkq, 0:CC8].rearrange("p (g f) -> p g f", g=8), in1=EQ, op=ALU.mult),
                      r=[(pbq, bkq), (AR, "u7")], w=[(AR, "u7")])
                    pb, bk = bank()
                    for g in range(8):
                        tr(pb[0:C, bk, g * C:(g + 1) * C], NN[:, g, :], C, [(AR, "u6")], (pb, bk))
                    MM = u3(5)
                    A(lambda e, pb=pb, bk=bk: e.activation(out=MM, in_=pb[0:C, bk, 0:CC8].rearrange("p (g f) -> p g f", g=8), func=AF.Copy), r=[(pb, bk)], w=[(AR, "u5")])
                    UU = u3(8)
                    V(lambda e: e.tensor_tensor(out=UU, in0=MM, in1=IDENT[0:C, 0:C].unsqueeze(1).broadcast_to([C, 8, C]), op=ALU.add), r=[(AR, "u5"), IDENT], w=[(AR, "u8")])
                    cur = {"N": (NN, "u6"), "M": (MM, "u5"), "U": (UU, "u8")}
                    free = [(u3(9), "u9"), (u3(10), "u10"), (u3(11), "u11")]
                    for lev in range(1, LV):
                        (Nv, Nk), (Mv, Mk), (Uv, Uk) = cur["N"], cur["M"], cur["U"]
                        (N2, N2k) = free.pop(0)
                        pb, bk = bank()
                        for g in range(8):
                            P(lambda e, g=g, Mv=Mv, Nv=Nv: e.matmul(pb[0:C, bk, g * C:(g + 1) * C], lhsT=Mv[:, g, :], rhs=Nv[:, g, :], start=True, stop=True),
                              r=[(AR, Mk), (AR, Nk)], w=[(pb, bk)])
                        A(lambda e, pb=pb, bk=bk, N2=N2: e.activation(out=N2, in_=pb[0:C, bk, 0:CC8].rearrange("p (g f) -> p g f", g=8), func=AF.Copy),
                          r=[(pb, bk)], w=[(AR, N2k)])
                        if lev < LV - 1:
                            (M2, M2k) = free.pop(0)
                            pb2, bk2 = bank()
                            for g in range(8):
                                P(lambda e, g=g, Mv=Mv, Nv=Nv: e.matmul(pb2[0:C, bk2, g * C:(g + 1) * C], lhsT=Nv[:, g, :], rhs=Mv[:, g, :], start=True, stop=True),
                                  r=[(AR, Mk), (AR, Nk)], w=[(pb2, bk2)])
                            A(lambda e, pb2=pb2, bk2=bk2, M2=M2: e.activation(out=M2, in_=pb2[0:C, bk2, 0:CC8].rearrange("p (g f) -> p g f", g=8), func=AF.Copy),
                              r=[(pb2, bk2)], w=[(AR, M2k)])
                        (U2, U2k) = free.pop(0)
                        pb3, bk3 = bank()
                        for g in range(8):
                            P(lambda e, g=g, N2=N2, Uv=Uv: e.matmul(pb3[0:C, bk3, g * C:(g + 1) * C], lhsT=N2[:, g, :], rhs=Uv[:, g, :], start=True, stop=True),
                              r=[(AR, N2k), (AR, Uk)], w=[(pb3, bk3)])
                        V(lambda e, pb3=pb3, bk3=bk3, U2=U2, Uv=Uv: e.tensor_tensor(out=U2, in0=pb3[0:C, bk3, 0:CC8].rearrange("p (g f) -> p g f", g=8), in1=Uv, op=ALU.add),
                          r=[(pb3, bk3), (AR, Uk)], w=[(AR, U2k)])
                        free.append((Nv, Nk))
                        free.append((Uv, Uk))
                        if lev < LV - 1:
                            free.append((Mv, Mk))
                            cur = {"N": (N2, N2k), "M": (M2, M2k), "U": (U2, U2k)}
                        else:
                            cur = {"N": (N2, N2k), "M": (Mv, Mk), "U": (U2, U2k)}
                    (UU, Uk) = cur["U"]
                    used = {Uk, "u7"}
                    avail = [i for i in (5, 6, 8, 9, 10, 11) if "u%d" % i not in used]
                    QKV4 = AR.t[0:C, 0:1536].rearrange("p (a n) -> p a n", a=2)

                    def part4(o):
                        return QKV4[:, :, o:o + 256].rearrange("p a (h d) -> p a h d", h=4)
                    Kt, Vt = part4(256), part4(512)

                    def bc4(smap):
                        return smap.rearrange("p (a h) -> p a h", a=2).unsqueeze(3).broadcast_to([C, 2, 4, 64])
                    V(lambda e: e.tensor_tensor(out=Vt, in0=Vt, in1=bc4(SM[0:C, smo + 56:smo + 64]), op=ALU.mult), r=[kQKV, (SM, "be" + sfx)], w=[kQKV])
                    iK = avail.pop(0)
                    KBG = AR.t[0:C, iK * 512:iK * 512 + 512].rearrange("p (a h d) -> p a h d", a=2, h=4)
                    V(lambda e: e.tensor_tensor(out=KBG, in0=Kt, in1=bc4(kf), op=ALU.mult), r=[kQKV, (SM, "kf" + sfx)], w=[(AR, "u%d" % iK)])
                    V(lambda e: e.tensor_tensor(out=Kt, in0=Kt, in1=bc4(egd), op=ALU.mult), r=[kQKV, (SM, "egd" + sfx)], w=[kQKV])
                    pbv, bkv = bank()
                    pbw, bkw = bank()
                    for ci in range(2):
                        for h in range(4):
                            g = ci * 4 + h
                            P(lambda e, ci=ci, h=h, g=g: e.matmul(pbv[0:C, bkv, g * 64:(g + 1) * 64], lhsT=UU[:, g, :], rhs=Vt[:, ci, h, :], start=True, stop=True),
                              r=[(AR, Uk), kQKV], w=[(pbv, bkv)])
                            P(lambda e, ci=ci, h=h, g=g: e.matmul(pbw[0:64, bkw, g * C:(g + 1) * C], lhsT=KBG[:, ci, h, :], rhs=UU[:, g, :], start=True, stop=True),
                              r=[(AR, Uk), (AR, "u%d" % iK)], w=[(pbw, bkw)])
                    iV = avail.pop(0)
                    iW = avail.pop(0)
                    WV = AR.t[0:C, iV * 512:iV * 512 + 512].rearrange("p (a h d) -> p a h d", a=2, h=4)
                    WKT = AR.t[0:64, iW * 512:iW * 512 + CC8].rearrange("p (a h t) -> p a h t", a=2, h=4)
                    A(lambda e: e.activation(out=WV, in_=pbv[0:C, bkv, :].rearrange("p (a h d) -> p a h d", a=2, h=4), func=AF.Copy), r=[(pbv, bkv)], w=[(AR, "u%d" % iV)])
                    A(lambda e: e.activation(out=WKT, in_=pbw[0:64, bkw, 0:CC8].rearrange("p (a h t) -> p a h t", a=2, h=4), func=AF.Copy),
                      r=[(pbw, bkw)], w=[(AR, "u%d" % iW)])
                    if kind == "p":
                        fw.wait_until(lambda: turn[0] == bt)
                    iO = avail.pop(0)
                    OO = AR.t[0:C, iO * 512:iO * 512 + 512].rearrange("p (a h d) -> p a h d", a=2, h=4)
                    iU = avail.pop(0)
                    UT = AR.t[0:C, iU * 512:iU * 512 + 512].rearrange("p (a h d) -> p a h d", a=2, h=4)
                    SS = SSv
                    for ci in range(2):
                        if kind == "p":
                            Sv, Sk = SST[:, l, :, :], SST
                        else:
                            seq = bt * 2 + ci
                            Sv, Sk = SS, SSk
                            fw.dma(sp, lambda e, seq=seq: e.dma_start(out=SS, in_=D["st_delta"][l, seq].rearrange("h d e -> d h e")), writes=[Sk])
                        pb, bk = bank()
                        for h in range(4):
                            P(lambda e, ci=ci, h=h: e.matmul(pb[0:C, bk, h * 64:(h + 1) * 64], lhsT=WKT[:, ci, h, :], rhs=Sv[:, h, :], start=True, stop=True),
                              r=[(AR, "u%d" % iW), Sk], w=[(pb, bk)])
                        for h in range(4):
                            P(lambda e, ci=ci, h=h: e.matmul(pb[0:C, bk, 256 + h * 64:256 + (h + 1) * 64], lhsT=QKF[:, ci, h, :], rhs=Sv[:, h, :], start=True, stop=True),
                              r=[(AR, "QKF"), Sk], w=[(pb, bk)])
                        V(lambda e, ci=ci, pb=pb, bk=bk: e.tensor_tensor(out=UT[:, 0, :, :], in0=WV[:, ci, :, :], in1=pb[0:C, bk, 0:256].rearrange("p (h d) -> p h d", h=4),
                                                                        op=ALU.subtract), r=[(AR, "u%d" % iV), (pb, bk)], w=[(AR, "UTu")])
                        V(lambda e, ci=ci, pb=pb, bk=bk: e.tensor_tensor(out=UT[:, 1, :, :], in0=pb[0:C, bk, 256:512].rearrange("p (h d) -> p h d", h=4),
                                                                        in1=egc[:, ci * 4:ci * 4 + 4].unsqueeze(2).broadcast_to([C, 4, 64]), op=ALU.mult),
                          r=[(pb, bk), (SM, "egc" + sfx)], w=[(AR, "UTt")])
                        pb2, bk2 = bank()
                        for h in range(4):
                            P(lambda e, ci=ci, h=h: e.matmul(pb2[0:C, bk2, h * 64:(h + 1) * 64], lhsT=QKT[:, ci * 4 + h, :], rhs=UT[:, 0, h, :], start=True, stop=True),
                              r=[(AR, "u7"), (AR, "UTu")], w=[(pb2, bk2)])
                        for h in range(4):
                            P(lambda e, ci=ci, h=h: e.matmul(pb2[0:64, bk2, 256 + h * 64:256 + (h + 1) * 64], lhsT=Kt[:, ci, h, :], rhs=UT[:, 0, h, :], start=True, stop=True),
                              r=[kQKV, (AR, "UTu")], w=[(pb2, bk2)])
                        V(lambda e, ci=ci, pb2=pb2, bk2=bk2: e.tensor_tensor(out=OO[:, ci, :, :], in0=pb2[0:C, bk2, 0:256].rearrange("p (h d) -> p h d", h=4), in1=UT[:, 1, :, :],
                                                                            op=ALU.add), r=[(pb2, bk2), (AR, "UTt")], w=[(AR, "OO")])
                        V(lambda e, ci=ci, Sv=Sv: e.tensor_tensor(out=Sv, in0=Sv, in1=egt[:, ci * 4:ci * 4 + 4].unsqueeze(2).broadcast_to([64, 4, 64]), op=ALU.mult),
                          r=[Sk, (SM, "egt" + sfx)], w=[Sk])
                        V(lambda e, ci=ci, Sv=Sv, pb2=pb2, bk2=bk2: e.tensor_tensor(out=Sv, in0=Sv, in1=pb2[0:64, bk2, 256:512].rearrange("p (h d) -> p h d", h=4), op=ALU.add),
                          r=[Sk, (pb2, bk2)], w=[Sk])
                        if kind == "s":
                            fw.dma(sp, lambda e, seq=seq: e.dma_start(out=D["delta_s"][l, seq].rearrange("h d e -> d h e"), in_=SS), reads=[Sk], is_out=True)
                    if kind == "p":
                        turn[0] = bt + 1
                    iQ = avail.pop(0) if avail else iK
                    OS = AR.t[0:C, iK * 512:iK * 512 + 512].rearrange("p (g d) -> p g d", g=8)
                    OOg = AR.t[0:C, iO * 512:iO * 512 + 512].rearrange("p (g d) -> p g d", g=8)
                    V(lambda e: e.tensor_tensor(out=OS, in0=OOg, in1=OOg, op=ALU.mult), r=[(AR, "OO")], w=[(AR, "u%d" % iK)])
                    V(lambda e: e.tensor_reduce(out=SM[0:C, smo + 112:smo + 120], in_=OS, axis=AX.X, op=ALU.add), r=[(AR, "u%d" % iK)], w=[(SM, "os" + sfx)])
                    rstd_from_ss(SM[0:C, smo + 112:smo + 120], 64.0, SM[0:C, smo + 120:smo + 128], [(SM, "os" + sfx)], (SM, "ors" + sfx))
                    V(lambda e: e.tensor_tensor(out=OOg, in0=OOg, in1=SM[0:C, smo + 120:smo + 128].unsqueeze(2).broadcast_to([C, 8, 64]), op=ALU.mult), r=[(AR, "OO"), (SM, "ors" + sfx)], w=[(AR, "OO")])
                    V(lambda e: e.tensor_tensor(out=OOg, in0=OOg, in1=TOKC[0:C, l, 8:72].unsqueeze(1).broadcast_to([C, 8, 64]), op=ALU.mult), r=[(AR, "OO"), TOKC], w=[(AR, "OO")])
                    pb, bk = bank()
                    for ci in range(2):
                        for k in range(8):
                            P(lambda e, ci=ci, k=k: e.matmul(pb[0:C, bk, ci * 256:(ci + 1) * 256], lhsT=XN[:, k, tcol[ci]:tcol[ci] + C], rhs=w1[:, k, 8:264],
                                                            start=(k == 0), stop=(k == 7)), r=[XN, w1], w=[(pb, bk)])
                    GTv = AR.t[0:C, iK * 512:iK * 512 + 512]
                    A(lambda e, pb=pb, bk=bk: e.activation(out=GTv, in_=pb[0:C, bk, :], func=AF.Silu), r=[(pb, bk)], w=[(AR, "u%d" % iK)])
                    OOf = AR.t[0:C, iO * 512:iO * 512 + 512]
                    V(lambda e: e.tensor_tensor(out=OOf, in0=OOf, in1=GTv, op=ALU.mult), r=[(AR, "OO"), (AR, "u%d" % iK)], w=[(AR, "OO")])
                    pb, bk = bank()
                    for ci in range(2):
                        for cc in range(2):
                            tr(pb[:, bk, (cc * 2 + ci) * C:(cc * 2 + ci + 1) * C], OOf[:, ci * 256 + cc * 128:ci * 256 + (cc + 1) * 128], C, [(AR, "OO")], (pb, bk))
                    A(lambda e, pb=pb, bk=bk: e.activation(out=MIX[:, 0:2, t0:t0 + 2 * C], in_=pb[:, bk, 0:4 * C].rearrange("p (c n) -> p c n", c=2), func=AF.Copy),
                      r=[(pb, bk)], w=[(MIX, "A%d" % bt)])

                def stream_bcd():
                    XB = xpview(A2, 0, 2, 31)
                    kXB = (A2, "XB")
                    tails_in(XB, TB, "st_bconv", 31, 2, kXB)
                    for cc in range(2):
                        pb1, bk1 = proj_fm(w1, 264 + cc * 128)
                        pb2, bk2 = proj_fm(w1, 264 + 256 + cc * 128)
                        A(lambda e: e.activation(out=A2.t[:, 4500:4500 + ntok], in_=pb2[:, bk2, 0:ntok], func=AF.Sigmoid), r=[(pb2, bk2)], w=[(A2, "sg")])
                        V(lambda e, cc=cc: e.tensor_tensor(out=XB[:, cc, :, 30:30 + T], in0=ps3(pb1, bk1),
                                                           in1=A2.t[:, 4500:4500 + ntok].rearrange("p (s t) -> p s t", s=nseq), op=ALU.mult),
                          r=[(pb1, bk1), (A2, "sg")], w=[kXB])
                    _chk(fw, 2.3)
                    tails_out(XB, TB, "conf_p", "conf_s", 31, 2, kXB)
                    _chk(fw, 2.35)
                    YB = A2.t[:, 1300:1300 + 2 * ntok].rearrange("p (c s t) -> p c s t", c=2, s=nseq)
                    kYB = (A2, "YB")
                    conv_fm(lambda cc, j, T_: XB[:, cc, :, j:j + T_], 2, T, 31, lambda cc, j: DWB[:, l, cc, j:j + 1], lambda cc: VEC[:, l, 0, cc:cc + 1],
                            lambda cc: YB[:, cc, :, :], [kXB, DWB, VEC], kYB)
                    YBf = A2.t[:, 1300:1300 + 2 * ntok].rearrange("p (c n) -> p c n", c=2)
                    _chk(fw, 2.4)
                    for cc in range(2):
                        sq = A2.t[:, 2400:2400 + ntok]
                        A(lambda e, cc=cc: e.activation(out=sq, in_=YBf[:, cc, :], func=AF.Square), r=[kYB], w=[(A2, "sqB")])
                        pbs, bks = bank()
                        P(lambda e, cc=cc: e.matmul(pbs[:, bks, 0:ntok], lhsT=BONES[:, :], rhs=YBf[:, cc, :], start=True, stop=True), r=[BONES, kYB], w=[(pbs, bks)])
                        pbq, bkq = bank()
                        P(lambda e: e.matmul(pbq[:, bkq, 0:ntok], lhsT=BONES[:, :], rhs=sq, start=True, stop=True), r=[BONES, (A2, "sqB")], w=[(pbq, bkq)])
                        _chk(fw, 2.42)
                        dd = A2.t[:, 3000:3000 + ntok]
                        msq = A2.t[:, 3600:3600 + ntok]
                        V(lambda e, cc=cc: e.scalar_tensor_tensor(out=dd, in0=pbs[:, bks, 0:ntok], scalar=-1.0 / 64, in1=YBf[:, cc, :], op0=ALU.mult, op1=ALU.add),
                          r=[(pbs, bks), kYB], w=[(A2, "ddB")])
                        _chk(fw, 2.43)
                        V(lambda e: e.tensor_scalar(out=msq, in0=pbs[:, bks, 0:ntok], scalar1=1.0 / 64, scalar2=None, op0=ALU.mult), r=[(pbs, bks)], w=[(A2, "msqB")])
                        V(lambda e: e.tensor_tensor(out=msq, in0=msq, in1=msq, op=ALU.mult), r=[(A2, "msqB")], w=[(A2, "msqB")])
                        _chk(fw, 2.435)
                        V(lambda e: e.scalar_tensor_tensor(out=msq, in0=pbq[:, bkq, 0:ntok], scalar=1.0 / 64, in1=msq, op0=ALU.mult, op1=ALU.subtract),
                          r=[(pbq, bkq), (A2, "msqB")], w=[(A2, "msqB")])
                        _chk(fw, 2.44)
                        V(lambda e: e.tensor_scalar(out=msq, in0=msq, scalar1=EPS, scalar2=None, op0=ALU.add), r=[(A2, "msqB")], w=[(A2, "msqB")])
                        A(lambda e: e.activation(out=msq, in_=msq, func=AF.Sqrt), r=[(A2, "msqB")], w=[(A2, "msqB")])
                        _chk(fw, 2.45)
                        V(lambda e: e.reciprocal(out=msq, in_=msq), r=[(A2, "msqB")], w=[(A2, "msqB")])
                        V(lambda e: e.tensor_tensor(out=dd, in0=dd, in1=msq, op=ALU.mult), r=[(A2, "ddB"), (A2, "msqB")], w=[(A2, "ddB")])
                        _chk(fw, 2.46)
                        yoff = 5100 if cc == 0 else 4200
                        ynb = A2.t[:, yoff:yoff + ntok // 2].bitcast(BF16)
                        A(lambda e, cc=cc, ynb=ynb: e.activation(out=ynb, in_=dd, func=AF.Silu, scale=VEC[:, l, 1, cc:cc + 1], bias=VEC[:, l, 2, cc:cc + 1]),
                          r=[(A2, "ddB"), VEC], w=[(A2, "ynB%d" % cc)])
                    _chk(fw, 2.5)
                    yn_c = [A2.t[:, 5100:5100 + ntok // 2].bitcast(BF16), A2.t[:, 4200:4200 + ntok // 2].bitcast(BF16)]
                    for oc in range(2):
                        pb, bk = bank()
                        for kc in range(2):
                            P(lambda e, kc=kc, oc=oc: e.matmul(pb[:, bk, 0:ntok], lhsT=WPW[:, l, kc, oc * 128:(oc + 1) * 128], rhs=yn_c[kc], start=(kc == 0), stop=(kc == 1)),
                              r=[WPW, (A2, "ynB0"), (A2, "ynB1")], w=[(pb, bk)])
                        A(lambda e, oc=oc: e.activation(out=MIX[:, 2 + oc, 0:ntok], in_=pb[:, bk, 0:ntok], func=AF.Copy), r=[(pb, bk)], w=[(MIX, "c%d" % (2 + oc))])

                    _chk(fw, 3)
                    A2.collapse()
                    XC = xpview(A2, 0, 2, 16)
                    kXC = (A2, "XC")
                    tails_in(XC, TC, "st_pool", 16, 2, kXC)
                    for cc in range(2):
                        pb, bk = proj_fm(w2, cc * 128)
                        A(lambda e, cc=cc: e.activation(out=XC[:, cc, :, 15:15 + T], in_=ps3(pb, bk), func=AF.Copy), r=[(pb, bk)], w=[kXC])
                    tails_out(XC, TC, "pool_p", "pool_s", 16, 2, kXC)
                    LW = 15 + T

                    def sview(off, ln):
                        return A2.t[:, off:off + 2 * nseq * ln].rearrange("p (c s w) -> p c s w", c=2, s=nseq)
                    S2 = sview(1100, LW - 1)
                    S4 = sview(2200, LW - 3)
                    S8 = sview(3300, LW - 7)
                    S16 = sview(4400, LW - 15)
                    V(lambda e: e.tensor_tensor(out=S2, in0=XC[:, :, :, 1:LW], in1=XC[:, :, :, 0:LW - 1], op=ALU.add), r=[kXC], w=[(A2, "S2")])
                    V(lambda e: e.tensor_tensor(out=S4, in0=S2[:, :, :, 2:LW - 1], in1=S2[:, :, :, 0:LW - 3], op=ALU.add), r=[(A2, "S2")], w=[(A2, "S4")])
                    V(lambda e: e.tensor_tensor(out=S8, in0=S4[:, :, :, 4:LW - 3], in1=S4[:, :, :, 0:LW - 7], op=ALU.add), r=[(A2, "S4")], w=[(A2, "S8")])
                    V(lambda e: e.tensor_tensor(out=S16, in0=S8[:, :, :, 8:LW - 7], in1=S8[:, :, :, 0:LW - 15], op=ALU.add), r=[(A2, "S8")], w=[(A2, "S16")])
                    SEL = A2.t[:, 5500:5500 + 2 * ntok].rearrange("p (c s t) -> p c s t", c=2, s=nseq)
                    srcs = {(0, 0): (S2, 14, "S2"), (0, 1): (S4, 12, "S4"), (1, 0): (S8, 8, "S8"), (1, 1): (S16, 0, "S16")}
                    for (cc, hf), (sv, o, nm) in srcs.items():
                        ps_ = slice(64 * hf, 64 * hf + 64)
                        wv = float(WIN[cc][hf])
                        V(lambda e, cc=cc, ps_=ps_, sv=sv, o=o, wv=wv: e.tensor_scalar(out=SEL[ps_, cc, :, :], in0=sv[ps_, cc, :, o:o + T], scalar1=1.0 / wv,
                                                                                      scalar2=None, op0=ALU.mult), r=[(A2, nm)], w=[(A2, "SEL")])
                        if first:
                            V(lambda e, cc=cc, ps_=ps_, sv=sv, o=o: e.tensor_tensor(out=SEL[ps_, cc, 0, 0:16], in0=sv[ps_, cc, 0, o:o + 16],
                                                                                   in1=INVC0[ps_, cc, :], op=ALU.mult), r=[(A2, nm), INVC0], w=[(A2, "SEL")])
                    DB = A2.t[:, 1100:1100 + ntok].bitcast(BF16).rearrange("p (c s t) -> p c s t", c=2, s=nseq)
                    V(lambda e: e.tensor_tensor(out=DB, in0=SEL, in1=XC[:, :, :, 15:15 + T], op=ALU.subtract), r=[(A2, "SEL"), kXC, (A2, "S2")], w=[(A2, "S2")])
                    DBf = A2.t[:, 1100:1100 + ntok].bitcast(BF16).rearrange("p (c n) -> p c n", c=2)
                    for cc in range(2):
                        pb, bk = bank()
                        P(lambda e, cc=cc: e.matmul(pb[:, bk, 0:ntok], lhsT=WBD[:, l, 0, cc, :], rhs=DBf[:, cc, :], start=True, stop=True), r=[WBD, (A2, "S2")], w=[(pb, bk)])
                        A(lambda e, cc=cc: e.activation(out=MIX[:, 4 + cc, 0:ntok], in_=pb[:, bk, 0:ntok], func=AF.Copy, scale=VEC[:, l, 3, cc:cc + 1]),
                          r=[(pb, bk), VEC], w=[(MIX, "c%d" % (4 + cc))])

                    _chk(fw, 4)
                    A2.collapse()
                    XD = xpview(A2, 0, 2, 4)
                    kXD = (A2, "XD")
                    tails_in(XD, TD, "st_lconv", 4, 2, kXD)

                    def reg(i):
                        return A2.t[:, 1100 + i * 1024:1100 + i * 1024 + 2 * ntok].rearrange("p (c n) -> p c n", c=2)
                    GD, XR, RR, II, AA = reg(0), reg(1), reg(2), reg(3), reg(4)
                    for cc in range(2):
                        pb, bk = proj_fm(w2, 256 + cc * 128)
                        A(lambda e, cc=cc: e.activation(out=GD[:, cc, :], in_=pb[:, bk, 0:ntok], func=AF.Gelu_apprx_tanh), r=[(pb, bk)], w=[(A2, "GD")])
                        pb, bk = proj_fm(w2, 512 + cc * 128)
                        A(lambda e, cc=cc: e.activation(out=XD[:, cc, :, 3:3 + T], in_=ps3(pb, bk), func=AF.Copy), r=[(pb, bk)], w=[kXD])
                    tails_out(XD, TD, "lconv_p", "lconv_s", 4, 2, kXD)
                    XR4 = A2.t[:, 1100 + 1024:1100 + 1024 + 2 * ntok].rearrange("p (c s t) -> p c s t", c=2, s=nseq)
                    conv_fm(lambda cc, j, T_: XD[:, cc, :, j:j + T_], 2, T, 4, lambda cc, j: CWD[:, l, cc, j:j + 1], lambda cc: VEC[:, l, 4, cc:cc + 1],
                            lambda cc: XR4[:, cc, :, :], [kXD, CWD, VEC], (A2, "XR"))
                    XRB = A2.t[:, 6220:6220 + ntok].bitcast(BF16).rearrange("p (c n) -> p c n", c=2)
                    V(lambda e: e.tensor_copy(out=XRB, in_=XR), r=[(A2, "XR")], w=[(A2, "XRB")])
                    for cc in range(2):
                        for (wi, dstv, bi, nm) in ((1, RR, 5, "RR"), (2, II, 6, "II")):
                            pb, bk = bank()
                            P(lambda e, cc=cc, wi=wi: e.matmul(pb[:, bk, 0:ntok], lhsT=WBD[:, l, wi, cc, :], rhs=XRB[:, cc, :], start=True, stop=True),
                              r=[WBD, (A2, "XRB")], w=[(pb, bk)])
                            A(lambda e, cc=cc, dstv=dstv, bi=bi, pb=pb, bk=bk: e.activation(out=dstv[:, cc, :], in_=pb[:, bk, 0:ntok], func=AF.Sigmoid,
                                                                                       bias=VEC[:, l, bi, cc:cc + 1]), r=[(pb, bk), VEC], w=[(A2, nm)])
                        A(lambda e, cc=cc: e.activation(out=AA[:, cc, :], in_=RR[:, cc, :], func=AF.Exp, scale=NSP[:, l, cc:cc + 1]), r=[(A2, "RR"), NSP], w=[(A2, "AA")])
                    A(lambda e: e.activation(out=RR, in_=AA, func=AF.Square), r=[(A2, "AA")], w=[(A2, "RR")])
                    V(lambda e: e.tensor_scalar(out=RR, in0=RR, scalar1=-1.0, scalar2=1.0, op0=ALU.mult, op1=ALU.add), r=[(A2, "RR")], w=[(A2, "RR")])
                    A(lambda e: e.activation(out=RR, in_=RR, func=AF.Sqrt), r=[(A2, "RR")], w=[(A2, "RR")])
                    V(lambda e: e.tensor_tensor(out=II, in0=II, in1=XR, op=ALU.mult), r=[(A2, "II"), (A2, "XR")], w=[(A2, "II")])
                    V(lambda e: e.tensor_tensor(out=II, in0=II, in1=RR, op=ALU.mult), r=[(A2, "II"), (A2, "RR")], w=[(A2, "II")])
                    HH = XR
                    if kind == "p":
                        for cc in range(2):
                            V(lambda e, cc=cc: e.tensor_tensor_scan(out=HH[:, cc, :], data0=AA[:, cc, :], data1=II[:, cc, :], initial=HST[:, l, cc:cc + 1],
                                                                    op0=ALU.mult, op1=ALU.add), r=[(A2, "AA"), (A2, "II"), HST], w=[(A2, "XR")])
                        V(lambda e: e.tensor_copy(out=HST[:, l, :], in_=HH[:, :, T - 1]), r=[(A2, "XR")], w=[HST])
                        if last_p:
                            store_fm_to_tm(lambda cc: HST[:, l, cc:cc + 1], [HST], 1, 2, D["lh_p"][l])
                    else:
                        H0 = A2.t[:, 6740:6772].rearrange("p (c s) -> p c s", c=2)
                        load_tm_to_fm(D["st_lh"][l], 16, 2, lambda cc: H0[:, cc, :], [(A2, "H0")])
                        AA4 = A2.t[:, 1100 + 4 * 1024:1100 + 4 * 1024 + 2 * ntok].rearrange("p (c s t) -> p c s t", c=2, s=nseq)
                        II4 = A2.t[:, 1100 + 3 * 1024:1100 + 3 * 1024 + 2 * ntok].rearrange("p (c s t) -> p c s t", c=2, s=nseq)
                        V(lambda e: e.tensor_tensor(out=H0, in0=H0, in1=AA4[:, :, :, 0], op=ALU.mult), r=[(A2, "H0"), (A2, "AA")], w=[(A2, "H0")])
                        V(lambda e: e.tensor_tensor(out=II4[:, :, :, 0], in0=II4[:, :, :, 0], in1=H0, op=ALU.add), r=[(A2, "II"), (A2, "H0")], w=[(A2, "II")])
                        V(lambda e: e.tensor_scalar(out=AA4[:, :, :, 0], in0=AA4[:, :, :, 0], scalar1=0.0, scalar2=None, op0=ALU.mult), r=[(A2, "AA")], w=[(A2, "AA")])
                        for cc in range(2):
                            V(lambda e, cc=cc: e.tensor_tensor_scan(out=HH[:, cc, :], data0=AA[:, cc, :], data1=II[:, cc, :], initial=0.0,
                                                                    op0=ALU.mult, op1=ALU.add), r=[(A2, "AA"), (A2, "II")], w=[(A2, "XR")])
                        store_fm_to_tm(lambda cc: XR4[:, cc, :, T - 1], [(A2, "XR")], 16, 2, D["lh_s"][l])
                    V(lambda e: e.tensor_tensor(out=MIX[:, 6:8, 0:ntok], in0=GD, in1=HH, op=ALU.mult), r=[(A2, "GD"), (A2, "XR")], w=[(MIX, "c6"), (MIX, "c7")])


                    if INTERLEAVE:
                        A2.collapse()
                        fw.wait_until(lambda: "prep" in flags)
                        for bt in range(1, nbatch, 2):
                            delta_batch(bt, A2, 112, "_1", A2.t[0:64, 6144:6400].rearrange("p (h d) -> p h d", h=4), (A2, "SS"))

                def stream_a():
                    _chk(fw, 5)
                    XA = xpview(A1, 0, 6, 4)
                    kXA = (A1, "XA")
                    tails_in(XA, TA, "st_dconv", 4, 6, kXA)
                    for cc in range(6):
                        pb, bk = proj_fm(w0, cc * 128)
                        A(lambda e, cc=cc: e.activation(out=XA[:, cc, :, 3:3 + T], in_=ps3(pb, bk), func=AF.Copy), r=[(pb, bk)], w=[kXA])
                    tails_out(XA, TA, "dconv_p", "dconv_s", 4, 6, kXA)
                    YA = A1.t[:, 3100:3100 + 6 * ntok].rearrange("p (c s t) -> p c s t", c=6, s=nseq)
                    YAf = A1.t[:, 3100:3100 + 6 * ntok].rearrange("p (c n) -> p c n", c=6)
                    kYA = (A1, "YA")
                    conv_fm(lambda cc, j, T_: XA[:, cc, :, j:j + T_], 6, T, 4, lambda cc, j: CWA[:, l, cc, j:j + 1], None,
                            lambda cc: YA[:, cc, :, :], [kXA, CWA], kYA)
                    A(lambda e: e.activation(out=YAf, in_=YAf, func=AF.Silu), r=[kYA], w=[kYA])

                    flags["prep"] = 1
                    for bt in (range(0, nbatch, 2) if INTERLEAVE else range(nbatch)):
                        delta_batch(bt, A3, 0, "", SSB[:, :, :], SSB)

                if INTERLEAVE:
                    run_streams(fw, [stream_a, stream_bcd], [1, 1])
                else:
                    stream_bcd()
                    stream_a()
                if last_p:
                    fw.dma(sp, lambda e: e.dma_start(out=D["delta_p"][l].rearrange("h d e -> d h e"), in_=SST[:, l, :, :]), reads=[SST], is_out=True)
                _chk(fw, 6)
                MIX.collapse()
                out_proj(MIX, [wo], 8, "norm_mix_post", l, NB)

                _chk(fw, 7)
                for a_ in (A1, A2, A3):
                    a_.collapse()
                prenorm(lambda tb: X[:, tb, :], NB, 1, l)
                wq = load_w(D["w_xq"][l], 1024)
                wxo = load_w(D["w_xo"][l], 1024)
                QF = A1.t[:, 0:2048].bitcast(BF16).rearrange("p (c n) -> p c n", c=8)
                for c in range(8):
                    pb, bk = proj_fm(wq, c * 128)
                    A(lambda e, c=c, pb=pb, bk=bk: e.activation(out=QF[:, c, 0:ntok], in_=pb[:, bk, 0:ntok], func=AF.Copy, scale=1.0 / 16.0), r=[(pb, bk)], w=[(A1, "QF")])
                KS = A1.t[:, 2048:4096].rearrange("p (a n) -> p a n", a=2)
                KTS = A2.t[:, 0:1024].bitcast(BF16).rearrange("p (c m) -> p c m", c=8)
                VS = A2.t[:, 1024:2048].bitcast(BF16).rearrange("p (a n) -> p a n", a=2)
                PEX = A1.t[:, 4096:5120].rearrange("p (h m) -> p h m", h=4)
                PT = A2.t[:, 2048:2560].bitcast(BF16).rearrange("p (c t) -> p c t", c=8)
                QPAD = A2.t[:, 2560:4608].bitcast(BF16).rearrange("p (c s t) -> p c s t", c=2, s=16)

                def load_kv(kap, vap):
                    fw.dma(sp, lambda e: e.dma_start(out=KS, in_=kap.rearrange("(a p) n -> p a n", p=128)), writes=[(A1, "KS")])
                    fw.dma(pool, lambda e: e.dma_start(out=VS, in_=vap.rearrange("(a p) n -> p a n", p=128)), writes=[(A2, "VS")])
                    pq = quad()
                    for a in range(2):
                        for c in range(8):
                            tr(pq[:, c // 2, (c % 2) * 256 + a * 128:(c % 2) * 256 + a * 128 + 128], KS[:, a, c * 128:(c + 1) * 128], 128, [(A1, "KS")], pq)
                    A(lambda e, pq=pq: e.activation(out=KTS, in_=pq.t[:, :, :].rearrange("p b (c m) -> p (b c) m", c=2), func=AF.Copy), r=[pq], w=[(A2, "KTS")])

                def softmax_pv(pqs, col0, ncol, vfn):
                    sc = pqs.t[:, :, 0:256]
                    V(lambda e: e.tensor_reduce(out=SM[:, 128:132], in_=sc, axis=AX.X, op=ALU.max), r=[pqs], w=[(SM, "mx")])
                    V(lambda e: e.tensor_scalar(out=SM[:, 128:132], in0=SM[:, 128:132], scalar1=-1.0, scalar2=None, op0=ALU.mult), r=[(SM, "mx")], w=[(SM, "mx")])
                    for h in range(4):
                        A(lambda e, h=h: e.activation(out=PEX[:, h, :], in_=pqs[:, h, 0:256], func=AF.Exp, bias=SM[:, 128 + h:129 + h], accum_out=SM[:, 132 + h:133 + h]),
                          r=[pqs, (SM, "mx")], w=[(A1, "PEX"), (SM, "sm")])
                    V(lambda e: e.reciprocal(out=SM[:, 132:136], in_=SM[:, 132:136]), r=[(SM, "sm")], w=[(SM, "sm")])
                    V(lambda e: e.tensor_tensor(out=PEX, in0=PEX, in1=SM[:, 132:136].unsqueeze(2).broadcast_to([128, 4, 256]), op=ALU.mult), r=[(A1, "PEX"), (SM, "sm")], w=[(A1, "PEX")])
                    pq2 = quad()
                    for h in range(4):
                        for a in range(2):
                            c = h * 2 + a
                            tr(pq2[:, c // 4, (c % 4) * 128:(c % 4) * 128 + 128], PEX[:, h, a * 128:(a + 1) * 128], 128, [(A1, "PEX")], pq2)
                    A(lambda e, pq2=pq2: e.activation(out=PT, in_=pq2.t[:, 0:2, :].rearrange("p b (c t) -> p (b c) t", c=4), func=AF.Copy), r=[pq2], w=[(A2, "PT")])
                    vfn()

                if kind == "p":
                    load_kv(D["mk_p"][l], D["mv_p"][l])
                    for tb in range(NB):
                        pqs = quad()
                        for h in range(4):
                            for dc in range(2):
                                P(lambda e, h=h, dc=dc, tb=tb: e.matmul(pqs[:, h, 0:256], lhsT=QF[:, 2 * h + dc, tb * 128:(tb + 1) * 128], rhs=KTS[:, 2 * h + dc, :],
                                                                       start=(dc == 0), stop=(dc == 1)), r=[(A1, "QF"), (A2, "KTS")], w=[pqs])

                        def pv(tb=tb):
                            pq3 = quad()
                            for h in range(4):
                                for ec in range(2):
                                    c = 2 * h + ec
                                    for a in range(2):
                                        P(lambda e, h=h, ec=ec, a=a, c=c: e.matmul(pq3[:, c // 4, (c % 4) * 128:(c % 4) * 128 + 128], lhsT=VS[:, a, h * 256 + ec * 128:h * 256 + ec * 128 + 128],
                                                                                  rhs=PT[:, 2 * h + a, :], start=(a == 0), stop=(a == 1)), r=[(A2, "VS"), (A2, "PT")], w=[pq3])
                            A(lambda e, pq3=pq3: e.activation(out=MIX[:, :, tb * 128:(tb + 1) * 128], in_=pq3.t[:, 0:2, :].rearrange("p b (c t) -> p (b c) t", c=4), func=AF.Copy),
                              r=[pq3], w=[(MIX, tb)])
                        softmax_pv(pqs, tb * 128, 128, pv)
                else:
                    G(lambda e: e.memset(A2.t[:, 2560:4608], 0.0), w=[(A2, "QPAD")])
                    pqs = PQ[0]
                    pq3 = PQ[1]
                    for h in range(4):
                        for s in range(16):
                            fw.dma(sp, lambda e, s=s, h=h: e.dma_start(out=KS[:, :, 0:256], in_=D["ck"][l, s][:, h * 256:(h + 1) * 256].rearrange("(a p) n -> p a n", p=128)),
                                   writes=[(A1, "KS")])
                            pb, bk = PQ[1], s % 4
                            for a in range(2):
                                for dc in range(2):
                                    tr(pb[:, bk, dc * 256 + a * 128:dc * 256 + a * 128 + 128], KS[:, a, dc * 128:(dc + 1) * 128], 128, [(A1, "KS")], (pb, bk))
                            kt = A2.t[:, 0:256].bitcast(BF16).rearrange("p (c m) -> p c m", c=2)
                            A(lambda e, pb=pb, bk=bk: e.activation(out=kt, in_=pb[:, bk, :].rearrange("p (c m) -> p c m", c=2), func=AF.Copy), r=[(pb, bk)], w=[(A2, "KTS")])
                            for dc in range(2):
                                V(lambda e, s=s, dc=dc, h=h: e.tensor_copy(out=QPAD[:, dc, s, s * 8:(s + 1) * 8], in_=QF[:, 2 * h + dc, s * 8:(s + 1) * 8]),
                                  r=[(A1, "QF")], w=[(A2, "QPAD")])
                            for dc in range(2):
                                P(lambda e, s=s, dc=dc, h=h: e.matmul(pqs[:, h, 0:256], lhsT=QPAD[:, dc, s, :], rhs=kt[:, dc, :], start=(s == 0 and dc == 0), stop=(s == 15 and dc == 1)),
                                  r=[(A2, "QPAD"), (A2, "KTS")], w=[(pqs, h)])

                    def pv_s():
                        for s in range(16):
                            fw.dma(pool, lambda e, s=s: e.dma_start(out=VS, in_=D["cv"][l, s].rearrange("(a p) n -> p a n", p=128)), writes=[(A2, "VS")])
                            pbv, bkv = bank()
                            for h in range(4):
                                for ec in range(2):
                                    c = 2 * h + ec
                                    for a in range(2):
                                        P(lambda e, h=h, ec=ec, a=a, c=c, s=s: e.matmul(pbv[:, bkv, c * 8:(c + 1) * 8], lhsT=VS[:, a, h * 256 + ec * 128:h * 256 + ec * 128 + 128],
                                                                                       rhs=PT[:, 2 * h + a, s * 8:(s + 1) * 8], start=(a == 0), stop=(a == 1)),
                                          r=[(A2, "VS"), (A2, "PT")], w=[(pbv, bkv)])
                            A(lambda e, s=s, pbv=pbv, bkv=bkv: e.activation(out=MIX[:, :, s * 8:(s + 1) * 8], in_=pbv[:, bkv, 0:64].rearrange("p (c t) -> p c t", c=8), func=AF.Copy),
                              r=[(pbv, bkv)], w=[(MIX, "s%d" % s)])
                    softmax_pv(pqs, 0, 128, pv_s)
                MIX.collapse()
                out_proj(MIX, [wxo], 8, "norm_x_post", l, NB)

                _chk(fw, 8)
                for a_ in (A1, A2, A3):
                    a_.collapse()
                prenorm(lambda tb: X[:, tb, :], NB, 2, l)
                HID = A1.t[:, 0:5632].bitcast(BF16).rearrange("p (m n) -> p m n", m=22)
                for gp in range(6):
                    ncol = 512 if gp < 5 else 256
                    slot = wslot()
                    load_w(D["w_ffn_in"][l][:, gp * 512:gp * 512 + ncol], ncol, 0, slot)
                    load_w(D["w_ffn_in"][l][:, FFN + gp * 512:FFN + gp * 512 + ncol], ncol, 512, slot)
                    for mi in range(ncol // 128):
                        m = gp * 4 + mi
                        pg, bg = proj_fm(slot, mi * 128)
                        pu, bu = proj_fm(slot, 512 + mi * 128)
                        sg = A2.t[:, (m % 2) * 512:(m % 2) * 512 + ntok]
                        A(lambda e, pg=pg, bg=bg, sg=sg: e.activation(out=sg, in_=pg[:, bg, 0:ntok], func=AF.Silu), r=[(pg, bg)], w=[(A2, "sg%d" % (m % 2))])
                        V(lambda e, pu=pu, bu=bu, sg=sg, m=m: e.tensor_tensor(out=HID[:, m, 0:ntok], in0=pu[:, bu, 0:ntok], in1=sg, op=ALU.mult),
                          r=[(pu, bu), (A2, "sg%d" % (m % 2))], w=[(A1, "h%d" % m)])
                A1.collapse()
                fo = [load_w(D["w_ffn_out"][l][0:1024, :], 1024), load_w(D["w_ffn_out"][l][1024:2048, :], 1024), load_w(D["w_ffn_out"][l][2048:2816, :], 1024)]

                A2.collapse()
                _gb[0] += 1
                gb = GBC[_gb[0] % 2]
                fw.dma(sp, lambda e: e.dma_start(out=gb[:, :], in_=D["norm_ffn_post"][l:l + 1, :].broadcast_to([128, 1024])), writes=[gb])
                for tb in range(NB):
                    pq = quad()
                    for n in range(2):
                        for k in range(22):
                            P(lambda e, n=n, k=k, tb=tb: e.matmul(pq[:, n, :], lhsT=HID[:, k, tb * 128:(tb + 1) * 128], rhs=fo[k // 8][:, k % 8, n * 512:(n + 1) * 512],
                                                                 start=(k == 0), stop=(k == 21)), r=[A1, fo[k // 8]], w=[pq])
                    for n in range(2):
                        A(lambda e, n=n, pq=pq: e.activation(out=A2.t[:, 1024 + n * 512:1024 + (n + 1) * 512], in_=pq[:, n, :], func=AF.Copy), r=[pq], w=[(A2, "ycp")])
                        A(lambda e, n=n: e.activation(out=A2.t[:, 0:512], in_=A2.t[:, 1024 + n * 512:1024 + (n + 1) * 512], func=AF.Square, accum_out=SM[:, 16 + n:17 + n]),
                          r=[(A2, "ycp")], w=[(A2, "junk"), (SM, "ss2")])
                    V(lambda e: e.tensor_tensor(out=SM[:, 18:19], in0=SM[:, 16:17], in1=SM[:, 17:18], op=ALU.add), r=[(SM, "ss2")], w=[(SM, "ss3")])
                    rstd_from_ss(SM[:, 18:19], 1024.0, SM[:, 19:20], [(SM, "ss3")], (SM, "rs3"))
                    for n in range(2):
                        V(lambda e, n=n: e.scalar_tensor_tensor(out=A2.t[:, 2048 + n * 512:2048 + (n + 1) * 512], in0=A2.t[:, 1024 + n * 512:1024 + (n + 1) * 512], scalar=SM[:, 19:20],
                                                                      in1=gb[:, n * 512:(n + 1) * 512], op0=ALU.mult, op1=ALU.mult),
                          r=[(A2, "ycp"), (SM, "rs3"), gb], w=[(A2, "yn")])
                    V(lambda e, tb=tb: e.tensor_tensor(out=X[:, tb, :], in0=X[:, tb, :], in1=A2.t[:, 2048:3072], op=ALU.add), r=[X, (A2, "yn")], w=[X])

            fw.dma(sp, lambda e: e.dma_start(out=ydst.rearrange("(tb p) d -> p tb d", p=128), in_=X[:, 0:NB, :]), reads=[X], is_out=True)

        fw.finish()
        print(f"[kernel] built {fw.ninst} instructions, {fw.n_dma_sems} dma sems, sbuf free {nc.sbuf_bytes_remaining}")
    return nc


_W_KEYS = ["norm_mix_pre", "norm_mix_post", "w_in", "conv_qkv", "a_log", "dt_bias", "onorm_a", "dw_b", "dwbias_b", "gn_gain_b",
           "gn_bias_b", "w_pw_b", "w_pool", "scale_pool", "conv_d", "conv_bias_d", "w_rg", "b_rg", "w_ig", "b_ig", "lam_d", "w_out",
           "norm_x_pre", "norm_x_post", "norm_mem", "w_xq", "w_xkv", "w_xo", "norm_ffn_pre", "norm_ffn_post", "w_ffn_in", "w_ffn_out"]


def make_in_map(inp, c):
    f = lambda a: np.ascontiguousarray(np.asarray(a, dtype=np.float32))
    s = slice(16 * c, 16 * c + 16)
    m = dict(
        xp=f(inp["x_prompt"][c]), xs=f(inp["x_sample"][s]).reshape(128, 1024), mem=f(inp["mem_prompt"][c]),
        st_delta=f(inp["state_delta"][:, s]), st_dconv=f(inp["state_delta_conv"][:, s]).reshape(4, 48, 768),
        st_bconv=f(inp["state_conf_conv"][:, s]).reshape(4, 480, 256), st_pool=f(inp["state_pool"][:, s]).reshape(4, 240, 256),
        st_lconv=f(inp["state_lru_conv"][:, s]).reshape(4, 48, 256), st_lh=f(inp["state_lru_h"][:, s]),
        ck=f(inp["cache_mem_k"][:, s]).reshape(4, 16, 256, 1024), cv=f(inp["cache_mem_v"][:, s]).reshape(4, 16, 256, 1024))
    for k in _W_KEYS:
        m[k] = f(inp[k])
    return m


def gather(results):
    n = len(results)
    cat = lambda k, ax: np.concatenate([r[k] for r in results], axis=ax)
    y_p = np.stack([r["y_p"] for r in results], 0)
    y_s = np.concatenate([r["y_s"].reshape(16, 8, 1024) for r in results], 0)
    delta_p = np.stack([r["delta_p"] for r in results], 1)
    delta_s = cat("delta_s", 1)
    dconv_p = np.stack([r["dconv_p"] for r in results], 1)
    dconv_s = np.concatenate([r["dconv_s"].reshape(4, 16, 3, 768) for r in results], 1)
    conf_p = np.stack([r["conf_p"] for r in results], 1)
    conf_s = np.concatenate([r["conf_s"].reshape(4, 16, 30, 256) for r in results], 1)
    pool_p = np.stack([r["pool_p"] for r in results], 1)
    pool_s = np.concatenate([r["pool_s"].reshape(4, 16, 15, 256) for r in results], 1)
    lconv_p = np.stack([r["lconv_p"] for r in results], 1)
    lconv_s = np.concatenate([r["lconv_s"].reshape(4, 16, 3, 256) for r in results], 1)
    lh_p = np.stack([r["lh_p"].reshape(4, 256) for r in results], 1)
    lh_s = cat("lh_s", 1)
    mk_p = np.stack([r["mk_p"].reshape(4, 256, 4, 256) for r in results], 1)
    mv_p = np.stack([r["mv_p"].reshape(4, 256, 4, 256) for r in results], 1)
    outs = (y_p, y_s, delta_p, delta_s, dconv_p, dconv_s, conf_p, conf_s, pool_p, pool_s, lconv_p, lconv_s, lh_p, lh_s, mk_p, mv_p)
    return tuple(np.ascontiguousarray(o.astype(np.float32)) for o in outs)


def kernel(**inputs):
    nc = build()
    in_maps = [make_in_map(inputs, c) for c in range(8)]
    res = run_bass_kernel_spmd(nc, in_maps, core_ids=list(range(8)))
    return gather(res.results)
```

```python
import numpy as np
from contextlib import ExitStack
import concourse.bass as bass
import concourse.mybir as mybir
from concourse.bass_utils import run_bass_kernel_spmd

F32 = mybir.dt.float32
BF16 = mybir.dt.bfloat16
I32 = mybir.dt.int32
AF = mybir.ActivationFunctionType
ALU = mybir.AluOpType
AX = mybir.AxisListType

EPS = 1e-6
NLAYER = 4
OFF_B, OFF_C, OFF_D = 1032, 1544, 1800
FFN = 2816


class Rec:
    __slots__ = ("w", "r", "wm", "dsem", "dcount")

    def __init__(self):
        self.w = None
        self.r = {}
        self.wm = {}
        self.dsem = {}
        self.dcount = {}


class Buf:
    def __init__(self, t, name):
        self.t = t
        self.name = name
        self.whole = Rec()
        self.parts = {}

    def __getitem__(self, k):
        return self.t[k]

    def recs(self, key):
        if key is None:
            return [self.whole] + list(self.parts.values())
        if key not in self.parts:
            self.parts[key] = Rec()
        return [self.whole, self.parts[key]]

    def own(self, key):
        if key is None:
            return self.whole
        if key not in self.parts:
            self.parts[key] = Rec()
        return self.parts[key]

    def collapse(self):
        for rec in self.parts.values():
            for (sem, val) in rec.r.values():
                k = id(sem)
                if k not in self.whole.r or self.whole.r[k][1] < val:
                    self.whole.r[k] = (sem, val)
            wevs = list(rec.wm.values())
            if rec.w is not None:
                wevs.append(rec.w)
            for (sem, val) in wevs:
                k = id(sem)
                if k not in self.whole.wm or self.whole.wm[k][1] < val:
                    self.whole.wm[k] = (sem, val)
        self.parts = {}


class Eng:
    def __init__(self, name, h):
        self.name = name
        self.h = h
        self.sem = None
        self.count = 0
        self.seen = {}


class FW:
    def __init__(self, nc, stack):
        self.nc = nc
        self.stack = stack
        self.pe = Eng("pe", nc.tensor)
        self.act = Eng("act", nc.scalar)
        self.dve = Eng("dve", nc.vector)
        self.pool = Eng("pool", nc.gpsimd)
        self.sp = Eng("sp", nc.sync)
        self.engs = [self.pe, self.act, self.dve, self.pool, self.sp]
        for e in self.engs:
            e.sem = stack.enter_context(nc.semaphore("s_" + e.name))
        self.n_dma_sems = 0
        self.out_events = {}
        self.nbuf = 0
        self.ninst = 0
        self.dead = False
        self.hook = None
        self.stream_idx = {}
        self.yield_now = None

    def wait_until(self, cond):
        if cond():
            return
        if self.yield_now is None:
            raise RuntimeError("wait_until outside interleaved emission")
        n = 0
        while not cond():
            self.yield_now()
            n += 1
            if n > 10_000_000:
                raise RuntimeError("wait_until: never satisfied")

    def sbuf(self, shape, dtype, name=None):
        self.nbuf += 1
        name = name or f"b{self.nbuf}"
        t = self.stack.enter_context(self.nc.sbuf_tensor(name, list(shape), dtype))
        return Buf(t, name)

    def psum(self, shape, dtype, name=None):
        self.nbuf += 1
        name = name or f"p{self.nbuf}"
        t = self.stack.enter_context(self.nc.psum_tensor(name, list(shape), dtype))
        return Buf(t, name)

    def _dsem(self, rec, kind):
        if kind not in rec.dsem:
            self.n_dma_sems += 1
            rec.dsem[kind] = self.stack.enter_context(self.nc.semaphore(f"d{self.n_dma_sems}"))
            rec.dcount[kind] = 0
        return rec.dsem[kind]

    def _collect(self, reads, writes):
        need = {}

        def add(ev):
            if ev is None:
                return
            sem, val = ev
            k = id(sem)
            if k not in need or need[k][1] < val:
                need[k] = (sem, val)

        for (b, key) in reads:
            for rec in b.recs(key):
                add(rec.w)
                for ev in rec.wm.values():
                    add(ev)
        for (b, key) in writes:
            for rec in b.recs(key):
                add(rec.w)
                for ev in rec.wm.values():
                    add(ev)
                for ev in rec.r.values():
                    add(ev)
        return need

    def _emit_waits(self, eng, need):
        for k, (sem, val) in need.items():
            if eng is self.pe and sem is self.pe.sem:
                continue
            if eng.seen.get(k, 0) >= val:
                continue
            eng.seen[k] = val
            eng.h.wait_ge(sem, val)

    @staticmethod
    def _norm(lst):
        out = []
        for x in lst:
            if isinstance(x, Buf):
                out.append((x, None))
            else:
                out.append(x)
        return out

    def op(self, eng, fn, reads=(), writes=()):
        if self.dead:
            return None
        reads = self._norm(reads)
        writes = self._norm(writes)
        need = self._collect(reads, writes)
        self._emit_waits(eng, need)
        ins = fn(eng.h)
        self.ninst += 1
        eng.count += 1
        ins.then_inc(eng.sem, 1)
        ev = (eng.sem, eng.count)
        for (b, key) in reads:
            b.own(key).r[id(eng.sem)] = ev
        for (b, key) in writes:
            if key is None:
                b.parts = {}
            rec = b.own(key)
            rec.w = ev
            rec.r = {}
            if key is None:
                rec.wm = {}
        if self.hook is not None:
            self.hook()
        return ins

    def dma(self, q, fn, reads=(), writes=(), is_out=False):
        if self.dead:
            return None
        reads = self._norm(reads)
        writes = self._norm(writes)
        need = self._collect(reads, writes)
        self._emit_waits(q, need)
        ins = fn(q.h)
        self.ninst += 1
        if writes:
            b, key = writes[0]
        else:
            b, key = reads[0]
        rec = b.own(key)
        kind = "sw" if q is self.pool else "hw"
        sem = self._dsem(rec, kind)
        rec.dcount[kind] += 16
        ins.then_inc(sem, 16)
        ev = (sem, rec.dcount[kind])
        self.last_dma_ev = ev
        for (b2, key2) in reads:
            b2.own(key2).r[id(sem)] = ev
        for (b2, key2) in writes:
            if key2 is None:
                b2.parts = {}
            r2 = b2.own(key2)
            r2.w = ev
            r2.r = {}
            if key2 is None:
                r2.wm = {}
        if is_out:
            self.out_events[id(sem)] = ev
        return ins

    def finish(self):
        for sem, val in self.out_events.values():
            self.sp.h.wait_ge(sem, val)
        for e in self.engs:
            if e is not self.sp and e.count > 0:
                self.sp.h.wait_ge(e.sem, e.count)


IN_SHAPES = dict(
    xp=[2048, 1024], xs=[128, 1024], mem=[256, 1024],
    st_delta=[4, 16, 4, 64, 64], st_dconv=[4, 48, 768], st_bconv=[4, 480, 256], st_pool=[4, 240, 256],
    st_lconv=[4, 48, 256], st_lh=[4, 16, 256], ck=[4, 16, 256, 1024], cv=[4, 16, 256, 1024],
    norm_mix_pre=[4, 1024], norm_mix_post=[4, 1024], w_in=[4, 1024, 2312], conv_qkv=[4, 4, 768], a_log=[4, 4],
    dt_bias=[4, 4], onorm_a=[4, 64], dw_b=[4, 31, 256], dwbias_b=[4, 256], gn_gain_b=[4, 256], gn_bias_b=[4, 256],
    w_pw_b=[4, 256, 256], w_pool=[4, 4, 64, 64], scale_pool=[4, 256], conv_d=[4, 4, 256], conv_bias_d=[4, 256],
    w_rg=[4, 4, 64, 64], b_rg=[4, 256], w_ig=[4, 4, 64, 64], b_ig=[4, 256], lam_d=[4, 256], w_out=[4, 1024, 1024],
    norm_x_pre=[4, 1024], norm_x_post=[4, 1024], norm_mem=[4, 1024], w_xq=[4, 1024, 1024], w_xkv=[4, 1024, 2048],
    w_xo=[4, 1024, 1024], norm_ffn_pre=[4, 1024], norm_ffn_post=[4, 1024], w_ffn_in=[4, 1024, 5632],
    w_ffn_out=[4, 2816, 1024])
OUT_SHAPES = dict(
    y_p=[2048, 1024], y_s=[128, 1024], delta_p=[4, 4, 64, 64], delta_s=[4, 16, 4, 64, 64], dconv_p=[4, 3, 768],
    dconv_s=[4, 48, 768], conf_p=[4, 30, 256], conf_s=[4, 480, 256], pool_p=[4, 15, 256], pool_s=[4, 240, 256],
    lconv_p=[4, 3, 256], lconv_s=[4, 48, 256], lh_p=[4, 1, 256], lh_s=[4, 16, 256], mk_p=[4, 256, 1024],
    mv_p=[4, 256, 1024])


class _Stop(Exception):
    pass


import threading


def run_streams(fw, fns, weights=None):
    n = len(fns)
    sems = [threading.Semaphore(0) for _ in range(n)]
    main_sem = threading.Semaphore(0)
    done = [False] * n
    errs = []
    idx = {}
    cnt = [0] * n
    weights = weights or [1] * n

    def hook(force=False):
        i = idx.get(threading.get_ident())
        if i is None:
            return
        cnt[i] += 1
        if cnt[i] < weights[i] and not force:
            return
        cnt[i] = 0
        j = i
        for d in range(1, n + 1):
            j = (i + d) % n
            if not done[j]:
                break
        if j == i:
            if force:
                raise RuntimeError("yield_now: no other live stream (emission-order deadlock)")
            return
        sems[j].release()
        sems[i].acquire()

    def runner(i):
        idx[threading.get_ident()] = i
        sems[i].acquire()
        try:
            fns[i]()
        except BaseException as e:
            errs.append(e)
        done[i] = True
        alive = [j for j in range(n) if not done[j]]
        if alive:
            sems[alive[0]].release()
        else:
            main_sem.release()

    ths = [threading.Thread(target=runner, args=(i,)) for i in range(n)]
    old = fw.hook
    fw.hook = hook
    fw.yield_now = lambda: hook(True)
    fw.stream_idx = idx
    for t in ths:
        t.start()
    sems[0].release()
    main_sem.acquire()
    for t in ths:
        t.join()
    fw.hook = old
    fw.yield_now = None
    fw.stream_idx = {}
    if errs:
        raise errs[0]


import os
STOP = float(os.environ.get("KSTOP", "99"))
INTERLEAVE = os.environ.get("KINTER", "1") == "1"


def _chk(fw, level):
    if STOP <= level:
        fw.dead = True


def build(NL=NLAYER, tts=None):
    nc = bass.Bass("TRN2", target_bir_lowering=False)
    D = {}
    for k, s in IN_SHAPES.items():
        D[k] = nc.dram_tensor(k, list(s), F32, kind="ExternalInput").ap()
    for k, s in OUT_SHAPES.items():
        D[k] = nc.dram_tensor(k, list(s), F32, kind="ExternalOutput").ap()
    if tts is None:
        tts = [("p", i) for i in range(4)] + [("s", 0)]

    with ExitStack() as st:
        fw = FW(nc, st)
        pe, act, dve, pool, sp = fw.pe, fw.act, fw.dve, fw.pool, fw.sp

        def V(fn, r=(), w=()):
            return fw.op(dve, fn, r, w)

        def A(fn, r=(), w=()):
            return fw.op(act, fn, r, w)

        def P(fn, r=(), w=()):
            return fw.op(pe, fn, r, w)

        def G(fn, r=(), w=()):
            return fw.op(pool, fn, r, w)

        X = fw.sbuf([128, 4, 1024], F32, "X")
        NW = 4
        WS = [fw.sbuf([128, 8, 1024], BF16, f"WS{i}") for i in range(NW)]
        XN = fw.sbuf([128, 8, 512], BF16, "XN")
        MIX = fw.sbuf([128, 8, 512], BF16, "MIX")
        A1 = fw.sbuf([128, 6200], F32, "A1")
        A2 = fw.sbuf([128, 6800], F32, "A2")
        A3 = fw.sbuf([128, 6144], F32, "A3")
        GBC = [fw.sbuf([128, 1024], F32, f"GBC{i}") for i in range(2)]
        STG = fw.sbuf([128, 1024], F32, "STG")
        SM = fw.sbuf([128, 256], F32, "SM")
        TST = fw.sbuf([128, 128], F32, "TST")
        TST2 = fw.sbuf([128, 128], F32, "TST2")
        PQ = [fw.psum([128, 4, 512], F32, f"PQ{i}") for i in range(2)]
        IDENT = fw.sbuf([128, 128], F32, "IDENT")
        ONES = fw.sbuf([128, 128], F32, "ONES")
        TRIU = fw.sbuf([128, 128], F32, "TRIU")
        NEGSL = fw.sbuf([128, 128], F32, "NEGSL")
        BONES = fw.sbuf([128, 128], F32, "BONES")
        INVC0 = fw.sbuf([128, 2, 16], F32, "INVC0")
        SSB = fw.sbuf([64, 4, 64], F32, "SSB")
        GPRE = fw.sbuf([128, NLAYER, 4, 8], F32, "GPRE")
        CWA = fw.sbuf([128, NLAYER, 6, 4], F32, "CWA")
        DWB = fw.sbuf([128, NLAYER, 2, 31], F32, "DWB")
        CWD = fw.sbuf([128, NLAYER, 2, 4], F32, "CWD")
        VEC = fw.sbuf([128, NLAYER, 8, 2], F32, "VEC")
        NSP = fw.sbuf([128, NLAYER, 2], F32, "NSP")
        WPW = fw.sbuf([128, NLAYER, 2, 256], BF16, "WPW")
        WBD = fw.sbuf([128, NLAYER, 3, 2, 128], BF16, "WBD")
        TOKC = fw.sbuf([128, NLAYER, 72], F32, "TOKC")
        TA = fw.sbuf([128, NLAYER, 6, 3], F32, "TA")
        TB = fw.sbuf([128, NLAYER, 2, 30], F32, "TB")
        TC = fw.sbuf([128, NLAYER, 2, 15], F32, "TC")
        TD = fw.sbuf([128, NLAYER, 2, 3], F32, "TD")
        HST = fw.sbuf([128, NLAYER, 2], F32, "HST")
        SST = fw.sbuf([64, NLAYER, 4, 64], F32, "SST")

        _pb = [0, 0, 0]

        def bank():
            si = fw.stream_idx.get(threading.get_ident())
            if si is None:
                i = _pb[0] % 8
                _pb[0] += 1
                return PQ[i // 4], i % 4
            i = _pb[1 + si] % 4
            _pb[1 + si] += 1
            return PQ[si], i

        _pq = [0]

        def quad():
            si = fw.stream_idx.get(threading.get_ident())
            if si is not None:
                return PQ[si]
            i = _pq[0] % 2
            _pq[0] += 1
            return PQ[i]

        _ws = [0]

        def wslot():
            i = _ws[0] % NW
            _ws[0] += 1
            return WS[i]

        SCR = nc.dram_tensor("wscratch", [NLAYER, 15, 128, 8 * 1024], BF16).ap()
        scr_ev = {}
        wpass = [0]

        def load_w(src_ap, ncols, col0=0, slot=None, wid=None, final=True):
            if slot is None:
                slot = wslot()
            if wid is None or wpass[0] == 0:
                nk = src_ap.shape[0] // 128
                fw.dma(pool, lambda e: e.dma_start(out=slot[:, 0:nk, col0:col0 + ncols],
                                                   in_=src_ap.rearrange("(k p) n -> p k n", p=128)), writes=[slot])
                if wid is not None and final and not fw.dead:
                    fw.dma(sp, lambda e: e.dma_start(out=SCR[wid[0], wid[1]], in_=slot.t[:, :, :].rearrange("p k n -> p (k n)")), reads=[slot])
                    scr_ev[wid] = fw.last_dma_ev
            elif final and not fw.dead:
                sem, val = scr_ev[wid]
                fw._emit_waits(sp, {id(sem): (sem, val)})
                fw.dma(sp, lambda e: e.dma_start(out=slot.t[:, :, :].rearrange("p k n -> p (k n)"), in_=SCR[wid[0], wid[1]]), writes=[slot])
            return slot

        G(lambda e: e.memset(ONES[:, :], 1.0), w=[ONES])
        G(lambda e: e.affine_select(out=IDENT[:, :], in_=ONES[:, :], pattern=[[-1, 128]], compare_op=ALU.is_equal,
                                    fill=0.0, base=0, channel_multiplier=1), r=[ONES], w=[IDENT])
        G(lambda e: e.affine_select(out=TRIU[:, :], in_=ONES[:, :], pattern=[[1, 128]], compare_op=ALU.is_ge,
                                    fill=0.0, base=0, channel_multiplier=-1), r=[ONES], w=[TRIU])
        G(lambda e: e.memset(BONES[:, :], -1.0), w=[BONES])
        G(lambda e: e.affine_select(out=NEGSL[:, :], in_=BONES[:, :], pattern=[[-1, 128]], compare_op=ALU.is_ge,
                                    fill=0.0, base=-1, channel_multiplier=1), r=[BONES], w=[NEGSL])
        G(lambda e: e.memset(BONES[:, :], 0.0), w=[BONES])
        G(lambda e: e.memset(BONES[0:64, 0:64], 1.0), w=[BONES])
        G(lambda e: e.memset(BONES[64:128, 64:128], 1.0), w=[BONES])
        for b_ in (TA, TB, TC, TD, HST, SST, WBD, SM):
            G(lambda e, b_=b_: e.memset(b_.t[:].rearrange(" ".join(["p"] + [f"a{i}" for i in range(len(b_.t.shape) - 1)]) + " -> p (" + " ".join(
                [f"a{i}" for i in range(len(b_.t.shape) - 1)]) + ")"), 0.0), w=[b_])
        WIN = [[2, 4], [8, 16]]
        IOT = A3
        G(lambda e: e.iota(IOT.t[:, 0:16].bitcast(I32), pattern=[[1, 16]], base=1, channel_multiplier=0), w=[IOT])
        V(lambda e: e.tensor_copy(out=IOT.t[:, 16:32], in_=IOT.t[:, 0:16].bitcast(I32)), r=[IOT], w=[IOT])
        for cc in range(2):
            for hf in range(2):
                ps = slice(64 * hf, 64 * hf + 64)
                wv = float(WIN[cc][hf])
                V(lambda e, ps=ps, cc=cc, wv=wv: e.tensor_scalar(out=INVC0[ps, cc, :], in0=IOT.t[ps, 16:32], scalar1=wv,
                                                                 scalar2=None, op0=ALU.min), r=[IOT], w=[INVC0])
        V(lambda e: e.reciprocal(out=INVC0[:, :, :], in_=INVC0[:, :, :]), r=[INVC0], w=[INVC0])

        def sdma(out_ap, in_ap, wbuf, q=None):
            fw.dma(q or sp, lambda e: e.dma_start(out=out_ap, in_=in_ap, allow_slow_non_contiguous=True), writes=[wbuf])

        for l in range(NL):
            for i, nm in enumerate(["norm_mix_pre", "norm_x_pre", "norm_ffn_pre", "norm_mem"]):
                sdma(GPRE[:, l, i, :], D[nm][l].rearrange("(c p) -> p c", p=128), GPRE)
            for j in range(4):
                sdma(CWA[:, l, :, j], D["conv_qkv"][l, j].rearrange("(c p) -> p c", p=128), CWA)
                sdma(CWD[:, l, :, j], D["conv_d"][l, j].rearrange("(c p) -> p c", p=128), CWD)
            for cc in range(2):
                sdma(DWB[:, l, cc, :], D["dw_b"][l, :, cc * 128:(cc + 1) * 128].rearrange("j p -> p j"), DWB)
            for i, nm in enumerate(["dwbias_b", "gn_gain_b", "gn_bias_b", "scale_pool", "conv_bias_d", "b_rg", "b_ig", "lam_d"]):
                sdma(VEC[:, l, i, :], D[nm][l].rearrange("(c p) -> p c", p=128), VEC)
            sdma(WPW[:, l, :, :], D["w_pw_b"][l].rearrange("(c p) n -> p c n", p=128), WPW, q=pool)
            for i, nm in enumerate(["w_pool", "w_rg", "w_ig"]):
                for gi in range(4):
                    hf, cc = gi % 2, gi // 2
                    sdma(WBD[64 * hf:64 * hf + 64, l, i, cc, 64 * hf:64 * hf + 64], D[nm][l, gi], WBD, q=pool)
            sdma(TOKC[:, l, 0:4], D["dt_bias"][l:l + 1, :].broadcast_to([128, 4]), TOKC)
            sdma(TOKC[:, l, 4:8], D["a_log"][l:l + 1, :].broadcast_to([128, 4]), TOKC)
            sdma(TOKC[:, l, 8:72], D["onorm_a"][l:l + 1, :].broadcast_to([128, 64]), TOKC)
        for l in range(NL):
            A(lambda e, l=l: e.activation(out=TOKC[:, l, 4:8], in_=TOKC[:, l, 4:8], func=AF.Exp), r=[TOKC], w=[TOKC])
            V(lambda e, l=l: e.tensor_scalar(out=TOKC[:, l, 4:8], in0=TOKC[:, l, 4:8], scalar1=-1.0, scalar2=None, op0=ALU.mult),
              r=[TOKC], w=[TOKC])
            A(lambda e, l=l: e.activation(out=NSP[:, l, :], in_=VEC[:, l, 7, :], func=AF.Exp, scale=-1.0), r=[VEC], w=[NSP])
            V(lambda e, l=l: e.tensor_scalar(out=NSP[:, l, :], in0=NSP[:, l, :], scalar1=1.0, scalar2=None, op0=ALU.add), r=[NSP], w=[NSP])
            A(lambda e, l=l: e.activation(out=NSP[:, l, :], in_=NSP[:, l, :], func=AF.Ln), r=[NSP], w=[NSP])
            V(lambda e, l=l: e.tensor_scalar(out=NSP[:, l, :], in0=NSP[:, l, :], scalar1=-8.0, scalar2=None, op0=ALU.mult), r=[NSP], w=[NSP])

        _chk(fw, 1)

        def rstd_from_ss(ss_ap, n, out_ap, bufs_r, buf_w):
            V(lambda e: e.tensor_scalar(out=out_ap, in0=ss_ap, scalar1=1.0 / n, scalar2=EPS, op0=ALU.mult, op1=ALU.add), r=bufs_r, w=[buf_w])
            A(lambda e: e.activation(out=out_ap, in_=out_ap, func=AF.Sqrt), r=[buf_w], w=[buf_w])
            V(lambda e: e.reciprocal(out=out_ap, in_=out_ap), r=[buf_w], w=[buf_w])

        def tr(out_ap, in_ap, np_, rbufs, wbuf):
            P(lambda e: e.transpose(out=out_ap, in_=in_ap, identity=IDENT[0:np_, 0:np_]), r=list(rbufs) + [IDENT], w=[wbuf])

        def prenorm(src_tm, NB, gi, l, dst=XN):
            for tb in range(NB):
                junk = A2
                A(lambda e, tb=tb: e.activation(out=junk.t[:, 0:1024], in_=src_tm(tb), func=AF.Square, accum_out=SM[:, tb:tb + 1]),
                  r=[X], w=[(junk, "junk"), (SM, "ss")])
                rstd_from_ss(SM[:, tb:tb + 1], 1024.0, SM[:, 8 + tb:9 + tb], [(SM, "ss")], (SM, "rs"))
                V(lambda e, tb=tb: e.tensor_scalar(out=junk.t[:, 1024:2048], in0=src_tm(tb), scalar1=SM[:, 8 + tb:9 + tb], scalar2=None,
                                                   op0=ALU.mult), r=[X, (SM, "rs")], w=[(junk, "xs")])
                pq = quad()
                for c in range(8):
                    tr(pq[:, c // 4, (c % 4) * 128:(c % 4) * 128 + 128], junk.t[:, 1024 + c * 128:1024 + (c + 1) * 128], 128, [(junk, "xs")], pq)
                src = pq.t[:, 0:2, :].rearrange("p a (b t) -> p (a b) t", b=4)
                V(lambda e, tb=tb, src=src: e.tensor_tensor(out=dst[:, :, tb * 128:(tb + 1) * 128], in0=src,
                                                            in1=GPRE[:, l, gi, :].unsqueeze(2).broadcast_to([128, 8, 128]), op=ALU.mult),
                  r=[pq, GPRE], w=[(dst, tb)])
            A2.collapse()

        _gb = [0]

        def out_proj(src_fm, slots, nk, gain_name, l, NB, src_key=None):
            A2.collapse()
            _gb[0] += 1
            gb = GBC[_gb[0] % 2]
            fw.dma(sp, lambda e: e.dma_start(out=gb[:, :], in_=D[gain_name][l:l + 1, :].broadcast_to([128, 1024])), writes=[gb])
            for tb in range(NB):
                pq = quad()
                for n in range(2):
                    for k in range(nk):
                        P(lambda e, n=n, k=k, tb=tb: e.matmul(pq[:, n, :], lhsT=src_fm[:, k, tb * 128:(tb + 1) * 128],
                                                             rhs=slots[k // 8][:, k % 8, n * 512:(n + 1) * 512], start=(k == 0), stop=(k == nk - 1)),
                          r=[(src_fm, src_key) if src_key is None else (src_fm, tb), slots[k // 8]], w=[pq])
                for n in range(2):
                    A(lambda e, n=n: e.activation(out=A2.t[:, 1024 + n * 512:1024 + (n + 1) * 512], in_=pq[:, n, :], func=AF.Copy), r=[pq], w=[(A2, "ycp")])
                    A(lambda e, n=n: e.activation(out=A2.t[:, 0:512], in_=A2.t[:, 1024 + n * 512:1024 + (n + 1) * 512], func=AF.Square, accum_out=SM[:, 16 + n:17 + n]),
                      r=[(A2, "ycp")], w=[(A2, "junk"), (SM, "ss2")])
                V(lambda e: e.tensor_tensor(out=SM[:, 18:19], in0=SM[:, 16:17], in1=SM[:, 17:18], op=ALU.add), r=[(SM, "ss2")], w=[(SM, "ss3")])
                rstd_from_ss(SM[:, 18:19], 1024.0, SM[:, 19:20], [(SM, "ss3")], (SM, "rs3"))
                for n in range(2):
                    V(lambda e, n=n: e.scalar_tensor_tensor(out=A2.t[:, 2048 + n * 512:2048 + (n + 1) * 512], in0=A2.t[:, 1024 + n * 512:1024 + (n + 1) * 512], scalar=SM[:, 19:20],
                                                            in1=gb[:, n * 512:(n + 1) * 512], op0=ALU.mult, op1=ALU.mult),
                      r=[(A2, "ycp"), (SM, "rs3"), gb], w=[(A2, "yn")])
                V(lambda e, tb=tb: e.tensor_tensor(out=X[:, tb, :], in0=X[:, tb, :], in1=A2.t[:, 2048:3072], op=ALU.add), r=[X, (A2, "yn")], w=[X])

        def _stg():
            si = fw.stream_idx.get(threading.get_ident())
            if si == 1:
                return 768, (STG, "b"), TST2
            if si == 0:
                return 0, (STG, "a"), TST
            return 0, (STG, None), TST

        def load_tm_to_fm(dram_ap, R, ncc, dst_fn, dst_bufs, sg=None):
            c0, sk, _ = _stg()
            fw.dma(sp, lambda e: e.dma_start(out=STG[0:R, c0:c0 + ncc * 128], in_=dram_ap), writes=[sk])
            for cc in range(ncc):
                pb, bk = bank()
                tr(pb[:, bk, 0:R], STG[0:R, c0 + cc * 128:c0 + (cc + 1) * 128], R, [sk], (pb, bk))
                srcp = pb[:, bk, 0:R] if sg is None else pb[:, bk, 0:R].rearrange("p (s w) -> p s w", s=sg)
                A(lambda e, cc=cc, srcp=srcp: e.activation(out=dst_fn(cc), in_=srcp, func=AF.Copy), r=[(pb, bk)], w=dst_bufs)

        def store_fm_to_tm(src_fn, src_bufs, R, ncc, dram_ap, sg=None):
            c0, sk, tst = _stg()
            for cc in range(ncc):
                pb, bk = bank()
                if sg is None:
                    tr(pb[0:R, bk, 0:128], src_fn(cc), 128, src_bufs, (pb, bk))
                else:
                    V(lambda e, cc=cc: e.tensor_copy(out=tst[:, 0:R].rearrange("p (s w) -> p s w", s=sg), in_=src_fn(cc)), r=src_bufs, w=[tst])
                    tr(pb[0:R, bk, 0:128], tst[:, 0:R], 128, [tst], (pb, bk))
                A(lambda e, cc=cc, pb=pb, bk=bk: e.activation(out=STG[0:R, c0 + cc * 128:c0 + (cc + 1) * 128], in_=pb[0:R, bk, 0:128], func=AF.Copy),
                  r=[(pb, bk)], w=[sk])
            fw.dma(sp, lambda e: e.dma_start(out=dram_ap, in_=STG[0:R, c0:c0 + ncc * 128]), reads=[sk], is_out=True)

        def conv_fm(xpv, ncc, T, W, wfn, bfn, outv, rbufs, wbuf):
            for cc in range(ncc):
                V(lambda e, cc=cc: e.tensor_scalar(out=outv(cc), in0=xpv(cc, 0, T), scalar1=wfn(cc, 0), scalar2=(bfn(cc) if bfn else None),
                                                   op0=ALU.mult, op1=(ALU.add if bfn else ALU.bypass)), r=rbufs, w=[wbuf])
                for j in range(1, W):
                    V(lambda e, cc=cc, j=j: e.scalar_tensor_tensor(out=outv(cc), in0=xpv(cc, j, T), scalar=wfn(cc, j), in1=outv(cc),
                                                                  op0=ALU.mult, op1=ALU.add), r=rbufs, w=[wbuf])

        for l in range(NL):
            fw.dma(sp, lambda e: e.dma_start(out=X[:, 0:2, :], in_=D["mem"].rearrange("(tb p) d -> p tb d", p=128)), writes=[X])
            prenorm(lambda tb: X[:, tb, :], 2, 3, l)
            slots = [load_w(D["w_xkv"][l][:, 0:1024], 1024), load_w(D["w_xkv"][l][:, 1024:2048], 1024)]
            for kv in range(2):
                for tb in range(2):
                    pq = quad()
                    for n in range(2):
                        for k in range(8):
                            P(lambda e, n=n, k=k, tb=tb, kv=kv: e.matmul(pq[:, n, :], lhsT=XN[:, k, tb * 128:(tb + 1) * 128],
                                                                        rhs=slots[kv][:, k, n * 512:(n + 1) * 512], start=(k == 0), stop=(k == 7)),
                              r=[(XN, tb), slots[kv]], w=[pq])
                    A(lambda e, pq=pq: e.activation(out=STG[:, :], in_=pq.t[:, 0:2, :].rearrange("p a b -> p (a b)"), func=AF.Copy), r=[pq], w=[STG])
                    dst = D["mk_p" if kv == 0 else "mv_p"][l, tb * 128:(tb + 1) * 128, :]
                    fw.dma(sp, lambda e, dst=dst: e.dma_start(out=dst, in_=STG[:, :]), reads=[STG], is_out=True)
        for sem, val in list(fw.out_events.values()):
            if not fw.dead:
                sp.h.wait_ge(sem, val)
                pool.h.wait_ge(sem, val)
        _chk(fw, 2)

        for (kind, ti) in tts:
            if kind == "p":
                ntok, NB, nseq, T = 512, 4, 1, 512
                xsrc = D["xp"][ti * 512:(ti + 1) * 512, :]
                ydst = D["y_p"][ti * 512:(ti + 1) * 512, :]
            else:
                ntok, NB, nseq, T = 128, 1, 16, 8
                xsrc = D["xs"]
                ydst = D["y_s"]
            first = (kind == "p" and ti == 0)
            last_p = (kind == "p" and ti == 3)
            fw.dma(sp, lambda e: e.dma_start(out=X[:, 0:NB, :], in_=xsrc.rearrange("(tb p) d -> p tb d", p=128)), writes=[X])

            for l in range(NL):
                prenorm(lambda tb: X[:, tb, :], NB, 0, l)
                w0 = load_w(D["w_in"][l][:, 0:768], 768, wid=(l, 0))
                w1 = load_w(D["w_in"][l][:, 768:1544], 776, wid=(l, 1))
                w2 = load_w(D["w_in"][l][:, 1544:2312], 768, wid=(l, 2))
                wo = load_w(D["w_out"][l], 1024, wid=(l, 3))
                for a_ in (A1, A2, A3):
                    a_.collapse()
                _chk(fw, 2.2)

                def xpview(arena, off, ncc, W):
                    sz = ncc * nseq * (W - 1 + T)
                    return arena.t[:, off:off + sz].rearrange("p (c s w) -> p c s w", c=ncc, s=nseq)

                def proj_fm(slot, col, M=128):
                    pb, bk = bank()
                    for k in range(8):
                        P(lambda e, k=k: e.matmul(pb[0:M, bk, 0:ntok], lhsT=slot[:, k, col:col + M], rhs=XN[:, k, 0:ntok], start=(k == 0), stop=(k == 7)),
                          r=[XN, slot], w=[(pb, bk)])
                    return pb, bk

                def ps3(pb, bk, M=128):
                    return pb[0:M, bk, 0:ntok].rearrange("p (s t) -> p s t", s=nseq)

                def tails_in(xv, TT_, st_name, W, ncc, key):
                    if kind == "p":
                        V(lambda e: e.tensor_copy(out=xv[:, :, 0, 0:W - 1], in_=TT_[:, l, :, :]), r=[TT_], w=[key])
                    else:
                        R_all = 16 * (W - 1)
                        ng = 1 if R_all <= 128 else R_all // 120
                        sg = 16 // ng
                        for g in range(ng):
                            R = sg * (W - 1)
                            load_tm_to_fm(D[st_name][l, g * R:(g + 1) * R, :], R, ncc,
                                          lambda cc, g=g: xv[:, cc, g * sg:(g + 1) * sg, 0:W - 1], [key], sg=sg)

                def tails_out(xv, TT_, out_p, out_s, W, ncc, key):
                    if kind == "p":
                        V(lambda e: e.tensor_copy(out=TT_[:, l, :, :], in_=xv[:, :, 0, T:T + W - 1]), r=[key], w=[TT_])
                        if last_p:
                            store_fm_to_tm(lambda cc: xv[:, cc, 0, T:T + W - 1], [key], W - 1, ncc, D[out_p][l])
                    else:
                        R_all = 16 * (W - 1)
                        ng = 1 if R_all <= 128 else R_all // 120
                        sg = 16 // ng
                        for g in range(ng):
                            R = sg * (W - 1)
                            store_fm_to_tm(lambda cc, g=g: xv[:, cc, g * sg:(g + 1) * sg, T:T + W - 1], [key], R, ncc,
                                           D[out_s][l, g * R:(g + 1) * R, :], sg=sg)

                C = 64 if kind == "p" else 8
                LV = 6 if kind == "p" else 3
                nbatch = ntok // (2 * C)
                flags = {}
                turn = [0]
                YAf_g = A1.t[:, 3100:3100 + 6 * ntok].rearrange("p (c n) -> p c n", c=6)
                kYA_g = (A1, "YA")

                def delta_batch(bt, AR, smo, sfx, SSv, SSk):
                    YAf, kYA = YAf_g, kYA_g
                    AR.collapse()
                    t0 = bt * 2 * C
                    tcol = [t0 + ci * C for ci in range(2)]
                    QKV = AR.t[0:C, 0:1536].rearrange("p (a n) -> p a n", a=2)
                    kQKV = (AR, "QKV")
                    pq = quad()
                    for ci in range(2):
                        for cc in range(6):
                            col = cc * 128
                            tr(pq[0:C, 2 * ci + col // 512, col % 512:col % 512 + 128], YAf[:, cc, tcol[ci]:tcol[ci] + C], 128, [kYA], pq)
                    src = pq.t[0:C, :, :].rearrange("p (a b) n -> p a (b n)", a=2)[:, :, 0:768]
                    A(lambda e, src=src: e.activation(out=QKV, in_=src, func=AF.Copy), r=[pq], w=[kQKV])
                    SQ = AR.t[0:C, 1536:2560].rearrange("p (g d) -> p g d", d=64)
                    QK3 = AR.t[0:C, 0:1536].rearrange("p (a n) -> p a n", a=2)[:, :, 0:512].rearrange("p a (g d) -> p a g d", d=64)
                    SQ4 = AR.t[0:C, 1536:2560].rearrange("p (a g d) -> p a g d", a=2, d=64)
                    V(lambda e: e.tensor_tensor(out=SQ4, in0=QK3, in1=QK3, op=ALU.mult), r=[kQKV], w=[(AR, "SQ")])
                    V(lambda e: e.tensor_reduce(out=SM[0:C, smo + 32:smo + 48], in_=SQ, axis=AX.X, op=ALU.add), r=[(AR, "SQ")], w=[(SM, "l2" + sfx)])
                    V(lambda e: e.tensor_scalar(out=SM[0:C, smo + 32:smo + 48], in0=SM[0:C, smo + 32:smo + 48], scalar1=EPS, scalar2=None, op0=ALU.add), r=[(SM, "l2" + sfx)], w=[(SM, "l2" + sfx)])
                    A(lambda e: e.activation(out=SM[0:C, smo + 32:smo + 48], in_=SM[0:C, smo + 32:smo + 48], func=AF.Sqrt), r=[(SM, "l2" + sfx)], w=[(SM, "l2" + sfx)])
                    V(lambda e: e.reciprocal(out=SM[0:C, smo + 32:smo + 48], in_=SM[0:C, smo + 32:smo + 48]), r=[(SM, "l2" + sfx)], w=[(SM, "l2" + sfx)])
                    l2v = SM[0:C, smo + 32:smo + 48].rearrange("p (a g) -> p a g", a=2)
                    V(lambda e: e.tensor_scalar(out=l2v[:, :, 0:4], in0=l2v[:, :, 0:4], scalar1=0.125, scalar2=None, op0=ALU.mult), r=[(SM, "l2" + sfx)], w=[(SM, "l2" + sfx)])
                    V(lambda e: e.tensor_tensor(out=QK3, in0=QK3, in1=l2v.unsqueeze(3).broadcast_to([C, 2, 8, 64]), op=ALU.mult), r=[kQKV, (SM, "l2" + sfx)], w=[kQKV])
                    QKF = AR.t[0:64, 1536:1536 + 16 * C].rearrange("p (a g t) -> p a g t", a=2, g=8)
                    pq = quad()
                    for ci in range(2):
                        for g in range(8):
                            tr(pq[0:64, ci, g * C:(g + 1) * C], QKV[:, ci, g * 64:(g + 1) * 64], C, [kQKV], pq)
                    A(lambda e, pq=pq: e.activation(out=QKF, in_=pq.t[0:64, 0:2, 0:8 * C].rearrange("p a (g t) -> p a g t", g=8), func=AF.Copy),
                      r=[pq, (AR, "SQ")], w=[(AR, "QKF")])
                    pb, bk = bank()
                    for ci in range(2):
                        for k in range(8):
                            P(lambda e, ci=ci, k=k: e.matmul(pb[0:C, bk, ci * 8:ci * 8 + 8], lhsT=XN[:, k, tcol[ci]:tcol[ci] + C], rhs=w1[:, k, 0:8],
                                                            start=(k == 0), stop=(k == 7)), r=[XN, w1], w=[(pb, bk)])
                    GBv = pb[0:C, bk, 0:16].rearrange("p (a g) -> p a g", a=2)
                    gg = SM[0:C, smo + 48:smo + 56].rearrange("p (a g) -> p a g", a=2)
                    be = SM[0:C, smo + 56:smo + 64].rearrange("p (a g) -> p a g", a=2)
                    V(lambda e: e.tensor_tensor(out=gg, in0=GBv[:, :, 0:4], in1=TOKC[0:C, l, 0:4].unsqueeze(1).broadcast_to([C, 2, 4]), op=ALU.add),
                      r=[(pb, bk), TOKC], w=[(SM, "gg" + sfx)])
                    A(lambda e: e.activation(out=be, in_=GBv[:, :, 4:8], func=AF.Sigmoid), r=[(pb, bk)], w=[(SM, "be" + sfx)])
                    V(lambda e: e.tensor_scalar(out=gg, in0=gg, scalar1=30.0, scalar2=None, op0=ALU.min), r=[(SM, "gg" + sfx)], w=[(SM, "gg" + sfx)])
                    A(lambda e: e.activation(out=gg, in_=gg, func=AF.Exp), r=[(SM, "gg" + sfx)], w=[(SM, "gg" + sfx)])
                    V(lambda e: e.tensor_scalar(out=gg, in0=gg, scalar1=1.0, scalar2=None, op0=ALU.add), r=[(SM, "gg" + sfx)], w=[(SM, "gg" + sfx)])
                    A(lambda e: e.activation(out=gg, in_=gg, func=AF.Ln), r=[(SM, "gg" + sfx)], w=[(SM, "gg" + sfx)])
                    V(lambda e: e.tensor_tensor(out=gg, in0=gg, in1=TOKC[0:C, l, 4:8].unsqueeze(1).broadcast_to([C, 2, 4]), op=ALU.mult),
                      r=[(SM, "gg" + sfx), TOKC], w=[(SM, "gg" + sfx)])
                    pb, bk = bank()
                    P(lambda e: e.matmul(pb[0:C, bk, 0:8], lhsT=TRIU[0:C, 0:C], rhs=SM[0:C, smo + 48:smo + 56], start=True, stop=True), r=[TRIU, (SM, "gg" + sfx)], w=[(pb, bk)])
                    P(lambda e: e.matmul(pb[0:64, bk, 8:16], lhsT=ONES[0:C, 0:64], rhs=SM[0:C, smo + 48:smo + 56], start=True, stop=True), r=[ONES, (SM, "gg" + sfx)], w=[(pb, bk)])
                    gc = SM[0:C, smo + 64:smo + 72]
                    gt = SM[0:64, smo + 72:smo + 80]
                    V(lambda e: e.tensor_copy(out=gc, in_=pb[0:C, bk, 0:8]), r=[(pb, bk)], w=[(SM, "gc" + sfx)])
                    V(lambda e: e.tensor_copy(out=gt, in_=pb[0:64, bk, 8:16]), r=[(pb, bk)], w=[(SM, "gt" + sfx)])
                    egc = SM[0:C, smo + 80:smo + 88]
                    egd = SM[0:C, smo + 88:smo + 96]
                    egt = SM[0:64, smo + 96:smo + 104]
                    kf = SM[0:C, smo + 104:smo + 112]
                    A(lambda e: e.activation(out=egc, in_=gc, func=AF.Exp), r=[(SM, "gc" + sfx)], w=[(SM, "egc" + sfx)])
                    A(lambda e: e.activation(out=egt, in_=gt, func=AF.Exp), r=[(SM, "gt" + sfx)], w=[(SM, "egt" + sfx)])
                    V(lambda e: e.tensor_tensor(out=egd, in0=SM[0:C, smo + 72:smo + 80], in1=gc, op=ALU.subtract), r=[(SM, "gt" + sfx), (SM, "gc" + sfx)], w=[(SM, "egd" + sfx)])
                    A(lambda e: e.activation(out=egd, in_=egd, func=AF.Exp), r=[(SM, "egd" + sfx)], w=[(SM, "egd" + sfx)])
                    V(lambda e: e.tensor_tensor(out=kf, in0=SM[0:C, smo + 56:smo + 64], in1=egc, op=ALU.mult), r=[(SM, "be" + sfx), (SM, "egc" + sfx)], w=[(SM, "kf" + sfx)])
                    CC8 = 8 * C

                    def u3(i):
                        return AR.t[0:C, i * 512:i * 512 + CC8].rearrange("p (g f) -> p g f", g=8)
                    DG = u3(5)
                    V(lambda e: e.tensor_tensor(out=DG, in0=IDENT[0:C, 0:C].unsqueeze(1).broadcast_to([C, 8, C]),
                                                in1=gc.unsqueeze(2).broadcast_to([C, 8, C]), op=ALU.mult), r=[IDENT, (SM, "gc" + sfx)], w=[(AR, "u5")])
                    pb, bk = bank()
                    for g in range(8):
                        P(lambda e, g=g: e.matmul(pb[0:C, bk, g * C:(g + 1) * C], lhsT=ONES[0:C, 0:C], rhs=DG[:, g, :], start=True, stop=True),
                          r=[ONES, (AR, "u5")], w=[(pb, bk)])
                    EE = u3(6)
                    V(lambda e: e.tensor_tensor(out=EE, in0=pb[0:C, bk, 0:CC8].rearrange("p (g f) -> p g f", g=8), in1=gc.unsqueeze(2).broadcast_to([C, 8, C]),
                                                op=ALU.subtract), r=[(pb, bk), (SM, "gc" + sfx)], w=[(AR, "u6")])
                    A(lambda e: e.activation(out=EE, in_=EE, func=AF.Abs), r=[(AR, "u6")], w=[(AR, "u6")])
                    A(lambda e: e.activation(out=EE, in_=EE, func=AF.Exp, scale=-1.0), r=[(AR, "u6")], w=[(AR, "u6")])
                    EN = u3(5)
                    EQ = u3(7)
                    V(lambda e: e.tensor_tensor(out=EN, in0=EE, in1=NEGSL[0:C, 0:C].unsqueeze(1).broadcast_to([C, 8, C]), op=ALU.mult),
                      r=[(AR, "u6"), NEGSL], w=[(AR, "u5")])
                    V(lambda e: e.tensor_tensor(out=EQ, in0=EE, in1=TRIU[0:C, 0:C].unsqueeze(1).broadcast_to([C, 8, C]), op=ALU.mult),
                      r=[(AR, "u6"), TRIU], w=[(AR, "u7")])
                    pbk, bkk = bank()
                    pbq, bkq = bank()
                    for ci in range(2):
                        for h in range(4):
                            g = ci * 4 + h
                            P(lambda e, ci=ci, h=h, g=g: e.matmul(pbk[0:C, bkk, g * C:(g + 1) * C], lhsT=QKF[:, ci, 4 + h, :], rhs=QKF[:, ci, 4 + h, :],
                                                                 start=True, stop=True), r=[(AR, "QKF")], w=[(pbk, bkk)])
                            P(lambda e, ci=ci, h=h, g=g: e.matmul(pbq[0:C, bkq, g * C:(g + 1) * C], lhsT=QKF[:, ci, 4 + h, :], rhs=QKF[:, ci, h, :],
                                                                 start=True, stop=True), r=[(AR, "QKF")], w=[(pbq, bkq)])
                    NN = u3(6)
                    V(lambda e: e.tensor_tensor(out=NN, in0=pbk[0:C, bkk, 0:CC8].rearrange("p (g f) -> p g f", g=8), in1=EN, op=ALU.mult),
                      r=[(pbk, bkk), (AR, "u5")], w=[(AR, "u6")])
                    V(lambda e: e.tensor_tensor(out=NN, in0=NN, in1=SM[0:C, smo + 56:smo + 64].unsqueeze(2).broadcast_to([C, 8, C]), op=ALU.mult),
                      r=[(AR, "u6"), (SM, "be" + sfx)], w=[(AR, "u6")])
                    QKT = u3(7)
                    V(lambda e: e.tensor_tensor(out=QKT, in0=pbq[0:C, bkq, 0:CC8].rearrange("p (g f) -> p g f", g=8), in1=EQ, op=ALU.mult),
                      r=[(pbq, bkq), (AR, "u7")], w=[(AR, "u7")])
                    pb, bk = bank()
                    for g in range(8):
                        tr(pb[0:C, bk, g * C:(g + 1) * C], NN[:, g, :], C, [(AR, "u6")], (pb, bk))
                    MM = u3(5)
                    A(lambda e, pb=pb, bk=bk: e.activation(out=MM, in_=pb[0:C, bk, 0:CC8].rearrange("p (g f) -> p g f", g=8), func=AF.Copy), r=[(pb, bk)], w=[(AR, "u5")])
                    UU = u3(8)
                    V(lambda e: e.tensor_tensor(out=UU, in0=MM, in1=IDENT[0:C, 0:C].unsqueeze(1).broadcast_to([C, 8, C]), op=ALU.add), r=[(AR, "u5"), IDENT], w=[(AR, "u8")])
                    cur = {"N": (NN, "u6"), "M": (MM, "u5"), "U": (UU, "u8")}
                    free = [(u3(9), "u9"), (u3(10), "u10"), (u3(11), "u11")]
                    for lev in range(1, LV):
                        (Nv, Nk), (Mv, Mk), (Uv, Uk) = cur["N"], cur["M"], cur["U"]
                        (N2, N2k) = free.pop(0)
                        pb, bk = bank()
                        for g in range(8):
                            P(lambda e, g=g, Mv=Mv, Nv=Nv: e.matmul(pb[0:C, bk, g * C:(g + 1) * C], lhsT=Mv[:, g, :], rhs=Nv[:, g, :], start=True, stop=True),
                              r=[(AR, Mk), (AR, Nk)], w=[(pb, bk)])
                        A(lambda e, pb=pb, bk=bk, N2=N2: e.activation(out=N2, in_=pb[0:C, bk, 0:CC8].rearrange("p (g f) -> p g f", g=8), func=AF.Copy),
                          r=[(pb, bk)], w=[(AR, N2k)])
                        if lev < LV - 1:
                            (M2, M2k) = free.pop(0)
                            pb2, bk2 = bank()
                            for g in range(8):
                                P(lambda e, g=g, Mv=Mv, Nv=Nv: e.matmul(pb2[0:C, bk2, g * C:(g + 1) * C], lhsT=Nv[:, g, :], rhs=Mv[:, g, :], start=True, stop=True),
                                  r=[(AR, Mk), (AR, Nk)], w=[(pb2, bk2)])
                            A(lambda e, pb2=pb2, bk2=bk2, M2=M2: e.activation(out=M2, in_=pb2[0:C, bk2, 0:CC8].rearrange("p (g f) -> p g f", g=8), func=AF.Copy),
                              r=[(pb2, bk2)], w=[(AR, M2k)])
                        (U2, U2k) = free.pop(0)
                        pb3, bk3 = bank()
                        for g in range(8):
                            P(lambda e, g=g, N2=N2, Uv=Uv: e.matmul(pb3[0:C, bk3, g * C:(g + 1) * C], lhsT=N2[:, g, :], rhs=Uv[:, g, :], start=True, stop=True),
                              r=[(AR, N2k), (AR, Uk)], w=[(pb3, bk3)])
                        V(lambda e, pb3=pb3, bk3=bk3, U2=U2, Uv=Uv: e.tensor_tensor(out=U2, in0=pb3[0:C, bk3, 0:CC8].rearrange("p (g f) -> p g f", g=8), in1=Uv, op=ALU.add),
                          r=[(pb3, bk3), (AR, Uk)], w=[(AR, U2k)])
                        free.append((Nv, Nk))
                        free.append((Uv, Uk))
                        if lev < LV - 1:
                            free.append((Mv, Mk))
                            cur = {"N": (N2, N2k), "M": (M2, M2k), "U": (U2, U2k)}
                        else:
                            cur = {"N": (N2, N2k), "M": (Mv, Mk), "U": (U2, U2k)}
                    (UU, Uk) = cur["U"]
                    used = {Uk, "u7"}
                    avail = [i for i in (5, 6, 8, 9, 10, 11) if "u%d" % i not in used]
                    QKV4 = AR.t[0:C, 0:1536].rearrange("p (a n) -> p a n", a=2)

                    def part4(o):
                        return QKV4[:, :, o:o + 256].rearrange("p a (h d) -> p a h d", h=4)
                    Kt, Vt = part4(256), part4(512)

                    def bc4(smap):
                        return smap.rearrange("p (a h) -> p a h", a=2).unsqueeze(3).broadcast_to([C, 2, 4, 64])
                    V(lambda e: e.tensor_tensor(out=Vt, in0=Vt, in1=bc4(SM[0:C, smo + 56:smo + 64]), op=ALU.mult), r=[kQKV, (SM, "be" + sfx)], w=[kQKV])
                    iK = avail.pop(0)
                    KBG = AR.t[0:C, iK * 512:iK * 512 + 512].rearrange("p (a h d) -> p a h d", a=2, h=4)
                    V(lambda e: e.tensor_tensor(out=KBG, in0=Kt, in1=bc4(kf), op=ALU.mult), r=[kQKV, (SM, "kf" + sfx)], w=[(AR, "u%d" % iK)])
                    V(lambda e: e.tensor_tensor(out=Kt, in0=Kt, in1=bc4(egd), op=ALU.mult), r=[kQKV, (SM, "egd" + sfx)], w=[kQKV])
                    pbv, bkv = bank()
                    pbw, bkw = bank()
                    for ci in range(2):
                        for h in range(4):
                            g = ci * 4 + h
                            P(lambda e, ci=ci, h=h, g=g: e.matmul(pbv[0:C, bkv, g * 64:(g + 1) * 64], lhsT=UU[:, g, :], rhs=Vt[:, ci, h, :], start=True, stop=True),
                              r=[(AR, Uk), kQKV], w=[(pbv, bkv)])
                            P(lambda e, ci=ci, h=h, g=g: e.matmul(pbw[0:64, bkw, g * C:(g + 1) * C], lhsT=KBG[:, ci, h, :], rhs=UU[:, g, :], start=True, stop=True),
                              r=[(AR, Uk), (AR, "u%d" % iK)], w=[(pbw, bkw)])
                    iV = avail.pop(0)
                    iW = avail.pop(0)
                    WV = AR.t[0:C, iV * 512:iV * 512 + 512].rearrange("p (a h d) -> p a h d", a=2, h=4)
                    WKT = AR.t[0:64, iW * 512:iW * 512 + CC8].rearrange("p (a h t) -> p a h t", a=2, h=4)
                    A(lambda e: e.activation(out=WV, in_=pbv[0:C, bkv, :].rearrange("p (a h d) -> p a h d", a=2, h=4), func=AF.Copy), r=[(pbv, bkv)], w=[(AR, "u%d" % iV)])
                    A(lambda e: e.activation(out=WKT, in_=pbw[0:64, bkw, 0:CC8].rearrange("p (a h t) -> p a h t", a=2, h=4), func=AF.Copy),
                      r=[(pbw, bkw)], w=[(AR, "u%d" % iW)])
                    if kind == "p":
                        fw.wait_until(lambda: turn[0] == bt)
                    iO = avail.pop(0)
                    OO = AR.t[0:C, iO * 512:iO * 512 + 512].rearrange("p (a h d) -> p a h d", a=2, h=4)
                    iU = avail.pop(0)
                    UT = AR.t[0:C, iU * 512:iU * 512 + 512].rearrange("p (a h d) -> p a h d", a=2, h=4)
                    SS = SSv
                    for ci in range(2):
                        if kind == "p":
                            Sv, Sk = SST[:, l, :, :], SST
                        else:
                            seq = bt * 2 + ci
                            Sv, Sk = SS, SSk
                            fw.dma(sp, lambda e, seq=seq: e.dma_start(out=SS, in_=D["st_delta"][l, seq].rearrange("h d e -> d h e")), writes=[Sk])
                        pb, bk = bank()
                        for h in range(4):
                            P(lambda e, ci=ci, h=h: e.matmul(pb[0:C, bk, h * 64:(h + 1) * 64], lhsT=WKT[:, ci, h, :], rhs=Sv[:, h, :], start=True, stop=True),
                              r=[(AR, "u%d" % iW), Sk], w=[(pb, bk)])
                        for h in range(4):
                            P(lambda e, ci=ci, h=h: e.matmul(pb[0:C, bk, 256 + h * 64:256 + (h + 1) * 64], lhsT=QKF[:, ci, h, :], rhs=Sv[:, h, :], start=True, stop=True),
                              r=[(AR, "QKF"), Sk], w=[(pb, bk)])
                        V(lambda e, ci=ci, pb=pb, bk=bk: e.tensor_tensor(out=UT[:, 0, :, :], in0=WV[:, ci, :, :], in1=pb[0:C, bk, 0:256].rearrange("p (h d) -> p h d", h=4),
                                                                        op=ALU.subtract), r=[(AR, "u%d" % iV), (pb, bk)], w=[(AR, "UTu")])
                        V(lambda e, ci=ci, pb=pb, bk=bk: e.tensor_tensor(out=UT[:, 1, :, :], in0=pb[0:C, bk, 256:512].rearrange("p (h d) -> p h d", h=4),
                                                                        in1=egc[:, ci * 4:ci * 4 + 4].unsqueeze(2).broadcast_to([C, 4, 64]), op=ALU.mult),
                          r=[(pb, bk), (SM, "egc" + sfx)], w=[(AR, "UTt")])
                        pb2, bk2 = bank()
                        for h in range(4):
                            P(lambda e, ci=ci, h=h: e.matmul(pb2[0:C, bk2, h * 64:(h + 1) * 64], lhsT=QKT[:, ci * 4 + h, :], rhs=UT[:, 0, h, :], start=True, stop=True),
                              r=[(AR, "u7"), (AR, "UTu")], w=[(pb2, bk2)])
                        for h in range(4):
                            P(lambda e, ci=ci, h=h: e.matmul(pb2[0:64, bk2, 256 + h * 64:256 + (h + 1) * 64], lhsT=Kt[:, ci, h, :], rhs=UT[:, 0, h, :], start=True, stop=True),
                              r=[kQKV, (AR, "UTu")], w=[(pb2, bk2)])
                        V(lambda e, ci=ci, pb2=pb2, bk2=bk2: e.tensor_tensor(out=OO[:, ci, :, :], in0=pb2[0:C, bk2, 0:256].rearrange("p (h d) -> p h d", h=4), in1=UT[:, 1, :, :],
                                                                            op=ALU.add), r=[(pb2, bk2), (AR, "UTt")], w=[(AR, "OO")])
                        V(lambda e, ci=ci, Sv=Sv: e.tensor_tensor(out=Sv, in0=Sv, in1=egt[:, ci * 4:ci * 4 + 4].unsqueeze(2).broadcast_to([64, 4, 64]), op=ALU.mult),
                          r=[Sk, (SM, "egt" + sfx)], w=[Sk])
                        V(lambda e, ci=ci, Sv=Sv, pb2=pb2, bk2=bk2: e.tensor_tensor(out=Sv, in0=Sv, in1=pb2[0:64, bk2, 256:512].rearrange("p (h d) -> p h d", h=4), op=ALU.add),
                          r=[Sk, (pb2, bk2)], w=[Sk])
                        if kind == "s":
                            fw.dma(sp, lambda e, seq=seq: e.dma_start(out=D["delta_s"][l, seq].rearrange("h d e -> d h e"), in_=SS), reads=[Sk], is_out=True)
                    if kind == "p":
                        turn[0] = bt + 1
                    iQ = avail.pop(0) if avail else iK
                    OS = AR.t[0:C, iK * 512:iK * 512 + 512].rearrange("p (g d) -> p g d", g=8)
                    OOg = AR.t[0:C, iO * 512:iO * 512 + 512].rearrange("p (g d) -> p g d", g=8)
                    V(lambda e: e.tensor_tensor(out=OS, in0=OOg, in1=OOg, op=ALU.mult), r=[(AR, "OO")], w=[(AR, "u%d" % iK)])
                    V(lambda e: e.tensor_reduce(out=SM[0:C, smo + 112:smo + 120], in_=OS, axis=AX.X, op=ALU.add), r=[(AR, "u%d" % iK)], w=[(SM, "os" + sfx)])
                    rstd_from_ss(SM[0:C, smo + 112:smo + 120], 64.0, SM[0:C, smo + 120:smo + 128], [(SM, "os" + sfx)], (SM, "ors" + sfx))
                    V(lambda e: e.tensor_tensor(out=OOg, in0=OOg, in1=SM[0:C, smo + 120:smo + 128].unsqueeze(2).broadcast_to([C, 8, 64]), op=ALU.mult), r=[(AR, "OO"), (SM, "ors" + sfx)], w=[(AR, "OO")])
                    V(lambda e: e.tensor_tensor(out=OOg, in0=OOg, in1=TOKC[0:C, l, 8:72].unsqueeze(1).broadcast_to([C, 8, 64]), op=ALU.mult), r=[(AR, "OO"), TOKC], w=[(AR, "OO")])
                    pb, bk = bank()
                    for ci in range(2):
                        for k in range(8):
                            P(lambda e, ci=ci, k=k: e.matmul(pb[0:C, bk, ci * 256:(ci + 1) * 256], lhsT=XN[:, k, tcol[ci]:tcol[ci] + C], rhs=w1[:, k, 8:264],
                                                            start=(k == 0), stop=(k == 7)), r=[XN, w1], w=[(pb, bk)])
                    GTv = AR.t[0:C, iK * 512:iK * 512 + 512]
                    A(lambda e, pb=pb, bk=bk: e.activation(out=GTv, in_=pb[0:C, bk, :], func=AF.Silu), r=[(pb, bk)], w=[(AR, "u%d" % iK)])
                    OOf = AR.t[0:C, iO * 512:iO * 512 + 512]
                    V(lambda e: e.tensor_tensor(out=OOf, in0=OOf, in1=GTv, op=ALU.mult), r=[(AR, "OO"), (AR, "u%d" % iK)], w=[(AR, "OO")])
                    pb, bk = bank()
                    for ci in range(2):
                        for cc in range(2):
                            tr(pb[:, bk, (cc * 2 + ci) * C:(cc * 2 + ci + 1) * C], OOf[:, ci * 256 + cc * 128:ci * 256 + (cc + 1) * 128], C, [(AR, "OO")], (pb, bk))
                    A(lambda e, pb=pb, bk=bk: e.activation(out=MIX[:, 0:2, t0:t0 + 2 * C], in_=pb[:, bk, 0:4 * C].rearrange("p (c n) -> p c n", c=2), func=AF.Copy),
                      r=[(pb, bk)], w=[(MIX, "A%d" % bt)])

                def stream_bcd():
                    XB = xpview(A2, 0, 2, 31)
                    kXB = (A2, "XB")
                    tails_in(XB, TB, "st_bconv", 31, 2, kXB)
                    for cc in range(2):
                        pb1, bk1 = proj_fm(w1, 264 + cc * 128)
                        pb2, bk2 = proj_fm(w1, 264 + 256 + cc * 128)
                        A(lambda e: e.activation(out=A2.t[:, 4500:4500 + ntok], in_=pb2[:, bk2, 0:ntok], func=AF.Sigmoid), r=[(pb2, bk2)], w=[(A2, "sg")])
                        V(lambda e, cc=cc: e.tensor_tensor(out=XB[:, cc, :, 30:30 + T], in0=ps3(pb1, bk1),
                                                           in1=A2.t[:, 4500:4500 + ntok].rearrange("p (s t) -> p s t", s=nseq), op=ALU.mult),
                          r=[(pb1, bk1), (A2, "sg")], w=[kXB])
                    _chk(fw, 2.3)
                    tails_out(XB, TB, "conf_p", "conf_s", 31, 2, kXB)
                    _chk(fw, 2.35)
                    YB = A2.t[:, 1300:1300 + 2 * ntok].rearrange("p (c s t) -> p c s t", c=2, s=nseq)
                    kYB = (A2, "YB")
                    conv_fm(lambda cc, j, T_: XB[:, cc, :, j:j + T_], 2, T, 31, lambda cc, j: DWB[:, l, cc, j:j + 1], lambda cc: VEC[:, l, 0, cc:cc + 1],
                            lambda cc: YB[:, cc, :, :], [kXB, DWB, VEC], kYB)
                    YBf = A2.t[:, 1300:1300 + 2 * ntok].rearrange("p (c n) -> p c n", c=2)
                    _chk(fw, 2.4)
                    for cc in range(2):
                        sq = A2.t[:, 2400:2400 + ntok]
                        A(lambda e, cc=cc: e.activation(out=sq, in_=YBf[:, cc, :], func=AF.Square), r=[kYB], w=[(A2, "sqB")])
                        pbs, bks = bank()
                        P(lambda e, cc=cc: e.matmul(pbs[:, bks, 0:ntok], lhsT=BONES[:, :], rhs=YBf[:, cc, :], start=True, stop=True), r=[BONES, kYB], w=[(pbs, bks)])
                        pbq, bkq = bank()
                        P(lambda e: e.matmul(pbq[:, bkq, 0:ntok], lhsT=BONES[:, :], rhs=sq, start=True, stop=True), r=[BONES, (A2, "sqB")], w=[(pbq, bkq)])
                        _chk(fw, 2.42)
                        dd = A2.t[:, 3000:3000 + ntok]
                        msq = A2.t[:, 3600:3600 + ntok]
                        V(lambda e, cc=cc: e.scalar_tensor_tensor(out=dd, in0=pbs[:, bks, 0:ntok], scalar=-1.0 / 64, in1=YBf[:, cc, :], op0=ALU.mult, op1=ALU.add),
                          r=[(pbs, bks), kYB], w=[(A2, "ddB")])
                        _chk(fw, 2.43)
                        V(lambda e: e.tensor_scalar(out=msq, in0=pbs[:, bks, 0:ntok], scalar1=1.0 / 64, scalar2=None, op0=ALU.mult), r=[(pbs, bks)], w=[(A2, "msqB")])
                        V(lambda e: e.tensor_tensor(out=msq, in0=msq, in1=msq, op=ALU.mult), r=[(A2, "msqB")], w=[(A2, "msqB")])
                        _chk(fw, 2.435)
                        V(lambda e: e.scalar_tensor_tensor(out=msq, in0=pbq[:, bkq, 0:ntok], scalar=1.0 / 64, in1=msq, op0=ALU.mult, op1=ALU.subtract),
                          r=[(pbq, bkq), (A2, "msqB")], w=[(A2, "msqB")])
                        _chk(fw, 2.44)
                        V(lambda e: e.tensor_scalar(out=msq, in0=msq, scalar1=EPS, scalar2=None, op0=ALU.add), r=[(A2, "msqB")], w=[(A2, "msqB")])
                        A(lambda e: e.activation(out=msq, in_=msq, func=AF.Sqrt), r=[(A2, "msqB")], w=[(A2, "msqB")])
                        _chk(fw, 2.45)
                        V(lambda e: e.reciprocal(out=msq, in_=msq), r=[(A2, "msqB")], w=[(A2, "msqB")])
                        V(lambda e: e.tensor_tensor(out=dd, in0=dd, in1=msq, op=ALU.mult), r=[(A2, "ddB"), (A2, "msqB")], w=[(A2, "ddB")])
                        _chk(fw, 2.46)
                        yoff = 5100 if cc == 0 else 4200
                        ynb = A2.t[:, yoff:yoff + ntok // 2].bitcast(BF16)
                        A(lambda e, cc=cc, ynb=ynb: e.activation(out=ynb, in_=dd, func=AF.Silu, scale=VEC[:, l, 1, cc:cc + 1], bias=VEC[:, l, 2, cc:cc + 1]),
                          r=[(A2, "ddB"), VEC], w=[(A2, "ynB%d" % cc)])
                    _chk(fw, 2.5)
                    yn_c = [A2.t[:, 5100:5100 + ntok // 2].bitcast(BF16), A2.t[:, 4200:4200 + ntok // 2].bitcast(BF16)]
                    for oc in range(2):
                        pb, bk = bank()
                        for kc in range(2):
                            P(lambda e, kc=kc, oc=oc: e.matmul(pb[:, bk, 0:ntok], lhsT=WPW[:, l, kc, oc * 128:(oc + 1) * 128], rhs=yn_c[kc], start=(kc == 0), stop=(kc == 1)),
                              r=[WPW, (A2, "ynB0"), (A2, "ynB1")], w=[(pb, bk)])
                        A(lambda e, oc=oc: e.activation(out=MIX[:, 2 + oc, 0:ntok], in_=pb[:, bk, 0:ntok], func=AF.Copy), r=[(pb, bk)], w=[(MIX, "c%d" % (2 + oc))])

                    _chk(fw, 3)
                    A2.collapse()
                    XC = xpview(A2, 0, 2, 16)
                    kXC = (A2, "XC")
                    tails_in(XC, TC, "st_pool", 16, 2, kXC)
                    for cc in range(2):
                        pb, bk = proj_fm(w2, cc * 128)
                        A(lambda e, cc=cc: e.activation(out=XC[:, cc, :, 15:15 + T], in_=ps3(pb, bk), func=AF.Copy), r=[(pb, bk)], w=[kXC])
                    tails_out(XC, TC, "pool_p", "pool_s", 16, 2, kXC)
                    LW = 15 + T

                    def sview(off, ln):
                        return A2.t[:, off:off + 2 * nseq * ln].rearrange("p (c s w) -> p c s w", c=2, s=nseq)
                    S2 = sview(1100, LW - 1)
                    S4 = sview(2200, LW - 3)
                    S8 = sview(3300, LW - 7)
                    S16 = sview(4400, LW - 15)
                    V(lambda e: e.tensor_tensor(out=S2, in0=XC[:, :, :, 1:LW], in1=XC[:, :, :, 0:LW - 1], op=ALU.add), r=[kXC], w=[(A2, "S2")])
                    V(lambda e: e.tensor_tensor(out=S4, in0=S2[:, :, :, 2:LW - 1], in1=S2[:, :, :, 0:LW - 3], op=ALU.add), r=[(A2, "S2")], w=[(A2, "S4")])
                    V(lambda e: e.tensor_tensor(out=S8, in0=S4[:, :, :, 4:LW - 3], in1=S4[:, :, :, 0:LW - 7], op=ALU.add), r=[(A2, "S4")], w=[(A2, "S8")])
                    V(lambda e: e.tensor_tensor(out=S16, in0=S8[:, :, :, 8:LW - 7], in1=S8[:, :, :, 0:LW - 15], op=ALU.add), r=[(A2, "S8")], w=[(A2, "S16")])
                    SEL = A2.t[:, 5500:5500 + 2 * ntok].rearrange("p (c s t) -> p c s t", c=2, s=nseq)
                    srcs = {(0, 0): (S2, 14, "S2"), (0, 1): (S4, 12, "S4"), (1, 0): (S8, 8, "S8"), (1, 1): (S16, 0, "S16")}
                    for (cc, hf), (sv, o, nm) in srcs.items():
                        ps_ = slice(64 * hf, 64 * hf + 64)
                        wv = float(WIN[cc][hf])
                        V(lambda e, cc=cc, ps_=ps_, sv=sv, o=o, wv=wv: e.tensor_scalar(out=SEL[ps_, cc, :, :], in0=sv[ps_, cc, :, o:o + T], scalar1=1.0 / wv,
                                                                                      scalar2=None, op0=ALU.mult), r=[(A2, nm)], w=[(A2, "SEL")])
                        if first:
                            V(lambda e, cc=cc, ps_=ps_, sv=sv, o=o: e.tensor_tensor(out=SEL[ps_, cc, 0, 0:16], in0=sv[ps_, cc, 0, o:o + 16],
                                                                                   in1=INVC0[ps_, cc, :], op=ALU.mult), r=[(A2, nm), INVC0], w=[(A2, "SEL")])
                    DB = A2.t[:, 1100:1100 + ntok].bitcast(BF16).rearrange("p (c s t) -> p c s t", c=2, s=nseq)
                    V(lambda e: e.tensor_tensor(out=DB, in0=SEL, in1=XC[:, :, :, 15:15 + T], op=ALU.subtract), r=[(A2, "SEL"), kXC, (A2, "S2")], w=[(A2, "S2")])
                    DBf = A2.t[:, 1100:1100 + ntok].bitcast(BF16).rearrange("p (c n) -> p c n", c=2)
                    for cc in range(2):
                        pb, bk = bank()
                        P(lambda e, cc=cc: e.matmul(pb[:, bk, 0:ntok], lhsT=WBD[:, l, 0, cc, :], rhs=DBf[:, cc, :], start=True, stop=True), r=[WBD, (A2, "S2")], w=[(pb, bk)])
                        A(lambda e, cc=cc: e.activation(out=MIX[:, 4 + cc, 0:ntok], in_=pb[:, bk, 0:ntok], func=AF.Copy, scale=VEC[:, l, 3, cc:cc + 1]),
                          r=[(pb, bk), VEC], w=[(MIX, "c%d" % (4 + cc))])

                    _chk(fw, 4)
                    A2.collapse()
                    XD = xpview(A2, 0, 2, 4)
                    kXD = (A2, "XD")
                    tails_in(XD, TD, "st_lconv", 4, 2, kXD)

                    def reg(i):
                        return A2.t[:, 1100 + i * 1024:1100 + i * 1024 + 2 * ntok].rearrange("p (c n) -> p c n", c=2)
                    GD, XR, RR, II, AA = reg(0), reg(1), reg(2), reg(3), reg(4)
                    for cc in range(2):
                        pb, bk = proj_fm(w2, 256 + cc * 128)
                        A(lambda e, cc=cc: e.activation(out=GD[:, cc, :], in_=pb[:, bk, 0:ntok], func=AF.Gelu_apprx_tanh), r=[(pb, bk)], w=[(A2, "GD")])
                        pb, bk = proj_fm(w2, 512 + cc * 128)
                        A(lambda e, cc=cc: e.activation(out=XD[:, cc, :, 3:3 + T], in_=ps3(pb, bk), func=AF.Copy), r=[(pb, bk)], w=[kXD])
                    tails_out(XD, TD, "lconv_p", "lconv_s", 4, 2, kXD)
                    XR4 = A2.t[:, 1100 + 1024:1100 + 1024 + 2 * ntok].rearrange("p (c s t) -> p c s t", c=2, s=nseq)
                    conv_fm(lambda cc, j, T_: XD[:, cc, :, j:j + T_], 2, T, 4, lambda cc, j: CWD[:, l, cc, j:j + 1], lambda cc: VEC[:, l, 4, cc:cc + 1],
                            lambda cc: XR4[:, cc, :, :], [kXD, CWD, VEC], (A2, "XR"))
                    XRB = A2.t[:, 6220:6220 + ntok].bitcast(BF16).rearrange("p (c n) -> p c n", c=2)
                    V(lambda e: e.tensor_copy(out=XRB, in_=XR), r=[(A2, "XR")], w=[(A2, "XRB")])
                    for cc in range(2):
                        for (wi, dstv, bi, nm) in ((1, RR, 5, "RR"), (2, II, 6, "II")):
                            pb, bk = bank()
                            P(lambda e, cc=cc, wi=wi: e.matmul(pb[:, bk, 0:ntok], lhsT=WBD[:, l, wi, cc, :], rhs=XRB[:, cc, :], start=True, stop=True),
                              r=[WBD, (A2, "XRB")], w=[(pb, bk)])
                            A(lambda e, cc=cc, dstv=dstv, bi=bi, pb=pb, bk=bk: e.activation(out=dstv[:, cc, :], in_=pb[:, bk, 0:ntok], func=AF.Sigmoid,
                                                                                       bias=VEC[:, l, bi, cc:cc + 1]), r=[(pb, bk), VEC], w=[(A2, nm)])
                        A(lambda e, cc=cc: e.activation(out=AA[:, cc, :], in_=RR[:, cc, :], func=AF.Exp, scale=NSP[:, l, cc:cc + 1]), r=[(A2, "RR"), NSP], w=[(A2, "AA")])
                    A(lambda e: e.activation(out=RR, in_=AA, func=AF.Square), r=[(A2, "AA")], w=[(A2, "RR")])
                    V(lambda e: e.tensor_scalar(out=RR, in0=RR, scalar1=-1.0, scalar2=1.0, op0=ALU.mult, op1=ALU.add), r=[(A2, "RR")], w=[(A2, "RR")])
                    A(lambda e: e.activation(out=RR, in_=RR, func=AF.Sqrt), r=[(A2, "RR")], w=[(A2, "RR")])
                    V(lambda e: e.tensor_tensor(out=II, in0=II, in1=XR, op=ALU.mult), r=[(A2, "II"), (A2, "XR")], w=[(A2, "II")])
                    V(lambda e: e.tensor_tensor(out=II, in0=II, in1=RR, op=ALU.mult), r=[(A2, "II"), (A2, "RR")], w=[(A2, "II")])
                    HH = XR
                    if kind == "p":
                        for cc in range(2):
                            V(lambda e, cc=cc: e.tensor_tensor_scan(out=HH[:, cc, :], data0=AA[:, cc, :], data1=II[:, cc, :], initial=HST[:, l, cc:cc + 1],
                                                                    op0=ALU.mult, op1=ALU.add), r=[(A2, "AA"), (A2, "II"), HST], w=[(A2, "XR")])
                        V(lambda e: e.tensor_copy(out=HST[:, l, :], in_=HH[:, :, T - 1]), r=[(A2, "XR")], w=[HST])
                        if last_p:
                            store_fm_to_tm(lambda cc: HST[:, l, cc:cc + 1], [HST], 1, 2, D["lh_p"][l])
                    else:
                        H0 = A2.t[:, 6740:6772].rearrange("p (c s) -> p c s", c=2)
                        load_tm_to_fm(D["st_lh"][l], 16, 2, lambda cc: H0[:, cc, :], [(A2, "H0")])
                        AA4 = A2.t[:, 1100 + 4 * 1024:1100 + 4 * 1024 + 2 * ntok].rearrange("p (c s t) -> p c s t", c=2, s=nseq)
                        II4 = A2.t[:, 1100 + 3 * 1024:1100 + 3 * 1024 + 2 * ntok].rearrange("p (c s t) -> p c s t", c=2, s=nseq)
                        V(lambda e: e.tensor_tensor(out=H0, in0=H0, in1=AA4[:, :, :, 0], op=ALU.mult), r=[(A2, "H0"), (A2, "AA")], w=[(A2, "H0")])
                        V(lambda e: e.tensor_tensor(out=II4[:, :, :, 0], in0=II4[:, :, :, 0], in1=H0, op=ALU.add), r=[(A2, "II"), (A2, "H0")], w=[(A2, "II")])
                        V(lambda e: e.tensor_scalar(out=AA4[:, :, :, 0], in0=AA4[:, :, :, 0], scalar1=0.0, scalar2=None, op0=ALU.mult), r=[(A2, "AA")], w=[(A2, "AA")])
                        for cc in range(2):
                            V(lambda e, cc=cc: e.tensor_tensor_scan(out=HH[:, cc, :], data0=AA[:, cc, :], data1=II[:, cc, :], initial=0.0,
                                                                    op0=ALU.mult, op1=ALU.add), r=[(A2, "AA"), (A2, "II")], w=[(A2, "XR")])
                        store_fm_to_tm(lambda cc: XR4[:, cc, :, T - 1], [(A2, "XR")], 16, 2, D["lh_s"][l])
                    V(lambda e: e.tensor_tensor(out=MIX[:, 6:8, 0:ntok], in0=GD, in1=HH, op=ALU.mult), r=[(A2, "GD"), (A2, "XR")], w=[(MIX, "c6"), (MIX, "c7")])


                    if INTERLEAVE:
                        A2.collapse()
                        fw.wait_until(lambda: "prep" in flags)
                        for bt in range(1, nbatch, 2):
                            delta_batch(bt, A2, 112, "_1", A2.t[0:64, 6144:6400].rearrange("p (h d) -> p h d", h=4), (A2, "SS"))

                def stream_a():
                    _chk(fw, 5)
                    XA = xpview(A1, 0, 6, 4)
                    kXA = (A1, "XA")
                    tails_in(XA, TA, "st_dconv", 4, 6, kXA)
                    for cc in range(6):
                        pb, bk = proj_fm(w0, cc * 128)
                        A(lambda e, cc=cc: e.activation(out=XA[:, cc, :, 3:3 + T], in_=ps3(pb, bk), func=AF.Copy), r=[(pb, bk)], w=[kXA])
                    tails_out(XA, TA, "dconv_p", "dconv_s", 4, 6, kXA)
                    YA = A1.t[:, 3100:3100 + 6 * ntok].rearrange("p (c s t) -> p c s t", c=6, s=nseq)
                    YAf = A1.t[:, 3100:3100 + 6 * ntok].rearrange("p (c n) -> p c n", c=6)
                    kYA = (A1, "YA")
                    conv_fm(lambda cc, j, T_: XA[:, cc, :, j:j + T_], 6, T, 4, lambda cc, j: CWA[:, l, cc, j:j + 1], None,
                            lambda cc: YA[:, cc, :, :], [kXA, CWA], kYA)
                    A(lambda e: e.activation(out=YAf, in_=YAf, func=AF.Silu), r=[kYA], w=[kYA])

                    flags["prep"] = 1
                    for bt in (range(0, nbatch, 2) if INTERLEAVE else range(nbatch)):
                        delta_batch(bt, A3, 0, "", SSB[:, :, :], SSB)

                if INTERLEAVE:
                    run_streams(fw, [stream_a, stream_bcd], [1, 1])
                else:
                    stream_bcd()
                    stream_a()
                if last_p:
                    fw.dma(sp, lambda e: e.dma_start(out=D["delta_p"][l].rearrange("h d e -> d h e"), in_=SST[:, l, :, :]), reads=[SST], is_out=True)
                _chk(fw, 6)
                MIX.collapse()
                out_proj(MIX, [wo], 8, "norm_mix_post", l, NB)

                _chk(fw, 7)
                for a_ in (A1, A2, A3):
                    a_.collapse()
                prenorm(lambda tb: X[:, tb, :], NB, 1, l)
                wq = load_w(D["w_xq"][l], 1024, wid=(l, 4))
                wxo = load_w(D["w_xo"][l], 1024, wid=(l, 5))
                QF = A1.t[:, 0:2048].bitcast(BF16).rearrange("p (c n) -> p c n", c=8)
                for c in range(8):
                    pb, bk = proj_fm(wq, c * 128)
                    A(lambda e, c=c, pb=pb, bk=bk: e.activation(out=QF[:, c, 0:ntok], in_=pb[:, bk, 0:ntok], func=AF.Copy, scale=1.0 / 16.0), r=[(pb, bk)], w=[(A1, "QF")])
                KS = A1.t[:, 2048:4096].rearrange("p (a n) -> p a n", a=2)
                KTS = A2.t[:, 0:1024].bitcast(BF16).rearrange("p (c m) -> p c m", c=8)
                VS = A2.t[:, 1024:2048].bitcast(BF16).rearrange("p (a n) -> p a n", a=2)
                PEX = A1.t[:, 4096:5120].rearrange("p (h m) -> p h m", h=4)
                PT = A2.t[:, 2048:2560].bitcast(BF16).rearrange("p (c t) -> p c t", c=8)
                QPAD = A2.t[:, 2560:4608].bitcast(BF16).rearrange("p (c s t) -> p c s t", c=2, s=16)

                def load_kv(kap, vap):
                    fw.dma(sp, lambda e: e.dma_start(out=KS, in_=kap.rearrange("(a p) n -> p a n", p=128)), writes=[(A1, "KS")])
                    fw.dma(pool, lambda e: e.dma_start(out=VS, in_=vap.rearrange("(a p) n -> p a n", p=128)), writes=[(A2, "VS")])
                    pq = quad()
                    for a in range(2):
                        for c in range(8):
                            tr(pq[:, c // 2, (c % 2) * 256 + a * 128:(c % 2) * 256 + a * 128 + 128], KS[:, a, c * 128:(c + 1) * 128], 128, [(A1, "KS")], pq)
                    A(lambda e, pq=pq: e.activation(out=KTS, in_=pq.t[:, :, :].rearrange("p b (c m) -> p (b c) m", c=2), func=AF.Copy), r=[pq], w=[(A2, "KTS")])

                def softmax_pv(pqs, col0, ncol, vfn):
                    sc = pqs.t[:, :, 0:256]
                    V(lambda e: e.tensor_reduce(out=SM[:, 128:132], in_=sc, axis=AX.X, op=ALU.max), r=[pqs], w=[(SM, "mx")])
                    V(lambda e: e.tensor_scalar(out=SM[:, 128:132], in0=SM[:, 128:132], scalar1=-1.0, scalar2=None, op0=ALU.mult), r=[(SM, "mx")], w=[(SM, "mx")])
                    for h in range(4):
                        A(lambda e, h=h: e.activation(out=PEX[:, h, :], in_=pqs[:, h, 0:256], func=AF.Exp, bias=SM[:, 128 + h:129 + h], accum_out=SM[:, 132 + h:133 + h]),
                          r=[pqs, (SM, "mx")], w=[(A1, "PEX"), (SM, "sm")])
                    V(lambda e: e.reciprocal(out=SM[:, 132:136], in_=SM[:, 132:136]), r=[(SM, "sm")], w=[(SM, "sm")])
                    V(lambda e: e.tensor_tensor(out=PEX, in0=PEX, in1=SM[:, 132:136].unsqueeze(2).broadcast_to([128, 4, 256]), op=ALU.mult), r=[(A1, "PEX"), (SM, "sm")], w=[(A1, "PEX")])
                    pq2 = quad()
                    for h in range(4):
                        for a in range(2):
                            c = h * 2 + a
                            tr(pq2[:, c // 4, (c % 4) * 128:(c % 4) * 128 + 128], PEX[:, h, a * 128:(a + 1) * 128], 128, [(A1, "PEX")], pq2)
                    A(lambda e, pq2=pq2: e.activation(out=PT, in_=pq2.t[:, 0:2, :].rearrange("p b (c t) -> p (b c) t", c=4), func=AF.Copy), r=[pq2], w=[(A2, "PT")])
                    vfn()

                if kind == "p":
                    load_kv(D["mk_p"][l], D["mv_p"][l])
                    for tb in range(NB):
                        pqs = quad()
                        for h in range(4):
                            for dc in range(2):
                                P(lambda e, h=h, dc=dc, tb=tb: e.matmul(pqs[:, h, 0:256], lhsT=QF[:, 2 * h + dc, tb * 128:(tb + 1) * 128], rhs=KTS[:, 2 * h + dc, :],
                                                                       start=(dc == 0), stop=(dc == 1)), r=[(A1, "QF"), (A2, "KTS")], w=[pqs])

                        def pv(tb=tb):
                            pq3 = quad()
                            for h in range(4):
                                for ec in range(2):
                                    c = 2 * h + ec
                                    for a in range(2):
                                        P(lambda e, h=h, ec=ec, a=a, c=c: e.matmul(pq3[:, c // 4, (c % 4) * 128:(c % 4) * 128 + 128], lhsT=VS[:, a, h * 256 + ec * 128:h * 256 + ec * 128 + 128],
                                                                                  rhs=PT[:, 2 * h + a, :], start=(a == 0), stop=(a == 1)), r=[(A2, "VS"), (A2, "PT")], w=[pq3])
                            A(lambda e, pq3=pq3: e.activation(out=MIX[:, :, tb * 128:(tb + 1) * 128], in_=pq3.t[:, 0:2, :].rearrange("p b (c t) -> p (b c) t", c=4), func=AF.Copy),
                              r=[pq3], w=[(MIX, tb)])
                        softmax_pv(pqs, tb * 128, 128, pv)
                else:
                    G(lambda e: e.memset(A2.t[:, 2560:4608], 0.0), w=[(A2, "QPAD")])
                    pqs = PQ[0]
                    pq3 = PQ[1]
                    for h in range(4):
                        for s in range(16):
                            fw.dma(sp, lambda e, s=s, h=h: e.dma_start(out=KS[:, :, 0:256], in_=D["ck"][l, s][:, h * 256:(h + 1) * 256].rearrange("(a p) n -> p a n", p=128)),
                                   writes=[(A1, "KS")])
                            pb, bk = PQ[1], s % 4
                            for a in range(2):
                                for dc in range(2):
                                    tr(pb[:, bk, dc * 256 + a * 128:dc * 256 + a * 128 + 128], KS[:, a, dc * 128:(dc + 1) * 128], 128, [(A1, "KS")], (pb, bk))
                            kt = A2.t[:, 0:256].bitcast(BF16).rearrange("p (c m) -> p c m", c=2)
                            A(lambda e, pb=pb, bk=bk: e.activation(out=kt, in_=pb[:, bk, :].rearrange("p (c m) -> p c m", c=2), func=AF.Copy), r=[(pb, bk)], w=[(A2, "KTS")])
                            for dc in range(2):
                                V(lambda e, s=s, dc=dc, h=h: e.tensor_copy(out=QPAD[:, dc, s, s * 8:(s + 1) * 8], in_=QF[:, 2 * h + dc, s * 8:(s + 1) * 8]),
                                  r=[(A1, "QF")], w=[(A2, "QPAD")])
                            for dc in range(2):
                                P(lambda e, s=s, dc=dc, h=h: e.matmul(pqs[:, h, 0:256], lhsT=QPAD[:, dc, s, :], rhs=kt[:, dc, :], start=(s == 0 and dc == 0), stop=(s == 15 and dc == 1)),
                                  r=[(A2, "QPAD"), (A2, "KTS")], w=[(pqs, h)])

                    def pv_s():
                        for s in range(16):
                            fw.dma(pool, lambda e, s=s: e.dma_start(out=VS, in_=D["cv"][l, s].rearrange("(a p) n -> p a n", p=128)), writes=[(A2, "VS")])
                            pbv, bkv = bank()
                            for h in range(4):
                                for ec in range(2):
                                    c = 2 * h + ec
                                    for a in range(2):
                                        P(lambda e, h=h, ec=ec, a=a, c=c, s=s: e.matmul(pbv[:, bkv, c * 8:(c + 1) * 8], lhsT=VS[:, a, h * 256 + ec * 128:h * 256 + ec * 128 + 128],
                                                                                       rhs=PT[:, 2 * h + a, s * 8:(s + 1) * 8], start=(a == 0), stop=(a == 1)),
                                          r=[(A2, "VS"), (A2, "PT")], w=[(pbv, bkv)])
                            A(lambda e, s=s, pbv=pbv, bkv=bkv: e.activation(out=MIX[:, :, s * 8:(s + 1) * 8], in_=pbv[:, bkv, 0:64].rearrange("p (c t) -> p c t", c=8), func=AF.Copy),
                              r=[(pbv, bkv)], w=[(MIX, "s%d" % s)])
                    softmax_pv(pqs, 0, 128, pv_s)
                MIX.collapse()
                out_proj(MIX, [wxo], 8, "norm_x_post", l, NB)

                _chk(fw, 8)
                for a_ in (A1, A2, A3):
                    a_.collapse()
                prenorm(lambda tb: X[:, tb, :], NB, 2, l)
                HID = A1.t[:, 0:5632].bitcast(BF16).rearrange("p (m n) -> p m n", m=22)
                for gp in range(6):
                    ncol = 512 if gp < 5 else 256
                    slot = wslot()
                    load_w(D["w_ffn_in"][l][:, gp * 512:gp * 512 + ncol], ncol, 0, slot, wid=(l, 6 + gp), final=False)
                    load_w(D["w_ffn_in"][l][:, FFN + gp * 512:FFN + gp * 512 + ncol], ncol, 512, slot, wid=(l, 6 + gp), final=True)
                    for mi in range(ncol // 128):
                        m = gp * 4 + mi
                        pg, bg = proj_fm(slot, mi * 128)
                        pu, bu = proj_fm(slot, 512 + mi * 128)
                        sg = A2.t[:, (m % 2) * 512:(m % 2) * 512 + ntok]
                        A(lambda e, pg=pg, bg=bg, sg=sg: e.activation(out=sg, in_=pg[:, bg, 0:ntok], func=AF.Silu), r=[(pg, bg)], w=[(A2, "sg%d" % (m % 2))])
                        V(lambda e, pu=pu, bu=bu, sg=sg, m=m: e.tensor_tensor(out=HID[:, m, 0:ntok], in0=pu[:, bu, 0:ntok], in1=sg, op=ALU.mult),
                          r=[(pu, bu), (A2, "sg%d" % (m % 2))], w=[(A1, "h%d" % m)])
                A1.collapse()
                fo = [load_w(D["w_ffn_out"][l][0:1024, :], 1024, wid=(l, 12)), load_w(D["w_ffn_out"][l][1024:2048, :], 1024, wid=(l, 13)), load_w(D["w_ffn_out"][l][2048:2816, :], 1024, wid=(l, 14))]

                A2.collapse()
                _gb[0] += 1
                gb = GBC[_gb[0] % 2]
                fw.dma(sp, lambda e: e.dma_start(out=gb[:, :], in_=D["norm_ffn_post"][l:l + 1, :].broadcast_to([128, 1024])), writes=[gb])
                for tb in range(NB):
                    pq = quad()
                    for n in range(2):
                        for k in range(22):
                            P(lambda e, n=n, k=k, tb=tb: e.matmul(pq[:, n, :], lhsT=HID[:, k, tb * 128:(tb + 1) * 128], rhs=fo[k // 8][:, k % 8, n * 512:(n + 1) * 512],
                                                                 start=(k == 0), stop=(k == 21)), r=[A1, fo[k // 8]], w=[pq])
                    for n in range(2):
                        A(lambda e, n=n, pq=pq: e.activation(out=A2.t[:, 1024 + n * 512:1024 + (n + 1) * 512], in_=pq[:, n, :], func=AF.Copy), r=[pq], w=[(A2, "ycp")])
                        A(lambda e, n=n: e.activation(out=A2.t[:, 0:512], in_=A2.t[:, 1024 + n * 512:1024 + (n + 1) * 512], func=AF.Square, accum_out=SM[:, 16 + n:17 + n]),
                          r=[(A2, "ycp")], w=[(A2, "junk"), (SM, "ss2")])
                    V(lambda e: e.tensor_tensor(out=SM[:, 18:19], in0=SM[:, 16:17], in1=SM[:, 17:18], op=ALU.add), r=[(SM, "ss2")], w=[(SM, "ss3")])
                    rstd_from_ss(SM[:, 18:19], 1024.0, SM[:, 19:20], [(SM, "ss3")], (SM, "rs3"))
                    for n in range(2):
                        V(lambda e, n=n: e.scalar_tensor_tensor(out=A2.t[:, 2048 + n * 512:2048 + (n + 1) * 512], in0=A2.t[:, 1024 + n * 512:1024 + (n + 1) * 512], scalar=SM[:, 19:20],
                                                                      in1=gb[:, n * 512:(n + 1) * 512], op0=ALU.mult, op1=ALU.mult),
                          r=[(A2, "ycp"), (SM, "rs3"), gb], w=[(A2, "yn")])
                    V(lambda e, tb=tb: e.tensor_tensor(out=X[:, tb, :], in0=X[:, tb, :], in1=A2.t[:, 2048:3072], op=ALU.add), r=[X, (A2, "yn")], w=[X])

            fw.dma(sp, lambda e: e.dma_start(out=ydst.rearrange("(tb p) d -> p tb d", p=128), in_=X[:, 0:NB, :]), reads=[X], is_out=True)
            wpass[0] += 1

        fw.finish()
        print(f"[kernel] built {fw.ninst} instructions, {fw.n_dma_sems} dma sems, sbuf free {nc.sbuf_bytes_remaining}")
    return nc


_W_KEYS = ["norm_mix_pre", "norm_mix_post", "w_in", "conv_qkv", "a_log", "dt_bias", "onorm_a", "dw_b", "dwbias_b", "gn_gain_b",
           "gn_bias_b", "w_pw_b", "w_pool", "scale_pool", "conv_d", "conv_bias_d", "w_rg", "b_rg", "w_ig", "b_ig", "lam_d", "w_out",
           "norm_x_pre", "norm_x_post", "norm_mem", "w_xq", "w_xkv", "w_xo", "norm_ffn_pre", "norm_ffn_post", "w_ffn_in", "w_ffn_out"]


def make_in_map(inp, c):
    f = lambda a: np.ascontiguousarray(np.asarray(a, dtype=np.float32))
    s = slice(16 * c, 16 * c + 16)
    m = dict(
        xp=f(inp["x_prompt"][c]), xs=f(inp["x_sample"][s]).reshape(128, 1024), mem=f(inp["mem_prompt"][c]),
        st_delta=f(inp["state_delta"][:, s]), st_dconv=f(inp["state_delta_conv"][:, s]).reshape(4, 48, 768),
        st_bconv=f(inp["state_conf_conv"][:, s]).reshape(4, 480, 256), st_pool=f(inp["state_pool"][:, s]).reshape(4, 240, 256),
        st_lconv=f(inp["state_lru_conv"][:, s]).reshape(4, 48, 256), st_lh=f(inp["state_lru_h"][:, s]),
        ck=f(inp["cache_mem_k"][:, s]).reshape(4, 16, 256, 1024), cv=f(inp["cache_mem_v"][:, s]).reshape(4, 16, 256, 1024))
    for k in _W_KEYS:
        m[k] = f(inp[k])
    return m


def gather(results):
    n = len(results)
    cat = lambda k, ax: np.concatenate([r[k] for r in results], axis=ax)
    y_p = np.stack([r["y_p"] for r in results], 0)
    y_s = np.concatenate([r["y_s"].reshape(16, 8, 1024) for r in results], 0)
    delta_p = np.stack([r["delta_p"] for r in results], 1)
    delta_s = cat("delta_s", 1)
    dconv_p = np.stack([r["dconv_p"] for r in results], 1)
    dconv_s = np.concatenate([r["dconv_s"].reshape(4, 16, 3, 768) for r in results], 1)
    conf_p = np.stack([r["conf_p"] for r in results], 1)
    conf_s = np.concatenate([r["conf_s"].reshape(4, 16, 30, 256) for r in results], 1)
    pool_p = np.stack([r["pool_p"] for r in results], 1)
    pool_s = np.concatenate([r["pool_s"].reshape(4, 16, 15, 256) for r in results], 1)
    lconv_p = np.stack([r["lconv_p"] for r in results], 1)
    lconv_s = np.concatenate([r["lconv_s"].reshape(4, 16, 3, 256) for r in results], 1)
    lh_p = np.stack([r["lh_p"].reshape(4, 256) for r in results], 1)
    lh_s = cat("lh_s", 1)
    mk_p = np.stack([r["mk_p"].reshape(4, 256, 4, 256) for r in results], 1)
    mv_p = np.stack([r["mv_p"].reshape(4, 256, 4, 256) for r in results], 1)
    outs = (y_p, y_s, delta_p, delta_s, dconv_p, dconv_s, conf_p, conf_s, pool_p, pool_s, lconv_p, lconv_s, lh_p, lh_s, mk_p, mv_p)
    return tuple(np.ascontiguousarray(o.astype(np.float32)) for o in outs)


def kernel(**inputs):
    nc = build()
    in_maps = [make_in_map(inputs, c) for c in range(8)]
    res = run_bass_kernel_spmd(nc, in_maps, core_ids=list(range(8)))
    return gather(res.results)
```

```python
import numpy as np
from contextlib import ExitStack
import concourse.bass as bass
import concourse.mybir as mybir
from concourse.bass_utils import run_bass_kernel_spmd

F32 = mybir.dt.float32
BF16 = mybir.dt.bfloat16
I32 = mybir.dt.int32
AF = mybir.ActivationFunctionType
ALU = mybir.AluOpType
AX = mybir.AxisListType

EPS = 1e-6
NLAYER = 4
OFF_B, OFF_C, OFF_D = 1032, 1544, 1800
FFN = 2816


class Rec:
    __slots__ = ("w", "r", "wm", "dsem", "dcount")

    def __init__(self):
        self.w = None
        self.r = {}
        self.wm = {}
        self.dsem = {}
        self.dcount = {}


class Buf:
    def __init__(self, t, name):
        self.t = t
        self.name = name
        self.whole = Rec()
        self.parts = {}

    def __getitem__(self, k):
        return self.t[k]

    def recs(self, key):
        if key is None:
            return [self.whole] + list(self.parts.values())
        if key not in self.parts:
            self.parts[key] = Rec()
        return [self.whole, self.parts[key]]

    def own(self, key):
        if key is None:
            return self.whole
        if key not in self.parts:
            self.parts[key] = Rec()
        return self.parts[key]

    def collapse(self):
        for rec in self.parts.values():
            for (sem, val) in rec.r.values():
                k = id(sem)
                if k not in self.whole.r or self.whole.r[k][1] < val:
                    self.whole.r[k] = (sem, val)
            wevs = list(rec.wm.values())
            if rec.w is not None:
                wevs.append(rec.w)
            for (sem, val) in wevs:
                k = id(sem)
                if k not in self.whole.wm or self.whole.wm[k][1] < val:
                    self.whole.wm[k] = (sem, val)
        self.parts = {}


class Eng:
    def __init__(self, name, h):
        self.name = name
        self.h = h
        self.sem = None
        self.count = 0
        self.seen = {}


class FW:
    def __init__(self, nc, stack):
        self.nc = nc
        self.stack = stack
        self.pe = Eng("pe", nc.tensor)
        self.act = Eng("act", nc.scalar)
        self.dve = Eng("dve", nc.vector)
        self.pool = Eng("pool", nc.gpsimd)
        self.sp = Eng("sp", nc.sync)
        self.engs = [self.pe, self.act, self.dve, self.pool, self.sp]
        for e in self.engs:
            e.sem = stack.enter_context(nc.semaphore("s_" + e.name))
        self.n_dma_sems = 0
        self.out_events = {}
        self.nbuf = 0
        self.ninst = 0
        self.dead = False
        self.hook = None
        self.stream_idx = {}
        self.yield_now = None

    def wait_until(self, cond):
        if cond():
            return
        if self.yield_now is None:
            raise RuntimeError("wait_until outside interleaved emission")
        n = 0
        while not cond():
            self.yield_now()
            n += 1
            if n > 10_000_000:
                raise RuntimeError("wait_until: never satisfied")

    def sbuf(self, shape, dtype, name=None):
        self.nbuf += 1
        name = name or f"b{self.nbuf}"
        t = self.stack.enter_context(self.nc.sbuf_tensor(name, list(shape), dtype))
        return Buf(t, name)

    def psum(self, shape, dtype, name=None):
        self.nbuf += 1
        name = name or f"p{self.nbuf}"
        t = self.stack.enter_context(self.nc.psum_tensor(name, list(shape), dtype))
        return Buf(t, name)

    def _dsem(self, rec, kind):
        if kind not in rec.dsem:
            self.n_dma_sems += 1
            rec.dsem[kind] = self.stack.enter_context(self.nc.semaphore(f"d{self.n_dma_sems}"))
            rec.dcount[kind] = 0
        return rec.dsem[kind]

    def _collect(self, reads, writes):
        need = {}

        def add(ev):
            if ev is None:
                return
            sem, val = ev
            k = id(sem)
            if k not in need or need[k][1] < val:
                need[k] = (sem, val)

        for (b, key) in reads:
            for rec in b.recs(key):
                add(rec.w)
                for ev in rec.wm.values():
                    add(ev)
        for (b, key) in writes:
            for rec in b.recs(key):
                add(rec.w)
                for ev in rec.wm.values():
                    add(ev)
                for ev in rec.r.values():
                    add(ev)
        return need

    def _emit_waits(self, eng, need):
        for k, (sem, val) in need.items():
            if eng is self.pe and sem is self.pe.sem:
                continue
            if eng.seen.get(k, 0) >= val:
                continue
            eng.seen[k] = val
            eng.h.wait_ge(sem, val)

    @staticmethod
    def _norm(lst):
        out = []
        for x in lst:
            if isinstance(x, Buf):
                out.append((x, None))
            else:
                out.append(x)
        return out

    def op(self, eng, fn, reads=(), writes=()):
        if self.dead:
            return None
        reads = self._norm(reads)
        writes = self._norm(writes)
        need = self._collect(reads, writes)
        self._emit_waits(eng, need)
        ins = fn(eng.h)
        self.ninst += 1
        eng.count += 1
        ins.then_inc(eng.sem, 1)
        ev = (eng.sem, eng.count)
        for (b, key) in reads:
            b.own(key).r[id(eng.sem)] = ev
        for (b, key) in writes:
            if key is None:
                b.parts = {}
            rec = b.own(key)
            rec.w = ev
            rec.r = {}
            if key is None:
                rec.wm = {}
        if self.hook is not None:
            self.hook()
        return ins

    def dma(self, q, fn, reads=(), writes=(), is_out=False):
        if self.dead:
            return None
        reads = self._norm(reads)
        writes = self._norm(writes)
        need = self._collect(reads, writes)
        self._emit_waits(q, need)
        ins = fn(q.h)
        self.ninst += 1
        if writes:
            b, key = writes[0]
        else:
            b, key = reads[0]
        rec = b.own(key)
        kind = "sw" if q is self.pool else "hw"
        sem = self._dsem(rec, kind)
        rec.dcount[kind] += 16
        ins.then_inc(sem, 16)
        ev = (sem, rec.dcount[kind])
        self.last_dma_ev = ev
        for (b2, key2) in reads:
            b2.own(key2).r[id(sem)] = ev
        for (b2, key2) in writes:
            if key2 is None:
                b2.parts = {}
            r2 = b2.own(key2)
            r2.w = ev
            r2.r = {}
            if key2 is None:
                r2.wm = {}
        if is_out:
            self.out_events[id(sem)] = ev
        return ins

    def finish(self):
        for sem, val in self.out_events.values():
            self.sp.h.wait_ge(sem, val)
        for e in self.engs:
            if e is not self.sp and e.count > 0:
                self.sp.h.wait_ge(e.sem, e.count)


IN_SHAPES = dict(
    xp=[2048, 1024], xs=[128, 1024], mem=[256, 1024],
    st_delta=[4, 16, 4, 64, 64], st_dconv=[4, 48, 768], st_bconv=[4, 480, 256], st_pool=[4, 240, 256],
    st_lconv=[4, 48, 256], st_lh=[4, 16, 256], ck=[4, 16, 256, 1024], cv=[4, 16, 256, 1024],
    norm_mix_pre=[4, 1024], norm_mix_post=[4, 1024], w_in=[4, 1024, 2312], conv_qkv=[4, 4, 768], a_log=[4, 4],
    dt_bias=[4, 4], onorm_a=[4, 64], dw_b=[4, 31, 256], dwbias_b=[4, 256], gn_gain_b=[4, 256], gn_bias_b=[4, 256],
    w_pw_b=[4, 256, 256], w_pool=[4, 4, 64, 64], scale_pool=[4, 256], conv_d=[4, 4, 256], conv_bias_d=[4, 256],
    w_rg=[4, 4, 64, 64], b_rg=[4, 256], w_ig=[4, 4, 64, 64], b_ig=[4, 256], lam_d=[4, 256], w_out=[4, 1024, 1024],
    norm_x_pre=[4, 1024], norm_x_post=[4, 1024], norm_mem=[4, 1024], w_xq=[4, 1024, 1024], w_xkv=[4, 1024, 2048],
    w_xo=[4, 1024, 1024], norm_ffn_pre=[4, 1024], norm_ffn_post=[4, 1024], w_ffn_in=[4, 1024, 5632],
    w_ffn_out=[4, 2816, 1024])
OUT_SHAPES = dict(
    y_p=[2048, 1024], y_s=[128, 1024], delta_p=[4, 4, 64, 64], delta_s=[4, 16, 4, 64, 64], dconv_p=[4, 3, 768],
    dconv_s=[4, 48, 768], conf_p=[4, 30, 256], conf_s=[4, 480, 256], pool_p=[4, 15, 256], pool_s=[4, 240, 256],
    lconv_p=[4, 3, 256], lconv_s=[4, 48, 256], lh_p=[4, 1, 256], lh_s=[4, 16, 256], mk_p=[4, 256, 1024],
    mv_p=[4, 256, 1024])


class _Stop(Exception):
    pass


import threading


def run_streams(fw, fns, weights=None):
    n = len(fns)
    sems = [threading.Semaphore(0) for _ in range(n)]
    main_sem = threading.Semaphore(0)
    done = [False] * n
    errs = []
    idx = {}
    cnt = [0] * n
    weights = weights or [1] * n

    def hook(force=False):
        i = idx.get(threading.get_ident())
        if i is None:
            return
        cnt[i] += 1
        if cnt[i] < weights[i] and not force:
            return
        cnt[i] = 0
        j = i
        for d in range(1, n + 1):
            j = (i + d) % n
            if not done[j]:
                break
        if j == i:
            if force:
                raise RuntimeError("yield_now: no other live stream (emission-order deadlock)")
            return
        sems[j].release()
        sems[i].acquire()

    def runner(i):
        idx[threading.get_ident()] = i
        sems[i].acquire()
        try:
            fns[i]()
        except BaseException as e:
            errs.append(e)
        done[i] = True
        alive = [j for j in range(n) if not done[j]]
        if alive:
            sems[alive[0]].release()
        else:
            main_sem.release()

    ths = [threading.Thread(target=runner, args=(i,)) for i in range(n)]
    old = fw.hook
    fw.hook = hook
    fw.yield_now = lambda: hook(True)
    fw.stream_idx = idx
    for t in ths:
        t.start()
    sems[0].release()
    main_sem.acquire()
    for t in ths:
        t.join()
    fw.hook = old
    fw.yield_now = None
    fw.stream_idx = {}
    if errs:
        raise errs[0]


import os
STOP = float(os.environ.get("KSTOP", "99"))
INTERLEAVE = os.environ.get("KINTER", "1") == "1"
USE_F32R = os.environ.get("KF32R", "0") == "1"
F32R = mybir.dt.float32r


def RR(ap):
    return ap.bitcast(F32R) if USE_F32R else ap


def _chk(fw, level):
    if STOP <= level:
        fw.dead = True


def build(NL=NLAYER, tts=None):
    nc = bass.Bass("TRN2", target_bir_lowering=False)
    D = {}
    for k, s in IN_SHAPES.items():
        D[k] = nc.dram_tensor(k, list(s), F32, kind="ExternalInput").ap()
    for k, s in OUT_SHAPES.items():
        D[k] = nc.dram_tensor(k, list(s), F32, kind="ExternalOutput").ap()
    if tts is None:
        tts = [("p", i) for i in range(4)] + [("s", 0)]

    with ExitStack() as st:
        fw = FW(nc, st)
        pe, act, dve, pool, sp = fw.pe, fw.act, fw.dve, fw.pool, fw.sp

        def V(fn, r=(), w=()):
            return fw.op(dve, fn, r, w)

        def A(fn, r=(), w=()):
            return fw.op(act, fn, r, w)

        def P(fn, r=(), w=()):
            return fw.op(pe, fn, r, w)

        def G(fn, r=(), w=()):
            return fw.op(pool, fn, r, w)

        X = fw.sbuf([128, 4, 1024], F32, "X")
        NW = 4
        WS = [fw.sbuf([128, 8, 1024], BF16, f"WS{i}") for i in range(NW)]
        XN = fw.sbuf([128, 8, 512], BF16, "XN")
        MIX = fw.sbuf([128, 8, 512], BF16, "MIX")
        A1 = fw.sbuf([128, 6200], F32, "A1")
        A2 = fw.sbuf([128, 6800], F32, "A2")
        A3 = fw.sbuf([128, 6144], F32, "A3")
        GBC = [fw.sbuf([128, 1024], F32, f"GBC{i}") for i in range(2)]
        STG = fw.sbuf([128, 1024], F32, "STG")
        SM = fw.sbuf([128, 256], F32, "SM")
        TST = fw.sbuf([128, 128], F32, "TST")
        TST2 = fw.sbuf([128, 128], F32, "TST2")
        PQ = [fw.psum([128, 4, 512], F32, f"PQ{i}") for i in range(2)]
        IDENT = fw.sbuf([128, 128], F32, "IDENT")
        ONES = fw.sbuf([128, 128], F32, "ONES")
        TRIU = fw.sbuf([128, 128], F32, "TRIU")
        NEGSL = fw.sbuf([128, 128], F32, "NEGSL")
        BONES = fw.sbuf([128, 128], F32, "BONES")
        INVC0 = fw.sbuf([128, 2, 16], F32, "INVC0")
        SSB = fw.sbuf([64, 4, 64], F32, "SSB")
        GPRE = fw.sbuf([128, NLAYER, 4, 8], F32, "GPRE")
        CWA = fw.sbuf([128, NLAYER, 6, 4], F32, "CWA")
        DWB = fw.sbuf([128, NLAYER, 2, 31], F32, "DWB")
        CWD = fw.sbuf([128, NLAYER, 2, 4], F32, "CWD")
        VEC = fw.sbuf([128, NLAYER, 8, 2], F32, "VEC")
        NSP = fw.sbuf([128, NLAYER, 2], F32, "NSP")
        WPW = fw.sbuf([128, NLAYER, 2, 256], BF16, "WPW")
        WBD = fw.sbuf([128, NLAYER, 3, 2, 128], BF16, "WBD")
        TOKC = fw.sbuf([128, NLAYER, 72], F32, "TOKC")
        TA = fw.sbuf([128, NLAYER, 6, 3], F32, "TA")
        TB = fw.sbuf([128, NLAYER, 2, 30], F32, "TB")
        TC = fw.sbuf([128, NLAYER, 2, 15], F32, "TC")
        TD = fw.sbuf([128, NLAYER, 2, 3], F32, "TD")
        HST = fw.sbuf([128, NLAYER, 2], F32, "HST")
        SST = fw.sbuf([64, NLAYER, 4, 64], F32, "SST")

        _pb = [0, 0, 0]

        def bank():
            si = fw.stream_idx.get(threading.get_ident())
            if si is None:
                i = _pb[0] % 8
                _pb[0] += 1
                return PQ[i // 4], i % 4
            i = _pb[1 + si] % 4
            _pb[1 + si] += 1
            return PQ[si], i

        _pq = [0]

        def quad():
            si = fw.stream_idx.get(threading.get_ident())
            if si is not None:
                return PQ[si]
            i = _pq[0] % 2
            _pq[0] += 1
            return PQ[i]

        _ws = [0]

        def wslot():
            i = _ws[0] % NW
            _ws[0] += 1
            return WS[i]

        SCR = nc.dram_tensor("wscratch", [NLAYER, 15, 128, 8 * 1024], BF16).ap()
        scr_ev = {}
        wpass = [0]

        def load_w(src_ap, ncols, col0=0, slot=None, wid=None, final=True):
            if slot is None:
                slot = wslot()
            if wid is None or wpass[0] == 0:
                nk = src_ap.shape[0] // 128
                fw.dma(pool, lambda e: e.dma_start(out=slot[:, 0:nk, col0:col0 + ncols],
                                                   in_=src_ap.rearrange("(k p) n -> p k n", p=128)), writes=[slot])
                if wid is not None and final and not fw.dead:
                    fw.dma(sp, lambda e: e.dma_start(out=SCR[wid[0], wid[1]], in_=slot.t[:, :, :].rearrange("p k n -> p (k n)")), reads=[slot])
                    scr_ev[wid] = fw.last_dma_ev
            elif final and not fw.dead:
                sem, val = scr_ev[wid]
                fw._emit_waits(sp, {id(sem): (sem, val)})
                fw.dma(sp, lambda e: e.dma_start(out=slot.t[:, :, :].rearrange("p k n -> p (k n)"), in_=SCR[wid[0], wid[1]]), writes=[slot])
            return slot

        G(lambda e: e.memset(ONES[:, :], 1.0), w=[ONES])
        G(lambda e: e.affine_select(out=IDENT[:, :], in_=ONES[:, :], pattern=[[-1, 128]], compare_op=ALU.is_equal,
                                    fill=0.0, base=0, channel_multiplier=1), r=[ONES], w=[IDENT])
        G(lambda e: e.affine_select(out=TRIU[:, :], in_=ONES[:, :], pattern=[[1, 128]], compare_op=ALU.is_ge,
                                    fill=0.0, base=0, channel_multiplier=-1), r=[ONES], w=[TRIU])
        G(lambda e: e.memset(BONES[:, :], -1.0), w=[BONES])
        G(lambda e: e.affine_select(out=NEGSL[:, :], in_=BONES[:, :], pattern=[[-1, 128]], compare_op=ALU.is_ge,
                                    fill=0.0, base=-1, channel_multiplier=1), r=[BONES], w=[NEGSL])
        for ws_ in WS:
            G(lambda e, ws_=ws_: e.memset(ws_.t[:, :, :].rearrange("p k n -> p (k n)"), 0.0), w=[ws_])
        G(lambda e: e.memset(BONES[:, :], 0.0), w=[BONES])
        G(lambda e: e.memset(BONES[0:64, 0:64], 1.0), w=[BONES])
        G(lambda e: e.memset(BONES[64:128, 64:128], 1.0), w=[BONES])
        for b_ in (TA, TB, TC, TD, HST, SST, WBD, SM):
            G(lambda e, b_=b_: e.memset(b_.t[:].rearrange(" ".join(["p"] + [f"a{i}" for i in range(len(b_.t.shape) - 1)]) + " -> p (" + " ".join(
                [f"a{i}" for i in range(len(b_.t.shape) - 1)]) + ")"), 0.0), w=[b_])
        WIN = [[2, 4], [8, 16]]
        IOT = A1
        G(lambda e: e.iota(IOT.t[:, 0:16].bitcast(I32), pattern=[[1, 16]], base=1, channel_multiplier=0), w=[IOT])
        V(lambda e: e.tensor_copy(out=IOT.t[:, 16:32], in_=IOT.t[:, 0:16].bitcast(I32)), r=[IOT], w=[IOT])
        for cc in range(2):
            for hf in range(2):
                ps = slice(64 * hf, 64 * hf + 64)
                wv = float(WIN[cc][hf])
                V(lambda e, ps=ps, cc=cc, wv=wv: e.tensor_scalar(out=INVC0[ps, cc, :], in0=IOT.t[ps, 16:32], scalar1=wv,
                                                                 scalar2=None, op0=ALU.min), r=[IOT], w=[INVC0])
        V(lambda e: e.reciprocal(out=INVC0[:, :, :], in_=INVC0[:, :, :]), r=[INVC0], w=[INVC0])

        def sdma(out_ap, in_ap, wbuf, q=None):
            fw.dma(q or sp, lambda e: e.dma_start(out=out_ap, in_=in_ap, allow_slow_non_contiguous=True), writes=[wbuf])

        for l in range(NL):
            for i, nm in enumerate(["norm_mix_pre", "norm_x_pre", "norm_ffn_pre", "norm_mem"]):
                sdma(GPRE[:, l, i, :], D[nm][l].rearrange("(c p) -> p c", p=128), GPRE)
            for j in range(4):
                sdma(CWA[:, l, :, j], D["conv_qkv"][l, j].rearrange("(c p) -> p c", p=128), CWA)
                sdma(CWD[:, l, :, j], D["conv_d"][l, j].rearrange("(c p) -> p c", p=128), CWD)
            for cc in range(2):
                sdma(DWB[:, l, cc, :], D["dw_b"][l, :, cc * 128:(cc + 1) * 128].rearrange("j p -> p j"), DWB)
            for i, nm in enumerate(["dwbias_b", "gn_gain_b", "gn_bias_b", "scale_pool", "conv_bias_d", "b_rg", "b_ig", "lam_d"]):
                sdma(VEC[:, l, i, :], D[nm][l].rearrange("(c p) -> p c", p=128), VEC)
            sdma(WPW[:, l, :, :], D["w_pw_b"][l].rearrange("(c p) n -> p c n", p=128), WPW, q=pool)
            for i, nm in enumerate(["w_pool", "w_rg", "w_ig"]):
                for gi in range(4):
                    hf, cc = gi % 2, gi // 2
                    sdma(WBD[64 * hf:64 * hf + 64, l, i, cc, 64 * hf:64 * hf + 64], D[nm][l, gi], WBD, q=pool)
            sdma(TOKC[:, l, 0:4], D["dt_bias"][l:l + 1, :].broadcast_to([128, 4]), TOKC)
            sdma(TOKC[:, l, 4:8], D["a_log"][l:l + 1, :].broadcast_to([128, 4]), TOKC)
            sdma(TOKC[:, l, 8:72], D["onorm_a"][l:l + 1, :].broadcast_to([128, 64]), TOKC)
        for l in range(NL):
            A(lambda e, l=l: e.activation(out=TOKC[:, l, 4:8], in_=TOKC[:, l, 4:8], func=AF.Exp), r=[TOKC], w=[TOKC])
            V(lambda e, l=l: e.tensor_scalar(out=TOKC[:, l, 4:8], in0=TOKC[:, l, 4:8], scalar1=-1.0, scalar2=None, op0=ALU.mult),
              r=[TOKC], w=[TOKC])
            A(lambda e, l=l: e.activation(out=NSP[:, l, :], in_=VEC[:, l, 7, :], func=AF.Exp, scale=-1.0), r=[VEC], w=[NSP])
            V(lambda e, l=l: e.tensor_scalar(out=NSP[:, l, :], in0=NSP[:, l, :], scalar1=1.0, scalar2=None, op0=ALU.add), r=[NSP], w=[NSP])
            A(lambda e, l=l: e.activation(out=NSP[:, l, :], in_=NSP[:, l, :], func=AF.Ln), r=[NSP], w=[NSP])
            V(lambda e, l=l: e.tensor_scalar(out=NSP[:, l, :], in0=NSP[:, l, :], scalar1=-8.0, scalar2=None, op0=ALU.mult), r=[NSP], w=[NSP])

        _chk(fw, 1)

        def rstd_from_ss(ss_ap, n, out_ap, bufs_r, buf_w):
            V(lambda e: e.tensor_scalar(out=out_ap, in0=ss_ap, scalar1=1.0 / n, scalar2=EPS, op0=ALU.mult, op1=ALU.add), r=bufs_r, w=[buf_w])
            A(lambda e: e.activation(out=out_ap, in_=out_ap, func=AF.Sqrt), r=[buf_w], w=[buf_w])
            V(lambda e: e.reciprocal(out=out_ap, in_=out_ap), r=[buf_w], w=[buf_w])

        def tr(out_ap, in_ap, np_, rbufs, wbuf):
            P(lambda e: e.transpose(out=out_ap, in_=in_ap, identity=IDENT[0:np_, 0:np_]), r=list(rbufs) + [IDENT], w=[wbuf])

        def prenorm(src_tm, NB, gi, l, dst=XN):
            for tb in range(NB):
                junk = A2
                A(lambda e, tb=tb: e.activation(out=junk.t[:, 0:1024], in_=src_tm(tb), func=AF.Square, accum_out=SM[:, tb:tb + 1]),
                  r=[X], w=[(junk, "junk"), (SM, "ss")])
                rstd_from_ss(SM[:, tb:tb + 1], 1024.0, SM[:, 8 + tb:9 + tb], [(SM, "ss")], (SM, "rs"))
                V(lambda e, tb=tb: e.tensor_scalar(out=junk.t[:, 1024:2048], in0=src_tm(tb), scalar1=SM[:, 8 + tb:9 + tb], scalar2=None,
                                                   op0=ALU.mult), r=[X, (SM, "rs")], w=[(junk, "xs")])
                pq = quad()
                for c in range(8):
                    tr(pq[:, c // 4, (c % 4) * 128:(c % 4) * 128 + 128], junk.t[:, 1024 + c * 128:1024 + (c + 1) * 128], 128, [(junk, "xs")], pq)
                src = pq.t[:, 0:2, :].rearrange("p a (b t) -> p (a b) t", b=4)
                V(lambda e, tb=tb, src=src: e.tensor_tensor(out=dst[:, :, tb * 128:(tb + 1) * 128], in0=src,
                                                            in1=GPRE[:, l, gi, :].unsqueeze(2).broadcast_to([128, 8, 128]), op=ALU.mult),
                  r=[pq, GPRE], w=[(dst, tb)])
            A2.collapse()

        _gb = [0]

        def out_proj(src_fm, slots, nk, gain_name, l, NB, src_key=None):
            A2.collapse()
            _gb[0] += 1
            gb = GBC[_gb[0] % 2]
            fw.dma(sp, lambda e: e.dma_start(out=gb[:, :], in_=D[gain_name][l:l + 1, :].broadcast_to([128, 1024])), writes=[gb])
            for tb in range(NB):
                pq = quad()
                for n in range(2):
                    for k in range(nk):
                        P(lambda e, n=n, k=k, tb=tb: e.matmul(pq[:, n, :], lhsT=src_fm[:, k, tb * 128:(tb + 1) * 128],
                                                             rhs=slots[k // 8][:, k % 8, n * 512:(n + 1) * 512], start=(k == 0), stop=(k == nk - 1)),
                          r=[(src_fm, src_key) if src_key is None else (src_fm, tb), slots[k // 8]], w=[pq])
                for n in range(2):
                    A(lambda e, n=n: e.activation(out=A2.t[:, 1024 + n * 512:1024 + (n + 1) * 512], in_=pq[:, n, :], func=AF.Copy), r=[pq], w=[(A2, "ycp")])
                    A(lambda e, n=n: e.activation(out=A2.t[:, 0:512], in_=A2.t[:, 1024 + n * 512:1024 + (n + 1) * 512], func=AF.Square, accum_out=SM[:, 16 + n:17 + n]),
                      r=[(A2, "ycp")], w=[(A2, "junk"), (SM, "ss2")])
                V(lambda e: e.tensor_tensor(out=SM[:, 18:19], in0=SM[:, 16:17], in1=SM[:, 17:18], op=ALU.add), r=[(SM, "ss2")], w=[(SM, "ss3")])
                rstd_from_ss(SM[:, 18:19], 1024.0, SM[:, 19:20], [(SM, "ss3")], (SM, "rs3"))
                for n in range(2):
                    V(lambda e, n=n: e.scalar_tensor_tensor(out=A2.t[:, 2048 + n * 512:2048 + (n + 1) * 512], in0=A2.t[:, 1024 + n * 512:1024 + (n + 1) * 512], scalar=SM[:, 19:20],
                                                            in1=gb[:, n * 512:(n + 1) * 512], op0=ALU.mult, op1=ALU.mult),
                      r=[(A2, "ycp"), (SM, "rs3"), gb], w=[(A2, "yn")])
                V(lambda e, tb=tb: e.tensor_tensor(out=X[:, tb, :], in0=X[:, tb, :], in1=A2.t[:, 2048:3072], op=ALU.add), r=[X, (A2, "yn")], w=[X])

        def _stg():
            si = fw.stream_idx.get(threading.get_ident())
            if si == 1:
                return 768, (STG, "b"), TST2
            if si == 0:
                return 0, (STG, "a"), TST
            return 0, (STG, None), TST

        def load_tm_to_fm(dram_ap, R, ncc, dst_fn, dst_bufs, sg=None):
            c0, sk, _ = _stg()
            fw.dma(sp, lambda e: e.dma_start(out=STG[0:R, c0:c0 + ncc * 128], in_=dram_ap), writes=[sk])
            for cc in range(ncc):
                pb, bk = bank()
                tr(pb[:, bk, 0:R], STG[0:R, c0 + cc * 128:c0 + (cc + 1) * 128], R, [sk], (pb, bk))
                srcp = pb[:, bk, 0:R] if sg is None else pb[:, bk, 0:R].rearrange("p (s w) -> p s w", s=sg)
                A(lambda e, cc=cc, srcp=srcp: e.activation(out=dst_fn(cc), in_=srcp, func=AF.Copy), r=[(pb, bk)], w=dst_bufs)

        def store_fm_to_tm(src_fn, src_bufs, R, ncc, dram_ap, sg=None):
            c0, sk, tst = _stg()
            for cc in range(ncc):
                pb, bk = bank()
                if sg is None:
                    tr(pb[0:R, bk, 0:128], src_fn(cc), 128, src_bufs, (pb, bk))
                else:
                    V(lambda e, cc=cc: e.tensor_copy(out=tst[:, 0:R].rearrange("p (s w) -> p s w", s=sg), in_=src_fn(cc)), r=src_bufs, w=[tst])
                    tr(pb[0:R, bk, 0:128], tst[:, 0:R], 128, [tst], (pb, bk))
                A(lambda e, cc=cc, pb=pb, bk=bk: e.activation(out=STG[0:R, c0 + cc * 128:c0 + (cc + 1) * 128], in_=pb[0:R, bk, 0:128], func=AF.Copy),
                  r=[(pb, bk)], w=[sk])
            fw.dma(sp, lambda e: e.dma_start(out=dram_ap, in_=STG[0:R, c0:c0 + ncc * 128]), reads=[sk], is_out=True)

        def conv_fm(xpv, ncc, T, W, wfn, bfn, outv, rbufs, wbuf):
            for cc in range(ncc):
                V(lambda e, cc=cc: e.tensor_scalar(out=outv(cc), in0=xpv(cc, 0, T), scalar1=wfn(cc, 0), scalar2=(bfn(cc) if bfn else None),
                                                   op0=ALU.mult, op1=(ALU.add if bfn else ALU.bypass)), r=rbufs, w=[wbuf])
                for j in range(1, W):
                    V(lambda e, cc=cc, j=j: e.scalar_tensor_tensor(out=outv(cc), in0=xpv(cc, j, T), scalar=wfn(cc, j), in1=outv(cc),
                                                                  op0=ALU.mult, op1=ALU.add), r=rbufs, w=[wbuf])

        for l in range(NL):
            fw.dma(sp, lambda e: e.dma_start(out=X[:, 0:2, :], in_=D["mem"].rearrange("(tb p) d -> p tb d", p=128)), writes=[X])
            prenorm(lambda tb: X[:, tb, :], 2, 3, l)
            slots = [load_w(D["w_xkv"][l][:, 0:1024], 1024), load_w(D["w_xkv"][l][:, 1024:2048], 1024)]
            for kv in range(2):
                for tb in range(2):
                    pq = quad()
                    for n in range(2):
                        for k in range(8):
                            P(lambda e, n=n, k=k, tb=tb, kv=kv: e.matmul(pq[:, n, :], lhsT=XN[:, k, tb * 128:(tb + 1) * 128],
                                                                        rhs=slots[kv][:, k, n * 512:(n + 1) * 512], start=(k == 0), stop=(k == 7)),
                              r=[(XN, tb), slots[kv]], w=[pq])
                    A(lambda e, pq=pq: e.activation(out=STG[:, :], in_=pq.t[:, 0:2, :].rearrange("p a b -> p (a b)"), func=AF.Copy), r=[pq], w=[STG])
                    dst = D["mk_p" if kv == 0 else "mv_p"][l, tb * 128:(tb + 1) * 128, :]
                    fw.dma(sp, lambda e, dst=dst: e.dma_start(out=dst, in_=STG[:, :]), reads=[STG], is_out=True)
        for sem, val in list(fw.out_events.values()):
            if not fw.dead:
                sp.h.wait_ge(sem, val)
                pool.h.wait_ge(sem, val)
        _chk(fw, 2)

        for (kind, ti) in tts:
            if kind == "p":
                ntok, NB, nseq, T = 512, 4, 1, 512
                xsrc = D["xp"][ti * 512:(ti + 1) * 512, :]
                ydst = D["y_p"][ti * 512:(ti + 1) * 512, :]
            else:
                ntok, NB, nseq, T = 128, 1, 16, 8
                xsrc = D["xs"]
                ydst = D["y_s"]
            first = (kind == "p" and ti == 0)
            last_p = (kind == "p" and ti == 3)
            fw.dma(sp, lambda e: e.dma_start(out=X[:, 0:NB, :], in_=xsrc.rearrange("(tb p) d -> p tb d", p=128)), writes=[X])

            for l in range(NL):
                prenorm(lambda tb: X[:, tb, :], NB, 0, l)
                w0 = load_w(D["w_in"][l][:, 0:768], 768, wid=(l, 0))
                w1 = load_w(D["w_in"][l][:, 768:1544], 776, wid=(l, 1))
                w2 = load_w(D["w_in"][l][:, 1544:2312], 768, wid=(l, 2))
                wo = load_w(D["w_out"][l], 1024, wid=(l, 3))
                for a_ in (A1, A2, A3):
                    a_.collapse()
                _chk(fw, 2.2)

                def xpview(arena, off, ncc, W):
                    sz = ncc * nseq * (W - 1 + T)
                    return arena.t[:, off:off + sz].rearrange("p (c s w) -> p c s w", c=ncc, s=nseq)

                def proj_fm(slot, col, M=128):
                    pb, bk = bank()
                    for k in range(8):
                        P(lambda e, k=k: e.matmul(pb[0:M, bk, 0:ntok], lhsT=slot[:, k, col:col + M], rhs=XN[:, k, 0:ntok], start=(k == 0), stop=(k == 7)),
                          r=[XN, slot], w=[(pb, bk)])
                    return pb, bk

                def ps3(pb, bk, M=128):
                    return pb[0:M, bk, 0:ntok].rearrange("p (s t) -> p s t", s=nseq)

                def tails_in(xv, TT_, st_name, W, ncc, key):
                    if kind == "p":
                        V(lambda e: e.tensor_copy(out=xv[:, :, 0, 0:W - 1], in_=TT_[:, l, :, :]), r=[TT_], w=[key])
                    else:
                        R_all = 16 * (W - 1)
                        ng = 1 if R_all <= 128 else R_all // 120
                        sg = 16 // ng
                        for g in range(ng):
                            R = sg * (W - 1)
                            load_tm_to_fm(D[st_name][l, g * R:(g + 1) * R, :], R, ncc,
                                          lambda cc, g=g: xv[:, cc, g * sg:(g + 1) * sg, 0:W - 1], [key], sg=sg)

                def tails_out(xv, TT_, out_p, out_s, W, ncc, key):
                    if kind == "p":
                        V(lambda e: e.tensor_copy(out=TT_[:, l, :, :], in_=xv[:, :, 0, T:T + W - 1]), r=[key], w=[TT_])
                        if last_p:
                            store_fm_to_tm(lambda cc: xv[:, cc, 0, T:T + W - 1], [key], W - 1, ncc, D[out_p][l])
                    else:
                        R_all = 16 * (W - 1)
                        ng = 1 if R_all <= 128 else R_all // 120
                        sg = 16 // ng
                        for g in range(ng):
                            R = sg * (W - 1)
                            store_fm_to_tm(lambda cc, g=g: xv[:, cc, g * sg:(g + 1) * sg, T:T + W - 1], [key], R, ncc,
                                           D[out_s][l, g * R:(g + 1) * R, :], sg=sg)

                C = 64 if kind == "p" else 8
                LV = 6 if kind == "p" else 3
                nbatch = ntok // (2 * C)
                flags = {}
                turn = [0]
                YAf_g = A1.t[:, 3100:3100 + 6 * ntok].rearrange("p (c n) -> p c n", c=6)
                kYA_g = (A1, "YA")

                def delta_batch(bt, AR, smo, sfx, SSv, SSk):
                    YAf, kYA = YAf_g, kYA_g
                    AR.collapse()
                    t0 = bt * 2 * C
                    tcol = [t0 + ci * C for ci in range(2)]
                    QKV = AR.t[0:C, 0:1536].rearrange("p (a n) -> p a n", a=2)
                    kQKV = (AR, "QKV")
                    pq = quad()
                    for ci in range(2):
                        for cc in range(6):
                            col = cc * 128
                            tr(pq[0:C, 2 * ci + col // 512, col % 512:col % 512 + 128], YAf[:, cc, tcol[ci]:tcol[ci] + C], 128, [kYA], pq)
                    src = pq.t[0:C, :, :].rearrange("p (a b) n -> p a (b n)", a=2)[:, :, 0:768]
                    A(lambda e, src=src: e.activation(out=RR(QKV), in_=src, func=AF.Copy), r=[pq], w=[kQKV])
                    SQ = AR.t[0:C, 1536:2560].rearrange("p (g d) -> p g d", d=64)
                    QK3 = AR.t[0:C, 0:1536].rearrange("p (a n) -> p a n", a=2)[:, :, 0:512].rearrange("p a (g d) -> p a g d", d=64)
                    SQ4 = AR.t[0:C, 1536:2560].rearrange("p (a g d) -> p a g d", a=2, d=64)
                    V(lambda e: e.tensor_tensor(out=RR(SQ4), in0=QK3, in1=QK3, op=ALU.mult), r=[kQKV], w=[(AR, "SQ")])
                    V(lambda e: e.tensor_reduce(out=SM[0:C, smo + 32:smo + 48], in_=SQ, axis=AX.X, op=ALU.add), r=[(AR, "SQ")], w=[(SM, "l2" + sfx)])
                    V(lambda e: e.tensor_scalar(out=SM[0:C, smo + 32:smo + 48], in0=SM[0:C, smo + 32:smo + 48], scalar1=EPS, scalar2=None, op0=ALU.add), r=[(SM, "l2" + sfx)], w=[(SM, "l2" + sfx)])
                    A(lambda e: e.activation(out=SM[0:C, smo + 32:smo + 48], in_=SM[0:C, smo + 32:smo + 48], func=AF.Sqrt), r=[(SM, "l2" + sfx)], w=[(SM, "l2" + sfx)])
                    V(lambda e: e.reciprocal(out=SM[0:C, smo + 32:smo + 48], in_=SM[0:C, smo + 32:smo + 48]), r=[(SM, "l2" + sfx)], w=[(SM, "l2" + sfx)])
                    l2v = SM[0:C, smo + 32:smo + 48].rearrange("p (a g) -> p a g", a=2)
                    V(lambda e: e.tensor_scalar(out=l2v[:, :, 0:4], in0=l2v[:, :, 0:4], scalar1=0.125, scalar2=None, op0=ALU.mult), r=[(SM, "l2" + sfx)], w=[(SM, "l2" + sfx)])
                    V(lambda e: e.tensor_tensor(out=RR(QK3), in0=QK3, in1=l2v.unsqueeze(3).broadcast_to([C, 2, 8, 64]), op=ALU.mult), r=[kQKV, (SM, "l2" + sfx)], w=[kQKV])
                    QKF = AR.t[0:64, 1536:1536 + 16 * C].rearrange("p (a g t) -> p a g t", a=2, g=8)
                    pq = quad()
                    for ci in range(2):
                        for g in range(8):
                            tr(pq[0:64, ci, g * C:(g + 1) * C], QKV[:, ci, g * 64:(g + 1) * 64], C, [kQKV], pq)
                    A(lambda e, pq=pq: e.activation(out=RR(QKF), in_=pq.t[0:64, 0:2, 0:8 * C].rearrange("p a (g t) -> p a g t", g=8), func=AF.Copy),
                      r=[pq, (AR, "SQ")], w=[(AR, "QKF")])
                    pb, bk = bank()
                    for ci in range(2):
                        for k in range(8):
                            P(lambda e, ci=ci, k=k: e.matmul(pb[0:C, bk, ci * 8:ci * 8 + 8], lhsT=XN[:, k, tcol[ci]:tcol[ci] + C], rhs=w1[:, k, 0:8],
                                                            start=(k == 0), stop=(k == 7)), r=[XN, w1], w=[(pb, bk)])
                    GBv = pb[0:C, bk, 0:16].rearrange("p (a g) -> p a g", a=2)
                    gg = SM[0:C, smo + 48:smo + 56].rearrange("p (a g) -> p a g", a=2)
                    be = SM[0:C, smo + 56:smo + 64].rearrange("p (a g) -> p a g", a=2)
                    V(lambda e: e.tensor_tensor(out=gg, in0=GBv[:, :, 0:4], in1=TOKC[0:C, l, 0:4].unsqueeze(1).broadcast_to([C, 2, 4]), op=ALU.add),
                      r=[(pb, bk), TOKC], w=[(SM, "gg" + sfx)])
                    A(lambda e: e.activation(out=be, in_=GBv[:, :, 4:8], func=AF.Sigmoid), r=[(pb, bk)], w=[(SM, "be" + sfx)])
                    V(lambda e: e.tensor_scalar(out=gg, in0=gg, scalar1=30.0, scalar2=None, op0=ALU.min), r=[(SM, "gg" + sfx)], w=[(SM, "gg" + sfx)])
                    A(lambda e: e.activation(out=gg, in_=gg, func=AF.Exp), r=[(SM, "gg" + sfx)], w=[(SM, "gg" + sfx)])
                    V(lambda e: e.tensor_scalar(out=gg, in0=gg, scalar1=1.0, scalar2=None, op0=ALU.add), r=[(SM, "gg" + sfx)], w=[(SM, "gg" + sfx)])
                    A(lambda e: e.activation(out=gg, in_=gg, func=AF.Ln), r=[(SM, "gg" + sfx)], w=[(SM, "gg" + sfx)])
                    V(lambda e: e.tensor_tensor(out=gg, in0=gg, in1=TOKC[0:C, l, 4:8].unsqueeze(1).broadcast_to([C, 2, 4]), op=ALU.mult),
                      r=[(SM, "gg" + sfx), TOKC], w=[(SM, "gg" + sfx)])
                    pb, bk = bank()
                    P(lambda e: e.matmul(pb[0:C, bk, 0:8], lhsT=TRIU[0:C, 0:C], rhs=SM[0:C, smo + 48:smo + 56], start=True, stop=True), r=[TRIU, (SM, "gg" + sfx)], w=[(pb, bk)])
                    P(lambda e: e.matmul(pb[0:64, bk, 8:16], lhsT=ONES[0:C, 0:64], rhs=SM[0:C, smo + 48:smo + 56], start=True, stop=True), r=[ONES, (SM, "gg" + sfx)], w=[(pb, bk)])
                    gc = SM[0:C, smo + 64:smo + 72]
                    gt = SM[0:64, smo + 72:smo + 80]
                    V(lambda e: e.tensor_copy(out=gc, in_=pb[0:C, bk, 0:8]), r=[(pb, bk)], w=[(SM, "gc" + sfx)])
                    V(lambda e: e.tensor_copy(out=gt, in_=pb[0:64, bk, 8:16]), r=[(pb, bk)], w=[(SM, "gt" + sfx)])
                    egc = SM[0:C, smo + 80:smo + 88]
                    egd = SM[0:C, smo + 88:smo + 96]
                    egt = SM[0:64, smo + 96:smo + 104]
                    kf = SM[0:C, smo + 104:smo + 112]
                    A(lambda e: e.activation(out=egc, in_=gc, func=AF.Exp), r=[(SM, "gc" + sfx)], w=[(SM, "egc" + sfx)])
                    A(lambda e: e.activation(out=egt, in_=gt, func=AF.Exp), r=[(SM, "gt" + sfx)], w=[(SM, "egt" + sfx)])
                    V(lambda e: e.tensor_tensor(out=egd, in0=SM[0:C, smo + 72:smo + 80], in1=gc, op=ALU.subtract), r=[(SM, "gt" + sfx), (SM, "gc" + sfx)], w=[(SM, "egd" + sfx)])
                    A(lambda e: e.activation(out=egd, in_=egd, func=AF.Exp), r=[(SM, "egd" + sfx)], w=[(SM, "egd" + sfx)])
                    V(lambda e: e.tensor_tensor(out=kf, in0=SM[0:C, smo + 56:smo + 64], in1=egc, op=ALU.mult), r=[(SM, "be" + sfx), (SM, "egc" + sfx)], w=[(SM, "kf" + sfx)])
                    CC8 = 8 * C

                    def u3(i):
                        return AR.t[0:C, i * 512:i * 512 + CC8].rearrange("p (g f) -> p g f", g=8)
                    pbt, bkt = bank()
                    tr(pbt[0:8, bkt, 0:C], gc, C, [(SM, "gc" + sfx)], (pbt, bkt))
                    GCT = GBC[0][0:8, 0:C]
                    V(lambda e: e.tensor_copy(out=GCT, in_=pbt[0:8, bkt, 0:C]), r=[(pbt, bkt)], w=[GBC[0]])
                    pb, bk = bank()
                    for g in range(8):
                        P(lambda e, g=g: e.matmul(pb[0:C, bk, g * C:(g + 1) * C], lhsT=IDENT[0:8, g:g + 1].broadcast_to([8, C]), rhs=GCT, start=True, stop=True),
                          r=[IDENT, GBC[0]], w=[(pb, bk)])
                    EE = u3(6)
                    V(lambda e: e.tensor_tensor(out=RR(EE), in0=pb[0:C, bk, 0:CC8].rearrange("p (g f) -> p g f", g=8), in1=gc.unsqueeze(2).broadcast_to([C, 8, C]),
                                                op=ALU.subtract), r=[(pb, bk), (SM, "gc" + sfx)], w=[(AR, "u6")])
                    A(lambda e: e.activation(out=RR(EE), in_=EE, func=AF.Abs), r=[(AR, "u6")], w=[(AR, "u6")])
                    A(lambda e: e.activation(out=RR(EE), in_=EE, func=AF.Exp, scale=-1.0), r=[(AR, "u6")], w=[(AR, "u6")])
                    EN = u3(5)
                    EQ = u3(7)
                    V(lambda e: e.tensor_tensor(out=RR(EN), in0=EE, in1=NEGSL[0:C, 0:C].unsqueeze(1).broadcast_to([C, 8, C]), op=ALU.mult),
                      r=[(AR, "u6"), NEGSL], w=[(AR, "u5")])
                    V(lambda e: e.tensor_tensor(out=RR(EQ), in0=EE, in1=TRIU[0:C, 0:C].unsqueeze(1).broadcast_to([C, 8, C]), op=ALU.mult),
                      r=[(AR, "u6"), TRIU], w=[(AR, "u7")])
                    pbk, bkk = bank()
                    pbq, bkq = bank()
                    for ci in range(2):
                        for h in range(4):
                            g = ci * 4 + h
                            P(lambda e, ci=ci, h=h, g=g: e.matmul(pbk[0:C, bkk, g * C:(g + 1) * C], lhsT=RR(QKF[:, ci, 4 + h, :]), rhs=RR(QKF[:, ci, 4 + h, :]),
                                                                 start=True, stop=True), r=[(AR, "QKF")], w=[(pbk, bkk)])
                            P(lambda e, ci=ci, h=h, g=g: e.matmul(pbq[0:C, bkq, g * C:(g + 1) * C], lhsT=RR(QKF[:, ci, 4 + h, :]), rhs=RR(QKF[:, ci, h, :]),
                                                                 start=True, stop=True), r=[(AR, "QKF")], w=[(pbq, bkq)])
                    NN = u3(6)
                    V(lambda e: e.tensor_tensor(out=RR(NN), in0=pbk[0:C, bkk, 0:CC8].rearrange("p (g f) -> p g f", g=8), in1=EN, op=ALU.mult),
                      r=[(pbk, bkk), (AR, "u5")], w=[(AR, "u6")])
                    V(lambda e: e.tensor_tensor(out=RR(NN), in0=NN, in1=SM[0:C, smo + 56:smo + 64].unsqueeze(2).broadcast_to([C, 8, C]), op=ALU.mult),
                      r=[(AR, "u6"), (SM, "be" + sfx)], w=[(AR, "u6")])
                    QKT = u3(7)
                    V(lambda e: e.tensor_tensor(out=RR(QKT), in0=pbq[0:C, bkq, 0:CC8].rearrange("p (g f) -> p g f", g=8), in1=EQ, op=ALU.mult),
                      r=[(pbq, bkq), (AR, "u7")], w=[(AR, "u7")])
                    pb, bk = bank()
                    for g in range(8):
                        tr(pb[0:C, bk, g * C:(g + 1) * C], NN[:, g, :], C, [(AR, "u6")], (pb, bk))
                    MM = u3(5)
                    A(lambda e, pb=pb, bk=bk: e.activation(out=RR(MM), in_=pb[0:C, bk, 0:CC8].rearrange("p (g f) -> p g f", g=8), func=AF.Copy), r=[(pb, bk)], w=[(AR, "u5")])
                    UU = u3(8)
                    V(lambda e: e.tensor_tensor(out=RR(UU), in0=MM, in1=IDENT[0:C, 0:C].unsqueeze(1).broadcast_to([C, 8, C]), op=ALU.add), r=[(AR, "u5"), IDENT], w=[(AR, "u8")])
                    cur = {"N": (NN, "u6"), "M": (MM, "u5"), "U": (UU, "u8")}
                    free = [(u3(9), "u9"), (u3(10), "u10"), (u3(11), "u11")]
                    for lev in range(1, LV):
                        (Nv, Nk), (Mv, Mk), (Uv, Uk) = cur["N"], cur["M"], cur["U"]
                        (N2, N2k) = free.pop(0)
                        pb, bk = bank()
                        for g in range(8):
                            P(lambda e, g=g, Mv=Mv, Nv=Nv: e.matmul(pb[0:C, bk, g * C:(g + 1) * C], lhsT=RR(Mv[:, g, :]), rhs=RR(Nv[:, g, :]), start=True, stop=True),
                              r=[(AR, Mk), (AR, Nk)], w=[(pb, bk)])
                        A(lambda e, pb=pb, bk=bk, N2=N2: e.activation(out=RR(N2), in_=pb[0:C, bk, 0:CC8].rearrange("p (g f) -> p g f", g=8), func=AF.Copy),
                          r=[(pb, bk)], w=[(AR, N2k)])
                        if lev < LV - 1:
                            (M2, M2k) = free.pop(0)
                            pb2, bk2 = bank()
                            for g in range(8):
                                P(lambda e, g=g, Mv=Mv, Nv=Nv: e.matmul(pb2[0:C, bk2, g * C:(g + 1) * C], lhsT=RR(Nv[:, g, :]), rhs=RR(Mv[:, g, :]), start=True, stop=True),
                                  r=[(AR, Mk), (AR, Nk)], w=[(pb2, bk2)])
                            A(lambda e, pb2=pb2, bk2=bk2, M2=M2: e.activation(out=RR(M2), in_=pb2[0:C, bk2, 0:CC8].rearrange("p (g f) -> p g f", g=8), func=AF.Copy),
                              r=[(pb2, bk2)], w=[(AR, M2k)])
                        (U2, U2k) = free.pop(0)
                        pb3, bk3 = bank()
                        for g in range(8):
                            P(lambda e, g=g, N2=N2, Uv=Uv: e.matmul(pb3[0:C, bk3, g * C:(g + 1) * C], lhsT=RR(N2[:, g, :]), rhs=RR(Uv[:, g, :]), start=True, stop=True),
                              r=[(AR, N2k), (AR, Uk)], w=[(pb3, bk3)])
                        V(lambda e, pb3=pb3, bk3=bk3, U2=U2, Uv=Uv: e.tensor_tensor(out=RR(U2), in0=pb3[0:C, bk3, 0:CC8].rearrange("p (g f) -> p g f", g=8), in1=Uv, op=ALU.add),
                          r=[(pb3, bk3), (AR, Uk)], w=[(AR, U2k)])
                        free.append((Nv, Nk))
                        free.append((Uv, Uk))
                        if lev < LV - 1:
                            free.append((Mv, Mk))
                            cur = {"N": (N2, N2k), "M": (M2, M2k), "U": (U2, U2k)}
                        else:
                            cur = {"N": (N2, N2k), "M": (Mv, Mk), "U": (U2, U2k)}
                    (UU, Uk) = cur["U"]
                    used = {Uk, "u7"}
                    avail = [i for i in (5, 6, 8, 9, 10, 11) if "u%d" % i not in used]
                    QKV4 = AR.t[0:C, 0:1536].rearrange("p (a n) -> p a n", a=2)

                    def part4(o):
                        return QKV4[:, :, o:o + 256].rearrange("p a (h d) -> p a h d", h=4)
                    Kt, Vt = part4(256), part4(512)

                    def bc4(smap):
                        return smap.rearrange("p (a h) -> p a h", a=2).unsqueeze(3).broadcast_to([C, 2, 4, 64])
                    V(lambda e: e.tensor_tensor(out=RR(Vt), in0=Vt, in1=bc4(SM[0:C, smo + 56:smo + 64]), op=ALU.mult), r=[kQKV, (SM, "be" + sfx)], w=[kQKV])
                    iK = avail.pop(0)
                    KBG = AR.t[0:C, iK * 512:iK * 512 + 512].rearrange("p (a h d) -> p a h d", a=2, h=4)
                    V(lambda e: e.tensor_tensor(out=RR(KBG), in0=Kt, in1=bc4(kf), op=ALU.mult), r=[kQKV, (SM, "kf" + sfx)], w=[(AR, "u%d" % iK)])
                    V(lambda e: e.tensor_tensor(out=RR(Kt), in0=Kt, in1=bc4(egd), op=ALU.mult), r=[kQKV, (SM, "egd" + sfx)], w=[kQKV])
                    pbv, bkv = bank()
                    pbw, bkw = bank()
                    for ci in range(2):
                        for h in range(4):
                            g = ci * 4 + h
                            P(lambda e, ci=ci, h=h, g=g: e.matmul(pbv[0:C, bkv, g * 64:(g + 1) * 64], lhsT=RR(UU[:, g, :]), rhs=RR(Vt[:, ci, h, :]), start=True, stop=True),
                              r=[(AR, Uk), kQKV], w=[(pbv, bkv)])
                            P(lambda e, ci=ci, h=h, g=g: e.matmul(pbw[0:64, bkw, g * C:(g + 1) * C], lhsT=RR(KBG[:, ci, h, :]), rhs=RR(UU[:, g, :]), start=True, stop=True),
                              r=[(AR, Uk), (AR, "u%d" % iK)], w=[(pbw, bkw)])
                    iV = avail.pop(0)
                    iW = avail.pop(0)
                    WV = AR.t[0:C, iV * 512:iV * 512 + 512].rearrange("p (a h d) -> p a h d", a=2, h=4)
                    WKT = AR.t[0:64, iW * 512:iW * 512 + CC8].rearrange("p (a h t) -> p a h t", a=2, h=4)
                    A(lambda e: e.activation(out=RR(WV), in_=pbv[0:C, bkv, :].rearrange("p (a h d) -> p a h d", a=2, h=4), func=AF.Copy), r=[(pbv, bkv)], w=[(AR, "u%d" % iV)])
                    A(lambda e: e.activation(out=RR(WKT), in_=pbw[0:64, bkw, 0:CC8].rearrange("p (a h t) -> p a h t", a=2, h=4), func=AF.Copy),
                      r=[(pbw, bkw)], w=[(AR, "u%d" % iW)])
                    if kind == "p":
                        fw.wait_until(lambda: turn[0] == bt)
                    iO = avail.pop(0)
                    OO = AR.t[0:C, iO * 512:iO * 512 + 512].rearrange("p (a h d) -> p a h d", a=2, h=4)
                    iU = avail.pop(0)
                    UT = AR.t[0:C, iU * 512:iU * 512 + 512].rearrange("p (a h d) -> p a h d", a=2, h=4)
                    SS = SSv
                    for ci in range(2):
                        if kind == "p":
                            Sv, Sk = SST[:, l, :, :], SST
                        else:
                            seq = bt * 2 + ci
                            Sv, Sk = SS, SSk
                            fw.dma(sp, lambda e, seq=seq: e.dma_start(out=SS, in_=D["st_delta"][l, seq].rearrange("h d e -> d h e")), writes=[Sk])
                        pb, bk = bank()
                        for h in range(4):
                            P(lambda e, ci=ci, h=h: e.matmul(pb[0:C, bk, h * 64:(h + 1) * 64], lhsT=WKT[:, ci, h, :], rhs=Sv[:, h, :], start=True, stop=True),
                              r=[(AR, "u%d" % iW), Sk], w=[(pb, bk)])
                        for h in range(4):
                            P(lambda e, ci=ci, h=h: e.matmul(pb[0:C, bk, 256 + h * 64:256 + (h + 1) * 64], lhsT=QKF[:, ci, h, :], rhs=Sv[:, h, :], start=True, stop=True),
                              r=[(AR, "QKF"), Sk], w=[(pb, bk)])
                        V(lambda e, ci=ci, pb=pb, bk=bk: e.tensor_tensor(out=RR(UT[:, 0, :, :]), in0=WV[:, ci, :, :], in1=pb[0:C, bk, 0:256].rearrange("p (h d) -> p h d", h=4),
                                                                        op=ALU.subtract), r=[(AR, "u%d" % iV), (pb, bk)], w=[(AR, "UTu")])
                        V(lambda e, ci=ci, pb=pb, bk=bk: e.tensor_tensor(out=RR(UT[:, 1, :, :]), in0=pb[0:C, bk, 256:512].rearrange("p (h d) -> p h d", h=4),
                                                                        in1=egc[:, ci * 4:ci * 4 + 4].unsqueeze(2).broadcast_to([C, 4, 64]), op=ALU.mult),
                          r=[(pb, bk), (SM, "egc" + sfx)], w=[(AR, "UTt")])
                        pb2, bk2 = bank()
                        for h in range(4):
                            P(lambda e, ci=ci, h=h: e.matmul(pb2[0:C, bk2, h * 64:(h + 1) * 64], lhsT=RR(QKT[:, ci * 4 + h, :]), rhs=RR(UT[:, 0, h, :]), start=True, stop=True),
                              r=[(AR, "u7"), (AR, "UTu")], w=[(pb2, bk2)])
                        for h in range(4):
                            P(lambda e, ci=ci, h=h: e.matmul(pb2[0:64, bk2, 256 + h * 64:256 + (h + 1) * 64], lhsT=RR(Kt[:, ci, h, :]), rhs=RR(UT[:, 0, h, :]), start=True, stop=True),
                              r=[kQKV, (AR, "UTu")], w=[(pb2, bk2)])
                        V(lambda e, ci=ci, pb2=pb2, bk2=bk2: e.tensor_tensor(out=RR(OO[:, ci, :, :]), in0=pb2[0:C, bk2, 0:256].rearrange("p (h d) -> p h d", h=4), in1=UT[:, 1, :, :],
                                                                            op=ALU.add), r=[(pb2, bk2), (AR, "UTt")], w=[(AR, "OO")])
                        V(lambda e, ci=ci, Sv=Sv: e.tensor_tensor(out=Sv, in0=Sv, in1=egt[:, ci * 4:ci * 4 + 4].unsqueeze(2).broadcast_to([64, 4, 64]), op=ALU.mult),
                          r=[Sk, (SM, "egt" + sfx)], w=[Sk])
                        V(lambda e, ci=ci, Sv=Sv, pb2=pb2, bk2=bk2: e.tensor_tensor(out=Sv, in0=Sv, in1=pb2[0:64, bk2, 256:512].rearrange("p (h d) -> p h d", h=4), op=ALU.add),
                          r=[Sk, (pb2, bk2)], w=[Sk])
                        if kind == "s":
                            fw.dma(sp, lambda e, seq=seq: e.dma_start(out=D["delta_s"][l, seq].rearrange("h d e -> d h e"), in_=SS), reads=[Sk], is_out=True)
                    if kind == "p":
                        turn[0] = bt + 1
                    iQ = avail.pop(0) if avail else iK
                    OS = AR.t[0:C, iK * 512:iK * 512 + 512].rearrange("p (g d) -> p g d", g=8)
                    OOg = AR.t[0:C, iO * 512:iO * 512 + 512].rearrange("p (g d) -> p g d", g=8)
                    V(lambda e: e.tensor_tensor(out=RR(OS), in0=OOg, in1=OOg, op=ALU.mult), r=[(AR, "OO")], w=[(AR, "u%d" % iK)])
                    V(lambda e: e.tensor_reduce(out=SM[0:C, smo + 112:smo + 120], in_=OS, axis=AX.X, op=ALU.add), r=[(AR, "u%d" % iK)], w=[(SM, "os" + sfx)])
                    rstd_from_ss(SM[0:C, smo + 112:smo + 120], 64.0, SM[0:C, smo + 120:smo + 128], [(SM, "os" + sfx)], (SM, "ors" + sfx))
                    V(lambda e: e.tensor_tensor(out=RR(OOg), in0=OOg, in1=SM[0:C, smo + 120:smo + 128].unsqueeze(2).broadcast_to([C, 8, 64]), op=ALU.mult), r=[(AR, "OO"), (SM, "ors" + sfx)], w=[(AR, "OO")])
                    V(lambda e: e.tensor_tensor(out=RR(OOg), in0=OOg, in1=TOKC[0:C, l, 8:72].unsqueeze(1).broadcast_to([C, 8, 64]), op=ALU.mult), r=[(AR, "OO"), TOKC], w=[(AR, "OO")])
                    pb, bk = bank()
                    for ci in range(2):
                        for k in range(8):
                            P(lambda e, ci=ci, k=k: e.matmul(pb[0:C, bk, ci * 256:(ci + 1) * 256], lhsT=XN[:, k, tcol[ci]:tcol[ci] + C], rhs=w1[:, k, 8:264],
                                                            start=(k == 0), stop=(k == 7)), r=[XN, w1], w=[(pb, bk)])
                    GTv = AR.t[0:C, iK * 512:iK * 512 + 512]
                    A(lambda e, pb=pb, bk=bk: e.activation(out=RR(GTv), in_=pb[0:C, bk, :], func=AF.Silu), r=[(pb, bk)], w=[(AR, "u%d" % iK)])
                    OOf = AR.t[0:C, iO * 512:iO * 512 + 512]
                    V(lambda e: e.tensor_tensor(out=RR(OOf), in0=OOf, in1=GTv, op=ALU.mult), r=[(AR, "OO"), (AR, "u%d" % iK)], w=[(AR, "OO")])
                    pb, bk = bank()
                    for ci in range(2):
                        for cc in range(2):
                            tr(pb[:, bk, (cc * 2 + ci) * C:(cc * 2 + ci + 1) * C], OOf[:, ci * 256 + cc * 128:ci * 256 + (cc + 1) * 128], C, [(AR, "OO")], (pb, bk))
                    A(lambda e, pb=pb, bk=bk: e.activation(out=MIX[:, 0:2, t0:t0 + 2 * C], in_=pb[:, bk, 0:4 * C].rearrange("p (c n) -> p c n", c=2), func=AF.Copy),
                      r=[(pb, bk)], w=[(MIX, "A%d" % bt)])

                def stream_bcd():
                    XB = xpview(A2, 0, 2, 31)
                    kXB = (A2, "XB")
                    tails_in(XB, TB, "st_bconv", 31, 2, kXB)
                    for cc in range(2):
                        pb1, bk1 = proj_fm(w1, 264 + cc * 128)
                        pb2, bk2 = proj_fm(w1, 264 + 256 + cc * 128)
                        A(lambda e: e.activation(out=A2.t[:, 4500:4500 + ntok], in_=pb2[:, bk2, 0:ntok], func=AF.Sigmoid), r=[(pb2, bk2)], w=[(A2, "sg")])
                        V(lambda e, cc=cc: e.tensor_tensor(out=XB[:, cc, :, 30:30 + T], in0=ps3(pb1, bk1),
                                                           in1=A2.t[:, 4500:4500 + ntok].rearrange("p (s t) -> p s t", s=nseq), op=ALU.mult),
                          r=[(pb1, bk1), (A2, "sg")], w=[kXB])
                    _chk(fw, 2.3)
                    tails_out(XB, TB, "conf_p", "conf_s", 31, 2, kXB)
                    _chk(fw, 2.35)
                    YB = A2.t[:, 1300:1300 + 2 * ntok].rearrange("p (c s t) -> p c s t", c=2, s=nseq)
                    kYB = (A2, "YB")
                    conv_fm(lambda cc, j, T_: XB[:, cc, :, j:j + T_], 2, T, 31, lambda cc, j: DWB[:, l, cc, j:j + 1], lambda cc: VEC[:, l, 0, cc:cc + 1],
                            lambda cc: YB[:, cc, :, :], [kXB, DWB, VEC], kYB)
                    YBf = A2.t[:, 1300:1300 + 2 * ntok].rearrange("p (c n) -> p c n", c=2)
                    _chk(fw, 2.4)
                    for cc in range(2):
                        sq = A2.t[:, 2400:2400 + ntok]
                        A(lambda e, cc=cc: e.activation(out=sq, in_=YBf[:, cc, :], func=AF.Square), r=[kYB], w=[(A2, "sqB")])
                        pbs, bks = bank()
                        P(lambda e, cc=cc: e.matmul(pbs[:, bks, 0:ntok], lhsT=BONES[:, :], rhs=YBf[:, cc, :], start=True, stop=True), r=[BONES, kYB], w=[(pbs, bks)])
                        pbq, bkq = bank()
                        P(lambda e: e.matmul(pbq[:, bkq, 0:ntok], lhsT=BONES[:, :], rhs=sq, start=True, stop=True), r=[BONES, (A2, "sqB")], w=[(pbq, bkq)])
                        _chk(fw, 2.42)
                        dd = A2.t[:, 3000:3000 + ntok]
                        msq = A2.t[:, 3600:3600 + ntok]
                        V(lambda e, cc=cc: e.scalar_tensor_tensor(out=dd, in0=pbs[:, bks, 0:ntok], scalar=-1.0 / 64, in1=YBf[:, cc, :], op0=ALU.mult, op1=ALU.add),
                          r=[(pbs, bks), kYB], w=[(A2, "ddB")])
                        _chk(fw, 2.43)
                        V(lambda e: e.tensor_scalar(out=msq, in0=pbs[:, bks, 0:ntok], scalar1=1.0 / 64, scalar2=None, op0=ALU.mult), r=[(pbs, bks)], w=[(A2, "msqB")])
                        V(lambda e: e.tensor_tensor(out=msq, in0=msq, in1=msq, op=ALU.mult), r=[(A2, "msqB")], w=[(A2, "msqB")])
                        _chk(fw, 2.435)
                        V(lambda e: e.scalar_tensor_tensor(out=msq, in0=pbq[:, bkq, 0:ntok], scalar=1.0 / 64, in1=msq, op0=ALU.mult, op1=ALU.subtract),
                          r=[(pbq, bkq), (A2, "msqB")], w=[(A2, "msqB")])
                        _chk(fw, 2.44)
                        V(lambda e: e.tensor_scalar(out=msq, in0=msq, scalar1=EPS, scalar2=None, op0=ALU.add), r=[(A2, "msqB")], w=[(A2, "msqB")])
                        A(lambda e: e.activation(out=msq, in_=msq, func=AF.Sqrt), r=[(A2, "msqB")], w=[(A2, "msqB")])
                        _chk(fw, 2.45)
                        V(lambda e: e.reciprocal(out=msq, in_=msq), r=[(A2, "msqB")], w=[(A2, "msqB")])
                        V(lambda e: e.tensor_tensor(out=dd, in0=dd, in1=msq, op=ALU.mult), r=[(A2, "ddB"), (A2, "msqB")], w=[(A2, "ddB")])
                        _chk(fw, 2.46)
                        yoff = 5100 if cc == 0 else 4200
                        ynb = A2.t[:, yoff:yoff + ntok // 2].bitcast(BF16)
                        A(lambda e, cc=cc, ynb=ynb: e.activation(out=ynb, in_=dd, func=AF.Silu, scale=VEC[:, l, 1, cc:cc + 1], bias=VEC[:, l, 2, cc:cc + 1]),
                          r=[(A2, "ddB"), VEC], w=[(A2, "ynB%d" % cc)])
                    _chk(fw, 2.5)
                    yn_c = [A2.t[:, 5100:5100 + ntok // 2].bitcast(BF16), A2.t[:, 4200:4200 + ntok // 2].bitcast(BF16)]
                    for oc in range(2):
                        pb, bk = bank()
                        for kc in range(2):
                            P(lambda e, kc=kc, oc=oc: e.matmul(pb[:, bk, 0:ntok], lhsT=WPW[:, l, kc, oc * 128:(oc + 1) * 128], rhs=yn_c[kc], start=(kc == 0), stop=(kc == 1)),
                              r=[WPW, (A2, "ynB0"), (A2, "ynB1")], w=[(pb, bk)])
                        A(lambda e, oc=oc: e.activation(out=MIX[:, 2 + oc, 0:ntok], in_=pb[:, bk, 0:ntok], func=AF.Copy), r=[(pb, bk)], w=[(MIX, "c%d" % (2 + oc))])

                    _chk(fw, 3)
                    A2.collapse()
                    XC = xpview(A2, 0, 2, 16)
                    kXC = (A2, "XC")
                    tails_in(XC, TC, "st_pool", 16, 2, kXC)
                    for cc in range(2):
                        pb, bk = proj_fm(w2, cc * 128)
                        A(lambda e, cc=cc: e.activation(out=XC[:, cc, :, 15:15 + T], in_=ps3(pb, bk), func=AF.Copy), r=[(pb, bk)], w=[kXC])
                    tails_out(XC, TC, "pool_p", "pool_s", 16, 2, kXC)
                    LW = 15 + T

                    def sview(off, ln):
                        return A2.t[:, off:off + 2 * nseq * ln].rearrange("p (c s w) -> p c s w", c=2, s=nseq)
                    S2 = sview(1100, LW - 1)
                    S4 = sview(2200, LW - 3)
                    S8 = sview(3300, LW - 7)
                    S16 = sview(4400, LW - 15)
                    V(lambda e: e.tensor_tensor(out=S2, in0=XC[:, :, :, 1:LW], in1=XC[:, :, :, 0:LW - 1], op=ALU.add), r=[kXC], w=[(A2, "S2")])
                    V(lambda e: e.tensor_tensor(out=S4, in0=S2[:, :, :, 2:LW - 1], in1=S2[:, :, :, 0:LW - 3], op=ALU.add), r=[(A2, "S2")], w=[(A2, "S4")])
                    V(lambda e: e.tensor_tensor(out=S8, in0=S4[:, :, :, 4:LW - 3], in1=S4[:, :, :, 0:LW - 7], op=ALU.add), r=[(A2, "S4")], w=[(A2, "S8")])
                    V(lambda e: e.tensor_tensor(out=S16, in0=S8[:, :, :, 8:LW - 7], in1=S8[:, :, :, 0:LW - 15], op=ALU.add), r=[(A2, "S8")], w=[(A2, "S16")])
                    SEL = A2.t[:, 5500:5500 + 2 * ntok].rearrange("p (c s t) -> p c s t", c=2, s=nseq)
                    srcs = {(0, 0): (S2, 14, "S2"), (0, 1): (S4, 12, "S4"), (1, 0): (S8, 8, "S8"), (1, 1): (S16, 0, "S16")}
                    for (cc, hf), (sv, o, nm) in srcs.items():
                        ps_ = slice(64 * hf, 64 * hf + 64)
                        wv = float(WIN[cc][hf])
                        V(lambda e, cc=cc, ps_=ps_, sv=sv, o=o, wv=wv: e.tensor_scalar(out=SEL[ps_, cc, :, :], in0=sv[ps_, cc, :, o:o + T], scalar1=1.0 / wv,
                                                                                      scalar2=None, op0=ALU.mult), r=[(A2, nm)], w=[(A2, "SEL")])
                        if first:
                            V(lambda e, cc=cc, ps_=ps_, sv=sv, o=o: e.tensor_tensor(out=SEL[ps_, cc, 0, 0:16], in0=sv[ps_, cc, 0, o:o + 16],
                                                                                   in1=INVC0[ps_, cc, :], op=ALU.mult), r=[(A2, nm), INVC0], w=[(A2, "SEL")])
                    DB = A2.t[:, 1100:1100 + ntok].bitcast(BF16).rearrange("p (c s t) -> p c s t", c=2, s=nseq)
                    V(lambda e: e.tensor_tensor(out=DB, in0=SEL, in1=XC[:, :, :, 15:15 + T], op=ALU.subtract), r=[(A2, "SEL"), kXC, (A2, "S2")], w=[(A2, "S2")])
                    DBf = A2.t[:, 1100:1100 + ntok].bitcast(BF16).rearrange("p (c n) -> p c n", c=2)
                    for cc in range(2):
                        pb, bk = bank()
                        P(lambda e, cc=cc: e.matmul(pb[:, bk, 0:ntok], lhsT=WBD[:, l, 0, cc, :], rhs=DBf[:, cc, :], start=True, stop=True), r=[WBD, (A2, "S2")], w=[(pb, bk)])
                        A(lambda e, cc=cc: e.activation(out=MIX[:, 4 + cc, 0:ntok], in_=pb[:, bk, 0:ntok], func=AF.Copy, scale=VEC[:, l, 3, cc:cc + 1]),
                          r=[(pb, bk), VEC], w=[(MIX, "c%d" % (4 + cc))])

                    _chk(fw, 4)
                    A2.collapse()
                    XD = xpview(A2, 0, 2, 4)
                    kXD = (A2, "XD")
                    tails_in(XD, TD, "st_lconv", 4, 2, kXD)

                    def reg(i):
                        return A2.t[:, 1100 + i * 1024:1100 + i * 1024 + 2 * ntok].rearrange("p (c n) -> p c n", c=2)
                    GD, XR, RR, II, AA = reg(0), reg(1), reg(2), reg(3), reg(4)
                    for cc in range(2):
                        pb, bk = proj_fm(w2, 256 + cc * 128)
                        A(lambda e, cc=cc: e.activation(out=GD[:, cc, :], in_=pb[:, bk, 0:ntok], func=AF.Gelu_apprx_tanh), r=[(pb, bk)], w=[(A2, "GD")])
                        pb, bk = proj_fm(w2, 512 + cc * 128)
                        A(lambda e, cc=cc: e.activation(out=XD[:, cc, :, 3:3 + T], in_=ps3(pb, bk), func=AF.Copy), r=[(pb, bk)], w=[kXD])
                    tails_out(XD, TD, "lconv_p", "lconv_s", 4, 2, kXD)
                    XR4 = A2.t[:, 1100 + 1024:1100 + 1024 + 2 * ntok].rearrange("p (c s t) -> p c s t", c=2, s=nseq)
                    conv_fm(lambda cc, j, T_: XD[:, cc, :, j:j + T_], 2, T, 4, lambda cc, j: CWD[:, l, cc, j:j + 1], lambda cc: VEC[:, l, 4, cc:cc + 1],
                            lambda cc: XR4[:, cc, :, :], [kXD, CWD, VEC], (A2, "XR"))
                    XRB = A2.t[:, 6220:6220 + ntok].bitcast(BF16).rearrange("p (c n) -> p c n", c=2)
                    V(lambda e: e.tensor_copy(out=XRB, in_=XR), r=[(A2, "XR")], w=[(A2, "XRB")])
                    for cc in range(2):
                        for (wi, dstv, bi, nm) in ((1, RR, 5, "RR"), (2, II, 6, "II")):
                            pb, bk = bank()
                            P(lambda e, cc=cc, wi=wi: e.matmul(pb[:, bk, 0:ntok], lhsT=WBD[:, l, wi, cc, :], rhs=XRB[:, cc, :], start=True, stop=True),
                              r=[WBD, (A2, "XRB")], w=[(pb, bk)])
                            A(lambda e, cc=cc, dstv=dstv, bi=bi, pb=pb, bk=bk: e.activation(out=dstv[:, cc, :], in_=pb[:, bk, 0:ntok], func=AF.Sigmoid,
                                                                                       bias=VEC[:, l, bi, cc:cc + 1]), r=[(pb, bk), VEC], w=[(A2, nm)])
                        A(lambda e, cc=cc: e.activation(out=AA[:, cc, :], in_=RR[:, cc, :], func=AF.Exp, scale=NSP[:, l, cc:cc + 1]), r=[(A2, "RR"), NSP], w=[(A2, "AA")])
                    A(lambda e: e.activation(out=RR, in_=AA, func=AF.Square), r=[(A2, "AA")], w=[(A2, "RR")])
                    V(lambda e: e.tensor_scalar(out=RR, in0=RR, scalar1=-1.0, scalar2=1.0, op0=ALU.mult, op1=ALU.add), r=[(A2, "RR")], w=[(A2, "RR")])
                    A(lambda e: e.activation(out=RR, in_=RR, func=AF.Sqrt), r=[(A2, "RR")], w=[(A2, "RR")])
                    V(lambda e: e.tensor_tensor(out=II, in0=II, in1=XR, op=ALU.mult), r=[(A2, "II"), (A2, "XR")], w=[(A2, "II")])
                    V(lambda e: e.tensor_tensor(out=II, in0=II, in1=RR, op=ALU.mult), r=[(A2, "II"), (A2, "RR")], w=[(A2, "II")])
                    HH = XR
                    if kind == "p":
                        for cc in range(2):
                            V(lambda e, cc=cc: e.tensor_tensor_scan(out=HH[:, cc, :], data0=AA[:, cc, :], data1=II[:, cc, :], initial=HST[:, l, cc:cc + 1],
                                                                    op0=ALU.mult, op1=ALU.add), r=[(A2, "AA"), (A2, "II"), HST], w=[(A2, "XR")])
                        V(lambda e: e.tensor_copy(out=HST[:, l, :], in_=HH[:, :, T - 1]), r=[(A2, "XR")], w=[HST])
                        if last_p:
                            store_fm_to_tm(lambda cc: HST[:, l, cc:cc + 1], [HST], 1, 2, D["lh_p"][l])
                    else:
                        H0 = A2.t[:, 6740:6772].rearrange("p (c s) -> p c s", c=2)
                        load_tm_to_fm(D["st_lh"][l], 16, 2, lambda cc: H0[:, cc, :], [(A2, "H0")])
                        AA4 = A2.t[:, 1100 + 4 * 1024:1100 + 4 * 1024 + 2 * ntok].rearrange("p (c s t) -> p c s t", c=2, s=nseq)
                        II4 = A2.t[:, 1100 + 3 * 1024:1100 + 3 * 1024 + 2 * ntok].rearrange("p (c s t) -> p c s t", c=2, s=nseq)
                        V(lambda e: e.tensor_tensor(out=H0, in0=H0, in1=AA4[:, :, :, 0], op=ALU.mult), r=[(A2, "H0"), (A2, "AA")], w=[(A2, "H0")])
                        V(lambda e: e.tensor_tensor(out=II4[:, :, :, 0], in0=II4[:, :, :, 0], in1=H0, op=ALU.add), r=[(A2, "II"), (A2, "H0")], w=[(A2, "II")])
                        V(lambda e: e.tensor_scalar(out=AA4[:, :, :, 0], in0=AA4[:, :, :, 0], scalar1=0.0, scalar2=None, op0=ALU.mult), r=[(A2, "AA")], w=[(A2, "AA")])
                        for cc in range(2):
                            V(lambda e, cc=cc: e.tensor_tensor_scan(out=HH[:, cc, :], data0=AA[:, cc, :], data1=II[:, cc, :], initial=0.0,
                                                                    op0=ALU.mult, op1=ALU.add), r=[(A2, "AA"), (A2, "II")], w=[(A2, "XR")])
                        store_fm_to_tm(lambda cc: XR4[:, cc, :, T - 1], [(A2, "XR")], 16, 2, D["lh_s"][l])
                    V(lambda e: e.tensor_tensor(out=MIX[:, 6:8, 0:ntok], in0=GD, in1=HH, op=ALU.mult), r=[(A2, "GD"), (A2, "XR")], w=[(MIX, "c6"), (MIX, "c7")])


                def stream_a():
                    _chk(fw, 5)
                    XA = xpview(A1, 0, 6, 4)
                    kXA = (A1, "XA")
                    tails_in(XA, TA, "st_dconv", 4, 6, kXA)
                    for cc in range(6):
                        pb, bk = proj_fm(w0, cc * 128)
                        A(lambda e, cc=cc: e.activation(out=XA[:, cc, :, 3:3 + T], in_=ps3(pb, bk), func=AF.Copy), r=[(pb, bk)], w=[kXA])
                    tails_out(XA, TA, "dconv_p", "dconv_s", 4, 6, kXA)
                    YA = A1.t[:, 3100:3100 + 6 * ntok].rearrange("p (c s t) -> p c s t", c=6, s=nseq)
                    YAf = A1.t[:, 3100:3100 + 6 * ntok].rearrange("p (c n) -> p c n", c=6)
                    kYA = (A1, "YA")
                    conv_fm(lambda cc, j, T_: XA[:, cc, :, j:j + T_], 6, T, 4, lambda cc, j: CWA[:, l, cc, j:j + 1], None,
                            lambda cc: YA[:, cc, :, :], [kXA, CWA], kYA)
                    A(lambda e: e.activation(out=YAf, in_=YAf, func=AF.Silu), r=[kYA], w=[kYA])

                    flags["prep"] = 1
                    for bt in range(nbatch):
                        delta_batch(bt, A3, 0, "", SSB[:, :, :], SSB)

                if INTERLEAVE:
                    run_streams(fw, [stream_a, stream_bcd], [3, 1])
                else:
                    stream_bcd()
                    stream_a()
                if last_p:
                    fw.dma(sp, lambda e: e.dma_start(out=D["delta_p"][l].rearrange("h d e -> d h e"), in_=SST[:, l, :, :]), reads=[SST], is_out=True)
                _chk(fw, 6)
                MIX.collapse()
                out_proj(MIX, [wo], 8, "norm_mix_post", l, NB)

                _chk(fw, 7)
                for a_ in (A1, A2, A3):
                    a_.collapse()
                prenorm(lambda tb: X[:, tb, :], NB, 1, l)
                wq = load_w(D["w_xq"][l], 1024, wid=(l, 4))
                wxo = load_w(D["w_xo"][l], 1024, wid=(l, 5))
                QF = A1.t[:, 0:2048].bitcast(BF16).rearrange("p (c n) -> p c n", c=8)
                for c in range(8):
                    pb, bk = proj_fm(wq, c * 128)
                    A(lambda e, c=c, pb=pb, bk=bk: e.activation(out=QF[:, c, 0:ntok], in_=pb[:, bk, 0:ntok], func=AF.Copy, scale=1.0 / 16.0), r=[(pb, bk)], w=[(A1, "QF")])
                KS = A1.t[:, 2048:4096].rearrange("p (a n) -> p a n", a=2)
                KTS = A2.t[:, 0:1024].bitcast(BF16).rearrange("p (c m) -> p c m", c=8)
                VS = A2.t[:, 1024:2048].bitcast(BF16).rearrange("p (a n) -> p a n", a=2)
                PEX = A1.t[:, 4096:5120].rearrange("p (h m) -> p h m", h=4)
                PT = A2.t[:, 2048:2560].bitcast(BF16).rearrange("p (c t) -> p c t", c=8)
                QPAD = A2.t[:, 2560:4608].bitcast(BF16).rearrange("p (c s t) -> p c s t", c=2, s=16)

                def load_kv(kap, vap):
                    fw.dma(sp, lambda e: e.dma_start(out=KS, in_=kap.rearrange("(a p) n -> p a n", p=128)), writes=[(A1, "KS")])
                    fw.dma(pool, lambda e: e.dma_start(out=VS, in_=vap.rearrange("(a p) n -> p a n", p=128)), writes=[(A2, "VS")])
                    pq = quad()
                    for a in range(2):
                        for c in range(8):
                            tr(pq[:, c // 2, (c % 2) * 256 + a * 128:(c % 2) * 256 + a * 128 + 128], KS[:, a, c * 128:(c + 1) * 128], 128, [(A1, "KS")], pq)
                    A(lambda e, pq=pq: e.activation(out=KTS, in_=pq.t[:, :, :].rearrange("p b (c m) -> p (b c) m", c=2), func=AF.Copy), r=[pq], w=[(A2, "KTS")])

                def softmax_pv(pqs, col0, ncol, vfn):
                    sc = pqs.t[:, :, 0:256]
                    V(lambda e: e.tensor_reduce(out=SM[:, 128:132], in_=sc, axis=AX.X, op=ALU.max), r=[pqs], w=[(SM, "mx")])
                    V(lambda e: e.tensor_scalar(out=SM[:, 128:132], in0=SM[:, 128:132], scalar1=-1.0, scalar2=None, op0=ALU.mult), r=[(SM, "mx")], w=[(SM, "mx")])
                    for h in range(4):
                        A(lambda e, h=h: e.activation(out=PEX[:, h, :], in_=pqs[:, h, 0:256], func=AF.Exp, bias=SM[:, 128 + h:129 + h], accum_out=SM[:, 132 + h:133 + h]),
                          r=[pqs, (SM, "mx")], w=[(A1, "PEX"), (SM, "sm")])
                    V(lambda e: e.reciprocal(out=SM[:, 132:136], in_=SM[:, 132:136]), r=[(SM, "sm")], w=[(SM, "sm")])
                    V(lambda e: e.tensor_tensor(out=PEX, in0=PEX, in1=SM[:, 132:136].unsqueeze(2).broadcast_to([128, 4, 256]), op=ALU.mult), r=[(A1, "PEX"), (SM, "sm")], w=[(A1, "PEX")])
                    pq2 = quad()
                    for h in range(4):
                        for a in range(2):
                            c = h * 2 + a
                            tr(pq2[:, c // 4, (c % 4) * 128:(c % 4) * 128 + 128], PEX[:, h, a * 128:(a + 1) * 128], 128, [(A1, "PEX")], pq2)
                    A(lambda e, pq2=pq2: e.activation(out=PT, in_=pq2.t[:, 0:2, :].rearrange("p b (c t) -> p (b c) t", c=4), func=AF.Copy), r=[pq2], w=[(A2, "PT")])
                    vfn()

                if kind == "p":
                    load_kv(D["mk_p"][l], D["mv_p"][l])
                    for tb in range(NB):
                        pqs = quad()
                        for h in range(4):
                            for dc in range(2):
                                P(lambda e, h=h, dc=dc, tb=tb: e.matmul(pqs[:, h, 0:256], lhsT=QF[:, 2 * h + dc, tb * 128:(tb + 1) * 128], rhs=KTS[:, 2 * h + dc, :],
                                                                       start=(dc == 0), stop=(dc == 1)), r=[(A1, "QF"), (A2, "KTS")], w=[pqs])

                        def pv(tb=tb):
                            pq3 = quad()
                            for h in range(4):
                                for ec in range(2):
                                    c = 2 * h + ec
                                    for a in range(2):
                                        P(lambda e, h=h, ec=ec, a=a, c=c: e.matmul(pq3[:, c // 4, (c % 4) * 128:(c % 4) * 128 + 128], lhsT=VS[:, a, h * 256 + ec * 128:h * 256 + ec * 128 + 128],
                                                                                  rhs=PT[:, 2 * h + a, :], start=(a == 0), stop=(a == 1)), r=[(A2, "VS"), (A2, "PT")], w=[pq3])
                            A(lambda e, pq3=pq3: e.activation(out=MIX[:, :, tb * 128:(tb + 1) * 128], in_=pq3.t[:, 0:2, :].rearrange("p b (c t) -> p (b c) t", c=4), func=AF.Copy),
                              r=[pq3], w=[(MIX, tb)])
                        softmax_pv(pqs, tb * 128, 128, pv)
                else:
                    G(lambda e: e.memset(A2.t[:, 2560:4608], 0.0), w=[(A2, "QPAD")])
                    pqs = PQ[0]
                    pq3 = PQ[1]
                    for h in range(4):
                        for s in range(16):
                            fw.dma(sp, lambda e, s=s, h=h: e.dma_start(out=KS[:, :, 0:256], in_=D["ck"][l, s][:, h * 256:(h + 1) * 256].rearrange("(a p) n -> p a n", p=128)),
                                   writes=[(A1, "KS")])
                            pb, bk = PQ[1], s % 4
                            for a in range(2):
                                for dc in range(2):
                                    tr(pb[:, bk, dc * 256 + a * 128:dc * 256 + a * 128 + 128], KS[:, a, dc * 128:(dc + 1) * 128], 128, [(A1, "KS")], (pb, bk))
                            kt = A2.t[:, 0:256].bitcast(BF16).rearrange("p (c m) -> p c m", c=2)
                            A(lambda e, pb=pb, bk=bk: e.activation(out=kt, in_=pb[:, bk, :].rearrange("p (c m) -> p c m", c=2), func=AF.Copy), r=[(pb, bk)], w=[(A2, "KTS")])
                            for dc in range(2):
                                V(lambda e, s=s, dc=dc, h=h: e.tensor_copy(out=QPAD[:, dc, s, s * 8:(s + 1) * 8], in_=QF[:, 2 * h + dc, s * 8:(s + 1) * 8]),
                                  r=[(A1, "QF")], w=[(A2, "QPAD")])
                            for dc in range(2):
                                P(lambda e, s=s, dc=dc, h=h: e.matmul(pqs[:, h, 0:256], lhsT=QPAD[:, dc, s, :], rhs=kt[:, dc, :], start=(s == 0 and dc == 0), stop=(s == 15 and dc == 1)),
                                  r=[(A2, "QPAD"), (A2, "KTS")], w=[(pqs, h)])

                    def pv_s():
                        for s in range(16):
                            fw.dma(pool, lambda e, s=s: e.dma_start(out=VS, in_=D["cv"][l, s].rearrange("(a p) n -> p a n", p=128)), writes=[(A2, "VS")])
                            pbv, bkv = bank()
                            for h in range(4):
                                for ec in range(2):
                                    c = 2 * h + ec
                                    for a in range(2):
                                        P(lambda e, h=h, ec=ec, a=a, c=c, s=s: e.matmul(pbv[:, bkv, c * 8:(c + 1) * 8], lhsT=VS[:, a, h * 256 + ec * 128:h * 256 + ec * 128 + 128],
                                                                                       rhs=PT[:, 2 * h + a, s * 8:(s + 1) * 8], start=(a == 0), stop=(a == 1)),
                                          r=[(A2, "VS"), (A2, "PT")], w=[(pbv, bkv)])
                            A(lambda e, s=s, pbv=pbv, bkv=bkv: e.activation(out=MIX[:, :, s * 8:(s + 1) * 8], in_=pbv[:, bkv, 0:64].rearrange("p (c t) -> p c t", c=8), func=AF.Copy),
                              r=[(pbv, bkv)], w=[(MIX, "s%d" % s)])
                    softmax_pv(pqs, 0, 128, pv_s)
                MIX.collapse()
                out_proj(MIX, [wxo], 8, "norm_x_post", l, NB)

                _chk(fw, 8)
                for a_ in (A1, A2, A3):
                    a_.collapse()
                prenorm(lambda tb: X[:, tb, :], NB, 2, l)
                HID = A1.t[:, 0:5632].bitcast(BF16).rearrange("p (m n) -> p m n", m=22)
                for gp in range(6):
                    ncol = 512 if gp < 5 else 256
                    slot = wslot()
                    load_w(D["w_ffn_in"][l][:, gp * 512:gp * 512 + ncol], ncol, 0, slot, wid=(l, 6 + gp), final=False)
                    load_w(D["w_ffn_in"][l][:, FFN + gp * 512:FFN + gp * 512 + ncol], ncol, 512, slot, wid=(l, 6 + gp), final=True)
                    for mi in range(ncol // 128):
                        m = gp * 4 + mi
                        pg, bg = proj_fm(slot, mi * 128)
                        pu, bu = proj_fm(slot, 512 + mi * 128)
                        sg = A2.t[:, (m % 2) * 512:(m % 2) * 512 + ntok]
                        A(lambda e, pg=pg, bg=bg, sg=sg: e.activation(out=sg, in_=pg[:, bg, 0:ntok], func=AF.Silu), r=[(pg, bg)], w=[(A2, "sg%d" % (m % 2))])
                        V(lambda e, pu=pu, bu=bu, sg=sg, m=m: e.tensor_tensor(out=HID[:, m, 0:ntok], in0=pu[:, bu, 0:ntok], in1=sg, op=ALU.mult),
                          r=[(pu, bu), (A2, "sg%d" % (m % 2))], w=[(A1, "h%d" % m)])
                A1.collapse()
                fo = [load_w(D["w_ffn_out"][l][0:1024, :], 1024, wid=(l, 12)), load_w(D["w_ffn_out"][l][1024:2048, :], 1024, wid=(l, 13)), load_w(D["w_ffn_out"][l][2048:2816, :], 1024, wid=(l, 14))]

                A2.collapse()
                _gb[0] += 1
                gb = GBC[_gb[0] % 2]
                fw.dma(sp, lambda e: e.dma_start(out=gb[:, :], in_=D["norm_ffn_post"][l:l + 1, :].broadcast_to([128, 1024])), writes=[gb])
                for tb in range(NB):
                    pq = quad()
                    for n in range(2):
                        for k in range(22):
                            P(lambda e, n=n, k=k, tb=tb: e.matmul(pq[:, n, :], lhsT=HID[:, k, tb * 128:(tb + 1) * 128], rhs=fo[k // 8][:, k % 8, n * 512:(n + 1) * 512],
                                                                 start=(k == 0), stop=(k == 21)), r=[A1, fo[k // 8]], w=[pq])
                    for n in range(2):
                        A(lambda e, n=n, pq=pq: e.activation(out=A2.t[:, 1024 + n * 512:1024 + (n + 1) * 512], in_=pq[:, n, :], func=AF.Copy), r=[pq], w=[(A2, "ycp")])
                        A(lambda e, n=n: e.activation(out=A2.t[:, 0:512], in_=A2.t[:, 1024 + n * 512:1024 + (n + 1) * 512], func=AF.Square, accum_out=SM[:, 16 + n:17 + n]),
                          r=[(A2, "ycp")], w=[(A2, "junk"), (SM, "ss2")])
                    V(lambda e: e.tensor_tensor(out=SM[:, 18:19], in0=SM[:, 16:17], in1=SM[:, 17:18], op=ALU.add), r=[(SM, "ss2")], w=[(SM, "ss3")])
                    rstd_from_ss(SM[:, 18:19], 1024.0, SM[:, 19:20], [(SM, "ss3")], (SM, "rs3"))
                    for n in range(2):
                        V(lambda e, n=n: e.scalar_tensor_tensor(out=A2.t[:, 2048 + n * 512:2048 + (n + 1) * 512], in0=A2.t[:, 1024 + n * 512:1024 + (n + 1) * 512], scalar=SM[:, 19:20],
                                                                      in1=gb[:, n * 512:(n + 1) * 512], op0=ALU.mult, op1=ALU.mult),
                          r=[(A2, "ycp"), (SM, "rs3"), gb], w=[(A2, "yn")])
                    V(lambda e, tb=tb: e.tensor_tensor(out=X[:, tb, :], in0=X[:, tb, :], in1=A2.t[:, 2048:3072], op=ALU.add), r=[X, (A2, "yn")], w=[X])

            fw.dma(sp, lambda e: e.dma_start(out=ydst.rearrange("(tb p) d -> p tb d", p=128), in_=X[:, 0:NB, :]), reads=[X], is_out=True)
            wpass[0] += 1

        fw.finish()
        print(f"[kernel] built {fw.ninst} instructions, {fw.n_dma_sems} dma sems, sbuf free {nc.sbuf_bytes_remaining}")
    return nc


_W_KEYS = ["norm_mix_pre", "norm_mix_post", "w_in", "conv_qkv", "a_log", "dt_bias", "onorm_a", "dw_b", "dwbias_b", "gn_gain_b",
           "gn_bias_b", "w_pw_b", "w_pool", "scale_pool", "conv_d", "conv_bias_d", "w_rg", "b_rg", "w_ig", "b_ig", "lam_d", "w_out",
           "norm_x_pre", "norm_x_post", "norm_mem", "w_xq", "w_xkv", "w_xo", "norm_ffn_pre", "norm_ffn_post", "w_ffn_in", "w_ffn_out"]


def make_in_map(inp, c):
    f = lambda a: np.ascontiguousarray(np.asarray(a, dtype=np.float32))
    s = slice(16 * c, 16 * c + 16)
    m = dict(
        xp=f(inp["x_prompt"][c]), xs=f(inp["x_sample"][s]).reshape(128, 1024), mem=f(inp["mem_prompt"][c]),
        st_delta=f(inp["state_delta"][:, s]), st_dconv=f(inp["state_delta_conv"][:, s]).reshape(4, 48, 768),
        st_bconv=f(inp["state_conf_conv"][:, s]).reshape(4, 480, 256), st_pool=f(inp["state_pool"][:, s]).reshape(4, 240, 256),
        st_lconv=f(inp["state_lru_conv"][:, s]).reshape(4, 48, 256), st_lh=f(inp["state_lru_h"][:, s]),
        ck=f(inp["cache_mem_k"][:, s]).reshape(4, 16, 256, 1024), cv=f(inp["cache_mem_v"][:, s]).reshape(4, 16, 256, 1024))
    for k in _W_KEYS:
        m[k] = f(inp[k])
    return m


def gather(results):
    n = len(results)
    cat = lambda k, ax: np.concatenate([r[k] for r in results], axis=ax)
    y_p = np.stack([r["y_p"] for r in results], 0)
    y_s = np.concatenate([r["y_s"].reshape(16, 8, 1024) for r in results], 0)
    delta_p = np.stack([r["delta_p"] for r in results], 1)
    delta_s = cat("delta_s", 1)
    dconv_p = np.stack([r["dconv_p"] for r in results], 1)
    dconv_s = np.concatenate([r["dconv_s"].reshape(4, 16, 3, 768) for r in results], 1)
    conf_p = np.stack([r["conf_p"] for r in results], 1)
    conf_s = np.concatenate([r["conf_s"].reshape(4, 16, 30, 256) for r in results], 1)
    pool_p = np.stack([r["pool_p"] for r in results], 1)
    pool_s = np.concatenate([r["pool_s"].reshape(4, 16, 15, 256) for r in results], 1)
    lconv_p = np.stack([r["lconv_p"] for r in results], 1)
    lconv_s = np.concatenate([r["lconv_s"].reshape(4, 16, 3, 256) for r in results], 1)
    lh_p = np.stack([r["lh_p"].reshape(4, 256) for r in results], 1)
    lh_s = cat("lh_s", 1)
    mk_p = np.stack([r["mk_p"].reshape(4, 256, 4, 256) for r in results], 1)
    mv_p = np.stack([r["mv_p"].reshape(4, 256, 4, 256) for r in results], 1)
    outs = (y_p, y_s, delta_p, delta_s, dconv_p, dconv_s, conf_p, conf_s, pool_p, pool_s, lconv_p, lconv_s, lh_p, lh_s, mk_p, mv_p)
    return tuple(np.ascontiguousarray(o.astype(np.float32)) for o in outs)


def kernel(**inputs):
    nc = build()
    in_maps = [make_in_map(inputs, c) for c in range(8)]
    res = run_bass_kernel_spmd(nc, in_maps, core_ids=list(range(8)))
    return gather(res.results)
```

```python
import numpy as np
from contextlib import ExitStack
import concourse.bass as bass
import concourse.mybir as mybir
from concourse.bass_utils import run_bass_kernel_spmd

F32 = mybir.dt.float32
BF16 = mybir.dt.bfloat16
I32 = mybir.dt.int32
AF = mybir.ActivationFunctionType
ALU = mybir.AluOpType
AX = mybir.AxisListType

EPS = 1e-6
NLAYER = 4
OFF_B, OFF_C, OFF_D = 1032, 1544, 1800
FFN = 2816


class Rec:
    __slots__ = ("w", "r", "wm", "dsem", "dcount")

    def __init__(self):
        self.w = None
        self.r = {}
        self.wm = {}
        self.dsem = {}
        self.dcount = {}


class Buf:
    def __init__(self, t, name):
        self.t = t
        self.name = name
        self.whole = Rec()
        self.parts = {}

    def __getitem__(self, k):
        return self.t[k]

    def recs(self, key):
        if key is None:
            return [self.whole] + list(self.parts.values())
        if key not in self.parts:
            self.parts[key] = Rec()
        return [self.whole, self.parts[key]]

    def own(self, key):
        if key is None:
            return self.whole
        if key not in self.parts:
            self.parts[key] = Rec()
        return self.parts[key]

    def collapse(self):
        for rec in self.parts.values():
            for (sem, val) in rec.r.values():
                k = id(sem)
                if k not in self.whole.r or self.whole.r[k][1] < val:
                    self.whole.r[k] = (sem, val)
            wevs = list(rec.wm.values())
            if rec.w is not None:
                wevs.append(rec.w)
            for (sem, val) in wevs:
                k = id(sem)
                if k not in self.whole.wm or self.whole.wm[k][1] < val:
                    self.whole.wm[k] = (sem, val)
        self.parts = {}


class Eng:
    def __init__(self, name, h):
        self.name = name
        self.h = h
        self.sem = None
        self.count = 0
        self.seen = {}


class FW:
    def __init__(self, nc, stack):
        self.nc = nc
        self.stack = stack
        self.pe = Eng("pe", nc.tensor)
        self.act = Eng("act", nc.scalar)
        self.dve = Eng("dve", nc.vector)
        self.pool = Eng("pool", nc.gpsimd)
        self.sp = Eng("sp", nc.sync)
        self.engs = [self.pe, self.act, self.dve, self.pool, self.sp]
        for e in self.engs:
            e.sem = stack.enter_context(nc.semaphore("s_" + e.name))
        self.n_dma_sems = 0
        self.out_events = {}
        self.nbuf = 0
        self.ninst = 0
        self.dead = False
        self.hook = None
        self.stream_idx = {}
        self.yield_now = None

    def wait_until(self, cond):
        if cond():
            return
        if self.yield_now is None:
            raise RuntimeError("wait_until outside interleaved emission")
        n = 0
        while not cond():
            self.yield_now()
            n += 1
            if n > 10_000_000:
                raise RuntimeError("wait_until: never satisfied")

    def sbuf(self, shape, dtype, name=None):
        self.nbuf += 1
        name = name or f"b{self.nbuf}"
        t = self.stack.enter_context(self.nc.sbuf_tensor(name, list(shape), dtype))
        return Buf(t, name)

    def psum(self, shape, dtype, name=None):
        self.nbuf += 1
        name = name or f"p{self.nbuf}"
        t = self.stack.enter_context(self.nc.psum_tensor(name, list(shape), dtype))
        return Buf(t, name)

    def _dsem(self, rec, kind):
        if kind not in rec.dsem:
            self.n_dma_sems += 1
            rec.dsem[kind] = self.stack.enter_context(self.nc.semaphore(f"d{self.n_dma_sems}"))
            rec.dcount[kind] = 0
        return rec.dsem[kind]

    def _collect(self, reads, writes):
        need = {}

        def add(ev):
            if ev is None:
                return
            sem, val = ev
            k = id(sem)
            if k not in need or need[k][1] < val:
                need[k] = (sem, val)

        for (b, key) in reads:
            for rec in b.recs(key):
                add(rec.w)
                for ev in rec.wm.values():
                    add(ev)
        for (b, key) in writes:
            for rec in b.recs(key):
                add(rec.w)
                for ev in rec.wm.values():
                    add(ev)
                for ev in rec.r.values():
                    add(ev)
        return need

    def _emit_waits(self, eng, need):
        for k, (sem, val) in need.items():
            if eng is self.pe and sem is self.pe.sem:
                continue
            if eng.seen.get(k, 0) >= val:
                continue
            eng.seen[k] = val
            eng.h.wait_ge(sem, val)

    @staticmethod
    def _norm(lst):
        out = []
        for x in lst:
            if isinstance(x, Buf):
                out.append((x, None))
            else:
                out.append(x)
        return out

    def op(self, eng, fn, reads=(), writes=()):
        if self.dead:
            return None
        reads = self._norm(reads)
        writes = self._norm(writes)
        need = self._collect(reads, writes)
        self._emit_waits(eng, need)
        ins = fn(eng.h)
        self.ninst += 1
        eng.count += 1
        ins.then_inc(eng.sem, 1)
        ev = (eng.sem, eng.count)
        for (b, key) in reads:
            b.own(key).r[id(eng.sem)] = ev
        for (b, key) in writes:
            if key is None:
                b.parts = {}
            rec = b.own(key)
            rec.w = ev
            rec.r = {}
            if key is None:
                rec.wm = {}
        if self.hook is not None:
            self.hook()
        return ins

    def dma(self, q, fn, reads=(), writes=(), is_out=False):
        if self.dead:
            return None
        reads = self._norm(reads)
        writes = self._norm(writes)
        need = self._collect(reads, writes)
        self._emit_waits(q, need)
        ins = fn(q.h)
        self.ninst += 1
        if writes:
            b, key = writes[0]
        else:
            b, key = reads[0]
        rec = b.own(key)
        kind = "sw" if q is self.pool else "hw"
        sem = self._dsem(rec, kind)
        rec.dcount[kind] += 16
        ins.then_inc(sem, 16)
        ev = (sem, rec.dcount[kind])
        self.last_dma_ev = ev
        for (b2, key2) in reads:
            b2.own(key2).r[id(sem)] = ev
        for (b2, key2) in writes:
            if key2 is None:
                b2.parts = {}
            r2 = b2.own(key2)
            r2.w = ev
            r2.r = {}
            if key2 is None:
                r2.wm = {}
        if is_out:
            self.out_events[id(sem)] = ev
        return ins

    def finish(self):
        for sem, val in self.out_events.values():
            self.sp.h.wait_ge(sem, val)
        for e in self.engs:
            if e is not self.sp and e.count > 0:
                self.sp.h.wait_ge(e.sem, e.count)


IN_SHAPES = dict(
    xp=[2048, 1024], xs=[128, 1024], mem=[256, 1024],
    st_delta=[4, 16, 4, 64, 64], st_dconv=[4, 48, 768], st_bconv=[4, 480, 256], st_pool=[4, 240, 256],
    st_lconv=[4, 48, 256], st_lh=[4, 16, 256], ck=[4, 16, 256, 1024], cv=[4, 16, 256, 1024],
    norm_mix_pre=[4, 1024], norm_mix_post=[4, 1024], w_in=[4, 1024, 2312], conv_qkv=[4, 4, 768], a_log=[4, 4],
    dt_bias=[4, 4], onorm_a=[4, 64], dw_b=[4, 31, 256], dwbias_b=[4, 256], gn_gain_b=[4, 256], gn_bias_b=[4, 256],
    w_pw_b=[4, 256, 256], w_pool=[4, 4, 64, 64], scale_pool=[4, 256], conv_d=[4, 4, 256], conv_bias_d=[4, 256],
    w_rg=[4, 4, 64, 64], b_rg=[4, 256], w_ig=[4, 4, 64, 64], b_ig=[4, 256], lam_d=[4, 256], w_out=[4, 1024, 1024],
    norm_x_pre=[4, 1024], norm_x_post=[4, 1024], norm_mem=[4, 1024], w_xq=[4, 1024, 1024], w_xkv=[4, 1024, 2048],
    w_xo=[4, 1024, 1024], norm_ffn_pre=[4, 1024], norm_ffn_post=[4, 1024], w_ffn_in=[4, 1024, 5632],
    w_ffn_out=[4, 2816, 1024])
OUT_SHAPES = dict(
    y_p=[2048, 1024], y_s=[128, 1024], delta_p=[4, 4, 64, 64], delta_s=[4, 16, 4, 64, 64], dconv_p=[4, 3, 768],
    dconv_s=[4, 48, 768], conf_p=[4, 30, 256], conf_s=[4, 480, 256], pool_p=[4, 15, 256], pool_s=[4, 240, 256],
    lconv_p=[4, 3, 256], lconv_s=[4, 48, 256], lh_p=[4, 1, 256], lh_s=[4, 16, 256], mk_p=[4, 256, 1024],
    mv_p=[4, 256, 1024])


class _Stop(Exception):
    pass


import threading


def run_streams(fw, fns, weights=None):
    n = len(fns)
    sems = [threading.Semaphore(0) for _ in range(n)]
    main_sem = threading.Semaphore(0)
    done = [False] * n
    errs = []
    idx = {}
    cnt = [0] * n
    weights = weights or [1] * n

    def hook(force=False):
        i = idx.get(threading.get_ident())
        if i is None:
            return
        cnt[i] += 1
        if cnt[i] < weights[i] and not force:
            return
        cnt[i] = 0
        j = i
        for d in range(1, n + 1):
            j = (i + d) % n
            if not done[j]:
                break
        if j == i:
            if force:
                raise RuntimeError("yield_now: no other live stream (emission-order deadlock)")
            return
        sems[j].release()
        sems[i].acquire()

    def runner(i):
        idx[threading.get_ident()] = i
        sems[i].acquire()
        try:
            fns[i]()
        except BaseException as e:
            errs.append(e)
        done[i] = True
        alive = [j for j in range(n) if not done[j]]
        if alive:
            sems[alive[0]].release()
        else:
            main_sem.release()

    ths = [threading.Thread(target=runner, args=(i,)) for i in range(n)]
    old = fw.hook
    fw.hook = hook
    fw.yield_now = lambda: hook(True)
    fw.stream_idx = idx
    for t in ths:
        t.start()
    sems[0].release()
    main_sem.acquire()
    for t in ths:
        t.join()
    fw.hook = old
    fw.yield_now = None
    fw.stream_idx = {}
    if errs:
        raise errs[0]


import os
STOP = float(os.environ.get("KSTOP", "99"))
INTERLEAVE = os.environ.get("KINTER", "1") == "1"
USE_F32R = os.environ.get("KF32R", "0") == "1"
F32R = mybir.dt.float32r


def RR(ap):
    return ap.bitcast(F32R) if USE_F32R else ap


def _chk(fw, level):
    if STOP <= level:
        fw.dead = True


def build(NL=NLAYER, tts=None):
    nc = bass.Bass("TRN2", target_bir_lowering=False)
    D = {}
    for k, s in IN_SHAPES.items():
        D[k] = nc.dram_tensor(k, list(s), F32, kind="ExternalInput").ap()
    for k, s in OUT_SHAPES.items():
        D[k] = nc.dram_tensor(k, list(s), F32, kind="ExternalOutput").ap()
    if tts is None:
        tts = [("p", i) for i in range(4)] + [("s", 0)]

    with ExitStack() as st:
        fw = FW(nc, st)
        pe, act, dve, pool, sp = fw.pe, fw.act, fw.dve, fw.pool, fw.sp

        def V(fn, r=(), w=()):
            return fw.op(dve, fn, r, w)

        def A(fn, r=(), w=()):
            return fw.op(act, fn, r, w)

        def P(fn, r=(), w=()):
            return fw.op(pe, fn, r, w)

        def G(fn, r=(), w=()):
            return fw.op(pool, fn, r, w)

        X = fw.sbuf([128, 4, 1024], F32, "X")
        NW = 4
        WS = [fw.sbuf([128, 8, 1024], BF16, f"WS{i}") for i in range(NW)]
        XN = fw.sbuf([128, 8, 512], BF16, "XN")
        MIX = fw.sbuf([128, 8, 512], BF16, "MIX")
        A1 = fw.sbuf([128, 6200], F32, "A1")
        A2 = fw.sbuf([128, 6800], F32, "A2")
        A3 = fw.sbuf([128, 6144], F32, "A3")
        GBC = [fw.sbuf([128, 1024], F32, f"GBC{i}") for i in range(2)]
        STG = fw.sbuf([128, 1024], F32, "STG")
        SM = fw.sbuf([128, 256], F32, "SM")
        TST = fw.sbuf([128, 128], F32, "TST")
        TST2 = fw.sbuf([128, 128], F32, "TST2")
        PQ = [fw.psum([128, 4, 512], F32, f"PQ{i}") for i in range(2)]
        IDENT = fw.sbuf([128, 128], F32, "IDENT")
        ONES = fw.sbuf([128, 128], F32, "ONES")
        TRIU = fw.sbuf([128, 128], F32, "TRIU")
        NEGSL = fw.sbuf([128, 128], F32, "NEGSL")
        BONES = fw.sbuf([128, 128], F32, "BONES")
        INVC0 = fw.sbuf([128, 2, 16], F32, "INVC0")
        SSB = fw.sbuf([64, 4, 64], F32, "SSB")
        GPRE = fw.sbuf([128, NLAYER, 4, 8], F32, "GPRE")
        CWA = fw.sbuf([128, NLAYER, 6, 4], F32, "CWA")
        DWB = fw.sbuf([128, NLAYER, 2, 31], F32, "DWB")
        CWD = fw.sbuf([128, NLAYER, 2, 4], F32, "CWD")
        VEC = fw.sbuf([128, NLAYER, 8, 2], F32, "VEC")
        NSP = fw.sbuf([128, NLAYER, 2], F32, "NSP")
        WPW = fw.sbuf([128, NLAYER, 2, 256], BF16, "WPW")
        WBD = fw.sbuf([128, NLAYER, 3, 2, 128], BF16, "WBD")
        TOKC = fw.sbuf([128, NLAYER, 72], F32, "TOKC")
        TA = fw.sbuf([128, NLAYER, 6, 3], F32, "TA")
        TB = fw.sbuf([128, NLAYER, 2, 30], F32, "TB")
        TC = fw.sbuf([128, NLAYER, 2, 15], F32, "TC")
        TD = fw.sbuf([128, NLAYER, 2, 3], F32, "TD")
        HST = fw.sbuf([128, NLAYER, 2], F32, "HST")
        SST = fw.sbuf([64, NLAYER, 4, 64], F32, "SST")

        _pb = [0, 0, 0]

        def bank():
            si = fw.stream_idx.get(threading.get_ident())
            if si is None:
                i = _pb[0] % 8
                _pb[0] += 1
                return PQ[i // 4], i % 4
            i = _pb[1 + si] % 4
            _pb[1 + si] += 1
            return PQ[si], i

        _pq = [0]

        def quad():
            si = fw.stream_idx.get(threading.get_ident())
            if si is not None:
                return PQ[si]
            i = _pq[0] % 2
            _pq[0] += 1
            return PQ[i]

        _ws = [0]

        def wslot():
            i = _ws[0] % NW
            _ws[0] += 1
            return WS[i]

        SCR = nc.dram_tensor("wscratch", [NLAYER, 15, 128, 8 * 1024], BF16).ap()
        scr_ev = {}
        wpass = [0]

        def load_w(src_ap, ncols, col0=0, slot=None, wid=None, final=True):
            if slot is None:
                slot = wslot()
            if wid is None or wpass[0] == 0:
                nk = src_ap.shape[0] // 128
                fw.dma(pool, lambda e: e.dma_start(out=slot[:, 0:nk, col0:col0 + ncols],
                                                   in_=src_ap.rearrange("(k p) n -> p k n", p=128)), writes=[slot])
                if wid is not None and final and not fw.dead:
                    fw.dma(sp, lambda e: e.dma_start(out=SCR[wid[0], wid[1]], in_=slot.t[:, :, :].rearrange("p k n -> p (k n)")), reads=[slot])
                    scr_ev[wid] = fw.last_dma_ev
            elif final and not fw.dead:
                sem, val = scr_ev[wid]
                fw._emit_waits(sp, {id(sem): (sem, val)})
                fw.dma(sp, lambda e: e.dma_start(out=slot.t[:, :, :].rearrange("p k n -> p (k n)"), in_=SCR[wid[0], wid[1]]), writes=[slot])
            return slot

        G(lambda e: e.memset(ONES[:, :], 1.0), w=[ONES])
        G(lambda e: e.affine_select(out=IDENT[:, :], in_=ONES[:, :], pattern=[[-1, 128]], compare_op=ALU.is_equal,
                                    fill=0.0, base=0, channel_multiplier=1), r=[ONES], w=[IDENT])
        G(lambda e: e.affine_select(out=TRIU[:, :], in_=ONES[:, :], pattern=[[1, 128]], compare_op=ALU.is_ge,
                                    fill=0.0, base=0, channel_multiplier=-1), r=[ONES], w=[TRIU])
        G(lambda e: e.memset(BONES[:, :], -1.0), w=[BONES])
        G(lambda e: e.affine_select(out=NEGSL[:, :], in_=BONES[:, :], pattern=[[-1, 128]], compare_op=ALU.is_ge,
                                    fill=0.0, base=-1, channel_multiplier=1), r=[BONES], w=[NEGSL])
        for ws_ in WS:
            G(lambda e, ws_=ws_: e.memset(ws_.t[:, :, :].rearrange("p k n -> p (k n)"), 0.0), w=[ws_])
        G(lambda e: e.memset(BONES[:, :], 0.0), w=[BONES])
        G(lambda e: e.memset(BONES[0:64, 0:64], 1.0), w=[BONES])
        G(lambda e: e.memset(BONES[64:128, 64:128], 1.0), w=[BONES])
        for b_ in (TA, TB, TC, TD, HST, SST, WBD, SM):
            G(lambda e, b_=b_: e.memset(b_.t[:].rearrange(" ".join(["p"] + [f"a{i}" for i in range(len(b_.t.shape) - 1)]) + " -> p (" + " ".join(
                [f"a{i}" for i in range(len(b_.t.shape) - 1)]) + ")"), 0.0), w=[b_])
        WIN = [[2, 4], [8, 16]]
        IOT = A1
        G(lambda e: e.iota(IOT.t[:, 0:16].bitcast(I32), pattern=[[1, 16]], base=1, channel_multiplier=0), w=[IOT])
        V(lambda e: e.tensor_copy(out=IOT.t[:, 16:32], in_=IOT.t[:, 0:16].bitcast(I32)), r=[IOT], w=[IOT])
        for cc in range(2):
            for hf in range(2):
                ps = slice(64 * hf, 64 * hf + 64)
                wv = float(WIN[cc][hf])
                V(lambda e, ps=ps, cc=cc, wv=wv: e.tensor_scalar(out=INVC0[ps, cc, :], in0=IOT.t[ps, 16:32], scalar1=wv,
                                                                 scalar2=None, op0=ALU.min), r=[IOT], w=[INVC0])
        V(lambda e: e.reciprocal(out=INVC0[:, :, :], in_=INVC0[:, :, :]), r=[INVC0], w=[INVC0])

        def sdma(out_ap, in_ap, wbuf, q=None):
            fw.dma(q or sp, lambda e: e.dma_start(out=out_ap, in_=in_ap, allow_slow_non_contiguous=True), writes=[wbuf])

        for l in range(NL):
            for i, nm in enumerate(["norm_mix_pre", "norm_x_pre", "norm_ffn_pre", "norm_mem"]):
                sdma(GPRE[:, l, i, :], D[nm][l].rearrange("(c p) -> p c", p=128), GPRE)
            for j in range(4):
                sdma(CWA[:, l, :, j], D["conv_qkv"][l, j].rearrange("(c p) -> p c", p=128), CWA)
                sdma(CWD[:, l, :, j], D["conv_d"][l, j].rearrange("(c p) -> p c", p=128), CWD)
            for cc in range(2):
                sdma(DWB[:, l, cc, :], D["dw_b"][l, :, cc * 128:(cc + 1) * 128].rearrange("j p -> p j"), DWB)
            for i, nm in enumerate(["dwbias_b", "gn_gain_b", "gn_bias_b", "scale_pool", "conv_bias_d", "b_rg", "b_ig", "lam_d"]):
                sdma(VEC[:, l, i, :], D[nm][l].rearrange("(c p) -> p c", p=128), VEC)
            sdma(WPW[:, l, :, :], D["w_pw_b"][l].rearrange("(c p) n -> p c n", p=128), WPW, q=pool)
            for i, nm in enumerate(["w_pool", "w_rg", "w_ig"]):
                for gi in range(4):
                    hf, cc = gi % 2, gi // 2
                    sdma(WBD[64 * hf:64 * hf + 64, l, i, cc, 64 * hf:64 * hf + 64], D[nm][l, gi], WBD, q=pool)
            sdma(TOKC[:, l, 0:4], D["dt_bias"][l:l + 1, :].broadcast_to([128, 4]), TOKC)
            sdma(TOKC[:, l, 4:8], D["a_log"][l:l + 1, :].broadcast_to([128, 4]), TOKC)
            sdma(TOKC[:, l, 8:72], D["onorm_a"][l:l + 1, :].broadcast_to([128, 64]), TOKC)
        for l in range(NL):
            A(lambda e, l=l: e.activation(out=TOKC[:, l, 4:8], in_=TOKC[:, l, 4:8], func=AF.Exp), r=[TOKC], w=[TOKC])
            V(lambda e, l=l: e.tensor_scalar(out=TOKC[:, l, 4:8], in0=TOKC[:, l, 4:8], scalar1=-1.0, scalar2=None, op0=ALU.mult),
              r=[TOKC], w=[TOKC])
            A(lambda e, l=l: e.activation(out=NSP[:, l, :], in_=VEC[:, l, 7, :], func=AF.Exp, scale=-1.0), r=[VEC], w=[NSP])
            V(lambda e, l=l: e.tensor_scalar(out=NSP[:, l, :], in0=NSP[:, l, :], scalar1=1.0, scalar2=None, op0=ALU.add), r=[NSP], w=[NSP])
            A(lambda e, l=l: e.activation(out=NSP[:, l, :], in_=NSP[:, l, :], func=AF.Ln), r=[NSP], w=[NSP])
            V(lambda e, l=l: e.tensor_scalar(out=NSP[:, l, :], in0=NSP[:, l, :], scalar1=-8.0, scalar2=None, op0=ALU.mult), r=[NSP], w=[NSP])

        _chk(fw, 1)

        def rstd_from_ss(ss_ap, n, out_ap, bufs_r, buf_w):
            V(lambda e: e.tensor_scalar(out=out_ap, in0=ss_ap, scalar1=1.0 / n, scalar2=EPS, op0=ALU.mult, op1=ALU.add), r=bufs_r, w=[buf_w])
            A(lambda e: e.activation(out=out_ap, in_=out_ap, func=AF.Sqrt), r=[buf_w], w=[buf_w])
            V(lambda e: e.reciprocal(out=out_ap, in_=out_ap), r=[buf_w], w=[buf_w])

        def tr(out_ap, in_ap, np_, rbufs, wbuf):
            P(lambda e: e.transpose(out=out_ap, in_=in_ap, identity=IDENT[0:np_, 0:np_]), r=list(rbufs) + [IDENT], w=[wbuf])

        def prenorm(src_tm, NB, gi, l, dst=XN):
            for tb in range(NB):
                junk = A2
                A(lambda e, tb=tb: e.activation(out=junk.t[:, 0:1024], in_=src_tm(tb), func=AF.Square, accum_out=SM[:, tb:tb + 1]),
                  r=[X], w=[(junk, "junk"), (SM, "ss")])
                rstd_from_ss(SM[:, tb:tb + 1], 1024.0, SM[:, 8 + tb:9 + tb], [(SM, "ss")], (SM, "rs"))
                V(lambda e, tb=tb: e.tensor_scalar(out=junk.t[:, 1024:2048], in0=src_tm(tb), scalar1=SM[:, 8 + tb:9 + tb], scalar2=None,
                                                   op0=ALU.mult), r=[X, (SM, "rs")], w=[(junk, "xs")])
                pq = quad()
                for c in range(8):
                    tr(pq[:, c // 4, (c % 4) * 128:(c % 4) * 128 + 128], junk.t[:, 1024 + c * 128:1024 + (c + 1) * 128], 128, [(junk, "xs")], pq)
                src = pq.t[:, 0:2, :].rearrange("p a (b t) -> p (a b) t", b=4)
                V(lambda e, tb=tb, src=src: e.tensor_tensor(out=dst[:, :, tb * 128:(tb + 1) * 128], in0=src,
                                                            in1=GPRE[:, l, gi, :].unsqueeze(2).broadcast_to([128, 8, 128]), op=ALU.mult),
                  r=[pq, GPRE], w=[(dst, tb)])
            A2.collapse()

        _gb = [0]

        def out_proj(src_fm, slots, nk, gain_name, l, NB, src_key=None):
            A2.collapse()
            _gb[0] += 1
            gb = GBC[_gb[0] % 2]
            fw.dma(sp, lambda e: e.dma_start(out=gb[:, :], in_=D[gain_name][l:l + 1, :].broadcast_to([128, 1024])), writes=[gb])
            for tb in range(NB):
                pq = quad()
                for n in range(2):
                    for k in range(nk):
                        P(lambda e, n=n, k=k, tb=tb: e.matmul(pq[:, n, :], lhsT=src_fm[:, k, tb * 128:(tb + 1) * 128],
                                                             rhs=slots[k // 8][:, k % 8, n * 512:(n + 1) * 512], start=(k == 0), stop=(k == nk - 1)),
                          r=[(src_fm, src_key) if src_key is None else (src_fm, tb), slots[k // 8]], w=[pq])
                for n in range(2):
                    A(lambda e, n=n: e.activation(out=A2.t[:, 1024 + n * 512:1024 + (n + 1) * 512], in_=pq[:, n, :], func=AF.Copy), r=[pq], w=[(A2, "ycp")])
                    A(lambda e, n=n: e.activation(out=A2.t[:, 0:512], in_=A2.t[:, 1024 + n * 512:1024 + (n + 1) * 512], func=AF.Square, accum_out=SM[:, 16 + n:17 + n]),
                      r=[(A2, "ycp")], w=[(A2, "junk"), (SM, "ss2")])
                V(lambda e: e.tensor_tensor(out=SM[:, 18:19], in0=SM[:, 16:17], in1=SM[:, 17:18], op=ALU.add), r=[(SM, "ss2")], w=[(SM, "ss3")])
                rstd_from_ss(SM[:, 18:19], 1024.0, SM[:, 19:20], [(SM, "ss3")], (SM, "rs3"))
                for n in range(2):
                    V(lambda e, n=n: e.scalar_tensor_tensor(out=A2.t[:, 2048 + n * 512:2048 + (n + 1) * 512], in0=A2.t[:, 1024 + n * 512:1024 + (n + 1) * 512], scalar=SM[:, 19:20],
                                                            in1=gb[:, n * 512:(n + 1) * 512], op0=ALU.mult, op1=ALU.mult),
                      r=[(A2, "ycp"), (SM, "rs3"), gb], w=[(A2, "yn")])
                V(lambda e, tb=tb: e.tensor_tensor(out=X[:, tb, :], in0=X[:, tb, :], in1=A2.t[:, 2048:3072], op=ALU.add), r=[X, (A2, "yn")], w=[X])

        def _stg():
            si = fw.stream_idx.get(threading.get_ident())
            if si == 1:
                return 768, (STG, "b"), TST2
            if si == 0:
                return 0, (STG, "a"), TST
            return 0, (STG, None), TST

        def load_tm_to_fm(dram_ap, R, ncc, dst_fn, dst_bufs, sg=None):
            c0, sk, _ = _stg()
            fw.dma(sp, lambda e: e.dma_start(out=STG[0:R, c0:c0 + ncc * 128], in_=dram_ap), writes=[sk])
            for cc in range(ncc):
                pb, bk = bank()
                tr(pb[:, bk, 0:R], STG[0:R, c0 + cc * 128:c0 + (cc + 1) * 128], R, [sk], (pb, bk))
                srcp = pb[:, bk, 0:R] if sg is None else pb[:, bk, 0:R].rearrange("p (s w) -> p s w", s=sg)
                A(lambda e, cc=cc, srcp=srcp: e.activation(out=dst_fn(cc), in_=srcp, func=AF.Copy), r=[(pb, bk)], w=dst_bufs)

        def store_fm_to_tm(src_fn, src_bufs, R, ncc, dram_ap, sg=None):
            c0, sk, tst = _stg()
            for cc in range(ncc):
                pb, bk = bank()
                if sg is None:
                    tr(pb[0:R, bk, 0:128], src_fn(cc), 128, src_bufs, (pb, bk))
                else:
                    V(lambda e, cc=cc: e.tensor_copy(out=tst[:, 0:R].rearrange("p (s w) -> p s w", s=sg), in_=src_fn(cc)), r=src_bufs, w=[tst])
                    tr(pb[0:R, bk, 0:128], tst[:, 0:R], 128, [tst], (pb, bk))
                A(lambda e, cc=cc, pb=pb, bk=bk: e.activation(out=STG[0:R, c0 + cc * 128:c0 + (cc + 1) * 128], in_=pb[0:R, bk, 0:128], func=AF.Copy),
                  r=[(pb, bk)], w=[sk])
            fw.dma(sp, lambda e: e.dma_start(out=dram_ap, in_=STG[0:R, c0:c0 + ncc * 128]), reads=[sk], is_out=True)

        def conv_fm(xpv, ncc, T, W, wfn, bfn, outv, rbufs, wbuf):
            for cc in range(ncc):
                V(lambda e, cc=cc: e.tensor_scalar(out=outv(cc), in0=xpv(cc, 0, T), scalar1=wfn(cc, 0), scalar2=(bfn(cc) if bfn else None),
                                                   op0=ALU.mult, op1=(ALU.add if bfn else ALU.bypass)), r=rbufs, w=[wbuf])
                for j in range(1, W):
                    V(lambda e, cc=cc, j=j: e.scalar_tensor_tensor(out=outv(cc), in0=xpv(cc, j, T), scalar=wfn(cc, j), in1=outv(cc),
                                                                  op0=ALU.mult, op1=ALU.add), r=rbufs, w=[wbuf])

        for l in range(NL):
            fw.dma(sp, lambda e: e.dma_start(out=X[:, 0:2, :], in_=D["mem"].rearrange("(tb p) d -> p tb d", p=128)), writes=[X])
            prenorm(lambda tb: X[:, tb, :], 2, 3, l)
            slots = [load_w(D["w_xkv"][l][:, 0:1024], 1024), load_w(D["w_xkv"][l][:, 1024:2048], 1024)]
            for kv in range(2):
                for tb in range(2):
                    pq = quad()
                    for n in range(2):
                        for k in range(8):
                            P(lambda e, n=n, k=k, tb=tb, kv=kv: e.matmul(pq[:, n, :], lhsT=XN[:, k, tb * 128:(tb + 1) * 128],
                                                                        rhs=slots[kv][:, k, n * 512:(n + 1) * 512], start=(k == 0), stop=(k == 7)),
                              r=[(XN, tb), slots[kv]], w=[pq])
                    A(lambda e, pq=pq: e.activation(out=STG[:, :], in_=pq.t[:, 0:2, :].rearrange("p a b -> p (a b)"), func=AF.Copy), r=[pq], w=[STG])
                    dst = D["mk_p" if kv == 0 else "mv_p"][l, tb * 128:(tb + 1) * 128, :]
                    fw.dma(sp, lambda e, dst=dst: e.dma_start(out=dst, in_=STG[:, :]), reads=[STG], is_out=True)
        for sem, val in list(fw.out_events.values()):
            if not fw.dead:
                sp.h.wait_ge(sem, val)
                pool.h.wait_ge(sem, val)
        _chk(fw, 2)

        for (kind, ti) in tts:
            if kind == "p":
                ntok, NB, nseq, T = 512, 4, 1, 512
                xsrc = D["xp"][ti * 512:(ti + 1) * 512, :]
                ydst = D["y_p"][ti * 512:(ti + 1) * 512, :]
            else:
                ntok, NB, nseq, T = 128, 1, 16, 8
                xsrc = D["xs"]
                ydst = D["y_s"]
            first = (kind == "p" and ti == 0)
            last_p = (kind == "p" and ti == 3)
            fw.dma(sp, lambda e: e.dma_start(out=X[:, 0:NB, :], in_=xsrc.rearrange("(tb p) d -> p tb d", p=128)), writes=[X])

            for l in range(NL):
                prenorm(lambda tb: X[:, tb, :], NB, 0, l)
                w0 = load_w(D["w_in"][l][:, 0:768], 768, wid=(l, 0))
                w1 = load_w(D["w_in"][l][:, 768:1544], 776, wid=(l, 1))
                w2 = load_w(D["w_in"][l][:, 1544:2312], 768, wid=(l, 2))
                wo = load_w(D["w_out"][l], 1024, wid=(l, 3))
                for a_ in (A1, A2, A3):
                    a_.collapse()
                _chk(fw, 2.2)

                def xpview(arena, off, ncc, W):
                    sz = ncc * nseq * (W - 1 + T)
                    return arena.t[:, off:off + sz].rearrange("p (c s w) -> p c s w", c=ncc, s=nseq)

                def proj_fm(slot, col, M=128):
                    pb, bk = bank()
                    for k in range(8):
                        P(lambda e, k=k: e.matmul(pb[0:M, bk, 0:ntok], lhsT=slot[:, k, col:col + M], rhs=XN[:, k, 0:ntok], start=(k == 0), stop=(k == 7)),
                          r=[XN, slot], w=[(pb, bk)])
                    return pb, bk

                def ps3(pb, bk, M=128):
                    return pb[0:M, bk, 0:ntok].rearrange("p (s t) -> p s t", s=nseq)

                def tails_in(xv, TT_, st_name, W, ncc, key):
                    if kind == "p":
                        V(lambda e: e.tensor_copy(out=xv[:, :, 0, 0:W - 1], in_=TT_[:, l, :, :]), r=[TT_], w=[key])
                    else:
                        R_all = 16 * (W - 1)
                        ng = 1 if R_all <= 128 else R_all // 120
                        sg = 16 // ng
                        for g in range(ng):
                            R = sg * (W - 1)
                            load_tm_to_fm(D[st_name][l, g * R:(g + 1) * R, :], R, ncc,
                                          lambda cc, g=g: xv[:, cc, g * sg:(g + 1) * sg, 0:W - 1], [key], sg=sg)

                def tails_out(xv, TT_, out_p, out_s, W, ncc, key):
                    if kind == "p":
                        V(lambda e: e.tensor_copy(out=TT_[:, l, :, :], in_=xv[:, :, 0, T:T + W - 1]), r=[key], w=[TT_])
                        if last_p:
                            store_fm_to_tm(lambda cc: xv[:, cc, 0, T:T + W - 1], [key], W - 1, ncc, D[out_p][l])
                    else:
                        R_all = 16 * (W - 1)
                        ng = 1 if R_all <= 128 else R_all // 120
                        sg = 16 // ng
                        for g in range(ng):
                            R = sg * (W - 1)
                            store_fm_to_tm(lambda cc, g=g: xv[:, cc, g * sg:(g + 1) * sg, T:T + W - 1], [key], R, ncc,
                                           D[out_s][l, g * R:(g + 1) * R, :], sg=sg)

                C = 64 if kind == "p" else 8
                LV = 6 if kind == "p" else 3
                nbatch = ntok // (2 * C)
                flags = {}
                turn = [0]
                YAf_g = A1.t[:, 3100:3100 + 6 * ntok].rearrange("p (c n) -> p c n", c=6)
                kYA_g = (A1, "YA")

                def delta_batch(bt, AR, smo, sfx, SSv, SSk):
                    YAf, kYA = YAf_g, kYA_g
                    AR.collapse()
                    t0 = bt * 2 * C
                    tcol = [t0 + ci * C for ci in range(2)]
                    QKV = AR.t[0:C, 0:1536].rearrange("p (a n) -> p a n", a=2)
                    kQKV = (AR, "QKV")
                    pq = quad()
                    for ci in range(2):
                        for cc in range(6):
                            col = cc * 128
                            tr(pq[0:C, 2 * ci + col // 512, col % 512:col % 512 + 128], YAf[:, cc, tcol[ci]:tcol[ci] + C], 128, [kYA], pq)
                    src = pq.t[0:C, :, :].rearrange("p (a b) n -> p a (b n)", a=2)[:, :, 0:768]
                    A(lambda e, src=src: e.activation(out=RR(QKV), in_=src, func=AF.Copy), r=[pq], w=[kQKV])
                    SQ = AR.t[0:C, 1536:2560].rearrange("p (g d) -> p g d", d=64)
                    QK3 = AR.t[0:C, 0:1536].rearrange("p (a n) -> p a n", a=2)[:, :, 0:512].rearrange("p a (g d) -> p a g d", d=64)
                    SQ4 = AR.t[0:C, 1536:2560].rearrange("p (a g d) -> p a g d", a=2, d=64)
                    V(lambda e: e.tensor_tensor(out=RR(SQ4), in0=QK3, in1=QK3, op=ALU.mult), r=[kQKV], w=[(AR, "SQ")])
                    V(lambda e: e.tensor_reduce(out=SM[0:C, smo + 32:smo + 48], in_=SQ, axis=AX.X, op=ALU.add), r=[(AR, "SQ")], w=[(SM, "l2" + sfx)])
                    V(lambda e: e.tensor_scalar(out=SM[0:C, smo + 32:smo + 48], in0=SM[0:C, smo + 32:smo + 48], scalar1=EPS, scalar2=None, op0=ALU.add), r=[(SM, "l2" + sfx)], w=[(SM, "l2" + sfx)])
                    A(lambda e: e.activation(out=SM[0:C, smo + 32:smo + 48], in_=SM[0:C, smo + 32:smo + 48], func=AF.Sqrt), r=[(SM, "l2" + sfx)], w=[(SM, "l2" + sfx)])
                    V(lambda e: e.reciprocal(out=SM[0:C, smo + 32:smo + 48], in_=SM[0:C, smo + 32:smo + 48]), r=[(SM, "l2" + sfx)], w=[(SM, "l2" + sfx)])
                    l2v = SM[0:C, smo + 32:smo + 48].rearrange("p (a g) -> p a g", a=2)
                    V(lambda e: e.tensor_scalar(out=l2v[:, :, 0:4], in0=l2v[:, :, 0:4], scalar1=0.125, scalar2=None, op0=ALU.mult), r=[(SM, "l2" + sfx)], w=[(SM, "l2" + sfx)])
                    V(lambda e: e.tensor_tensor(out=RR(QK3), in0=QK3, in1=l2v.unsqueeze(3).broadcast_to([C, 2, 8, 64]), op=ALU.mult), r=[kQKV, (SM, "l2" + sfx)], w=[kQKV])
                    QKF = AR.t[0:64, 1536:1536 + 16 * C].rearrange("p (a g t) -> p a g t", a=2, g=8)
                    pq = quad()
                    for ci in range(2):
                        for g in range(8):
                            tr(pq[0:64, ci, g * C:(g + 1) * C], QKV[:, ci, g * 64:(g + 1) * 64], C, [kQKV], pq)
                    A(lambda e, pq=pq: e.activation(out=RR(QKF), in_=pq.t[0:64, 0:2, 0:8 * C].rearrange("p a (g t) -> p a g t", g=8), func=AF.Copy),
                      r=[pq, (AR, "SQ")], w=[(AR, "QKF")])
                    pb, bk = bank()
                    for ci in range(2):
                        for k in range(8):
                            P(lambda e, ci=ci, k=k: e.matmul(pb[0:C, bk, ci * 8:ci * 8 + 8], lhsT=XN[:, k, tcol[ci]:tcol[ci] + C], rhs=w1[:, k, 0:8],
                                                            start=(k == 0), stop=(k == 7)), r=[XN, w1], w=[(pb, bk)])
                    GBv = pb[0:C, bk, 0:16].rearrange("p (a g) -> p a g", a=2)
                    gg = SM[0:C, smo + 48:smo + 56].rearrange("p (a g) -> p a g", a=2)
                    be = SM[0:C, smo + 56:smo + 64].rearrange("p (a g) -> p a g", a=2)
                    V(lambda e: e.tensor_tensor(out=gg, in0=GBv[:, :, 0:4], in1=TOKC[0:C, l, 0:4].unsqueeze(1).broadcast_to([C, 2, 4]), op=ALU.add),
                      r=[(pb, bk), TOKC], w=[(SM, "gg" + sfx)])
                    A(lambda e: e.activation(out=be, in_=GBv[:, :, 4:8], func=AF.Sigmoid), r=[(pb, bk)], w=[(SM, "be" + sfx)])
                    V(lambda e: e.tensor_scalar(out=gg, in0=gg, scalar1=30.0, scalar2=None, op0=ALU.min), r=[(SM, "gg" + sfx)], w=[(SM, "gg" + sfx)])
                    A(lambda e: e.activation(out=gg, in_=gg, func=AF.Exp), r=[(SM, "gg" + sfx)], w=[(SM, "gg" + sfx)])
                    V(lambda e: e.tensor_scalar(out=gg, in0=gg, scalar1=1.0, scalar2=None, op0=ALU.add), r=[(SM, "gg" + sfx)], w=[(SM, "gg" + sfx)])
                    A(lambda e: e.activation(out=gg, in_=gg, func=AF.Ln), r=[(SM, "gg" + sfx)], w=[(SM, "gg" + sfx)])
                    V(lambda e: e.tensor_tensor(out=gg, in0=gg, in1=TOKC[0:C, l, 4:8].unsqueeze(1).broadcast_to([C, 2, 4]), op=ALU.mult),
                      r=[(SM, "gg" + sfx), TOKC], w=[(SM, "gg" + sfx)])
                    pb, bk = bank()
                    P(lambda e: e.matmul(pb[0:C, bk, 0:8], lhsT=TRIU[0:C, 0:C], rhs=SM[0:C, smo + 48:smo + 56], start=True, stop=True), r=[TRIU, (SM, "gg" + sfx)], w=[(pb, bk)])
                    P(lambda e: e.matmul(pb[0:64, bk, 8:16], lhsT=ONES[0:C, 0:64], rhs=SM[0:C, smo + 48:smo + 56], start=True, stop=True), r=[ONES, (SM, "gg" + sfx)], w=[(pb, bk)])
                    gc = SM[0:C, smo + 64:smo + 72]
                    gt = SM[0:64, smo + 72:smo + 80]
                    V(lambda e: e.tensor_copy(out=gc, in_=pb[0:C, bk, 0:8]), r=[(pb, bk)], w=[(SM, "gc" + sfx)])
                    V(lambda e: e.tensor_copy(out=gt, in_=pb[0:64, bk, 8:16]), r=[(pb, bk)], w=[(SM, "gt" + sfx)])
                    egc = SM[0:C, smo + 80:smo + 88]
                    egd = SM[0:C, smo + 88:smo + 96]
                    egt = SM[0:64, smo + 96:smo + 104]
                    kf = SM[0:C, smo + 104:smo + 112]
                    A(lambda e: e.activation(out=egc, in_=gc, func=AF.Exp), r=[(SM, "gc" + sfx)], w=[(SM, "egc" + sfx)])
                    A(lambda e: e.activation(out=egt, in_=gt, func=AF.Exp), r=[(SM, "gt" + sfx)], w=[(SM, "egt" + sfx)])
                    V(lambda e: e.tensor_tensor(out=egd, in0=SM[0:C, smo + 72:smo + 80], in1=gc, op=ALU.subtract), r=[(SM, "gt" + sfx), (SM, "gc" + sfx)], w=[(SM, "egd" + sfx)])
                    A(lambda e: e.activation(out=egd, in_=egd, func=AF.Exp), r=[(SM, "egd" + sfx)], w=[(SM, "egd" + sfx)])
                    V(lambda e: e.tensor_tensor(out=kf, in0=SM[0:C, smo + 56:smo + 64], in1=egc, op=ALU.mult), r=[(SM, "be" + sfx), (SM, "egc" + sfx)], w=[(SM, "kf" + sfx)])
                    CC8 = 8 * C

                    def u3(i):
                        return AR.t[0:C, i * 512:i * 512 + CC8].rearrange("p (g f) -> p g f", g=8)
                    pbt, bkt = bank()
                    tr(pbt[0:8, bkt, 0:C], gc, C, [(SM, "gc" + sfx)], (pbt, bkt))
                    GCT = GBC[0][0:8, 0:C]
                    V(lambda e: e.tensor_copy(out=GCT, in_=pbt[0:8, bkt, 0:C]), r=[(pbt, bkt)], w=[GBC[0]])
                    pb, bk = bank()
                    for g in range(8):
                        P(lambda e, g=g: e.matmul(pb[0:C, bk, g * C:(g + 1) * C], lhsT=IDENT[0:8, g:g + 1].broadcast_to([8, C]), rhs=GCT, start=True, stop=True),
                          r=[IDENT, GBC[0]], w=[(pb, bk)])
                    EE = u3(6)
                    V(lambda e: e.tensor_tensor(out=RR(EE), in0=pb[0:C, bk, 0:CC8].rearrange("p (g f) -> p g f", g=8), in1=gc.unsqueeze(2).broadcast_to([C, 8, C]),
                                                op=ALU.subtract), r=[(pb, bk), (SM, "gc" + sfx)], w=[(AR, "u6")])
                    A(lambda e: e.activation(out=RR(EE), in_=EE, func=AF.Abs), r=[(AR, "u6")], w=[(AR, "u6")])
                    A(lambda e: e.activation(out=RR(EE), in_=EE, func=AF.Exp, scale=-1.0), r=[(AR, "u6")], w=[(AR, "u6")])
                    EN = u3(5)
                    EQ = u3(7)
                    V(lambda e: e.tensor_tensor(out=RR(EN), in0=EE, in1=NEGSL[0:C, 0:C].unsqueeze(1).broadcast_to([C, 8, C]), op=ALU.mult),
                      r=[(AR, "u6"), NEGSL], w=[(AR, "u5")])
                    V(lambda e: e.tensor_tensor(out=RR(EQ), in0=EE, in1=TRIU[0:C, 0:C].unsqueeze(1).broadcast_to([C, 8, C]), op=ALU.mult),
                      r=[(AR, "u6"), TRIU], w=[(AR, "u7")])
                    pbk, bkk = bank()
                    pbq, bkq = bank()
                    for ci in range(2):
                        for h in range(4):
                            g = ci * 4 + h
                            P(lambda e, ci=ci, h=h, g=g: e.matmul(pbk[0:C, bkk, g * C:(g + 1) * C], lhsT=RR(QKF[:, ci, 4 + h, :]), rhs=RR(QKF[:, ci, 4 + h, :]),
                                                                 start=True, stop=True), r=[(AR, "QKF")], w=[(pbk, bkk)])
                            P(lambda e, ci=ci, h=h, g=g: e.matmul(pbq[0:C, bkq, g * C:(g + 1) * C], lhsT=RR(QKF[:, ci, 4 + h, :]), rhs=RR(QKF[:, ci, h, :]),
                                                                 start=True, stop=True), r=[(AR, "QKF")], w=[(pbq, bkq)])
                    NN = u3(6)
                    V(lambda e: e.tensor_tensor(out=RR(NN), in0=pbk[0:C, bkk, 0:CC8].rearrange("p (g f) -> p g f", g=8), in1=EN, op=ALU.mult),
                      r=[(pbk, bkk), (AR, "u5")], w=[(AR, "u6")])
                    V(lambda e: e.tensor_tensor(out=RR(NN), in0=NN, in1=SM[0:C, smo + 56:smo + 64].unsqueeze(2).broadcast_to([C, 8, C]), op=ALU.mult),
                      r=[(AR, "u6"), (SM, "be" + sfx)], w=[(AR, "u6")])
                    QKT = u3(7)
                    V(lambda e: e.tensor_tensor(out=RR(QKT), in0=pbq[0:C, bkq, 0:CC8].rearrange("p (g f) -> p g f", g=8), in1=EQ, op=ALU.mult),
                      r=[(pbq, bkq), (AR, "u7")], w=[(AR, "u7")])
                    pb, bk = bank()
                    for g in range(8):
                        tr(pb[0:C, bk, g * C:(g + 1) * C], NN[:, g, :], C, [(AR, "u6")], (pb, bk))
                    MM = u3(5)
                    A(lambda e, pb=pb, bk=bk: e.activation(out=RR(MM), in_=pb[0:C, bk, 0:CC8].rearrange("p (g f) -> p g f", g=8), func=AF.Copy), r=[(pb, bk)], w=[(AR, "u5")])
                    UU = u3(8)
                    V(lambda e: e.tensor_tensor(out=RR(UU), in0=MM, in1=IDENT[0:C, 0:C].unsqueeze(1).broadcast_to([C, 8, C]), op=ALU.add), r=[(AR, "u5"), IDENT], w=[(AR, "u8")])
                    cur = {"N": (NN, "u6"), "M": (MM, "u5"), "U": (UU, "u8")}
                    free = [(u3(9), "u9"), (u3(10), "u10"), (u3(11), "u11")]
                    for lev in range(1, LV):
                        (Nv, Nk), (Mv, Mk), (Uv, Uk) = cur["N"], cur["M"], cur["U"]
                        (N2, N2k) = free.pop(0)
                        pb, bk = bank()
                        for g in range(8):
                            P(lambda e, g=g, Mv=Mv, Nv=Nv: e.matmul(pb[0:C, bk, g * C:(g + 1) * C], lhsT=RR(Mv[:, g, :]), rhs=RR(Nv[:, g, :]), start=True, stop=True),
                              r=[(AR, Mk), (AR, Nk)], w=[(pb, bk)])
                        A(lambda e, pb=pb, bk=bk, N2=N2: e.activation(out=RR(N2), in_=pb[0:C, bk, 0:CC8].rearrange("p (g f) -> p g f", g=8), func=AF.Copy),
                          r=[(pb, bk)], w=[(AR, N2k)])
                        if lev < LV - 1:
                            (M2, M2k) = free.pop(0)
                            pb2, bk2 = bank()
                            for g in range(8):
                                P(lambda e, g=g, Mv=Mv, Nv=Nv: e.matmul(pb2[0:C, bk2, g * C:(g + 1) * C], lhsT=RR(Nv[:, g, :]), rhs=RR(Mv[:, g, :]), start=True, stop=True),
                                  r=[(AR, Mk), (AR, Nk)], w=[(pb2, bk2)])
                            A(lambda e, pb2=pb2, bk2=bk2, M2=M2: e.activation(out=RR(M2), in_=pb2[0:C, bk2, 0:CC8].rearrange("p (g f) -> p g f", g=8), func=AF.Copy),
                              r=[(pb2, bk2)], w=[(AR, M2k)])
                        (U2, U2k) = free.pop(0)
                        pb3, bk3 = bank()
                        for g in range(8):
                            P(lambda e, g=g, N2=N2, Uv=Uv: e.matmul(pb3[0:C, bk3, g * C:(g + 1) * C], lhsT=RR(N2[:, g, :]), rhs=RR(Uv[:, g, :]), start=True, stop=True),
                              r=[(AR, N2k), (AR, Uk)], w=[(pb3, bk3)])
                        V(lambda e, pb3=pb3, bk3=bk3, U2=U2, Uv=Uv: e.tensor_tensor(out=RR(U2), in0=pb3[0:C, bk3, 0:CC8].rearrange("p (g f) -> p g f", g=8), in1=Uv, op=ALU.add),
                          r=[(pb3, bk3), (AR, Uk)], w=[(AR, U2k)])
                        free.append((Nv, Nk))
                        free.append((Uv, Uk))
                        if lev < LV - 1:
                            free.append((Mv, Mk))
                            cur = {"N": (N2, N2k), "M": (M2, M2k), "U": (U2, U2k)}
                        else:
                            cur = {"N": (N2, N2k), "M": (Mv, Mk), "U": (U2, U2k)}
                    (UU, Uk) = cur["U"]
                    used = {Uk, "u7"}
                    avail = [i for i in (5, 6, 8, 9, 10, 11) if "u%d" % i not in used]
                    QKV4 = AR.t[0:C, 0:1536].rearrange("p (a n) -> p a n", a=2)

                    def part4(o):
                        return QKV4[:, :, o:o + 256].rearrange("p a (h d) -> p a h d", h=4)
                    Kt, Vt = part4(256), part4(512)

                    def bc4(smap):
                        return smap.rearrange("p (a h) -> p a h", a=2).unsqueeze(3).broadcast_to([C, 2, 4, 64])
                    V(lambda e: e.tensor_tensor(out=RR(Vt), in0=Vt, in1=bc4(SM[0:C, smo + 56:smo + 64]), op=ALU.mult), r=[kQKV, (SM, "be" + sfx)], w=[kQKV])
                    iK = avail.pop(0)
                    KBG = AR.t[0:C, iK * 512:iK * 512 + 512].rearrange("p (a h d) -> p a h d", a=2, h=4)
                    V(lambda e: e.tensor_tensor(out=RR(KBG), in0=Kt, in1=bc4(kf), op=ALU.mult), r=[kQKV, (SM, "kf" + sfx)], w=[(AR, "u%d" % iK)])
                    V(lambda e: e.tensor_tensor(out=RR(Kt), in0=Kt, in1=bc4(egd), op=ALU.mult), r=[kQKV, (SM, "egd" + sfx)], w=[kQKV])
                    pbv, bkv = bank()
                    pbw, bkw = bank()
                    for ci in range(2):
                        for h in range(4):
                            g = ci * 4 + h
                            P(lambda e, ci=ci, h=h, g=g: e.matmul(pbv[0:C, bkv, g * 64:(g + 1) * 64], lhsT=RR(UU[:, g, :]), rhs=RR(Vt[:, ci, h, :]), start=True, stop=True),
                              r=[(AR, Uk), kQKV], w=[(pbv, bkv)])
                            P(lambda e, ci=ci, h=h, g=g: e.matmul(pbw[0:64, bkw, g * C:(g + 1) * C], lhsT=RR(KBG[:, ci, h, :]), rhs=RR(UU[:, g, :]), start=True, stop=True),
                              r=[(AR, Uk), (AR, "u%d" % iK)], w=[(pbw, bkw)])
                    iV = avail.pop(0)
                    iW = avail.pop(0)
                    WV = AR.t[0:C, iV * 512:iV * 512 + 512].rearrange("p (a h d) -> p a h d", a=2, h=4)
                    WKT = AR.t[0:64, iW * 512:iW * 512 + CC8].rearrange("p (a h t) -> p a h t", a=2, h=4)
                    A(lambda e: e.activation(out=RR(WV), in_=pbv[0:C, bkv, :].rearrange("p (a h d) -> p a h d", a=2, h=4), func=AF.Copy), r=[(pbv, bkv)], w=[(AR, "u%d" % iV)])
                    A(lambda e: e.activation(out=RR(WKT), in_=pbw[0:64, bkw, 0:CC8].rearrange("p (a h t) -> p a h t", a=2, h=4), func=AF.Copy),
                      r=[(pbw, bkw)], w=[(AR, "u%d" % iW)])
                    if kind == "p":
                        fw.wait_until(lambda: turn[0] == bt)
                    iO = avail.pop(0)
                    OO = AR.t[0:C, iO * 512:iO * 512 + 512].rearrange("p (a h d) -> p a h d", a=2, h=4)
                    iU = avail.pop(0)
                    UT = AR.t[0:C, iU * 512:iU * 512 + 512].rearrange("p (a h d) -> p a h d", a=2, h=4)
                    SS = SSv
                    for ci in range(2):
                        if kind == "p":
                            Sv, Sk = SST[:, l, :, :], SST
                        else:
                            seq = bt * 2 + ci
                            Sv, Sk = SS, SSk
                            fw.dma(sp, lambda e, seq=seq: e.dma_start(out=SS, in_=D["st_delta"][l, seq].rearrange("h d e -> d h e")), writes=[Sk])
                        pb, bk = bank()
                        for h in range(4):
                            P(lambda e, ci=ci, h=h: e.matmul(pb[0:C, bk, h * 64:(h + 1) * 64], lhsT=WKT[:, ci, h, :], rhs=Sv[:, h, :], start=True, stop=True),
                              r=[(AR, "u%d" % iW), Sk], w=[(pb, bk)])
                        for h in range(4):
                            P(lambda e, ci=ci, h=h: e.matmul(pb[0:C, bk, 256 + h * 64:256 + (h + 1) * 64], lhsT=QKF[:, ci, h, :], rhs=Sv[:, h, :], start=True, stop=True),
                              r=[(AR, "QKF"), Sk], w=[(pb, bk)])
                        V(lambda e, ci=ci, pb=pb, bk=bk: e.tensor_tensor(out=RR(UT[:, 0, :, :]), in0=WV[:, ci, :, :], in1=pb[0:C, bk, 0:256].rearrange("p (h d) -> p h d", h=4),
                                                                        op=ALU.subtract), r=[(AR, "u%d" % iV), (pb, bk)], w=[(AR, "UTu")])
                        V(lambda e, ci=ci, pb=pb, bk=bk: e.tensor_tensor(out=RR(UT[:, 1, :, :]), in0=pb[0:C, bk, 256:512].rearrange("p (h d) -> p h d", h=4),
                                                                        in1=egc[:, ci * 4:ci * 4 + 4].unsqueeze(2).broadcast_to([C, 4, 64]), op=ALU.mult),
                          r=[(pb, bk), (SM, "egc" + sfx)], w=[(AR, "UTt")])
                        pb2, bk2 = bank()
                        for h in range(4):
                            P(lambda e, ci=ci, h=h: e.matmul(pb2[0:C, bk2, h * 64:(h + 1) * 64], lhsT=RR(QKT[:, ci * 4 + h, :]), rhs=RR(UT[:, 0, h, :]), start=True, stop=True),
                              r=[(AR, "u7"), (AR, "UTu")], w=[(pb2, bk2)])
                        for h in range(4):
                            P(lambda e, ci=ci, h=h: e.matmul(pb2[0:64, bk2, 256 + h * 64:256 + (h + 1) * 64], lhsT=RR(Kt[:, ci, h, :]), rhs=RR(UT[:, 0, h, :]), start=True, stop=True),
                              r=[kQKV, (AR, "UTu")], w=[(pb2, bk2)])
                        V(lambda e, ci=ci, pb2=pb2, bk2=bk2: e.tensor_tensor(out=RR(OO[:, ci, :, :]), in0=pb2[0:C, bk2, 0:256].rearrange("p (h d) -> p h d", h=4), in1=UT[:, 1, :, :],
                                                                            op=ALU.add), r=[(pb2, bk2), (AR, "UTt")], w=[(AR, "OO")])
                        V(lambda e, ci=ci, Sv=Sv: e.tensor_tensor(out=Sv, in0=Sv, in1=egt[:, ci * 4:ci * 4 + 4].unsqueeze(2).broadcast_to([64, 4, 64]), op=ALU.mult),
                          r=[Sk, (SM, "egt" + sfx)], w=[Sk])
                        V(lambda e, ci=ci, Sv=Sv, pb2=pb2, bk2=bk2: e.tensor_tensor(out=Sv, in0=Sv, in1=pb2[0:64, bk2, 256:512].rearrange("p (h d) -> p h d", h=4), op=ALU.add),
                          r=[Sk, (pb2, bk2)], w=[Sk])
                        if kind == "s":
                            fw.dma(sp, lambda e, seq=seq: e.dma_start(out=D["delta_s"][l, seq].rearrange("h d e -> d h e"), in_=SS), reads=[Sk], is_out=True)
                    if kind == "p":
                        turn[0] = bt + 1
                    iQ = avail.pop(0) if avail else iK
                    OS = AR.t[0:C, iK * 512:iK * 512 + 512].rearrange("p (g d) -> p g d", g=8)
                    OOg = AR.t[0:C, iO * 512:iO * 512 + 512].rearrange("p (g d) -> p g d", g=8)
                    V(lambda e: e.tensor_tensor(out=RR(OS), in0=OOg, in1=OOg, op=ALU.mult), r=[(AR, "OO")], w=[(AR, "u%d" % iK)])
                    V(lambda e: e.tensor_reduce(out=SM[0:C, smo + 112:smo + 120], in_=OS, axis=AX.X, op=ALU.add), r=[(AR, "u%d" % iK)], w=[(SM, "os" + sfx)])
                    rstd_from_ss(SM[0:C, smo + 112:smo + 120], 64.0, SM[0:C, smo + 120:smo + 128], [(SM, "os" + sfx)], (SM, "ors" + sfx))
                    V(lambda e: e.tensor_tensor(out=RR(OOg), in0=OOg, in1=SM[0:C, smo + 120:smo + 128].unsqueeze(2).broadcast_to([C, 8, 64]), op=ALU.mult), r=[(AR, "OO"), (SM, "ors" + sfx)], w=[(AR, "OO")])
                    V(lambda e: e.tensor_tensor(out=RR(OOg), in0=OOg, in1=TOKC[0:C, l, 8:72].unsqueeze(1).broadcast_to([C, 8, 64]), op=ALU.mult), r=[(AR, "OO"), TOKC], w=[(AR, "OO")])
                    pb, bk = bank()
                    for ci in range(2):
                        for k in range(8):
                            P(lambda e, ci=ci, k=k: e.matmul(pb[0:C, bk, ci * 256:(ci + 1) * 256], lhsT=XN[:, k, tcol[ci]:tcol[ci] + C], rhs=w1[:, k, 8:264],
                                                            start=(k == 0), stop=(k == 7)), r=[XN, w1], w=[(pb, bk)])
                    GTv = AR.t[0:C, iK * 512:iK * 512 + 512]
                    A(lambda e, pb=pb, bk=bk: e.activation(out=RR(GTv), in_=pb[0:C, bk, :], func=AF.Silu), r=[(pb, bk)], w=[(AR, "u%d" % iK)])
                    OOf = AR.t[0:C, iO * 512:iO * 512 + 512]
                    V(lambda e: e.tensor_tensor(out=RR(OOf), in0=OOf, in1=GTv, op=ALU.mult), r=[(AR, "OO"), (AR, "u%d" % iK)], w=[(AR, "OO")])
                    pb, bk = bank()
                    for ci in range(2):
                        for cc in range(2):
                            tr(pb[:, bk, (cc * 2 + ci) * C:(cc * 2 + ci + 1) * C], OOf[:, ci * 256 + cc * 128:ci * 256 + (cc + 1) * 128], C, [(AR, "OO")], (pb, bk))
                    A(lambda e, pb=pb, bk=bk: e.activation(out=MIX[:, 0:2, t0:t0 + 2 * C], in_=pb[:, bk, 0:4 * C].rearrange("p (c n) -> p c n", c=2), func=AF.Copy),
                      r=[(pb, bk)], w=[(MIX, "A%d" % bt)])

                def stream_bcd():
                    XB = xpview(A2, 0, 2, 31)
                    kXB = (A2, "XB")
                    tails_in(XB, TB, "st_bconv", 31, 2, kXB)
                    for cc in range(2):
                        pb1, bk1 = proj_fm(w1, 264 + cc * 128)
                        pb2, bk2 = proj_fm(w1, 264 + 256 + cc * 128)
                        A(lambda e: e.activation(out=A2.t[:, 4500:4500 + ntok], in_=pb2[:, bk2, 0:ntok], func=AF.Sigmoid), r=[(pb2, bk2)], w=[(A2, "sg")])
                        V(lambda e, cc=cc: e.tensor_tensor(out=XB[:, cc, :, 30:30 + T], in0=ps3(pb1, bk1),
                                                           in1=A2.t[:, 4500:4500 + ntok].rearrange("p (s t) -> p s t", s=nseq), op=ALU.mult),
                          r=[(pb1, bk1), (A2, "sg")], w=[kXB])
                    _chk(fw, 2.3)
                    tails_out(XB, TB, "conf_p", "conf_s", 31, 2, kXB)
                    _chk(fw, 2.35)
                    YB = A2.t[:, 1300:1300 + 2 * ntok].rearrange("p (c s t) -> p c s t", c=2, s=nseq)
                    kYB = (A2, "YB")
                    conv_fm(lambda cc, j, T_: XB[:, cc, :, j:j + T_], 2, T, 31, lambda cc, j: DWB[:, l, cc, j:j + 1], lambda cc: VEC[:, l, 0, cc:cc + 1],
                            lambda cc: YB[:, cc, :, :], [kXB, DWB, VEC], kYB)
                    YBf = A2.t[:, 1300:1300 + 2 * ntok].rearrange("p (c n) -> p c n", c=2)
                    _chk(fw, 2.4)
                    for cc in range(2):
                        sq = A2.t[:, 2400:2400 + ntok]
                        A(lambda e, cc=cc: e.activation(out=sq, in_=YBf[:, cc, :], func=AF.Square), r=[kYB], w=[(A2, "sqB")])
                        pbs, bks = bank()
                        P(lambda e, cc=cc: e.matmul(pbs[:, bks, 0:ntok], lhsT=BONES[:, :], rhs=YBf[:, cc, :], start=True, stop=True), r=[BONES, kYB], w=[(pbs, bks)])
                        pbq, bkq = bank()
                        P(lambda e: e.matmul(pbq[:, bkq, 0:ntok], lhsT=BONES[:, :], rhs=sq, start=True, stop=True), r=[BONES, (A2, "sqB")], w=[(pbq, bkq)])
                        _chk(fw, 2.42)
                        dd = A2.t[:, 3000:3000 + ntok]
                        msq = A2.t[:, 3600:3600 + ntok]
                        V(lambda e, cc=cc: e.scalar_tensor_tensor(out=dd, in0=pbs[:, bks, 0:ntok], scalar=-1.0 / 64, in1=YBf[:, cc, :], op0=ALU.mult, op1=ALU.add),
                          r=[(pbs, bks), kYB], w=[(A2, "ddB")])
                        _chk(fw, 2.43)
                        V(lambda e: e.tensor_scalar(out=msq, in0=pbs[:, bks, 0:ntok], scalar1=1.0 / 64, scalar2=None, op0=ALU.mult), r=[(pbs, bks)], w=[(A2, "msqB")])
                        V(lambda e: e.tensor_tensor(out=msq, in0=msq, in1=msq, op=ALU.mult), r=[(A2, "msqB")], w=[(A2, "msqB")])
                        _chk(fw, 2.435)
                        V(lambda e: e.scalar_tensor_tensor(out=msq, in0=pbq[:, bkq, 0:ntok], scalar=1.0 / 64, in1=msq, op0=ALU.mult, op1=ALU.subtract),
                          r=[(pbq, bkq), (A2, "msqB")], w=[(A2, "msqB")])
                        _chk(fw, 2.44)
                        V(lambda e: e.tensor_scalar(out=msq, in0=msq, scalar1=EPS, scalar2=None, op0=ALU.add), r=[(A2, "msqB")], w=[(A2, "msqB")])
                        A(lambda e: e.activation(out=msq, in_=msq, func=AF.Sqrt), r=[(A2, "msqB")], w=[(A2, "msqB")])
                        _chk(fw, 2.45)
                        V(lambda e: e.reciprocal(out=msq, in_=msq), r=[(A2, "msqB")], w=[(A2, "msqB")])
                        V(lambda e: e.tensor_tensor(out=dd, in0=dd, in1=msq, op=ALU.mult), r=[(A2, "ddB"), (A2, "msqB")], w=[(A2, "ddB")])
                        _chk(fw, 2.46)
                        yoff = 5100 if cc == 0 else 4200
                        ynb = A2.t[:, yoff:yoff + ntok // 2].bitcast(BF16)
                        A(lambda e, cc=cc, ynb=ynb: e.activation(out=ynb, in_=dd, func=AF.Silu, scale=VEC[:, l, 1, cc:cc + 1], bias=VEC[:, l, 2, cc:cc + 1]),
                          r=[(A2, "ddB"), VEC], w=[(A2, "ynB%d" % cc)])
                    _chk(fw, 2.5)
                    yn_c = [A2.t[:, 5100:5100 + ntok // 2].bitcast(BF16), A2.t[:, 4200:4200 + ntok // 2].bitcast(BF16)]
                    for oc in range(2):
                        pb, bk = bank()
                        for kc in range(2):
                            P(lambda e, kc=kc, oc=oc: e.matmul(pb[:, bk, 0:ntok], lhsT=WPW[:, l, kc, oc * 128:(oc + 1) * 128], rhs=yn_c[kc], start=(kc == 0), stop=(kc == 1)),
                              r=[WPW, (A2, "ynB0"), (A2, "ynB1")], w=[(pb, bk)])
                        A(lambda e, oc=oc: e.activation(out=MIX[:, 2 + oc, 0:ntok], in_=pb[:, bk, 0:ntok], func=AF.Copy), r=[(pb, bk)], w=[(MIX, "c%d" % (2 + oc))])

                    _chk(fw, 3)
                    A2.collapse()
                    XC = xpview(A2, 0, 2, 16)
                    kXC = (A2, "XC")
                    tails_in(XC, TC, "st_pool", 16, 2, kXC)
                    for cc in range(2):
                        pb, bk = proj_fm(w2, cc * 128)
                        A(lambda e, cc=cc: e.activation(out=XC[:, cc, :, 15:15 + T], in_=ps3(pb, bk), func=AF.Copy), r=[(pb, bk)], w=[kXC])
                    tails_out(XC, TC, "pool_p", "pool_s", 16, 2, kXC)
                    LW = 15 + T

                    def sview(off, ln):
                        return A2.t[:, off:off + 2 * nseq * ln].rearrange("p (c s w) -> p c s w", c=2, s=nseq)
                    S2 = sview(1100, LW - 1)
                    S4 = sview(2200, LW - 3)
                    S8 = sview(3300, LW - 7)
                    S16 = sview(4400, LW - 15)
                    V(lambda e: e.tensor_tensor(out=S2, in0=XC[:, :, :, 1:LW], in1=XC[:, :, :, 0:LW - 1], op=ALU.add), r=[kXC], w=[(A2, "S2")])
                    V(lambda e: e.tensor_tensor(out=S4, in0=S2[:, :, :, 2:LW - 1], in1=S2[:, :, :, 0:LW - 3], op=ALU.add), r=[(A2, "S2")], w=[(A2, "S4")])
                    V(lambda e: e.tensor_tensor(out=S8, in0=S4[:, :, :, 4:LW - 3], in1=S4[:, :, :, 0:LW - 7], op=ALU.add), r=[(A2, "S4")], w=[(A2, "S8")])
                    V(lambda e: e.tensor_tensor(out=S16, in0=S8[:, :, :, 8:LW - 7], in1=S8[:, :, :, 0:LW - 15], op=ALU.add), r=[(A2, "S8")], w=[(A2, "S16")])
                    SEL = A2.t[:, 5500:5500 + 2 * ntok].rearrange("p (c s t) -> p c s t", c=2, s=nseq)
                    srcs = {(0, 0): (S2, 14, "S2"), (0, 1): (S4, 12, "S4"), (1, 0): (S8, 8, "S8"), (1, 1): (S16, 0, "S16")}
                    for (cc, hf), (sv, o, nm) in srcs.items():
                        ps_ = slice(64 * hf, 64 * hf + 64)
                        wv = float(WIN[cc][hf])
                        V(lambda e, cc=cc, ps_=ps_, sv=sv, o=o, wv=wv: e.tensor_scalar(out=SEL[ps_, cc, :, :], in0=sv[ps_, cc, :, o:o + T], scalar1=1.0 / wv,
                                                                                      scalar2=None, op0=ALU.mult), r=[(A2, nm)], w=[(A2, "SEL")])
                        if first:
                            V(lambda e, cc=cc, ps_=ps_, sv=sv, o=o: e.tensor_tensor(out=SEL[ps_, cc, 0, 0:16], in0=sv[ps_, cc, 0, o:o + 16],
                                                                                   in1=INVC0[ps_, cc, :], op=ALU.mult), r=[(A2, nm), INVC0], w=[(A2, "SEL")])
                    DB = A2.t[:, 1100:1100 + ntok].bitcast(BF16).rearrange("p (c s t) -> p c s t", c=2, s=nseq)
                    V(lambda e: e.tensor_tensor(out=DB, in0=SEL, in1=XC[:, :, :, 15:15 + T], op=ALU.subtract), r=[(A2, "SEL"), kXC, (A2, "S2")], w=[(A2, "S2")])
                    DBf = A2.t[:, 1100:1100 + ntok].bitcast(BF16).rearrange("p (c n) -> p c n", c=2)
                    for cc in range(2):
                        pb, bk = bank()
                        P(lambda e, cc=cc: e.matmul(pb[:, bk, 0:ntok], lhsT=WBD[:, l, 0, cc, :], rhs=DBf[:, cc, :], start=True, stop=True), r=[WBD, (A2, "S2")], w=[(pb, bk)])
                        A(lambda e, cc=cc: e.activation(out=MIX[:, 4 + cc, 0:ntok], in_=pb[:, bk, 0:ntok], func=AF.Copy, scale=VEC[:, l, 3, cc:cc + 1]),
                          r=[(pb, bk), VEC], w=[(MIX, "c%d" % (4 + cc))])

                    _chk(fw, 4)
                    A2.collapse()
                    XD = xpview(A2, 0, 2, 4)
                    kXD = (A2, "XD")
                    tails_in(XD, TD, "st_lconv", 4, 2, kXD)

                    def reg(i):
                        return A2.t[:, 1100 + i * 1024:1100 + i * 1024 + 2 * ntok].rearrange("p (c n) -> p c n", c=2)
                    GD, XR, RR, II, AA = reg(0), reg(1), reg(2), reg(3), reg(4)
                    for cc in range(2):
                        pb, bk = proj_fm(w2, 256 + cc * 128)
                        A(lambda e, cc=cc: e.activation(out=GD[:, cc, :], in_=pb[:, bk, 0:ntok], func=AF.Gelu_apprx_tanh), r=[(pb, bk)], w=[(A2, "GD")])
                        pb, bk = proj_fm(w2, 512 + cc * 128)
                        A(lambda e, cc=cc: e.activation(out=XD[:, cc, :, 3:3 + T], in_=ps3(pb, bk), func=AF.Copy), r=[(pb, bk)], w=[kXD])
                    tails_out(XD, TD, "lconv_p", "lconv_s", 4, 2, kXD)
                    XR4 = A2.t[:, 1100 + 1024:1100 + 1024 + 2 * ntok].rearrange("p (c s t) -> p c s t", c=2, s=nseq)
                    conv_fm(lambda cc, j, T_: XD[:, cc, :, j:j + T_], 2, T, 4, lambda cc, j: CWD[:, l, cc, j:j + 1], lambda cc: VEC[:, l, 4, cc:cc + 1],
                            lambda cc: XR4[:, cc, :, :], [kXD, CWD, VEC], (A2, "XR"))
                    XRB = A2.t[:, 6220:6220 + ntok].bitcast(BF16).rearrange("p (c n) -> p c n", c=2)
                    V(lambda e: e.tensor_copy(out=XRB, in_=XR), r=[(A2, "XR")], w=[(A2, "XRB")])
                    for cc in range(2):
                        for (wi, dstv, bi, nm) in ((1, RR, 5, "RR"), (2, II, 6, "II")):
                            pb, bk = bank()
                            P(lambda e, cc=cc, wi=wi: e.matmul(pb[:, bk, 0:ntok], lhsT=WBD[:, l, wi, cc, :], rhs=XRB[:, cc, :], start=True, stop=True),
                              r=[WBD, (A2, "XRB")], w=[(pb, bk)])
                            A(lambda e, cc=cc, dstv=dstv, bi=bi, pb=pb, bk=bk: e.activation(out=dstv[:, cc, :], in_=pb[:, bk, 0:ntok], func=AF.Sigmoid,
                                                                                       bias=VEC[:, l, bi, cc:cc + 1]), r=[(pb, bk), VEC], w=[(A2, nm)])
                        A(lambda e, cc=cc: e.activation(out=AA[:, cc, :], in_=RR[:, cc, :], func=AF.Exp, scale=NSP[:, l, cc:cc + 1]), r=[(A2, "RR"), NSP], w=[(A2, "AA")])
                    A(lambda e: e.activation(out=RR, in_=AA, func=AF.Square), r=[(A2, "AA")], w=[(A2, "RR")])
                    V(lambda e: e.tensor_scalar(out=RR, in0=RR, scalar1=-1.0, scalar2=1.0, op0=ALU.mult, op1=ALU.add), r=[(A2, "RR")], w=[(A2, "RR")])
                    A(lambda e: e.activation(out=RR, in_=RR, func=AF.Sqrt), r=[(A2, "RR")], w=[(A2, "RR")])
                    V(lambda e: e.tensor_tensor(out=II, in0=II, in1=XR, op=ALU.mult), r=[(A2, "II"), (A2, "XR")], w=[(A2, "II")])
                    V(lambda e: e.tensor_tensor(out=II, in0=II, in1=RR, op=ALU.mult), r=[(A2, "II"), (A2, "RR")], w=[(A2, "II")])
                    HH = XR
                    if kind == "p":
                        for cc in range(2):
                            V(lambda e, cc=cc: e.tensor_tensor_scan(out=HH[:, cc, :], data0=AA[:, cc, :], data1=II[:, cc, :], initial=HST[:, l, cc:cc + 1],
                                                                    op0=ALU.mult, op1=ALU.add), r=[(A2, "AA"), (A2, "II"), HST], w=[(A2, "XR")])
                        V(lambda e: e.tensor_copy(out=HST[:, l, :], in_=HH[:, :, T - 1]), r=[(A2, "XR")], w=[HST])
                        if last_p:
                            store_fm_to_tm(lambda cc: HST[:, l, cc:cc + 1], [HST], 1, 2, D["lh_p"][l])
                    else:
                        H0 = A2.t[:, 6740:6772].rearrange("p (c s) -> p c s", c=2)
                        load_tm_to_fm(D["st_lh"][l], 16, 2, lambda cc: H0[:, cc, :], [(A2, "H0")])
                        AA4 = A2.t[:, 1100 + 4 * 1024:1100 + 4 * 1024 + 2 * ntok].rearrange("p (c s t) -> p c s t", c=2, s=nseq)
                        II4 = A2.t[:, 1100 + 3 * 1024:1100 + 3 * 1024 + 2 * ntok].rearrange("p (c s t) -> p c s t", c=2, s=nseq)
                        V(lambda e: e.tensor_tensor(out=H0, in0=H0, in1=AA4[:, :, :, 0], op=ALU.mult), r=[(A2, "H0"), (A2, "AA")], w=[(A2, "H0")])
                        V(lambda e: e.tensor_tensor(out=II4[:, :, :, 0], in0=II4[:, :, :, 0], in1=H0, op=ALU.add), r=[(A2, "II"), (A2, "H0")], w=[(A2, "II")])
                        V(lambda e: e.tensor_scalar(out=AA4[:, :, :, 0], in0=AA4[:, :, :, 0], scalar1=0.0, scalar2=None, op0=ALU.mult), r=[(A2, "AA")], w=[(A2, "AA")])
                        for cc in range(2):
                            V(lambda e, cc=cc: e.tensor_tensor_scan(out=HH[:, cc, :], data0=AA[:, cc, :], data1=II[:, cc, :], initial=0.0,
                                                                    op0=ALU.mult, op1=ALU.add), r=[(A2, "AA"), (A2, "II")], w=[(A2, "XR")])
                        store_fm_to_tm(lambda cc: XR4[:, cc, :, T - 1], [(A2, "XR")], 16, 2, D["lh_s"][l])
                    V(lambda e: e.tensor_tensor(out=MIX[:, 6:8, 0:ntok], in0=GD, in1=HH, op=ALU.mult), r=[(A2, "GD"), (A2, "XR")], w=[(MIX, "c6"), (MIX, "c7")])


                def stream_a():
                    _chk(fw, 5)
                    XA = xpview(A1, 0, 6, 4)
                    kXA = (A1, "XA")
                    tails_in(XA, TA, "st_dconv", 4, 6, kXA)
                    for cc in range(6):
                        pb, bk = proj_fm(w0, cc * 128)
                        A(lambda e, cc=cc: e.activation(out=XA[:, cc, :, 3:3 + T], in_=ps3(pb, bk), func=AF.Copy), r=[(pb, bk)], w=[kXA])
                    tails_out(XA, TA, "dconv_p", "dconv_s", 4, 6, kXA)
                    YA = A1.t[:, 3100:3100 + 6 * ntok].rearrange("p (c s t) -> p c s t", c=6, s=nseq)
                    YAf = A1.t[:, 3100:3100 + 6 * ntok].rearrange("p (c n) -> p c n", c=6)
                    kYA = (A1, "YA")
                    conv_fm(lambda cc, j, T_: XA[:, cc, :, j:j + T_], 6, T, 4, lambda cc, j: CWA[:, l, cc, j:j + 1], None,
                            lambda cc: YA[:, cc, :, :], [kXA, CWA], kYA)
                    A(lambda e: e.activation(out=YAf, in_=YAf, func=AF.Silu), r=[kYA], w=[kYA])

                    flags["prep"] = 1
                    for bt in range(nbatch):
                        delta_batch(bt, A3, 0, "", SSB[:, :, :], SSB)

                if INTERLEAVE:
                    run_streams(fw, [stream_a, stream_bcd], [3, 1])
                else:
                    stream_bcd()
                    stream_a()
                if last_p:
                    fw.dma(sp, lambda e: e.dma_start(out=D["delta_p"][l].rearrange("h d e -> d h e"), in_=SST[:, l, :, :]), reads=[SST], is_out=True)
                _chk(fw, 6)
                MIX.collapse()
                out_proj(MIX, [wo], 8, "norm_mix_post", l, NB)

                _chk(fw, 7)
                for a_ in (A1, A2, A3):
                    a_.collapse()
                prenorm(lambda tb: X[:, tb, :], NB, 1, l)
                wq = load_w(D["w_xq"][l], 1024, wid=(l, 4))
                wxo = load_w(D["w_xo"][l], 1024, wid=(l, 5))
                QF = A1.t[:, 0:2048].bitcast(BF16).rearrange("p (c n) -> p c n", c=8)
                for c in range(8):
                    pb, bk = proj_fm(wq, c * 128)
                    A(lambda e, c=c, pb=pb, bk=bk: e.activation(out=QF[:, c, 0:ntok], in_=pb[:, bk, 0:ntok], func=AF.Copy, scale=1.0 / 16.0), r=[(pb, bk)], w=[(A1, "QF")])
                KS = A1.t[:, 2048:4096].rearrange("p (a n) -> p a n", a=2)
                KTS = A2.t[:, 0:1024].bitcast(BF16).rearrange("p (c m) -> p c m", c=8)
                VS = A2.t[:, 1024:2048].bitcast(BF16).rearrange("p (a n) -> p a n", a=2)
                PEX = A1.t[:, 4096:5120].rearrange("p (h m) -> p h m", h=4)
                PT = A2.t[:, 2048:2560].bitcast(BF16).rearrange("p (c t) -> p c t", c=8)
                QPAD = A2.t[:, 2560:4608].bitcast(BF16).rearrange("p (c s t) -> p c s t", c=2, s=16)

                def load_kv(kap, vap):
                    fw.dma(sp, lambda e: e.dma_start(out=KS, in_=kap.rearrange("(a p) n -> p a n", p=128)), writes=[(A1, "KS")])
                    fw.dma(pool, lambda e: e.dma_start(out=VS, in_=vap.rearrange("(a p) n -> p a n", p=128)), writes=[(A2, "VS")])
                    pq = quad()
                    for a in range(2):
                        for c in range(8):
                            tr(pq[:, c // 2, (c % 2) * 256 + a * 128:(c % 2) * 256 + a * 128 + 128], KS[:, a, c * 128:(c + 1) * 128], 128, [(A1, "KS")], pq)
                    A(lambda e, pq=pq: e.activation(out=KTS, in_=pq.t[:, :, :].rearrange("p b (c m) -> p (b c) m", c=2), func=AF.Copy), r=[pq], w=[(A2, "KTS")])

                def softmax_pv(pqs, col0, ncol, vfn):
                    sc = pqs.t[:, :, 0:256]
                    V(lambda e: e.tensor_reduce(out=SM[:, 128:132], in_=sc, axis=AX.X, op=ALU.max), r=[pqs], w=[(SM, "mx")])
                    V(lambda e: e.tensor_scalar(out=SM[:, 128:132], in0=SM[:, 128:132], scalar1=-1.0, scalar2=None, op0=ALU.mult), r=[(SM, "mx")], w=[(SM, "mx")])
                    for h in range(4):
                        A(lambda e, h=h: e.activation(out=PEX[:, h, :], in_=pqs[:, h, 0:256], func=AF.Exp, bias=SM[:, 128 + h:129 + h], accum_out=SM[:, 132 + h:133 + h]),
                          r=[pqs, (SM, "mx")], w=[(A1, "PEX"), (SM, "sm")])
                    V(lambda e: e.reciprocal(out=SM[:, 132:136], in_=SM[:, 132:136]), r=[(SM, "sm")], w=[(SM, "sm")])
                    V(lambda e: e.tensor_tensor(out=PEX, in0=PEX, in1=SM[:, 132:136].unsqueeze(2).broadcast_to([128, 4, 256]), op=ALU.mult), r=[(A1, "PEX"), (SM, "sm")], w=[(A1, "PEX")])
                    pq2 = quad()
                    for h in range(4):
                        for a in range(2):
                            c = h * 2 + a
                            tr(pq2[:, c // 4, (c % 4) * 128:(c % 4) * 128 + 128], PEX[:, h, a * 128:(a + 1) * 128], 128, [(A1, "PEX")], pq2)
                    A(lambda e, pq2=pq2: e.activation(out=PT, in_=pq2.t[:, 0:2, :].rearrange("p b (c t) -> p (b c) t", c=4), func=AF.Copy), r=[pq2], w=[(A2, "PT")])
                    vfn()

                if kind == "p":
                    load_kv(D["mk_p"][l], D["mv_p"][l])
                    for tb in range(NB):
                        pqs = quad()
                        for h in range(4):
                            for dc in range(2):
                                P(lambda e, h=h, dc=dc, tb=tb: e.matmul(pqs[:, h, 0:256], lhsT=QF[:, 2 * h + dc, tb * 128:(tb + 1) * 128], rhs=KTS[:, 2 * h + dc, :],
                                                                       start=(dc == 0), stop=(dc == 1)), r=[(A1, "QF"), (A2, "KTS")], w=[pqs])

                        def pv(tb=tb):
                            pq3 = quad()
                            for h in range(4):
                                for ec in range(2):
                                    c = 2 * h + ec
                                    for a in range(2):
                                        P(lambda e, h=h, ec=ec, a=a, c=c: e.matmul(pq3[:, c // 4, (c % 4) * 128:(c % 4) * 128 + 128], lhsT=VS[:, a, h * 256 + ec * 128:h * 256 + ec * 128 + 128],
                                                                                  rhs=PT[:, 2 * h + a, :], start=(a == 0), stop=(a == 1)), r=[(A2, "VS"), (A2, "PT")], w=[pq3])
                            A(lambda e, pq3=pq3: e.activation(out=MIX[:, :, tb * 128:(tb + 1) * 128], in_=pq3.t[:, 0:2, :].rearrange("p b (c t) -> p (b c) t", c=4), func=AF.Copy),
                              r=[pq3], w=[(MIX, tb)])
                        softmax_pv(pqs, tb * 128, 128, pv)
                else:
                    G(lambda e: e.memset(A2.t[:, 2560:4608], 0.0), w=[(A2, "QPAD")])
                    pqs = PQ[0]
                    pq3 = PQ[1]
                    for h in range(4):
                        for s in range(16):
                            ib = (h * 16 + s) % 2
                            KSi = A1.t[:, 2048 + ib * 512:2048 + (ib + 1) * 512].rearrange("p (a n) -> p a n", a=2)
                            kKS = (A1, "KS%d" % ib)
                            kKT = (A2, "KT%d" % ib)
                            fw.dma(sp, lambda e, s=s, h=h, KSi=KSi: e.dma_start(out=KSi, in_=D["ck"][l, s][:, h * 256:(h + 1) * 256].rearrange("(a p) n -> p a n", p=128)),
                                   writes=[kKS])
                            pb, bk = PQ[1], (h * 16 + s) % 4
                            for a in range(2):
                                for dc in range(2):
                                    tr(pb[:, bk, dc * 256 + a * 128:dc * 256 + a * 128 + 128], KSi[:, a, dc * 128:(dc + 1) * 128], 128, [kKS], (pb, bk))
                            kt = A2.t[:, ib * 256:(ib + 1) * 256].bitcast(BF16).rearrange("p (c m) -> p c m", c=2)
                            A(lambda e, pb=pb, bk=bk, kt=kt: e.activation(out=kt, in_=pb[:, bk, :].rearrange("p (c m) -> p c m", c=2), func=AF.Copy), r=[(pb, bk)], w=[kKT])
                            for dc in range(2):
                                V(lambda e, s=s, dc=dc, h=h: e.tensor_copy(out=QPAD[:, dc, s, s * 8:(s + 1) * 8], in_=QF[:, 2 * h + dc, s * 8:(s + 1) * 8]),
                                  r=[(A1, "QF")], w=[(A2, "QPAD")])
                            for dc in range(2):
                                P(lambda e, s=s, dc=dc, h=h, kt=kt: e.matmul(pqs[:, h, 0:256], lhsT=QPAD[:, dc, s, :], rhs=kt[:, dc, :], start=(s == 0 and dc == 0), stop=(s == 15 and dc == 1)),
                                  r=[(A2, "QPAD"), kKT], w=[(pqs, h)])

                    def pv_s():
                        for s in range(16):
                            ib = s % 2
                            VSi = (VS if ib == 0 else A2.t[:, 4608:5632].bitcast(BF16).rearrange("p (a n) -> p a n", a=2))
                            kVS = (A2, "VS%d" % ib)
                            fw.dma(pool, lambda e, s=s, VSi=VSi: e.dma_start(out=VSi, in_=D["cv"][l, s].rearrange("(a p) n -> p a n", p=128)), writes=[kVS])
                            pbv, bkv = bank()
                            for h in range(4):
                                for ec in range(2):
                                    c = 2 * h + ec
                                    for a in range(2):
                                        P(lambda e, h=h, ec=ec, a=a, c=c, s=s, VSi=VSi: e.matmul(pbv[:, bkv, c * 8:(c + 1) * 8], lhsT=VSi[:, a, h * 256 + ec * 128:h * 256 + ec * 128 + 128],
                                                                                             rhs=PT[:, 2 * h + a, s * 8:(s + 1) * 8], start=(a == 0), stop=(a == 1)),
                                          r=[kVS, (A2, "PT")], w=[(pbv, bkv)])
                            A(lambda e, s=s, pbv=pbv, bkv=bkv: e.activation(out=MIX[:, :, s * 8:(s + 1) * 8], in_=pbv[:, bkv, 0:64].rearrange("p (c t) -> p c t", c=8), func=AF.Copy),
                              r=[(pbv, bkv)], w=[(MIX, "s%d" % s)])
                    softmax_pv(pqs, 0, 128, pv_s)
                MIX.collapse()
                out_proj(MIX, [wxo], 8, "norm_x_post", l, NB)

                _chk(fw, 8)
                for a_ in (A1, A2, A3):
                    a_.collapse()
                prenorm(lambda tb: X[:, tb, :], NB, 2, l)
                HID = A1.t[:, 0:5632].bitcast(BF16).rearrange("p (m n) -> p m n", m=22)
                for gp in range(6):
                    ncol = 512 if gp < 5 else 256
                    slot = wslot()
                    load_w(D["w_ffn_in"][l][:, gp * 512:gp * 512 + ncol], ncol, 0, slot, wid=(l, 6 + gp), final=False)
                    load_w(D["w_ffn_in"][l][:, FFN + gp * 512:FFN + gp * 512 + ncol], ncol, 512, slot, wid=(l, 6 + gp), final=True)
                    for mi in range(ncol // 128):
                        m = gp * 4 + mi
                        pg, bg = proj_fm(slot, mi * 128)
                        pu, bu = proj_fm(slot, 512 + mi * 128)
                        sg = A2.t[:, (m % 2) * 512:(m % 2) * 512 + ntok]
                        A(lambda e, pg=pg, bg=bg, sg=sg: e.activation(out=sg, in_=pg[:, bg, 0:ntok], func=AF.Silu), r=[(pg, bg)], w=[(A2, "sg%d" % (m % 2))])
                        V(lambda e, pu=pu, bu=bu, sg=sg, m=m: e.tensor_tensor(out=HID[:, m, 0:ntok], in0=pu[:, bu, 0:ntok], in1=sg, op=ALU.mult),
                          r=[(pu, bu), (A2, "sg%d" % (m % 2))], w=[(A1, "h%d" % m)])
                A1.collapse()
                fo = [load_w(D["w_ffn_out"][l][0:1024, :], 1024, wid=(l, 12)), load_w(D["w_ffn_out"][l][1024:2048, :], 1024, wid=(l, 13)), load_w(D["w_ffn_out"][l][2048:2816, :], 1024, wid=(l, 14))]

                A2.collapse()
                _gb[0] += 1
                gb = GBC[_gb[0] % 2]
                fw.dma(sp, lambda e: e.dma_start(out=gb[:, :], in_=D["norm_ffn_post"][l:l + 1, :].broadcast_to([128, 1024])), writes=[gb])
                for tb in range(NB):
                    pq = quad()
                    for n in range(2):
                        for k in range(22):
                            P(lambda e, n=n, k=k, tb=tb: e.matmul(pq[:, n, :], lhsT=HID[:, k, tb * 128:(tb + 1) * 128], rhs=fo[k // 8][:, k % 8, n * 512:(n + 1) * 512],
                                                                 start=(k == 0), stop=(k == 21)), r=[A1, fo[k // 8]], w=[pq])
                    for n in range(2):
                        A(lambda e, n=n, pq=pq: e.activation(out=A2.t[:, 1024 + n * 512:1024 + (n + 1) * 512], in_=pq[:, n, :], func=AF.Copy), r=[pq], w=[(A2, "ycp")])
                        A(lambda e, n=n: e.activation(out=A2.t[:, 0:512], in_=A2.t[:, 1024 + n * 512:1024 + (n + 1) * 512], func=AF.Square, accum_out=SM[:, 16 + n:17 + n]),
                          r=[(A2, "ycp")], w=[(A2, "junk"), (SM, "ss2")])
                    V(lambda e: e.tensor_tensor(out=SM[:, 18:19], in0=SM[:, 16:17], in1=SM[:, 17:18], op=ALU.add), r=[(SM, "ss2")], w=[(SM, "ss3")])
                    rstd_from_ss(SM[:, 18:19], 1024.0, SM[:, 19:20], [(SM, "ss3")], (SM, "rs3"))
                    for n in range(2):
                        V(lambda e, n=n: e.scalar_tensor_tensor(out=A2.t[:, 2048 + n * 512:2048 + (n + 1) * 512], in0=A2.t[:, 1024 + n * 512:1024 + (n + 1) * 512], scalar=SM[:, 19:20],
                                                                      in1=gb[:, n * 512:(n + 1) * 512], op0=ALU.mult, op1=ALU.mult),
                          r=[(A2, "ycp"), (SM, "rs3"), gb], w=[(A2, "yn")])
                    V(lambda e, tb=tb: e.tensor_tensor(out=X[:, tb, :], in0=X[:, tb, :], in1=A2.t[:, 2048:3072], op=ALU.add), r=[X, (A2, "yn")], w=[X])

            fw.dma(sp, lambda e: e.dma_start(out=ydst.rearrange("(tb p) d -> p tb d", p=128), in_=X[:, 0:NB, :]), reads=[X], is_out=True)
            wpass[0] += 1

        fw.finish()
        print(f"[kernel] built {fw.ninst} instructions, {fw.n_dma_sems} dma sems, sbuf free {nc.sbuf_bytes_remaining}")
    return nc


_W_KEYS = ["norm_mix_pre", "norm_mix_post", "w_in", "conv_qkv", "a_log", "dt_bias", "onorm_a", "dw_b", "dwbias_b", "gn_gain_b",
           "gn_bias_b", "w_pw_b", "w_pool", "scale_pool", "conv_d", "conv_bias_d", "w_rg", "b_rg", "w_ig", "b_ig", "lam_d", "w_out",
           "norm_x_pre", "norm_x_post", "norm_mem", "w_xq", "w_xkv", "w_xo", "norm_ffn_pre", "norm_ffn_post", "w_ffn_in", "w_ffn_out"]


def make_in_map(inp, c):
    f = lambda a: np.ascontiguousarray(np.asarray(a, dtype=np.float32))
    s = slice(16 * c, 16 * c + 16)
    m = dict(
        xp=f(inp["x_prompt"][c]), xs=f(inp["x_sample"][s]).reshape(128, 1024), mem=f(inp["mem_prompt"][c]),
        st_delta=f(inp["state_delta"][:, s]), st_dconv=f(inp["state_delta_conv"][:, s]).reshape(4, 48, 768),
        st_bconv=f(inp["state_conf_conv"][:, s]).reshape(4, 480, 256), st_pool=f(inp["state_pool"][:, s]).reshape(4, 240, 256),
        st_lconv=f(inp["state_lru_conv"][:, s]).reshape(4, 48, 256), st_lh=f(inp["state_lru_h"][:, s]),
        ck=f(inp["cache_mem_k"][:, s]).reshape(4, 16, 256, 1024), cv=f(inp["cache_mem_v"][:, s]).reshape(4, 16, 256, 1024))
    for k in _W_KEYS:
        m[k] = f(inp[k])
    return m


def gather(results):
    n = len(results)
    cat = lambda k, ax: np.concatenate([r[k] for r in results], axis=ax)
    y_p = np.stack([r["y_p"] for r in results], 0)
    y_s = np.concatenate([r["y_s"].reshape(16, 8, 1024) for r in results], 0)
    delta_p = np.stack([r["delta_p"] for r in results], 1)
    delta_s = cat("delta_s", 1)
    dconv_p = np.stack([r["dconv_p"] for r in results], 1)
    dconv_s = np.concatenate([r["dconv_s"].reshape(4, 16, 3, 768) for r in results], 1)
    conf_p = np.stack([r["conf_p"] for r in results], 1)
    conf_s = np.concatenate([r["conf_s"].reshape(4, 16, 30, 256) for r in results], 1)
    pool_p = np.stack([r["pool_p"] for r in results], 1)
    pool_s = np.concatenate([r["pool_s"].reshape(4, 16, 15, 256) for r in results], 1)
    lconv_p = np.stack([r["lconv_p"] for r in results], 1)
    lconv_s = np.concatenate([r["lconv_s"].reshape(4, 16, 3, 256) for r in results], 1)
    lh_p = np.stack([r["lh_p"].reshape(4, 256) for r in results], 1)
    lh_s = cat("lh_s", 1)
    mk_p = np.stack([r["mk_p"].reshape(4, 256, 4, 256) for r in results], 1)
    mv_p = np.stack([r["mv_p"].reshape(4, 256, 4, 256) for r in results], 1)
    outs = (y_p, y_s, delta_p, delta_s, dconv_p, dconv_s, conf_p, conf_s, pool_p, pool_s, lconv_p, lconv_s, lh_p, lh_s, mk_p, mv_p)
    return tuple(np.ascontiguousarray(o.astype(np.float32)) for o in outs)


def kernel(**inputs):
    nc = build()
    in_maps = [make_in_map(inputs, c) for c in range(8)]
    res = run_bass_kernel_spmd(nc, in_maps, core_ids=list(range(8)))
    return gather(res.results)
```
